# Optimizing a Trainium2 kernel written in Bass

```python
import jax
import jax.numpy as jnp
from jax import lax
import numpy as np

D_MODEL = 1024
BATCH = 4
SEQ = 4096
DEPTH = 4

GRID_W = 64
CTX_LEN = 256
EPS = 1e-6

N_GROUPS = 4
GROUP_W = D_MODEL // N_GROUPS
HEAD_DIM = 64
N_HEADS = GROUP_W // HEAD_DIM

NA_ROWS = 8
NA_COLS = 16

MLA_Q_RANK = (3 * D_MODEL) // 16
MLA_KV_RANK = D_MODEL // 8
MLA_NOPE = HEAD_DIM
MLA_ROPE = HEAD_DIM // 2
MLA_V = HEAD_DIM
ROPE_THETA = 10000.0
Q_BLOCK = 128

RW_DECAY_RANK = 64
RW_AAA_RANK = 64
RW_GATE_RANK = 128
RW_GN_EPS = 64e-5

GLA_DK = HEAD_DIM // 2
GLA_DV = HEAD_DIM
GLA_GATE_RANK = 16
GLA_TAU = 16.0
GLA_CHUNK = 64

D_FF = 2816
CONV_W = 3

NA_SIZES = (GROUP_W, GROUP_W, GROUP_W)
MLA_SIZES = (MLA_Q_RANK, MLA_KV_RANK, MLA_ROPE)
RW_SIZES = (GROUP_W, GROUP_W, GROUP_W, RW_DECAY_RANK, RW_AAA_RANK, RW_GATE_RANK)
GLA_SIZES = (N_HEADS * GLA_DK, N_HEADS * GLA_DK, GROUP_W, GLA_GATE_RANK, GROUP_W)
GROUP_SIZES = (sum(NA_SIZES), sum(MLA_SIZES), sum(RW_SIZES), sum(GLA_SIZES))
D_IN = sum(GROUP_SIZES)

kernel_name = 'hybrid_natten_mla_rwkv7_gla_dit'

F32 = jnp.float32


def split_sizes(z, sizes):
    idx = [int(s) for s in np.cumsum(sizes)[:-1]]
    return jnp.split(z, idx, axis=-1)


def rmsnorm(x, g):
    x32 = x.astype(F32)
    y = x32 * lax.rsqrt(jnp.mean(x32 * x32, axis=-1, keepdims=True) + EPS)
    return (y * g.astype(F32)).astype(x.dtype)


def heads(t):
    B, T, F = t.shape
    return t.reshape(B, T, N_HEADS, F // N_HEADS).transpose(0, 2, 1, 3)


def merge_heads(t):
    B, H, T, d = t.shape
    return t.transpose(0, 2, 1, 3).reshape(B, T, H * d)


def centred_shift(z):
    zp = jnp.pad(z, ((0, 0), (1, 1), (0, 0)))
    return 0.5 * (zp[:, :-2] + zp[:, 2:])


def dwconv_centred(z, w, b):
    pad = CONV_W // 2
    zp = jnp.pad(z, ((0, 0), (pad, pad), (0, 0)))
    T = z.shape[1]
    out = b
    for tap in range(CONV_W):
        out = out + zp[:, tap:tap + T] * w[tap]
    return out


def rotate(x, pos):
    d = x.shape[-1]
    inv = ROPE_THETA ** (-jnp.arange(0, d, 2, dtype=F32) / d)
    ang = pos[:, None] * inv[None, :]
    cos = jnp.cos(ang)[None, :, None, :].astype(x.dtype)
    sin = jnp.sin(ang)[None, :, None, :].astype(x.dtype)
    x1, x2 = jnp.split(x, 2, axis=-1)
    return jnp.concatenate([x1 * cos - x2 * sin, x1 * sin + x2 * cos], axis=-1)


def axial_rope(x, row, col):
    xr, xc = jnp.split(x, 2, axis=-1)
    return jnp.concatenate([rotate(xr, row), rotate(xc, col)], axis=-1)


def softmax_attend(q, k, v):
    logits = jnp.einsum('bhqd,bhkd->bhqk', q, k, preferred_element_type=F32) * (q.shape[-1] ** -0.5)
    p = jax.nn.softmax(logits, axis=-1).astype(v.dtype)
    return jnp.einsum('bhqk,bhkd->bhqd', p, v)


def blockwise_attend(q, k, v):
    B, H, T, d = q.shape
    nb = T // Q_BLOCK
    qb = jnp.moveaxis(q.reshape(B, H, nb, Q_BLOCK, d), 2, 0)
    out = lax.map(lambda qq: softmax_attend(qq, k, v), qb)
    return jnp.moveaxis(out, 0, 2).reshape(B, H, T, v.shape[-1])


def natten_group(z_lat, z_ctx, rpb, need_ctx):
    B, T, _ = z_lat.shape
    rows = T // GRID_W
    kr = min(NA_ROWS, rows)
    q_c, k_c, v_c = [heads(t) for t in split_sizes(z_ctx, NA_SIZES)]

    def grid(t):
        return heads(t).reshape(B, N_HEADS, rows, GRID_W, HEAD_DIM)

    q_g, k_g, v_g = [grid(t) for t in split_sizes(z_lat, NA_SIZES)]
    r = jnp.arange(rows)
    row_idx = jnp.clip(r - kr // 2, 0, rows - kr)[:, None] + jnp.arange(kr)[None, :]
    j = jnp.arange(GRID_W)
    col_start = jnp.clip(j - NA_COLS // 2, 0, GRID_W - NA_COLS)
    col_in = (j[None, :] >= col_start[:, None]) & (j[None, :] < col_start[:, None] + NA_COLS)
    row_off = row_idx - r[:, None] + (NA_ROWS - 1)
    col_off = jnp.clip(j[None, :] - j[:, None], -(NA_COLS - 1), NA_COLS - 1) + (NA_COLS - 1)
    bias = rpb.astype(F32)[:, row_off][..., col_off]
    bias = jnp.where(col_in[:, None, :], bias.transpose(0, 1, 3, 2, 4), -jnp.inf)
    k_band = k_g[:, :, row_idx]
    v_band = v_g[:, :, row_idx]
    scale = HEAD_DIM ** -0.5
    l_win = jnp.einsum('bhrqd,bhrnkd->bhrqnk', q_g, k_band, preferred_element_type=F32) * scale + bias[None]
    l_ctx = jnp.einsum('bhrqd,bhld->bhrql', q_g, k_c, preferred_element_type=F32) * scale
    nwin = kr * GRID_W
    logits = jnp.concatenate([l_win.reshape(B, N_HEADS, rows, GRID_W, nwin), l_ctx], axis=-1)
    p = jax.nn.softmax(logits, axis=-1).astype(v_g.dtype)
    p_win = p[..., :nwin].reshape(B, N_HEADS, rows, GRID_W, kr, GRID_W)
    out = (jnp.einsum('bhrqnk,bhrnkd->bhrqd', p_win, v_band)
           + jnp.einsum('bhrql,bhld->bhrqd', p[..., nwin:], v_c))
    y_lat = merge_heads(out.reshape(B, N_HEADS, T, HEAD_DIM))
    y_ctx = merge_heads(softmax_attend(q_c, k_c, v_c)) if need_ctx else None
    return y_lat, y_ctx


def mla_group(z_lat, z_ctx, row, col, q_norm, w_uq, kv_norm, w_ukv, need_ctx):
    def project(z):
        B, T, _ = z.shape
        cq, ckv, k_rope = split_sizes(z, MLA_SIZES)
        q = (rmsnorm(cq, q_norm) @ w_uq).reshape(B, T, N_HEADS, MLA_NOPE + MLA_ROPE)
        kv = (rmsnorm(ckv, kv_norm) @ w_ukv).reshape(B, T, N_HEADS, MLA_NOPE + MLA_V)
        q_nope, q_rope = jnp.split(q, [MLA_NOPE], axis=-1)
        k_nope, v = jnp.split(kv, [MLA_NOPE], axis=-1)
        return q_nope, q_rope, k_nope, k_rope[:, :, None, :], v

    def assemble(q_nope, q_rope, k_nope, k_rope, v):
        k_rope = jnp.broadcast_to(k_rope, k_nope.shape[:-1] + (MLA_ROPE,))
        q = jnp.concatenate([q_nope, q_rope], axis=-1).transpose(0, 2, 1, 3)
        k = jnp.concatenate([k_nope, k_rope], axis=-1).transpose(0, 2, 1, 3)
        return q, k, v.transpose(0, 2, 1, 3)

    q_c, k_c, v_c = assemble(*project(z_ctx))
    qn, qr, kn, krp, vl = project(z_lat)
    q_l, k_l, v_l = assemble(qn, axial_rope(qr, row, col), kn, axial_rope(krp, row, col), vl)
    y_lat = blockwise_attend(q_l, jnp.concatenate([k_l, k_c], axis=2), jnp.concatenate([v_l, v_c], axis=2))
    y_ctx = merge_heads(softmax_attend(q_c, k_c, v_c)) if need_ctx else None
    return merge_heads(y_lat), y_ctx


def rwkv_scan(S0, r, w, k, v, a, b):
    def step(S, inp):
        rt, wt, kt, vt, at, bt = inp
        sa = jnp.einsum('bhvk,bhk->bhv', S, at)
        S = S * wt[:, :, None, :] + sa[..., None] * bt[:, :, None, :] + vt[..., None] * kt[:, :, None, :]
        return S, jnp.einsum('bhvk,bhk->bhv', S, rt)

    xs = tuple(jnp.moveaxis(t.astype(F32), 1, 0) for t in (r, w, k, v, a, b))
    S, y = lax.scan(step, S0, xs)
    return jnp.moveaxis(y, 0, 1), S


def rwkv_group(z_lat, z_ctx, mu, w0, w_up, a0, a_up, g_up, k_k, k_a, r_k, ln_w, ln_b, need_ctx):
    def prep(z):
        B, T, _ = z.shape
        z = z + mu * (centred_shift(z) - z)
        r, k, v, wd, ad, gd = split_sizes(z, RW_SIZES)

        def hd(t):
            return t.reshape(B, T, N_HEADS, HEAD_DIM)

        kk = hd(k * k_k).astype(F32)
        kk = kk * lax.rsqrt(jnp.sum(kk * kk, axis=-1, keepdims=True) + 1e-12)
        per_dir = []
        for d in range(2):
            w_raw = -jax.nn.softplus(-(w0[d] + jnp.tanh(wd) @ w_up[d]).astype(F32)) - 0.5
            decay = jnp.exp(-jnp.exp(w_raw))
            a = jax.nn.sigmoid((a0[d] + ad @ a_up[d]).astype(F32))
            k_d = k.astype(F32) * (1.0 + (a - 1.0) * k_a.astype(F32))
            per_dir.append((hd(decay), hd(k_d), hd(a)))
        g = jax.nn.sigmoid(gd) @ g_up
        return hd(r).astype(F32), hd(v).astype(F32), kk, per_dir, g

    def run(p, d, S0):
        r, v, kk, per_dir, _ = p
        decay, k_d, a = per_dir[d]
        seq = (r, decay, k_d, v, -kk, kk * a)
        if d == 1:
            seq = tuple(jnp.flip(t, 1) for t in seq)
        y, S = rwkv_scan(S0, *seq)
        return (jnp.flip(y, 1) if d == 1 else y), S

    def finish(p, ys, dtype):
        r, v, kk, per_dir, g = p
        y = ys[0] + ys[1]
        mean = jnp.mean(y, axis=-1, keepdims=True)
        var = jnp.mean(jnp.square(y - mean), axis=-1, keepdims=True)
        y = ((y - mean) * lax.rsqrt(var + RW_GN_EPS) * ln_w.astype(F32).reshape(N_HEADS, HEAD_DIM)
             + ln_b.astype(F32).reshape(N_HEADS, HEAD_DIM))
        for (_, k_d, _) in per_dir:
            y = y + jnp.sum(r * k_d * r_k.astype(F32), axis=-1, keepdims=True) * v
        B, T = y.shape[:2]
        return y.reshape(B, T, GROUP_W).astype(dtype) * g

    pc = prep(z_ctx)
    pl = prep(z_lat)
    S0 = jnp.zeros((z_lat.shape[0], N_HEADS, HEAD_DIM, HEAD_DIM), F32)
    yl, yc = [], []
    for d in range(2):
        y_c, S_c = run(pc, d, S0)
        y_l, _ = run(pl, d, S_c)
        yc.append(y_c)
        yl.append(y_l)
    y_lat = finish(pl, yl, z_lat.dtype)
    y_ctx = finish(pc, yc, z_ctx.dtype) if need_ctx else None
    return y_lat, y_ctx


def gla_chunked(S0, q, k, v, log_a):
    B, H, T, _ = q.shape
    dv = v.shape[-1]
    nc = T // GLA_CHUNK

    def ch(t):
        return t.astype(F32).reshape(B, H, nc, GLA_CHUNK, t.shape[-1])

    q, k, v, log_a = ch(q), ch(k), ch(v), ch(log_a)
    b = jnp.cumsum(log_a, axis=3)
    b_last = b[:, :, :, -1:, :]
    q_e = q * jnp.exp(b)
    k_e = k * jnp.exp(-b)
    k_s = k * jnp.exp(b_last - b)
    causal = jnp.tril(jnp.ones((GLA_CHUNK, GLA_CHUNK), dtype=bool))
    A = jnp.where(causal, jnp.einsum('bhnid,bhnjd->bhnij', q_e, k_e), 0.0)
    o_intra = jnp.einsum('bhnij,bhnjv->bhniv', A, v)
    chunk_kv = jnp.einsum('bhnjd,bhnjv->bhndv', k_s, v)
    chunk_decay = jnp.exp(b_last[:, :, :, 0, :])

    def step(S, inp):
        dec, kv = inp
        return dec[..., None] * S + kv, S

    S_fin, S_prev = lax.scan(step, S0, (jnp.moveaxis(chunk_decay, 2, 0), jnp.moveaxis(chunk_kv, 2, 0)))
    o_inter = jnp.einsum('bhnid,nbhdv->bhniv', q_e, S_prev)
    return (o_intra + o_inter).reshape(B, H, T, dv), S_fin


def gla_group(z_lat, z_ctx, gate_up, gate_b, norm_g, need_ctx):
    def prep(z):
        q, k, v, gd, og = split_sizes(z, GLA_SIZES)
        log_a = [heads(jax.nn.log_sigmoid((gd @ gate_up[d] + gate_b[d]).astype(F32)) / GLA_TAU)
                 for d in range(2)]
        return heads(q) * (GLA_DK ** -0.5), heads(k), heads(v), log_a, og

    def run(p, d, S0):
        q, k, v, log_a, _ = p
        seq = (q, k, v, log_a[d])
        if d == 1:
            seq = tuple(jnp.flip(t, 2) for t in seq)
        o, S = gla_chunked(S0, *seq)
        return (jnp.flip(o, 2) if d == 1 else o), S

    def finish(p, os, dtype):
        o = (os[0] + os[1]).transpose(0, 2, 1, 3)
        o = rmsnorm(o, norm_g.reshape(N_HEADS, GLA_DV))
        B, T = o.shape[:2]
        return o.reshape(B, T, GROUP_W).astype(dtype) * jax.nn.silu(p[4])

    pc = prep(z_ctx)
    pl = prep(z_lat)
    S0 = jnp.zeros((z_lat.shape[0], N_HEADS, GLA_DK, GLA_DV), F32)
    ol, oc = [], []
    for d in range(2):
        o_c, S_c = run(pc, d, S0)
        o_l, _ = run(pl, d, S_c)
        oc.append(o_c)
        ol.append(o_l)
    y_lat = finish(pl, ol, z_lat.dtype)
    y_ctx = finish(pc, oc, z_ctx.dtype) if need_ctx else None
    return y_lat, y_ctx


def token_mixers(hl, hc, row, col, need_ctx, w_in, w_out, na_rpb, mla_q_norm, mla_w_uq, mla_kv_norm,
                 mla_w_ukv, rw_mu, rw_w0, rw_w_up, rw_a0, rw_a_up, rw_g_up, rw_k_k, rw_k_a, rw_r_k,
                 rw_ln_w, rw_ln_b, gla_gate_up, gla_gate_b, gla_norm):
    zl = split_sizes(hl @ w_in, GROUP_SIZES)
    zc = split_sizes(hc @ w_in, GROUP_SIZES)
    outs = [
        natten_group(zl[0], zc[0], na_rpb, need_ctx),
        mla_group(zl[1], zc[1], row, col, mla_q_norm, mla_w_uq, mla_kv_norm, mla_w_ukv, need_ctx),
        rwkv_group(zl[2], zc[2], rw_mu, rw_w0, rw_w_up, rw_a0, rw_a_up, rw_g_up, rw_k_k, rw_k_a,
                   rw_r_k, rw_ln_w, rw_ln_b, need_ctx),
        gla_group(zl[3], zc[3], gla_gate_up, gla_gate_b, gla_norm, need_ctx),
    ]
    y_lat = jnp.concatenate([o[0] for o in outs], axis=-1) @ w_out
    y_ctx = (jnp.concatenate([o[1] for o in outs], axis=-1) @ w_out) if need_ctx else None
    return y_lat, y_ctx


def conv_ffn(h, w_up, conv_w, conv_b, w_down):
    u = dwconv_centred(h @ w_up, conv_w, conv_b)
    val, gate = jnp.split(u, 2, axis=-1)
    return (jax.nn.silu(gate) * val) @ w_down


def modulation(cvec, w_mod, b_mod):
    return jnp.split(jax.nn.silu(cvec) @ w_mod + b_mod, 6, axis=-1)


def setup_inputs(seed: int = 0) -> dict:
    key = jax.random.key(seed)
    ks = iter(jax.random.split(key, 48))
    L = DEPTH

    def nrm(shape, scale):
        return scale * jax.random.normal(next(ks), shape, F32)

    return {
        'x': nrm((BATCH, SEQ, D_MODEL), 1.0),
        'c': nrm((BATCH, D_MODEL), 1.0),
        'ctx': nrm((BATCH, CTX_LEN, D_MODEL), 1.0),
        'c_ctx': nrm((D_MODEL,), 1.0),
        'w_mod': nrm((L, D_MODEL, 6 * D_MODEL), 0.5 * D_MODEL ** -0.5),
        'b_mod': nrm((L, 6 * D_MODEL), 0.02),
        'g_mix_pre': 1.0 + nrm((L, D_MODEL), 0.05),
        'g_mix_post': 1.0 + nrm((L, D_MODEL), 0.05),
        'g_ffn_pre': 1.0 + nrm((L, D_MODEL), 0.05),
        'g_ffn_post': 1.0 + nrm((L, D_MODEL), 0.05),
        'w_in': nrm((L, D_MODEL, D_IN), D_MODEL ** -0.5),
        'w_out': nrm((L, D_MODEL, D_MODEL), D_MODEL ** -0.5),
        'na_rpb': nrm((L, N_HEADS, 2 * NA_ROWS - 1, 2 * NA_COLS - 1), 0.1),
        'mla_q_norm': 1.0 + nrm((L, MLA_Q_RANK), 0.05),
        'mla_w_uq': nrm((L, MLA_Q_RANK, N_HEADS * (MLA_NOPE + MLA_ROPE)), MLA_Q_RANK ** -0.5),
        'mla_kv_norm': 1.0 + nrm((L, MLA_KV_RANK), 0.05),
        'mla_w_ukv': nrm((L, MLA_KV_RANK, N_HEADS * (MLA_NOPE + MLA_V)), MLA_KV_RANK ** -0.5),
        'rw_mu': jax.random.uniform(next(ks), (L, sum(RW_SIZES)), F32),
        'rw_w0': nrm((L, 2, GROUP_W), 0.5),
        'rw_w_up': nrm((L, 2, RW_DECAY_RANK, GROUP_W), 0.5 * RW_DECAY_RANK ** -0.5),
        'rw_a0': nrm((L, 2, GROUP_W), 0.5),
        'rw_a_up': nrm((L, 2, RW_AAA_RANK, GROUP_W), RW_AAA_RANK ** -0.5),
        'rw_g_up': nrm((L, RW_GATE_RANK, GROUP_W), RW_GATE_RANK ** -0.5),
        'rw_k_k': 0.85 + nrm((L, GROUP_W), 0.05),
        'rw_k_a': 1.0 + nrm((L, GROUP_W), 0.05),
        'rw_r_k': nrm((L, N_HEADS, HEAD_DIM), 0.1),
        'rw_ln_w': 1.0 + nrm((L, GROUP_W), 0.05),
        'rw_ln_b': nrm((L, GROUP_W), 0.02),
        'gla_gate_up': nrm((L, 2, GLA_GATE_RANK, N_HEADS * GLA_DK), GLA_GATE_RANK ** -0.5),
        'gla_gate_b': nrm((L, 2, N_HEADS * GLA_DK), 0.5),
        'gla_norm': 1.0 + nrm((L, GROUP_W), 0.05),
        'ffn_w_up': nrm((L, D_MODEL, 2 * D_FF), D_MODEL ** -0.5),
        'ffn_conv_w': nrm((L, CONV_W, 2 * D_FF), 0.3) + jnp.array([0.0, 1.0, 0.0], F32)[None, :, None],
        'ffn_conv_b': nrm((L, 2 * D_FF), 0.02),
        'ffn_w_down': nrm((L, D_FF, D_MODEL), D_FF ** -0.5),
    }


def reference(x, c, ctx, c_ctx, w_mod, b_mod, g_mix_pre, g_mix_post, g_ffn_pre, g_ffn_post, w_in, w_out,
              na_rpb, mla_q_norm, mla_w_uq, mla_kv_norm, mla_w_ukv, rw_mu, rw_w0, rw_w_up, rw_a0, rw_a_up,
              rw_g_up, rw_k_k, rw_k_a, rw_r_k, rw_ln_w, rw_ln_b, gla_gate_up, gla_gate_b, gla_norm,
              ffn_w_up, ffn_conv_w, ffn_conv_b, ffn_w_down):
    T = x.shape[1]
    t = jnp.arange(T)
    row = (t // GRID_W).astype(F32)
    col = (t % GRID_W).astype(F32)
    xl, xc = x, ctx
    for i in range(DEPTH):
        need_ctx = i < DEPTH - 1
        ml = [m[:, None, :] for m in modulation(c, w_mod[i], b_mod[i])]
        mc = modulation(c_ctx, w_mod[i], b_mod[i])
        hl = rmsnorm(xl, g_mix_pre[i]) * (1.0 + ml[1]) + ml[0]
        hc = rmsnorm(xc, g_mix_pre[i]) * (1.0 + mc[1]) + mc[0]
        yl, yc = token_mixers(hl, hc, row, col, need_ctx, w_in[i], w_out[i], na_rpb[i], mla_q_norm[i],
                              mla_w_uq[i], mla_kv_norm[i], mla_w_ukv[i], rw_mu[i], rw_w0[i], rw_w_up[i],
                              rw_a0[i], rw_a_up[i], rw_g_up[i], rw_k_k[i], rw_k_a[i], rw_r_k[i], rw_ln_w[i],
                              rw_ln_b[i], gla_gate_up[i], gla_gate_b[i], gla_norm[i])
        xl = xl + ml[2] * rmsnorm(yl, g_mix_post[i])
        hl = rmsnorm(xl, g_ffn_pre[i]) * (1.0 + ml[4]) + ml[3]
        xl = xl + ml[5] * rmsnorm(conv_ffn(hl, ffn_w_up[i], ffn_conv_w[i], ffn_conv_b[i], ffn_w_down[i]),
                                  g_ffn_post[i])
        if need_ctx:
            xc = xc + mc[2] * rmsnorm(yc, g_mix_post[i])
            hc = rmsnorm(xc, g_ffn_pre[i]) * (1.0 + mc[4]) + mc[3]
            xc = xc + mc[5] * rmsnorm(conv_ffn(hc, ffn_w_up[i], ffn_conv_w[i], ffn_conv_b[i], ffn_w_down[i]),
                                      g_ffn_post[i])
    return xl
```

```python
import contextlib
import numpy as np
import concourse.bass as bass
import concourse.mybir as mybir
from concourse.bass_utils import run_bass_kernel_spmd

F32 = mybir.dt.float32
BF16 = mybir.dt.bfloat16
ALU = mybir.AluOpType
AF = mybir.ActivationFunctionType
AX = mybir.AxisListType

D = 1024
NL_FULL = 4
TCTX = 256
TLAT = 4096
TALL = TCTX + TLAT
NT = TALL // 128
DIN = 2928
DFF = 2816
EPS = 1e-6


class Res:
    __slots__ = ("name", "w", "r")

    def __init__(self, name=""):
        self.name = name
        self.w = None
        self.r = {}


class T:
    def __init__(self, handle, name):
        self.h = handle
        self.res = Res(name)

    def __getitem__(self, k):
        return self.h[k]


class Prog:
    EPOCH = 30000
    KD = 6

    def __init__(self, nc, st):
        self.nc = nc
        self.st = st
        self.engs = ["pe", "act", "dve", "pool", "sp"]
        self.ops = {e: [] for e in self.engs}
        self.cnt = {e: 0 for e in self.engs}
        self.seen = {e: {} for e in self.engs}
        self.dcnt = {e: 0 for e in self.engs}
        self.sems = {}
        self.n_ops = 0
        self.gst = st
        self.last_ev = {}
        self.pending = {e: [] for e in self.engs}

    def sb(self, name, shape, dt=F32):
        self.n_ops += 1
        h = self.st.enter_context(self.nc.sbuf_tensor("sb%d_%s" % (self.n_ops, name), list(shape), dt))
        return T(h, name)

    def ps(self, name, shape=(128, 512), dt=F32):
        h = self.st.enter_context(self.nc.psum_tensor("ps_" + name, list(shape), dt))
        return T(h, name)

    def _sem(self, key):
        if key not in self.sems:
            self.sems[key] = self.gst.enter_context(self.nc.semaphore("s_" + "_".join(str(k) for k in key)))
        self.last_ev[key] = max(self.last_ev.get(key, 0), 0)
        return self.sems[key]

    def barrier(self):
        evs = [(k, v) for k, v in self.last_ev.items() if v > 0]
        for e in self.engs:
            self.pending[e] = list(evs)

    def barrier_light(self, ress):
        evs = [r.w for r in ress if r.w is not None]
        for e in self.engs:
            self.pending[e] = self.pending[e] + list(evs)

    @contextlib.contextmanager
    def phase(self):
        outer = self.st
        with contextlib.ExitStack() as ph:
            self.st = ph
            yield
            self.barrier()
            self.flush()
        self.st = outer

    def _deps(self, eng, reads, writes, is_dma):
        evs = []
        for ev in self.pending[eng]:
            evs.append((ev, "bar"))
        self.pending[eng] = []
        for t in reads:
            r = t.res if isinstance(t, T) else t
            if r.w is not None:
                evs.append((r.w, "raw"))
        for t in writes:
            r = t.res if isinstance(t, T) else t
            if r.w is not None:
                evs.append((r.w, "waw"))
            for ev in r.r.values():
                evs.append((ev, "war"))
        waits = {}
        for (key, val), kind in evs:
            if key[0] == "e" and key[1] == eng and not is_dma:
                if kind != "raw" or eng == "pe":
                    continue
            if self.seen[eng].get(key, 0) >= val:
                continue
            if waits.get(key, 0) < val:
                waits[key] = val
        for k, v in waits.items():
            self.seen[eng][k] = v
        return list(waits.items())

    def _mark(self, ev, reads, writes):
        for t in reads:
            r = t.res if isinstance(t, T) else t
            r.r[ev[0]] = ev
        for t in writes:
            r = t.res if isinstance(t, T) else t
            r.w = ev
            r.r = {}

    def op(self, eng, fn, reads=(), writes=()):
        waits = self._deps(eng, reads, writes, False)
        n = self.cnt[eng]
        key = ("e", eng, n // self.EPOCH)
        ev = (key, n % self.EPOCH + 1)
        self.cnt[eng] = n + 1
        self._sem(key)
        self.last_ev[key] = ev[1]
        self.ops[eng].append((waits, fn, key, 1))
        self._mark(ev, reads, writes)
        self.n_ops += 1

    def dma(self, q, out, in_, reads=(), writes=(), **kw):
        waits = self._deps(q, reads, writes, True)
        j = self.dcnt[q]
        self.dcnt[q] = j + 1
        key = ("d", q, j % self.KD)
        val = 16 * (j // self.KD + 1)
        if j >= self.KD and self.seen[q].get(key, 0) < val - 16:
            waits.append((key, val - 16))
            self.seen[q][key] = val - 16
        self._sem(key)
        self.last_ev[key] = val
        self.ops[q].append((waits, lambda e: e.dma_start(out=out, in_=in_, **kw), key, 16))
        self._mark((key, val), reads, writes)
        self.n_ops += 1

    def mm(self, out, lhsT, rhs, start, stop, reads, writes):
        self.op("pe", lambda e: e.matmul(out, lhsT, rhs, start=start, stop=stop), reads, writes)

    def tr(self, out, in_, ident, reads, writes):
        self.op("pe", lambda e: e.transpose(out, in_, ident), reads, writes)

    def act(self, eng, out, in_, func, reads, writes, bias=None, scale=None, accum_out=None):
        kw = {}
        if bias is not None:
            kw["bias"] = bias
        if scale is not None:
            kw["scale"] = scale
        if accum_out is not None:
            kw["accum_out"] = accum_out
        self.op(eng, lambda e: e.activation(out, in_, func, **kw), reads, writes)

    def tt(self, eng, out, in0, in1, op, reads, writes):
        self.op(eng, lambda e: e.tensor_tensor(out, in0, in1, op), reads, writes)

    def ts(self, eng, out, in0, s1, s2, op0, op1, reads, writes, accum_out=None):
        if op1 is None:
            self.op(eng, lambda e: e.tensor_scalar(out, in0, s1, None, op0), reads, writes)
        elif accum_out is None:
            self.op(eng, lambda e: e.tensor_scalar(out, in0, s1, s2, op0, op1), reads, writes)
        else:
            self.op(eng, lambda e: e.tensor_scalar(out, in0, s1, s2, op0, op1, accum_out), reads, writes)

    def stt(self, eng, out, in0, scalar, in1, op0, op1, reads, writes):
        self.op(eng, lambda e: e.scalar_tensor_tensor(out, in0, scalar, in1, op0, op1), reads, writes)

    def cp(self, eng, out, in_, reads, writes):
        if eng == "act":
            self.op(eng, lambda e: e.copy(out, in_), reads, writes)
        else:
            self.op(eng, lambda e: e.tensor_copy(out, in_), reads, writes)

    def red(self, eng, out, in_, op, reads, writes, axis=AX.X):
        self.op(eng, lambda e: e.tensor_reduce(out, in_, axis, op), reads, writes)

    def rstd(self, t, scale, eps, extra_reads=()):
        self.act("act", t[:, 1:2], t[:, 0:1], AF.Sqrt, [t, self.consts[eps]] + list(extra_reads), [t], bias=self.eps_ap(eps, t), scale=scale)
        self.op("dve", lambda e: e.reciprocal(t[:, 2:3], t[:, 1:2]), [t], [t])

    def eps_ap(self, eps, t):
        return self.consts[eps][0:t.h.shape[0], 0:1]

    def sumsq(self, dst_col, srcs, junk, reads):
        raise NotImplementedError

    def memset(self, eng, ap, val, writes):
        self.op(eng, lambda e: e.memset(ap, val), (), writes)

    def flush(self, final=False):
        nc = self.nc
        fin = [(k, v) for k, v in self.last_ev.items() if v > 0] if final else []
        sems = self.sems
        engmap = {"pe": "tensor", "act": "scalar", "dve": "vector", "pool": "gpsimd", "sp": "sync"}
        ops = self.ops
        with nc.Block() as block:
            def make(ename):
                def body(e):
                    for waits, fn, key, inc in ops[ename]:
                        for wk, wv in waits:
                            e.wait_ge(sems[wk], wv)
                        fn(e).then_inc(sems[key], inc)
                    if ename == "sp":
                        for wk, wv in fin:
                            e.wait_ge(sems[wk], wv)
                return body
            for ename in self.engs:
                getattr(block, engmap[ename])(make(ename))
        self.ops = {e: [] for e in self.engs}


NEG = -30000.0


def emit_na(P, L, need_ctx, dram, z_d, z_res, y_d, y_res, psum, ident):
    with P.phase():
        QT = P.sb("naQT", [128, 2, TALL], BF16)
        KT = P.sb("naKT", [128, 2, TALL], BF16)
        Ve = P.sb("naVe", [128, NT, 4, 66], BF16)
        Vo = P.sb("naVo", [128, NT - 1, 4, 66], BF16)
        btab = P.sb("btab", [128, 4, 14, 64])
        maskt = P.sb("mask", [128, 64])
        zt = [P.sb("zt%d" % i, [128, 768]) for i in range(2)]
        zo = [P.sb("zo%d" % i, [128, 256]) for i in range(2)]
        sw = [P.sb("sw%d" % i, [128, 4, 64]) for i in range(2)]
        pT = [P.sb("pT%d" % i, [128, 6, 64], BF16) for i in range(2)]
        rc = [P.sb("rc%d" % i, [64, 4]) for i in range(2)]
        yo = [P.sb("yo%d" % i, [64, 4, 64]) for i in range(2)]
        import os
        stage = float(os.environ.get("NA_STAGE", "9"))
        if stage < 0.5:
            return
        P.dma("sp", btab[:], dram["na_btab"][L], (), [btab])
        P.dma("sp", maskt[:], dram["na_mask"], (), [maskt])
        P.tt("dve", btab[:].rearrange("p h d q -> p (h d) q"), btab[:].rearrange("p h d q -> p (h d) q"),
             maskt[:, :].unsqueeze(1).to_broadcast([128, 56, 64]), ALU.add, [btab, maskt], [btab])
        if stage < 0.7:
            return
        P.memset("pool", Ve[:, :, :, 64:65], 1.0, [Ve])
        P.memset("pool", Vo[:, :, :, 64:65], 1.0, [Vo])
        if stage < 0.9:
            return
        for i in range(NT):
            b = i % 2
            P.dma("sp", zt[b][:], z_d[i * 128:(i + 1) * 128, 0:768], [z_res[i]], [zt[b]])
            pst = psum[i % 2]
            for c in range(4):
                P.tr(pst[:, c * 128:(c + 1) * 128], zt[b][:, c * 128:(c + 1) * 128], ident[:], [zt[b], ident], [pst])
            pv = pst[:, :].rearrange("p (c t) -> p c t", c=4)
            if stage > 0.915:
                P.cp("act", QT[:, :, i * 128:(i + 1) * 128], pv[:, 0:2, :], [pst], [QT])
            if stage > 0.925:
                P.cp("act", KT[:, :, i * 128:(i + 1) * 128], pv[:, 2:4, :], [pst], [KT])
            if stage > 0.935:
                P.cp("dve", Ve[:, i, :, 0:64], zt[b][:, 512:768].rearrange("p (h d) -> p h d", h=4), [zt[b]], [Ve])
            if i < NT - 1 and stage > 0.945:
                P.dma("pool", zo[b][:], z_d[64 + i * 128:64 + (i + 1) * 128, 512:768], [z_res[i], z_res[i + 1]], [zo[b]])
                if stage > 0.955:
                    P.cp("dve", Vo[:, i, :, 0:64], zo[b][:].rearrange("p (h d) -> p h d", h=4), [zo[b]], [Vo])

        cnt = [0]

        def na_rows(q0, keyspec, bias_di0, out_row0):
            n = cnt[0]
            cnt[0] += 1
            po = psum[4 + n % 2]
            nk = len(keyspec)
            for h in range(4):
                g, hp = h // 2, (h % 2) * 64
                pst = psum[(n * 4 + h) % 4]
                for c, (k0, Vt, ti) in enumerate(keyspec):
                    P.mm(pst[:, c * 64:(c + 1) * 64], KT[hp:hp + 64, g, k0:k0 + 128], QT[hp:hp + 64, g, q0:q0 + 64], True, True,
                         [KT, QT], [pst])
                p = pT[(n * 4 + h) % 2]
                c0 = 0
                if bias_di0 is not None:
                    s = sw[(n * 4 + h) % 2]
                    P.stt("dve", s[:], pst[:, 0:256].rearrange("p (c q) -> p c q", c=4), 0.125,
                          btab[:, h, bias_di0:bias_di0 + 7:2, :], ALU.mult, ALU.add, [pst, btab], [s])
                    P.act("act", p[:, 0:4, :], s[:], AF.Exp, [s], [p])
                    c0 = 4
                P.act("act", p[:, c0:nk, :], pst[:, c0 * 64:nk * 64].rearrange("p (c q) -> p c q", c=nk - c0), AF.Exp, [pst], [p],
                      scale=0.125)
                for c, (k0, Vt, ti) in enumerate(keyspec):
                    P.mm(po[0:64, h * 65:(h + 1) * 65], p[:, c, :], Vt[:, ti, h, 0:65], c == 0, c == nk - 1, [p, Vt], [po])
            r_ = rc[n % 2]
            y_ = yo[n % 2]
            pov = po[0:64, 0:260].rearrange("p (h e) -> p h e", h=4)
            P.op("dve", lambda e: e.reciprocal(r_[:, :], pov[:, :, 64]), [po], [r_])
            P.tt("dve", y_[:], pov[:, :, 0:64], r_[:, :].unsqueeze(2).to_broadcast([64, 4, 64]), ALU.mult, [po, r_], [y_])
            P.dma("sp", y_d[out_row0:out_row0 + 64, 0:256], y_[:].rearrange("p h d -> p (h d)"), [y_], [y_res[out_row0 // 128]])

        import os
        ctxkeys = [(0, Ve, 0), (128, Ve, 1)]
        if stage < 2:
            return
        if need_ctx:
            for qc in range(4):
                na_rows(qc * 64, ctxkeys, None, qc * 64)
        for r in range(64 if stage >= 3 else 0):
            w0 = min(max(r - 4, 0), 56)
            key0 = TCTX + w0 * 64
            ks = []
            for c in range(4):
                k0 = key0 + c * 128
                if k0 % 128 == 0:
                    ks.append((k0, Ve, k0 // 128))
                else:
                    ks.append((k0, Vo, (k0 - 64) // 128))
            na_rows(TCTX + r * 64, ks + ctxkeys, w0 - r + 7, TCTX + r * 64)


def emit_mla(P, L, need_ctx, dram, z_d, z_res, y_d, y_res, psum, ident):
    SC = 96.0 ** -0.5
    with P.phase():
        nT0 = P.sb("nT0", [128, TALL], BF16)
        nT1 = P.sb("nT1", [64, TALL], BF16)
        nkT = P.sb("nkT", [128, TALL], BF16)
        krT = P.sb("krT", [96, TALL], BF16)
        KTm = P.sb("KTm", [96, 4, TALL], BF16)
        Vm = P.sb("Vm", [128, NT, 4, 66], BF16)
        wst = P.sb("wst", [128, 768])
        wq = P.sb("wq", [128, 2, 4, 96], BF16)
        wqp = P.sb("wqp", [128, 2, 4, 96], BF16)
        wk = P.sb("wk", [128, 4, 96], BF16)
        wv = P.sb("wv", [128, 256], BF16)
        Pm = P.sb("Pm", [96, 96], BF16)
        qnb = P.sb("qnb", [128, 192])
        kvnb = P.sb("kvnb", [128, 128])
        zt = [P.sb("zt%d" % i, [128, 352]) for i in range(2)]
        nq = [P.sb("nq%d" % i, [128, 416]) for i in range(2)]
        stt_ = [P.sb("st%d" % i, [128, 8]) for i in range(2)]
        junk = P.sb("junk", [128, 192])
        ctab = [P.sb("ctab%d" % i, [96, 2, 512]) for i in range(2)]
        t1 = [P.sb("t1_%d" % i, [96, 512]) for i in range(2)]
        t2 = [P.sb("t2_%d" % i, [96, 512]) for i in range(2)]
        QTb = [P.sb("QTb%d" % i, [96, 512], BF16) for i in range(2)]
        pT = [P.sb("pT%d" % i, [128, 512], BF16) for i in range(3)]
        rc = [P.sb("rc%d" % i, [128, 4]) for i in range(2)]
        yo = [P.sb("yo%d" % i, [128, 4, 4, 64]) for i in range(2)]
        oT = [P.sb("oT%d" % i, [65, 512]) for i in range(2)]
        def ldw(dst_ap, dst_t, src, rows, cols):
            P.dma("sp", wst[0:rows, 0:cols], src, (), [wst])
            P.cp("act", dst_ap, wst[0:rows, 0:cols], [wst], [dst_t])
        ldw(wq[:, 0, :, :].rearrange("p h e -> p (h e)"), wq, dram["mla_wq_r"][L, 0:128, :], 128, 384)
        ldw(wq[0:64, 1, :, :].rearrange("p h e -> p (h e)"), wq, dram["mla_wq_r"][L, 128:192, :], 64, 384)
        ldw(wqp[:, 0, :, :].rearrange("p h e -> p (h e)"), wqp, dram["mla_wq_p"][L, 0:128, :], 128, 384)
        ldw(wqp[0:64, 1, :, :].rearrange("p h e -> p (h e)"), wqp, dram["mla_wq_p"][L, 128:192, :], 64, 384)
        ldw(wk[:, :, :].rearrange("p h e -> p (h e)"), wk, dram["mla_wk"][L], 128, 384)
        ldw(wv[:, :], wv, dram["mla_wv"][L], 128, 256)
        ldw(Pm[:, :], Pm, dram["mla_pm"], 96, 96)
        for n_ in nq:
            P.memset("pool", n_[:, 320:384], 0.0, [n_])
        P.dma("sp", qnb[:], dram["mla_q_norm"][L].partition_broadcast(128), (), [qnb])
        P.dma("sp", kvnb[:], dram["mla_kv_norm"][L].partition_broadcast(128), (), [kvnb])
        P.memset("pool", Vm[:, :, :, 64:65], 1.0, [Vm])
        for i in range(NT):
            b = i % 2
            z_, n_, s_ = zt[b], nq[b], stt_[b]
            P.dma("sp", z_[:], z_d[i * 128:(i + 1) * 128, 768:1120], [z_res[i]], [z_])
            P.act("act", junk[:, 0:192], z_[:, 0:192], AF.Square, [z_], [junk], accum_out=s_[:, 0:1])
            P.rstd(s_, 1.0 / 192, EPS, [junk])
            P.stt("dve", n_[:, 0:192], z_[:, 0:192], s_[:, 2:3], qnb[:], ALU.mult, ALU.mult, [z_, s_, qnb], [n_])
            P.act("act", junk[:, 0:128], z_[:, 192:320], AF.Square, [z_], [junk], accum_out=s_[:, 4:5])
            P.act("act", s_[:, 5:6], s_[:, 4:5], AF.Sqrt, [s_, junk, P.consts[EPS]], [s_], bias=P.consts[EPS][:, 0:1], scale=1.0 / 128)
            P.op("dve", lambda e, s_=s_: e.reciprocal(s_[:, 6:7], s_[:, 5:6]), [s_], [s_])
            P.stt("dve", n_[:, 192:320], z_[:, 192:320], s_[:, 6:7], kvnb[:], ALU.mult, ALU.mult, [z_, s_, kvnb], [n_])
            pst = psum[i % 2]
            P.tr(pst[:, 0:128], n_[:, 0:128], ident[:], [n_, ident], [pst])
            P.tr(pst[0:64, 128:256], n_[:, 128:192], ident[:], [n_, ident], [pst])
            P.tr(pst[:, 256:384], n_[:, 192:320], ident[:], [n_, ident], [pst])
            P.cp("pool", n_[:, 384:416], z_[:, 320:352], [z_], [n_])
            P.tr(pst[0:96, 384:512], n_[:, 320:416], ident[:], [n_, ident], [pst])
            ts_ = slice(i * 128, (i + 1) * 128)
            P.cp("act", nT0[:, ts_], pst[:, 0:128], [pst], [nT0])
            P.cp("act", nT1[:, ts_], pst[0:64, 128:256], [pst], [nT1])
            P.cp("act", nkT[:, ts_], pst[:, 256:384], [pst], [nkT])
            P.cp("act", krT[:, ts_], pst[0:96, 384:512], [pst], [krT])
        nblk = [(b0, min(512, TALL - b0)) for b0 in range(0, TALL, 512)]
        for bi, (b0, n) in enumerate(nblk):
            ct = ctab[bi % 2]
            P.dma("sp", ct[64:96, :, 0:n], dram["rope_tab"][:, :, b0:b0 + n].rearrange("c p t -> p c t"), (), [ct])
            for h in range(4):
                pst = psum[h % 2]
                P.mm(pst[0:96, 0:n], wk[:, h, :], nkT[:, b0:b0 + n], True, True, [wk, nkT], [pst])
                P.cp("act", KTm[0:64, h, b0:b0 + n], pst[0:64, 0:n], [pst], [KTm])
            pb = psum[2]
            P.mm(pb[0:96, 0:n], Pm[:, :], krT[:, b0:b0 + n], True, True, [Pm, krT], [pb])
            a1, a2 = t1[bi % 2], t2[bi % 2]
            P.tt("dve", a1[64:96, 0:n], krT[64:96, b0:b0 + n], ct[64:96, 0, 0:n], ALU.mult, [krT, ct], [a1])
            P.tt("dve", a2[64:96, 0:n], pb[64:96, 0:n], ct[64:96, 1, 0:n], ALU.mult, [pb, ct], [a2])
            P.tt("pool", KTm[64:96, :, b0:b0 + n], a1[64:96, 0:n].unsqueeze(1).to_broadcast([32, 4, n]),
                 a2[64:96, 0:n].unsqueeze(1).to_broadcast([32, 4, n]), ALU.add, [a1, a2], [KTm])
            for s in range(n // 128):
                ti = b0 // 128 + s
                pv = psum[3 + s % 2]
                P.mm(pv[:, 0:256], nkT[:, ti * 128:(ti + 1) * 128], wv[:, :], True, True, [nkT, wv], [pv])
                P.cp("act", Vm[:, ti, :, 0:64], pv[:, 0:256].rearrange("p (h d) -> p h d", h=4), [pv], [Vm])
        qblocks = []
        if need_ctx:
            qblocks.append((0, 256, [0, 1]))
        for b0 in range(TCTX, TALL, 512):
            qblocks.append((b0, 512, list(range(NT))))
        cnt = 0
        for qi, (b0, n, ktiles) in enumerate(qblocks):
            ct = ctab[qi % 2]
            P.dma("sp", ct[64:96, :, 0:n], dram["rope_tab"][:, :, b0:b0 + n].rearrange("c p t -> p c t"), (), [ct])
            y_ = yo[qi % 2]
            nsub = n // 128
            for h in range(4):
                pa, pb = psum[0], psum[1]
                P.mm(pa[0:96, 0:n], wq[:, 0, h, :], nT0[:, b0:b0 + n], True, False, [wq, nT0], [pa])
                P.mm(pa[0:96, 0:n], wq[0:64, 1, h, :], nT1[:, b0:b0 + n], False, True, [wq, nT1], [pa])
                P.mm(pb[0:96, 0:n], wqp[:, 0, h, :], nT0[:, b0:b0 + n], True, False, [wqp, nT0], [pb])
                P.mm(pb[0:96, 0:n], wqp[0:64, 1, h, :], nT1[:, b0:b0 + n], False, True, [wqp, nT1], [pb])
                q_ = QTb[(qi * 4 + h) % 2]
                a1, a2 = t1[h % 2], t2[h % 2]
                P.cp("act", q_[0:64, 0:n], pa[0:64, 0:n], [pa], [q_])
                P.tt("dve", a1[64:96, 0:n], pa[64:96, 0:n], ct[64:96, 0, 0:n], ALU.mult, [pa, ct], [a1])
                P.tt("dve", a2[64:96, 0:n], pb[64:96, 0:n], ct[64:96, 1, 0:n], ALU.mult, [pb, ct], [a2])
                P.tt("pool", q_[64:96, 0:n], a1[64:96, 0:n], a2[64:96, 0:n], ALU.add, [a1, a2], [q_])
                poT = psum[5 + (qi * 4 + h) % 2]
                for ki, kt in enumerate(ktiles):
                    pst = psum[2 + cnt % 3]
                    p = pT[cnt % 3]
                    cnt += 1
                    P.mm(pst[:, 0:n], KTm[:, h, kt * 128:(kt + 1) * 128], q_[:, 0:n], True, True, [KTm, q_], [pst])
                    P.act("act", p[:, 0:n], pst[:, 0:n], AF.Exp, [pst], [p], scale=SC)
                    P.mm(poT[0:65, 0:n], Vm[:, kt, h, 0:65], p[:, 0:n], ki == 0, ki == len(ktiles) - 1, [p, Vm], [poT])
                o_ = oT[h % 2]
                P.cp("dve", o_[:, 0:n], poT[0:65, 0:n], [poT], [o_])
                po = psum[7]
                for s in range(nsub):
                    P.tr(po[:, s * 65:(s + 1) * 65], o_[0:65, s * 128:(s + 1) * 128], ident[0:65, 0:65], [o_, ident], [po])
                r_ = rc[h % 2]
                pov = po[:, 0:nsub * 65].rearrange("p (s e) -> p s e", s=nsub)
                P.op("dve", lambda e, r_=r_, pov=pov, nsub=nsub: e.reciprocal(r_[:, 0:nsub], pov[:, :, 64]), [po], [r_])
                P.tt("dve", y_[:, 0:nsub, h, :], pov[:, :, 0:64], r_[:, 0:nsub].unsqueeze(2).to_broadcast([128, nsub, 64]), ALU.mult,
                     [po, r_], [y_])
            for s in range(nsub):
                r0 = b0 + s * 128
                P.dma("sp", y_d[r0:r0 + 128, 256:512], y_[:, s, :, :].rearrange("p h d -> p (h d)"), [y_], [y_res[r0 // 128]])


def emit_mixers(P, L, need_ctx, dram, z_d, z_res, y_d, y_res, psum, ident, dbg):
    if "skip_na" not in dbg:
        emit_na(P, L, need_ctx, dram, z_d, z_res, y_d, y_res, psum, ident)
    if "skip_mla" not in dbg:
        emit_mla(P, L, need_ctx, dram, z_d, z_res, y_d, y_res, psum, ident)
    if "skip_gla" not in dbg:
        emit_gla(P, L, need_ctx, dram, z_d, z_res, y_d, y_res, psum, ident)
    if "skip_rw" not in dbg:
        emit_rwkv(P, L, need_ctx, dram, z_d, z_res, y_d, y_res, psum, ident)


def rope_tables():
    t = np.arange(TLAT)
    row = (t // 64).astype(np.float32)
    col = (t % 64).astype(np.float32)
    inv = (10000.0 ** (-np.arange(0, 16, 2, dtype=np.float32) / 16)).astype(np.float32)
    C = np.ones((32, TALL), np.float32)
    S = np.zeros((32, TALL), np.float32)
    for part, pos in ((0, row), (1, col)):
        ang = (pos[:, None] * inv[None, :]).astype(np.float32)
        c, s = np.cos(ang).T, np.sin(ang).T
        C[part * 16:part * 16 + 8, TCTX:] = c
        C[part * 16 + 8:part * 16 + 16, TCTX:] = c
        S[part * 16:part * 16 + 8, TCTX:] = -s
        S[part * 16 + 8:part * 16 + 16, TCTX:] = s
    return np.stack([C, S], 0).astype(np.float32)


def rope_partner():
    idx = np.arange(32)
    return np.where((idx % 16) < 8, idx + 8, idx - 8)


def mix_host_inputs(inputs, m):
    part = rope_partner()
    m["rope_tab"] = rope_tables()
    m["scan_consts"] = scan_consts()
    for nm, shp in (("rw_mu", (1, 1024)), ("rw_k_k", (1, 256)), ("rw_k_a", (1, 256)), ("rw_w0", (2, 1, 256)), ("rw_a0", (2, 1, 256)),
                    ("rw_w_up", (2, 64, 256)), ("rw_a_up", (2, 64, 256)), ("rw_g_up", (128, 256)), ("rw_ln_w", (1, 256)),
                    ("rw_ln_b", (1, 256)), ("rw_r_k", (1, 256))):
        m[nm] = np.ascontiguousarray(np.asarray(inputs[nm], np.float32).reshape((NL_FULL,) + shp))
    m["gla_gate_up"] = np.asarray(inputs["gla_gate_up"], np.float32)
    m["gla_gate_b"] = np.asarray(inputs["gla_gate_b"], np.float32).reshape(NL_FULL, 2, 1, 128)
    m["gla_norm"] = np.asarray(inputs["gla_norm"], np.float32).reshape(NL_FULL, 1, 256)
    pm = np.zeros((96, 96), np.float32)
    pm[64 + part, 64 + np.arange(32)] = 1.0
    m["mla_pm"] = pm
    wuq = np.asarray(inputs["mla_w_uq"], np.float32).reshape(NL_FULL, 192, 4, 96)
    m["mla_wq_r"] = np.ascontiguousarray(wuq.reshape(NL_FULL, 192, 384))
    wp = np.concatenate([wuq[..., 0:64], wuq[..., 64:96][..., part]], axis=-1)
    m["mla_wq_p"] = np.ascontiguousarray(wp.reshape(NL_FULL, 192, 384))
    wukv = np.asarray(inputs["mla_w_ukv"], np.float32).reshape(NL_FULL, 128, 4, 128)
    wk = np.zeros((NL_FULL, 128, 4, 96), np.float32)
    wk[..., 0:64] = wukv[..., 0:64]
    m["mla_wk"] = np.ascontiguousarray(wk.reshape(NL_FULL, 128, 384))
    m["mla_wv"] = np.ascontiguousarray(wukv[..., 64:128].reshape(NL_FULL, 128, 256))
    m["mla_q_norm"] = np.asarray(inputs["mla_q_norm"], np.float32).reshape(NL_FULL, 1, 192)
    m["mla_kv_norm"] = np.asarray(inputs["mla_kv_norm"], np.float32).reshape(NL_FULL, 1, 128)
    rpb = np.asarray(inputs["na_rpb"], np.float32)
    p = np.arange(128)
    k = p % 64
    q = np.arange(64)
    coff = np.clip(k[:, None] - q[None, :], -15, 15) + 15
    di = np.arange(14)
    roff = di[None, :] + (p[:, None] >= 64)
    m["na_btab"] = np.ascontiguousarray(
        rpb[:, :, roff[:, :, None], coff[:, None, :]].transpose(0, 2, 1, 3, 4))
    cstart = np.clip(q - 8, 0, 48)
    inwin = (k[:, None] >= cstart[None, :]) & (k[:, None] < cstart[None, :] + 16)
    m["na_mask"] = np.where(inwin, 0.0, NEG).astype(np.float32)
    return m


MIX_SPECS = [
    ("rw_mu", (1, 1024), True), ("rw_k_k", (1, 256), True), ("rw_k_a", (1, 256), True), ("rw_w0", (2, 1, 256), True),
    ("rw_a0", (2, 1, 256), True), ("rw_w_up", (2, 64, 256), True), ("rw_a_up", (2, 64, 256), True), ("rw_g_up", (128, 256), True),
    ("rw_ln_w", (1, 256), True), ("rw_ln_b", (1, 256), True), ("rw_r_k", (1, 256), True),
    ("scan_consts", (2, 128, 768), False), ("gla_gate_up", (2, 16, 128), True), ("gla_gate_b", (2, 1, 128), True),
    ("gla_norm", (1, 256), True),
    ("rope_tab", (2, 32, TALL), False), ("mla_pm", (96, 96), False),
    ("mla_wq_r", (192, 384), True), ("mla_wq_p", (192, 384), True), ("mla_wk", (128, 384), True), ("mla_wv", (128, 256), True),
    ("mla_q_norm", (1, 192), True), ("mla_kv_norm", (1, 128), True),
    ("na_btab", (128, 4, 14, 64), True), ("na_mask", (128, 64), False),
]


def scan_consts():
    s = np.arange(128)[:, None]
    t = np.arange(128)[None, :]
    out = np.zeros((2, 128, 768), np.float32)
    for d in range(2):
        incl = (s <= t) if d == 0 else (s >= t)
        strict = (s < t) if d == 0 else (s > t)
        out[d, :, 0:128] = incl
        out[d, :, 128:256] = incl
        out[d, :, 256:384] = strict
        out[d, :, 384:512] = incl
        out[d, :, 512:640] = strict
        out[d, :, 640:768] = strict.T
    return out


def emit_scan(P, dram, psum, ident, pre_d, pre_res, loads, has_ab, yacc):
    ones = P.sb("ones", [128, 1])
    P.memset("pool", ones[:], 1.0, [ones])
    P.memset("pool", yacc[:], 0.0, [yacc])
    yres = [Res("yacc%d" % i) for i in range(NT)]

    def direction(d):
        B = psum[4 * d:4 * d + 4]
        Xin = [P.sb("Xin%d_%d" % (d, i), [128, 6, 256]) for i in range(2)]
        E = P.sb("E%d" % d, [128, 3, 256])
        W = [P.sb("W%d_%d" % (d, i), [128, 4, 256]) for i in range(2)]
        ft = P.sb("FT%d" % d, [128, 2, 4, 128])
        gm = P.sb("Gm%d" % d, [128, 4, 512])
        pCt = [P.sb("pC%d_%d" % (d, i), [128, 2]) for i in range(2)]
        H = P.sb("H%d" % d, [128, 2, 64])
        tmpH = P.sb("tmpH%d" % d, [128, 2, 64])
        cst = P.sb("cst%d" % d, [128, 768])
        P.dma("sp", cst[:], dram["scan_consts"][d], (), [cst])
        if has_ab:
            Xb = [P.sb("Xb%d_%d" % (d, i), [128, 4, 128]) for i in range(2)]
            Yb = [P.sb("Yb%d_%d" % (d, i), [128, 4, 128]) for i in range(2)]
            Pb = [P.sb("Pb%d_%d" % (d, i), [128, 4, 128]) for i in range(2)]
            Xs = P.sb("Xs%d" % d, [128, 256])
            Us = P.sb("Us%d" % d, [128, 256])
        order = list(range(NT)) if d == 0 else [1, 0] + list(range(NT - 1, 1, -1))
        Mc = cst[:, 0:128]
        MASK4 = cst[:, 128:640]
        P.memset("pool", H[:], 0.0, [H])
        yield
        for n, i in enumerate(order):
            b = n % 2
            xi, w_, pc = Xin[b], W[b], pCt[b]
            for (s0, ns, c0) in loads(d):
                P.dma("sp" if (s0 + d) % 2 == 0 else "pool", xi[:, s0:s0 + ns, :].rearrange("p s f -> p (s f)"),
                      pre_d[i * 128:(i + 1) * 128, c0:c0 + ns * 256], [pre_res[i]], [xi])
            pcl = B[0]
            P.mm(pcl[:, 0:256], Mc, xi[:, 3, :], True, True, [cst, xi], [pcl])
            pcp = B[1]
            for g in range(2):
                P.mm(pcp[:, 384 + g:385 + g], xi[:, 3, g * 128:(g + 1) * 128], ones[:, 0:1], True, True, [xi, ones], [pcp])
            yield
            P.act("act", E[:, 0, :], pcl[:, 0:256], AF.Exp, [pcl], [E])
            P.act("act", E[:, 1, :], pcl[:, 0:256], AF.Exp, [pcl], [E], scale=-1.0)
            P.act("act", pc[:, :], pcp[:, 384:386], AF.Exp, [pcp], [pc])
            if has_ab:
                P.tt("dve", E[:, 2, :], pcl[:, 0:256], xi[:, 3, :], ALU.subtract, [pcl, xi], [E])
                P.act("act", E[:, 2, :], E[:, 2, :], AF.Exp, [E], [E])
            yield
            P.tt("dve", w_[:, 0, :], xi[:, 0, :], E[:, 0, :], ALU.mult, [xi, E], [w_])
            P.tt("pool", w_[:, 1, :], xi[:, 1, :], E[:, 1, :], ALU.mult, [xi, E], [w_])
            if has_ab:
                P.tt("dve", w_[:, 2, :], xi[:, 4, :], E[:, 2, :], ALU.mult, [xi, E], [w_])
                P.tt("pool", w_[:, 3, :], xi[:, 5, :], E[:, 1, :], ALU.mult, [xi, E], [w_])
            yield
            slots = [(0, 0), (2, 1), (1, 2), (3, 3)] if has_ab else [(0, 0), (1, 2)]
            for g in range(2):
                pt = B[2 + g]
                for (ws, fs) in slots:
                    P.tr(pt[:, fs * 128:(fs + 1) * 128], w_[:, ws, g * 128:(g + 1) * 128], ident[:], [w_, ident], [pt])
            yield
            for g in range(2):
                pt = B[2 + g]
                if has_ab:
                    P.cp("act", ft[:, g, :, :].rearrange("p f t -> p (f t)"), pt[:, :], [pt], [ft])
                else:
                    P.cp("act", ft[:, g, 0, :], pt[:, 0:128], [pt], [ft])
                    P.cp("act", ft[:, g, 2, :], pt[:, 256:384], [pt], [ft])
            yield
            for h in range(4):
                g, hp = h // 2, (h % 2) * 64
                pg = B[2 + h % 2]
                if has_ab:
                    P.mm(pg[:, 0:256], ft[hp:hp + 64, g, 2, :], ft[hp:hp + 64, g, 0:2, :].rearrange("p f t -> p (f t)"), True, True, [ft], [pg])
                    P.mm(pg[:, 256:512], ft[hp:hp + 64, g, 3, :], ft[hp:hp + 64, g, 0:2, :].rearrange("p f t -> p (f t)"), True, True, [ft], [pg])
                    P.tt("dve", gm[:, h, :], pg[:, :], MASK4, ALU.mult, [pg, cst], [gm])
                else:
                    P.mm(pg[:, 0:128], ft[hp:hp + 64, g, 2, :], ft[hp:hp + 64, g, 0, :], True, True, [ft], [pg])
                    P.tt("dve", gm[:, h, 0:128], pg[:, 0:128], MASK4[:, 0:128], ALU.mult, [pg, cst], [gm])
                if h % 2 == 1:
                    yield
            if has_ab:
                p3 = B[0]
                for h in range(4):
                    P.tr(p3[:, h * 128:(h + 1) * 128], gm[:, h, 384:512], ident[:], [gm, ident], [p3])
                Xc = None
                Yc, Pc = Yb[0], Pb[0]
                yield
                for h in range(4):
                    P.cp("act", Yc[:, h, :], p3[:, h * 128:(h + 1) * 128], [p3], [Yc])
                    P.tt("dve", Pc[:, h, :], gm[:, h, 384:512], ident[:, :], ALU.add, [gm, ident], [Pc])
                yield
                for lv in range(1, 7):
                    Xsrc = (lambda h: gm[:, h, 384:512]) if Xc is None else (lambda h, Xc=Xc: Xc[:, h, :])
                    Xres = gm if Xc is None else Xc
                    Yn, Pn = Yb[lv % 2], Pb[lv % 2]
                    pY = B[3]
                    for h in range(4):
                        P.mm(pY[:, h * 128:(h + 1) * 128], Xsrc(h), Yc[:, h, :], True, True, [Yc, Xres], [pY])
                    if lv < 6:
                        Xn = Xb[lv % 2]
                        pX = B[2]
                        for h in range(4):
                            P.mm(pX[:, h * 128:(h + 1) * 128], Yc[:, h, :], Xsrc(h), True, True, [Yc, Xres], [pX])
                    yield
                    P.cp("dve", Yn[:].rearrange("p h s -> p (h s)"), pY[:, :], [pY], [Yn])
                    if lv < 6:
                        P.cp("act", Xn[:].rearrange("p h s -> p (h s)"), pX[:, :], [pX], [Xn])
                    yield
                    pP = B[1]
                    for h in range(4):
                        P.mm(pP[:, h * 128:(h + 1) * 128], Yn[:, h, :], Pc[:, h, :], True, True, [Yn, Pc], [pP])
                    yield
                    P.tt("dve", Pn[:].rearrange("p h s -> p (h s)"), pP[:, :], Pc[:].rearrange("p h s -> p (h s)"), ALU.add, [pP, Pc], [Pn])
                    yield
                    if lv < 6:
                        Xc = Xn
                    Yc, Pc = Yn, Pn
                pXs = B[0]
                for h in range(4):
                    g, hp = h // 2, (h % 2) * 64
                    P.mm(pXs[:, 256 + h * 64:256 + (h + 1) * 64], ft[hp:hp + 64, g, 1, :], H[hp:hp + 64, g, :], True, False, [ft, H], [pXs])
                    P.mm(pXs[:, 256 + h * 64:256 + (h + 1) * 64], gm[:, h, 128:256], xi[:, 2, h * 64:(h + 1) * 64], False, True, [gm, xi], [pXs])
                yield
                P.cp("act", Xs[:, :], pXs[:, 256:512], [pXs], [Xs])
                yield
                pU = B[2]
                for h in range(4):
                    P.mm(pU[:, h * 64:(h + 1) * 64], Pc[:, h, :], Xs[:, h * 64:(h + 1) * 64], True, True, [Pc, Xs], [pU])
                yield
                P.cp("dve", Us[:, :], pU[:, 0:256], [pU], [Us])
                yield
            pYo = B[1]
            for h in range(4):
                g, hp = h // 2, (h % 2) * 64
                vh = xi[:, 2, h * 64:(h + 1) * 64]
                P.mm(pYo[:, h * 64:(h + 1) * 64], ft[hp:hp + 64, g, 0, :], H[hp:hp + 64, g, :], True, False, [ft, H], [pYo])
                if has_ab:
                    P.mm(pYo[:, h * 64:(h + 1) * 64], gm[:, h, 256:384], Us[:, h * 64:(h + 1) * 64], False, False, [gm, Us], [pYo])
                P.mm(pYo[:, h * 64:(h + 1) * 64], gm[:, h, 0:128], vh, False, True, [gm, xi], [pYo])
            for h in range(4):
                g, hp = h // 2, (h % 2) * 64
                vh = xi[:, 2, h * 64:(h + 1) * 64]
                P.mm(pYo[hp:hp + 64, 256 + g * 64:256 + (g + 1) * 64], w_[:, 1, h * 64:(h + 1) * 64], vh, True, not has_ab, [w_, xi], [pYo])
                if has_ab:
                    P.mm(pYo[hp:hp + 64, 256 + g * 64:256 + (g + 1) * 64], w_[:, 3, h * 64:(h + 1) * 64], Us[:, h * 64:(h + 1) * 64], False, True,
                         [w_, Us], [pYo])
            yield
            P.tt("dve", yacc[:, i, :], pYo[:, 0:256], yacc[:, i, :], ALU.add, [pYo, yres[i]], [yres[i]])
            P.tt("dve", tmpH[:], pYo[:, 256:384].rearrange("p (g v) -> p g v", g=2), H[:], ALU.add, [pYo, H], [tmpH])
            P.tt("dve", H[:], tmpH[:], pc[:, :].unsqueeze(2).to_broadcast([128, 2, 64]), ALU.mult, [tmpH, pc], [H])
            yield

    gens = [direction(0), direction(1)]
    while gens:
        for g_ in list(gens):
            try:
                next(g_)
            except StopIteration:
                gens.remove(g_)
    yacc.res = Res("yacc_done")
    P.barrier_light(yres)


GLA_Z0 = 768 + 352 + 1024


def emit_gla(P, L, need_ctx, dram, z_d, z_res, y_d, y_res, psum, ident):
    pre_d = dram["gla_pre"]
    pre_res = dram.res("gla_pre")
    with P.phase():
        zt = [P.sb("zt%d" % i, [128, 528]) for i in range(2)]
        pre = [P.sb("pre%d" % i, [128, 5, 4, 64]) for i in range(2)]
        gdT = [P.sb("gdT%d" % i, [16, 128]) for i in range(2)]
        gup = P.sb("gup", [16, 2, 128])
        gb = P.sb("gb", [128, 2, 128])
        xg = [P.sb("xg%d" % i, [128, 128]) for i in range(2)]
        for p_ in pre:
            P.memset("pool", p_[:], 0.0, [p_])
        for d in range(2):
            P.dma("sp", gup[:, d, :], dram["gla_gate_up"][L, d], (), [gup])
            P.dma("sp", gb[:, d, :], dram["gla_gate_b"][L, d].partition_broadcast(128), (), [gb])
        for i in range(NT):
            b = i % 2
            z_, p_ = zt[b], pre[b]
            P.dma("sp", z_[:], z_d[i * 128:(i + 1) * 128, GLA_Z0:GLA_Z0 + 528], [z_res[i]], [z_])
            P.op("act", lambda e, z_=z_, p_=p_: e.mul(p_[:, 0, :, 0:32], z_[:, 0:128].rearrange("p (h d) -> p h d", h=4), 32.0 ** -0.5), [z_], [p_])
            P.cp("pool", p_[:, 1, :, 0:32], z_[:, 128:256].rearrange("p (h d) -> p h d", h=4), [z_], [p_])
            P.cp("pool", p_[:, 2, :, :], z_[:, 256:512].rearrange("p (h d) -> p h d", h=4), [z_], [p_])
            pt = psum[i % 2]
            P.tr(pt[0:16, 0:128], z_[:, 512:528], ident[:], [z_, ident], [pt])
            P.cp("act", gdT[b][:, :], pt[0:16, 0:128], [pt], [gdT[b]])
            for d in range(2):
                pl = psum[2 + d]
                x_ = xg[d]
                P.mm(pl[:, 0:128], gdT[b][:, :], gup[:, d, :], True, True, [gdT[b], gup], [pl])
                P.tt("dve", x_[:], pl[:, 0:128], gb[:, d, :], ALU.add, [pl, gb], [x_])
                P.act("act", x_[:], x_[:], AF.Exp, [x_], [x_], scale=-1.0)
                P.act("act", x_[:], x_[:], AF.Ln, [x_], [x_], bias=1.0)
                P.op("act", lambda e, x_=x_, p_=p_, d=d: e.mul(p_[:, 3 + d, :, 0:32], x_[:].rearrange("p (h d) -> p h d", h=4), -1.0 / 16.0), [x_], [p_])
            P.dma("pool", pre_d[i * 128:(i + 1) * 128, :], p_[:].rearrange("p s h d -> p (s h d)"), [p_], [pre_res[i]])
    with P.phase():
        yacc = P.sb("yacc", [128, NT, 256])
        emit_scan(P, dram, psum, ident, pre_d, pre_res, lambda d: [(0, 3, 0), (3, 1, 768 + d * 256)], False, yacc)
        gn = P.sb("gn", [128, 256])
        og = [P.sb("og%d" % i, [128, 256]) for i in range(2)]
        sq = [P.sb("sq%d" % i, [128, 256]) for i in range(2)]
        stg = [P.sb("stg%d" % i, [128, 12]) for i in range(2)]
        P.dma("sp", gn[:], dram["gla_norm"][L].partition_broadcast(128), (), [gn])
        for i in (range(NT) if need_ctx else range(2, NT)):
            b = i % 2
            o_, s_, g_ = og[b], sq[b], stg[b]
            yv = yacc[:, i, :]
            P.dma("sp", o_[:], z_d[i * 128:(i + 1) * 128, DIN - 256:DIN], [z_res[i]], [o_])
            P.act("act", o_[:], o_[:], AF.Silu, [o_], [o_])
            P.tt("pool", s_[:], yv, yv, ALU.mult, [yacc], [s_])
            P.red("dve", g_[:, 0:4], s_[:].rearrange("p (h d) -> p h d", h=4), ALU.add, [s_], [g_])
            P.act("act", g_[:, 4:8], g_[:, 0:4], AF.Sqrt, [g_, P.consts[EPS]], [g_], bias=P.consts[EPS][:, 0:1], scale=1.0 / 64)
            P.op("dve", lambda e, g_=g_: e.reciprocal(g_[:, 8:12], g_[:, 4:8]), [g_], [g_])
            P.tt("dve", s_[:].rearrange("p (h d) -> p h d", h=4), yv.rearrange("p (h d) -> p h d", h=4),
                 g_[:, 8:12].unsqueeze(2).to_broadcast([128, 4, 64]), ALU.mult, [yacc, g_], [s_])
            P.tt("pool", s_[:], s_[:], gn[:], ALU.mult, [s_, gn], [s_])
            P.tt("pool", s_[:], s_[:], o_[:], ALU.mult, [s_, o_], [s_])
            P.dma("sp", y_d[i * 128:(i + 1) * 128, 768:1024], s_[:], [s_], [y_res[i]])


RW_Z0 = 768 + 352


def emit_rwkv(P, L, need_ctx, dram, z_d, z_res, y_d, y_res, psum, ident):
    pre_d = dram["rw_pre"]
    pre_res = dram.res("rw_pre")
    C05 = float(-np.exp(-0.5))
    with P.phase():
        zc = [P.sb("zc%d" % i, [128, 1024]) for i in range(2)]
        zp = [P.sb("zp%d" % i, [128, 1024]) for i in range(2)]
        zn = [P.sb("zn%d" % i, [128, 1024]) for i in range(2)]
        zs = [P.sb("zs%d" % i, [128, 1024]) for i in range(2)]
        pre = [P.sb("pre%d" % i, [128, 10, 256]) for i in range(2)]
        mub = P.sb("mub", [128, 1024])
        kkb = P.sb("kkb", [128, 256])
        kab = P.sb("kab", [128, 256])
        w0b = P.sb("w0b", [128, 2, 256])
        a0b = P.sb("a0b", [128, 2, 256])
        wa = P.sb("wa", [128, 2, 256])
        gup = P.sb("gup", [128, 256])
        TWA = [P.sb("TWA%d" % i, [128, 128]) for i in range(2)]
        TG = [P.sb("TG%d" % i, [128, 128]) for i in range(2)]
        kt_ = [P.sb("kt%d" % i, [128, 256]) for i in range(2)]
        kn_ = [P.sb("kn%d" % i, [128, 256]) for i in range(2)]
        sq_ = P.sb("sq", [128, 256])
        xw = [P.sb("xw%d" % i, [128, 256]) for i in range(2)]
        xa = [P.sb("xa%d" % i, [128, 256]) for i in range(2)]
        st_ = [P.sb("st%d" % i, [128, 12]) for i in range(2)]
        P.dma("sp", mub[:], dram["rw_mu"][L].partition_broadcast(128), (), [mub])
        P.dma("sp", kkb[:], dram["rw_k_k"][L].partition_broadcast(128), (), [kkb])
        P.dma("sp", kab[:], dram["rw_k_a"][L].partition_broadcast(128), (), [kab])
        for d in range(2):
            P.dma("sp", w0b[:, d, :], dram["rw_w0"][L, d].partition_broadcast(128), (), [w0b])
            P.dma("sp", a0b[:, d, :], dram["rw_a0"][L, d].partition_broadcast(128), (), [a0b])
            P.dma("sp", wa[0:64, d, :], dram["rw_w_up"][L, d], (), [wa])
            P.dma("sp", wa[64:128, d, :], dram["rw_a_up"][L, d], (), [wa])
        P.dma("sp", gup[:], dram["rw_g_up"][L], (), [gup])
        for i in range(NT):
            b = i % 2
            c_, p_, n_, s_, pr = zc[b], zp[b], zn[b], zs[b], pre[b]
            t0 = i * 128
            P.dma("sp", c_[:], z_d[t0:t0 + 128, RW_Z0:RW_Z0 + 1024], [z_res[i]], [c_])
            if t0 in (0, TCTX):
                P.memset("pool", p_[:], 0.0, [p_])
                P.dma("pool", p_[1:128, :], z_d[t0:t0 + 127, RW_Z0:RW_Z0 + 1024], [z_res[i]], [p_])
            else:
                P.dma("pool", p_[:], z_d[t0 - 1:t0 + 127, RW_Z0:RW_Z0 + 1024], [z_res[i - 1], z_res[i]], [p_])
            if t0 + 128 in (TCTX, TALL):
                P.memset("pool", n_[:], 0.0, [n_])
                P.dma("sp", n_[0:127, :], z_d[t0 + 1:t0 + 128, RW_Z0:RW_Z0 + 1024], [z_res[i]], [n_])
            else:
                P.dma("sp", n_[:], z_d[t0 + 1:t0 + 129, RW_Z0:RW_Z0 + 1024], [z_res[i], z_res[i + 1]], [n_])
            P.tt("pool", p_[:], p_[:], n_[:], ALU.add, [p_, n_], [p_])
            P.stt("dve", p_[:], p_[:], 0.5, c_[:], ALU.mult, ALU.subtract, [p_, c_], [p_])
            P.tt("pool", p_[:], p_[:], mub[:], ALU.mult, [p_, mub], [p_])
            P.tt("dve", s_[:], p_[:], c_[:], ALU.add, [p_, c_], [s_])
            P.cp("act", pr[:, 0, :], s_[:, 0:256], [s_], [pr])
            P.cp("act", pr[:, 1, :], s_[:, 512:768], [s_], [pr])
            k_, kn, st = kt_[b], kn_[b], st_[b]
            P.tt("dve", k_[:], s_[:, 256:512], kkb[:], ALU.mult, [s_, kkb], [k_])
            P.tt("pool", sq_[:], k_[:], k_[:], ALU.mult, [k_], [sq_])
            P.red("dve", st[:, 0:4], sq_[:].rearrange("p (h d) -> p h d", h=4), ALU.add, [sq_], [st])
            P.act("act", st[:, 4:8], st[:, 0:4], AF.Sqrt, [st, P.consts[1e-12]], [st], bias=P.consts[1e-12][:, 0:1], scale=1.0)
            P.op("dve", lambda e, st=st: e.reciprocal(st[:, 8:12], st[:, 4:8]), [st], [st])
            P.tt("dve", kn[:].rearrange("p (h d) -> p h d", h=4), k_[:].rearrange("p (h d) -> p h d", h=4),
                 st[:, 8:12].unsqueeze(2).to_broadcast([128, 4, 64]), ALU.mult, [k_, st], [kn])
            P.op("act", lambda e, pr=pr, kn=kn: e.mul(pr[:, 2, :], kn[:], -1.0), [kn], [pr])
            pt = psum[i % 2]
            P.tr(pt[:, 0:128], s_[:, 768:896], ident[:], [s_, ident], [pt])
            P.tr(pt[:, 128:256], s_[:, 896:1024], ident[:], [s_, ident], [pt])
            tw, tg = TWA[b], TG[b]
            P.act("act", tw[0:64, :], pt[0:64, 0:128], AF.Tanh, [pt], [tw])
            P.cp("act", tw[64:128, :], pt[64:128, 0:128], [pt], [tw])
            P.act("act", tg[:, :], pt[:, 128:256], AF.Sigmoid, [pt], [tg])
            pgp = psum[6]
            P.mm(pgp[:, 0:256], tg[:, :], gup[:, :], True, True, [tg, gup], [pgp])
            P.cp("act", pr[:, 9, :], pgp[:, 0:256], [pgp], [pr])
            for d in range(2):
                pw, pa = psum[2 + d * 2], psum[3 + d * 2]
                P.mm(pw[:, 0:256], tw[0:64, :], wa[0:64, d, :], True, True, [tw, wa], [pw])
                P.mm(pa[:, 0:256], tw[64:128, :], wa[64:128, d, :], True, True, [tw, wa], [pa])
                w_, a_ = xw[d], xa[d]
                P.tt("dve", w_[:], pw[:, 0:256], w0b[:, d, :], ALU.add, [pw, w0b], [w_])
                P.act("act", w_[:], w_[:], AF.Sigmoid, [w_], [w_])
                P.op("act", lambda e, pr=pr, w_=w_, d=d: e.mul(pr[:, 3 + 3 * d, :], w_[:], C05), [w_], [pr])
                P.tt("dve", a_[:], pa[:, 0:256], a0b[:, d, :], ALU.add, [pa, a0b], [a_])
                P.act("act", a_[:], a_[:], AF.Sigmoid, [a_], [a_])
                P.tt("pool", pr[:, 5 + 3 * d, :], kn[:], a_[:], ALU.mult, [kn, a_], [pr])
                P.stt("dve", a_[:], a_[:], -1.0, kab[:], ALU.add, ALU.mult, [a_, kab], [a_])
                P.stt("dve", pr[:, 4 + 3 * d, :], a_[:], 1.0, s_[:, 256:512], ALU.add, ALU.mult, [a_, s_], [pr])
            P.dma("pool", pre_d[t0:t0 + 128, :], pr[:].rearrange("p s f -> p (s f)"), [pr], [pre_res[i]])
    import os
    if os.environ.get("RW_STAGE", "9") == "1":
        return
    with P.phase():
        yacc = P.sb("yacc", [128, NT, 256])
        emit_scan(P, dram, psum, ident, pre_d, pre_res,
                  lambda d: [(0, 1, 0), (2, 1, 256), (4, 1, 512), (3, 1, 768 + 768 * d), (1, 1, 1024 + 768 * d), (5, 1, 1280 + 768 * d)],
                  True, yacc)
        lnw = P.sb("lnw", [128, 256])
        lnb = P.sb("lnb", [128, 256])
        rkb = P.sb("rkb", [128, 256])
        fin = [P.sb("fin%d" % i, [128, 5, 256]) for i in range(2)]
        yc = [P.sb("yc%d" % i, [128, 256]) for i in range(2)]
        sq = [P.sb("sq%d" % i, [128, 256]) for i in range(2)]
        sf = [P.sb("sf%d" % i, [128, 24]) for i in range(2)]
        P.dma("sp", lnw[:], dram["rw_ln_w"][L].partition_broadcast(128), (), [lnw])
        P.dma("sp", lnb[:], dram["rw_ln_b"][L].partition_broadcast(128), (), [lnb])
        P.dma("sp", rkb[:], dram["rw_r_k"][L].partition_broadcast(128), (), [rkb])
        v4 = lambda ap: ap.rearrange("p (h d) -> p h d", h=4)
        bc4 = lambda ap: ap.unsqueeze(2).to_broadcast([128, 4, 64])
        for i in (range(NT) if need_ctx else range(2, NT)):
            b = i % 2
            f_, y_, s_, t_ = fin[b], yc[b], sq[b], sf[b]
            t0 = i * 128
            P.dma("sp", f_[:, 0:2, :].rearrange("p s f -> p (s f)"), pre_d[t0:t0 + 128, 0:512], [pre_res[i]], [f_])
            P.dma("pool", f_[:, 2, :], pre_d[t0:t0 + 128, 1024:1280], [pre_res[i]], [f_])
            P.dma("pool", f_[:, 3, :], pre_d[t0:t0 + 128, 1792:2048], [pre_res[i]], [f_])
            P.dma("sp", f_[:, 4, :], pre_d[t0:t0 + 128, 2304:2560], [pre_res[i]], [f_])
            yv = yacc[:, i, :]
            P.red("dve", t_[:, 0:4], v4(yv), ALU.add, [yacc], [t_])
            P.op("act", lambda e, t_=t_: e.mul(t_[:, 4:8], t_[:, 0:4], -1.0 / 64), [t_], [t_])
            P.tt("dve", v4(y_[:]), v4(yv), bc4(t_[:, 4:8]), ALU.add, [yacc, t_], [y_])
            P.tt("pool", s_[:], y_[:], y_[:], ALU.mult, [y_], [s_])
            P.red("dve", t_[:, 8:12], v4(s_[:]), ALU.add, [s_], [t_])
            P.act("act", t_[:, 12:16], t_[:, 8:12], AF.Sqrt, [t_, P.consts[64e-5]], [t_], bias=P.consts[64e-5][:, 0:1], scale=1.0 / 64)
            P.op("dve", lambda e, t_=t_: e.reciprocal(t_[:, 16:20], t_[:, 12:16]), [t_], [t_])
            P.tt("dve", v4(y_[:]), v4(y_[:]), bc4(t_[:, 16:20]), ALU.mult, [y_, t_], [y_])
            P.tt("pool", y_[:], y_[:], lnw[:], ALU.mult, [y_, lnw], [y_])
            P.tt("pool", y_[:], y_[:], lnb[:], ALU.add, [y_, lnb], [y_])
            P.tt("pool", s_[:], f_[:, 2, :], f_[:, 3, :], ALU.add, [f_], [s_])
            P.tt("dve", s_[:], s_[:], f_[:, 0, :], ALU.mult, [s_, f_], [s_])
            P.tt("pool", s_[:], s_[:], rkb[:], ALU.mult, [s_, rkb], [s_])
            P.red("dve", t_[:, 20:24], v4(s_[:]), ALU.add, [s_], [t_])
            P.tt("dve", v4(s_[:]), v4(f_[:, 1, :]), bc4(t_[:, 20:24]), ALU.mult, [f_, t_, s_], [s_])
            P.tt("pool", y_[:], y_[:], s_[:], ALU.add, [y_, s_], [y_])
            P.tt("pool", y_[:], y_[:], f_[:, 4, :], ALU.mult, [y_, f_], [y_])
            P.dma("sp", y_d[t0:t0 + 128, 512:768], y_[:], [y_], [y_res[i]])


N_CORES = 8
SCR_SPECS = {"gla_pre": 1280, "rw_pre": 2560}
W_SPECS = [
    ("w_mod", (D, 6 * D)), ("b_mod", (1, 6 * D)),
    ("g_mix_pre", (1, D)), ("g_mix_post", (1, D)), ("g_ffn_pre", (1, D)), ("g_ffn_post", (1, D)),
    ("w_in", (D, DIN)), ("w_out", (D, D)),
    ("ffn_w_up", (D, 2 * DFF)), ("ffn_conv_w", (3, 2 * DFF)), ("ffn_conv_b", (1, 2 * DFF)), ("ffn_w_down", (DFF, D)),
]


def build_program(NL=NL_FULL, dbg=None):
    dbg = dbg or {}
    NLW = dbg.get("nlw", NL_FULL)
    nc = bass.Bass("TRN2", target_bir_lowering=False)
    dram = {}

    def dtens(name, shape, dt=F32, kind="Internal"):
        return nc.dram_tensor(name, list(shape), dt, kind=kind).ap()

    def kind_of(tag, default="Internal"):
        if tag + "_in" in dbg:
            return "ExternalInput"
        if tag in dbg:
            return "ExternalOutput"
        return default

    dram["xall"] = dtens("xall", [TALL, D], kind="ExternalInput")
    dram["c2"] = dtens("c2", [2, D], kind="ExternalInput")
    dram["ident"] = dtens("ident", [128, 128], kind="ExternalInput")
    class LazyDram(dict):
        def __missing__(self, name):
            for nm, shp in W_SPECS:
                if nm == name:
                    self[name] = dtens(name, [NLW] + list(shp), kind="ExternalInput")
                    return self[name]
            for nm, shp, per_layer in MIX_SPECS:
                if nm == name:
                    self[name] = dtens(name, ([NLW] if per_layer else []) + list(shp), kind="ExternalInput")
                    return self[name]
            if name in SCR_SPECS:
                self[name] = dtens(name + "_scr", [TALL, SCR_SPECS[name]], kind=kind_of(name))
                return self[name]
            raise KeyError(name)

        def res(self, name):
            if not hasattr(self, "_res"):
                self._res = {}
            if name not in self._res:
                self._res[name] = [Res("%s%d" % (name, i)) for i in range(NT)]
            return self._res[name]
    dram = LazyDram(dram)
    out_ap = dtens("out", [TLAT, D], kind="ExternalOutput")
    z_d = dtens("z_scr", [TALL, DIN], kind=kind_of("z"))
    y_d = dtens("y_scr", [TALL, D], kind=kind_of("y"))
    xs_d = dtens("xs_scr", [TALL, D], kind=kind_of("xs"))
    mods_d = dtens("mods_scr", [NL_FULL, 2, 6 * D], kind=kind_of("mods"))
    h2T_d = dtens("h2T_scr", [128, 8, TALL], BF16)
    z_res = [Res("z%d" % i) for i in range(NT)]
    y_res = [Res("y%d" % i) for i in range(NT)]
    xs_res = [Res("x%d" % i) for i in range(NT)]
    h2_res = [Res("h2_%d" % i) for i in range(NT)]
    mods_res = [Res("mods%d" % i) for i in range(NL_FULL)]

    with contextlib.ExitStack() as st:
        P = Prog(nc, st)
        ident = P.sb("ident", [128, 128])
        P.dma("sp", ident[:], dram["ident"], (), [ident])
        psum = [P.ps("ps%d" % i) for i in range(8)]
        P.consts = {}
        for cv in (EPS, 1e-12, 64e-5):
            ct = P.sb("const%d" % len(P.consts), [128, 1])
            P.memset("pool", ct[:], cv, [ct])
            P.consts[cv] = ct
        c2T = P.sb("c2T", [128, 8, 2])
        for v in range(2):
            P.dma("sp", c2T[:, :, v], dram["c2"][v].rearrange("(kt p) -> p kt", p=128), (), [c2T],
                  allow_slow_non_contiguous=True)
        P.act("act", c2T[:], c2T[:], AF.Silu, [c2T], [c2T])

        def bcast_load(q, tile, row_ap, reads=()):
            P.dma(q, tile[:], row_ap.partition_broadcast(128), reads, [tile])

        def rms_tile(P, src, st_t, junk, A, Bv, dst, ncols=D):
            P.act("act", junk[:, 0:ncols], src[:, 0:ncols], AF.Square, [src], [junk], accum_out=st_t[:, 0:1])
            P.rstd(st_t, 1.0 / ncols, EPS, [junk])
            P.stt("dve", dst[:, 0:ncols], src[:, 0:ncols], st_t[:, 2:3], A[:, 0:ncols], ALU.mult, ALU.mult, [src, st_t, A], [dst])
            if Bv is not None:
                P.tt("pool", dst[:, 0:ncols], dst[:, 0:ncols], Bv[:, 0:ncols], ALU.add, [dst, Bv], [dst])

        def transpose8(P, src, dstT, banks):
            for half in range(2):
                pst = banks[half]
                for k4 in range(4):
                    kt = half * 4 + k4
                    P.tr(pst[:, k4 * 128:(k4 + 1) * 128], src[:, kt * 128:(kt + 1) * 128], ident[:], [src, ident], [pst])
                P.cp("act", dstT[:, half * 4:(half + 1) * 4, :], pst[:, :].rearrange("p (k t) -> p k t", k=4), [pst], [dstT])

        def load_cast(P, dst_ap, dst_t, src_ap, wst, idx, ncols):
            w = wst[idx % 2]
            P.dma("sp" if idx % 2 == 0 else "pool", w[:, 0:ncols], src_ap, (), [w])
            P.cp("act" if idx % 2 == 0 else "dve", dst_ap, w[:, 0:ncols], [w], [dst_t])

        for L in range(NL):
            need_ctx = L < NL_FULL - 1
            last = (L == NL - 1)
            mrow = lambda v, k: mods_d[L, v, k * D:(k + 1) * D]
            if "skip_p0" not in dbg:
              with P.phase():
                wst = [P.sb("wst%d" % i, [128, 512]) for i in range(2)]
                brow = P.sb("brow", [2, 6 * D])
                mrow_sb = [P.sb("mrow%d" % i, [2, 512]) for i in range(2)]
                P.dma("sp", brow[0:1, :], dram["b_mod"][L], (), [brow])
                P.dma("sp", brow[1:2, :], dram["b_mod"][L], (), [brow])
                for nb in range(12):
                    pst = psum[nb % 2]
                    for kt in range(8):
                        w = wst[(nb * 8 + kt) % 2]
                        P.dma("sp" if kt % 2 == 0 else "pool", w[:, 0:512],
                              dram["w_mod"][L, kt * 128:(kt + 1) * 128, nb * 512:(nb + 1) * 512], (), [w])
                        P.mm(pst[0:2, :], c2T[:, kt, :], w[:, 0:512], kt == 0, kt == 7, [c2T, w], [pst])
                    ms = mrow_sb[nb % 2]
                    P.tt("dve", ms[:, :], pst[0:2, :], brow[:, nb * 512:(nb + 1) * 512], ALU.add, [pst, brow], [ms])
                    P.dma("sp", mods_d[L, :, nb * 512:(nb + 1) * 512], ms[:, :], [ms], [mods_res[L]])

            if "skip_p1" not in dbg:
              with P.phase():
                x_src = dram["xall"] if L == 0 else xs_d
                wst = [P.sb("wst%d" % i, [128, 2048]) for i in range(2)]
                gB = P.sb("gB", [128, D])
                A1 = P.sb("A1", [128, D])
                B1 = P.sb("B1", [128, D])
                win_sb = P.sb("win", [128, 8, DIN], BF16)
                xt = [P.sb("xt%d" % i, [128, D]) for i in range(2)]
                xn = [P.sb("xn%d" % i, [128, D]) for i in range(2)]
                hT = [P.sb("hT%d" % i, [128, 8, 128], BF16) for i in range(2)]
                zt = [P.sb("zt%d" % i, [128, DIN]) for i in range(2)]
                st1 = [P.sb("st1_%d" % i, [128, 4]) for i in range(2)]
                junk = P.sb("junk", [128, D])
                bcast_load("sp", gB, dram["g_mix_pre"][L])
                idx = 0
                for kt in range(8):
                    for c0, cw in ((0, 2048), (2048, DIN - 2048)):
                        load_cast(P, win_sb[:, kt, c0:c0 + cw], win_sb, dram["w_in"][L, kt * 128:(kt + 1) * 128, c0:c0 + cw], wst, idx, cw)
                        idx += 1
                for i in range(NT):
                    v = 1 if i < 2 else 0
                    if i == 0 or i == 2:
                        bcast_load("sp", A1, mrow(v, 1), [mods_res[L]])
                        P.stt("dve", A1[:], A1[:], 1.0, gB[:], ALU.add, ALU.mult, [A1, gB], [A1])
                        bcast_load("sp", B1, mrow(v, 0), [mods_res[L]])
                    b = i % 2
                    P.dma("sp", xt[b][:], x_src[i * 128:(i + 1) * 128, :], [xs_res[i]], [xt[b]])
                    rms_tile(P, xt[b], st1[b], junk, A1, B1, xn[b])
                    transpose8(P, xn[b], hT[b], psum[0:2])
                    for cb in range(6):
                        c0 = cb * 512
                        cw = min(512, DIN - c0)
                        pst = psum[2 + cb]
                        for kt in range(8):
                            P.mm(pst[:, 0:cw], hT[b][:, kt, :], win_sb[:, kt, c0:c0 + cw], kt == 0, kt == 7, [hT[b], win_sb], [pst])
                        P.cp("dve" if cb % 2 == 0 else "act", zt[b][:, c0:c0 + cw], pst[:, 0:cw], [pst], [zt[b]])
                    P.dma("pool", z_d[i * 128:(i + 1) * 128, :], zt[b][:], [zt[b]], [z_res[i]])

            if "skip_mix" not in dbg:
                emit_mixers(P, L, need_ctx, dram, z_d, z_res, y_d, y_res, psum, ident, dbg)

            tiles5 = list(range(NT)) if need_ctx else list(range(2, NT))
            x_src = dram["xall"] if L == 0 else xs_d
            if "skip_p5" not in dbg:
              with P.phase():
                wst = [P.sb("wst%d" % i, [128, 1024]) for i in range(2)]
                wout_sb = P.sb("wout", [128, 8, D], BF16)
                gB = P.sb("gB", [128, D])
                G1 = P.sb("G1", [128, D])
                A2 = P.sb("A2", [128, D])
                B2 = P.sb("B2", [128, D])
                yt = [P.sb("yt%d" % i, [128, D]) for i in range(2)]
                xt = [P.sb("xt%d" % i, [128, D]) for i in range(2)]
                yT = [P.sb("yT%d" % i, [128, 8, 128], BF16) for i in range(2)]
                tmp = [P.sb("tmp%d" % i, [128, D]) for i in range(2)]
                xw = [P.sb("xw%d" % i, [128, D]) for i in range(2)]
                h2 = [P.sb("h2%d" % i, [128, D]) for i in range(2)]
                h2T = [P.sb("h2T%d" % i, [128, 8, 128], BF16) for i in range(2)]
                st5 = [P.sb("st5_%d" % i, [128, 8]) for i in range(2)]
                junk = P.sb("junk", [128, D])
                for kt in range(8):
                    load_cast(P, wout_sb[:, kt, :], wout_sb, dram["w_out"][L, kt * 128:(kt + 1) * 128, :], wst, kt, D)
                for i in tiles5:
                    v = 1 if i < 2 else 0
                    if i == tiles5[0] or i == 2:
                        bcast_load("sp", gB, dram["g_mix_post"][L])
                        bcast_load("sp", G1, mrow(v, 2), [mods_res[L]])
                        P.tt("dve", G1[:], G1[:], gB[:], ALU.mult, [G1, gB], [G1])
                        bcast_load("sp", gB, dram["g_ffn_pre"][L])
                        bcast_load("sp", A2, mrow(v, 4), [mods_res[L]])
                        P.stt("dve", A2[:], A2[:], 1.0, gB[:], ALU.add, ALU.mult, [A2, gB], [A2])
                        bcast_load("sp", B2, mrow(v, 3), [mods_res[L]])
                    b = i % 2
                    P.dma("sp", yt[b][:], y_d[i * 128:(i + 1) * 128, :], [y_res[i]], [yt[b]])
                    P.dma("pool", xt[b][:], x_src[i * 128:(i + 1) * 128, :], [xs_res[i]], [xt[b]])
                    transpose8(P, yt[b], yT[b], psum[0:2])
                    pb = [psum[2 + 2 * b], psum[3 + 2 * b]]
                    for half in range(2):
                        for kt in range(8):
                            P.mm(pb[half][:, :], yT[b][:, kt, :], wout_sb[:, kt, half * 512:(half + 1) * 512], kt == 0, kt == 7,
                                 [yT[b], wout_sb], [pb[half]])
                        P.act("act", junk[:, 0:512], pb[half][:, :], AF.Square, [pb[half]], [junk], accum_out=st5[b][:, 3 + half:4 + half])
                    P.tt("dve", st5[b][:, 0:1], st5[b][:, 3:4], st5[b][:, 4:5], ALU.add, [st5[b], junk], [st5[b]])
                    P.rstd(st5[b], 1.0 / D, EPS)
                    for half in range(2):
                        hs = slice(half * 512, (half + 1) * 512)
                        P.stt("dve", tmp[b][:, hs], pb[half][:, :], st5[b][:, 2:3], G1[:, hs], ALU.mult, ALU.mult,
                              [pb[half], st5[b], G1], [tmp[b]])
                    P.tt("pool", xw[b][:], tmp[b][:], xt[b][:], ALU.add, [tmp[b], xt[b]], [xw[b]])
                    P.dma("pool", xs_d[i * 128:(i + 1) * 128, :], xw[b][:], [xw[b]], [xs_res[i]])
                    rms_tile(P, xw[b], st5[b], junk, A2, B2, h2[b])
                    transpose8(P, h2[b], h2T[b], psum[6:8])
                    P.dma("sp", h2T_d[:, :, i * 128:(i + 1) * 128], h2T[b][:], [h2T[b]], [h2_res[i]])

            if "skip_p6" not in dbg:
              with P.phase():
                wst = [P.sb("wst%d" % i, [128, 1024]) for i in range(2)]
                wup_sb = P.sb("wup", [128, 8, 2 * DFF], BF16)
                wdn_sb = P.sb("wdn", [128, 22, D], BF16)
                cwt = P.sb("cwt", [128, 3, 44])
                cbt = P.sb("cbt", [128, 44])
                gB = P.sb("gB", [128, D])
                G2 = P.sb("G2", [128, D])
                h2blk = [P.sb("h2blk%d" % i, [128, 8, 258], BF16) for i in range(2)]
                cv = [[P.sb("cv%d_%d" % (i, j), [128, 256]) for j in range(2)] for i in range(2)]
                aT = P.sb("aT", [128, 22, 256], BF16)
                xt = [P.sb("xt%d" % i, [128, D]) for i in range(2)]
                tmp = [P.sb("tmp%d" % i, [128, D]) for i in range(2)]
                st6 = [P.sb("st6_%d" % i, [128, 8]) for i in range(2)]
                junk = P.sb("junk", [128, 512])
                idx = 0
                for kt in range(8):
                    for c0 in range(0, 2 * DFF, 1024):
                        cw = min(1024, 2 * DFF - c0)
                        load_cast(P, wup_sb[:, kt, c0:c0 + cw], wup_sb, dram["ffn_w_up"][L, kt * 128:(kt + 1) * 128, c0:c0 + cw], wst, idx, cw)
                        idx += 1
                for j in range(22):
                    load_cast(P, wdn_sb[:, j, :], wdn_sb, dram["ffn_w_down"][L, j * 128:(j + 1) * 128, :], wst, idx, D)
                    idx += 1
                for tap in range(3):
                    P.dma("sp", cwt[:, tap, :], dram["ffn_conv_w"][L, tap].rearrange("(j p) -> p j", p=128), (), [cwt],
                          allow_slow_non_contiguous=True)
                P.dma("sp", cbt[:, :], dram["ffn_conv_b"][L, 0].rearrange("(j p) -> p j", p=128), (), [cbt],
                      allow_slow_non_contiguous=True)
                blocks = list(range(17)) if need_ctx else list(range(1, 17))
                ucount = 0
                for bi in blocks:
                    v = 1 if bi == 0 else 0
                    if bi == blocks[0] or bi == 1:
                        bcast_load("sp", gB, dram["g_ffn_post"][L])
                        bcast_load("sp", G2, mrow(v, 5), [mods_res[L]])
                        P.tt("dve", G2[:], G2[:], gB[:], ALU.mult, [G2, gB], [G2])
                    t0 = bi * 256
                    hb = h2blk[bi % 2]
                    lval = t0 not in (0, TCTX)
                    rval = (t0 + 256) not in (TCTX, TALL)
                    if not lval:
                        P.memset("pool", hb[:, :, 0:1], 0.0, [hb])
                    if not rval:
                        P.memset("pool", hb[:, :, 257:258], 0.0, [hb])
                    a = t0 - 1 if lval else t0
                    e = t0 + 257 if rval else t0 + 256
                    rtiles = sorted(set([a // 128, (e - 1) // 128, t0 // 128, t0 // 128 + 1]))
                    P.dma("sp", hb[:, :, a - (t0 - 1):e - (t0 - 1)], h2T_d[:, :, a:e], [h2_res[r] for r in rtiles], [hb])
                    for j in range(22):
                        cs = cv[j % 2]
                        for part in range(2):
                            fi = part * 22 + j
                            f0 = part * DFF + j * 128
                            pst = psum[ucount % 4]
                            ucount += 1
                            for kt in range(8):
                                P.mm(pst[:, 0:258], wup_sb[:, kt, f0:f0 + 128], hb[:, kt, :], kt == 0, kt == 7, [wup_sb, hb], [pst])
                            c = cs[part]
                            P.act("act", c[:], pst[:, 1:257], AF.Identity, [pst, cwt, cbt], [c], bias=cbt[:, fi:fi + 1], scale=cwt[:, 1, fi:fi + 1])
                            P.stt("dve", c[:], pst[:, 0:256], cwt[:, 0, fi:fi + 1], c[:], ALU.mult, ALU.add, [pst, cwt, c], [c])
                            P.stt("dve", c[:], pst[:, 2:258], cwt[:, 2, fi:fi + 1], c[:], ALU.mult, ALU.add, [pst, cwt, c], [c])
                        P.act("act", cs[1][:], cs[1][:], AF.Silu, [cs[1]], [cs[1]])
                        P.tt("pool", aT[:, j, :], cs[0][:], cs[1][:], ALU.mult, [cs[0], cs[1]], [aT])
                    for s in range(2):
                        i = bi * 2 + s
                        b = s
                        pb = [psum[4 + 2 * s], psum[5 + 2 * s]]
                        P.dma("pool", xt[b][:], xs_d[i * 128:(i + 1) * 128, :], [xs_res[i]], [xt[b]])
                        for half in range(2):
                            for j in range(22):
                                P.mm(pb[half][:, :], aT[:, j, s * 128:(s + 1) * 128], wdn_sb[:, j, half * 512:(half + 1) * 512],
                                     j == 0, j == 21, [aT, wdn_sb], [pb[half]])
                            P.act("act", junk[:, 0:512], pb[half][:, :], AF.Square, [pb[half]], [junk], accum_out=st6[b][:, 3 + half:4 + half])
                        P.tt("dve", st6[b][:, 0:1], st6[b][:, 3:4], st6[b][:, 4:5], ALU.add, [st6[b], junk], [st6[b]])
                        P.rstd(st6[b], 1.0 / D, EPS)
                        for half in range(2):
                            hs = slice(half * 512, (half + 1) * 512)
                            P.stt("dve", tmp[b][:, hs], pb[half][:, :], st6[b][:, 2:3], G2[:, hs], ALU.mult, ALU.mult,
                                  [pb[half], st6[b], G2], [tmp[b]])
                        P.tt("pool", tmp[b][:], tmp[b][:], xt[b][:], ALU.add, [tmp[b], xt[b]], [tmp[b]])
                        if last and i >= 2:
                            P.dma("sp", out_ap[(i - 2) * 128:(i - 1) * 128, :], tmp[b][:], [tmp[b]], ())
                        else:
                            P.dma("sp", xs_d[i * 128:(i + 1) * 128, :], tmp[b][:], [tmp[b]], [xs_res[i]])

        P.flush(final=True)
    return nc


def host_inputs(inputs, b):
    m = {}
    m["xall"] = np.ascontiguousarray(np.concatenate([inputs["ctx"][b], inputs["x"][b]], axis=0), dtype=np.float32)
    m["c2"] = np.ascontiguousarray(np.stack([inputs["c"][b], inputs["c_ctx"]], axis=0), dtype=np.float32)
    m["ident"] = np.eye(128, dtype=np.float32)
    for name, shp in W_SPECS:
        m[name] = np.ascontiguousarray(np.asarray(inputs[name], dtype=np.float32).reshape([NL_FULL] + list(shp)))
    mix_host_inputs(inputs, m)
    return m


def used_inputs(nc, m):
    shapes = {}
    for alloc in nc.allocations:
        try:
            if alloc.kind == "ExternalInput":
                shapes[alloc.memorylocations[0].name] = tuple(alloc.tensor_shape)
        except Exception:
            pass
    out = {}
    for k, v in m.items():
        if k in shapes:
            shp = shapes[k]
            if tuple(v.shape) != shp:
                v = np.ascontiguousarray(v[0:shp[0]])
            assert tuple(v.shape) == shp, (k, v.shape, shp)
            out[k] = v
    return out


def kernel(**inputs):
    inputs = {k: np.asarray(v) for k, v in inputs.items()}
    nc = build_program()
    shared = host_inputs(inputs, 0)
    maps = []
    for b in range(4):
        m = dict(shared)
        m["xall"] = np.ascontiguousarray(np.concatenate([inputs["ctx"][b], inputs["x"][b]], axis=0), dtype=np.float32)
        m["c2"] = np.ascontiguousarray(np.stack([inputs["c"][b], inputs["c_ctx"]], axis=0), dtype=np.float32)
        maps.append(used_inputs(nc, m))
    in_maps = [maps[b % 4] for b in range(N_CORES)]
    res = run_bass_kernel_spmd(nc, in_maps, core_ids=list(range(N_CORES)))
    out = np.stack([np.asarray(res.results[b]["out"]) for b in range(4)], axis=0)
    return out.astype(np.float32)
```

```python
import contextlib
import numpy as np
import concourse.bass as bass
import concourse.mybir as mybir
from concourse.bass_utils import run_bass_kernel_spmd

F32 = mybir.dt.float32
BF16 = mybir.dt.bfloat16
ALU = mybir.AluOpType
AF = mybir.ActivationFunctionType
AX = mybir.AxisListType

D = 1024
NL_FULL = 4
TCTX = 256
TLAT = 4096
TALL = TCTX + TLAT
NT = TALL // 128
DIN = 2928
DFF = 2816
EPS = 1e-6


class Res:
    __slots__ = ("name", "w", "r")

    def __init__(self, name=""):
        self.name = name
        self.w = None
        self.r = {}


class T:
    def __init__(self, handle, name):
        self.h = handle
        self.res = Res(name)

    def __getitem__(self, k):
        return self.h[k]


class Prog:
    EPOCH = 30000
    KD = 6

    def __init__(self, nc, st):
        self.nc = nc
        self.st = st
        self.engs = ["pe", "act", "dve", "pool", "sp"]
        self.ops = {e: [] for e in self.engs}
        self.cnt = {e: 0 for e in self.engs}
        self.seen = {e: {} for e in self.engs}
        self.dcnt = {e: 0 for e in self.engs}
        self.sems = {}
        self.n_ops = 0
        self.gst = st
        self.last_ev = {}
        self.pending = {e: [] for e in self.engs}

    def sb(self, name, shape, dt=F32):
        self.n_ops += 1
        h = self.st.enter_context(self.nc.sbuf_tensor("sb%d_%s" % (self.n_ops, name), list(shape), dt))
        return T(h, name)

    def ps(self, name, shape=(128, 512), dt=F32):
        h = self.st.enter_context(self.nc.psum_tensor("ps_" + name, list(shape), dt))
        return T(h, name)

    def _sem(self, key):
        if key not in self.sems:
            self.sems[key] = self.gst.enter_context(self.nc.semaphore("s_" + "_".join(str(k) for k in key)))
        self.last_ev[key] = max(self.last_ev.get(key, 0), 0)
        return self.sems[key]

    def barrier(self):
        evs = [(k, v) for k, v in self.last_ev.items() if v > 0]
        for e in self.engs:
            self.pending[e] = list(evs)

    def barrier_light(self, ress):
        evs = [r.w for r in ress if r.w is not None]
        for e in self.engs:
            self.pending[e] = self.pending[e] + list(evs)

    @contextlib.contextmanager
    def phase(self):
        outer = self.st
        with contextlib.ExitStack() as ph:
            self.st = ph
            yield
            self.barrier()
            self.flush()
        self.st = outer

    def _deps(self, eng, reads, writes, is_dma):
        evs = []
        for ev in self.pending[eng]:
            evs.append((ev, "bar"))
        self.pending[eng] = []
        for t in reads:
            r = t.res if isinstance(t, T) else t
            if r.w is not None:
                evs.append((r.w, "raw"))
        for t in writes:
            r = t.res if isinstance(t, T) else t
            if r.w is not None:
                evs.append((r.w, "waw"))
            for ev in r.r.values():
                evs.append((ev, "war"))
        waits = {}
        for (key, val), kind in evs:
            if key[0] == "e" and key[1] == eng and not is_dma:
                if kind != "raw" or eng == "pe":
                    continue
            if self.seen[eng].get(key, 0) >= val:
                continue
            if waits.get(key, 0) < val:
                waits[key] = val
        for k, v in waits.items():
            self.seen[eng][k] = v
        return list(waits.items())

    def _mark(self, ev, reads, writes):
        for t in reads:
            r = t.res if isinstance(t, T) else t
            r.r[ev[0]] = ev
        for t in writes:
            r = t.res if isinstance(t, T) else t
            r.w = ev
            r.r = {}

    def op(self, eng, fn, reads=(), writes=()):
        waits = self._deps(eng, reads, writes, False)
        n = self.cnt[eng]
        key = ("e", eng, n // self.EPOCH)
        ev = (key, n % self.EPOCH + 1)
        self.cnt[eng] = n + 1
        self._sem(key)
        self.last_ev[key] = ev[1]
        self.ops[eng].append((waits, fn, key, 1))
        self._mark(ev, reads, writes)
        self.n_ops += 1

    def dma(self, q, out, in_, reads=(), writes=(), **kw):
        waits = self._deps(q, reads, writes, True)
        j = self.dcnt[q]
        self.dcnt[q] = j + 1
        key = ("d", q, j % self.KD)
        val = 16 * (j // self.KD + 1)
        if j >= self.KD and self.seen[q].get(key, 0) < val - 16:
            waits.append((key, val - 16))
            self.seen[q][key] = val - 16
        self._sem(key)
        self.last_ev[key] = val
        self.ops[q].append((waits, lambda e: e.dma_start(out=out, in_=in_, **kw), key, 16))
        self._mark((key, val), reads, writes)
        self.n_ops += 1

    def mm(self, out, lhsT, rhs, start, stop, reads, writes):
        self.op("pe", lambda e: e.matmul(out, lhsT, rhs, start=start, stop=stop), reads, writes)

    def tr(self, out, in_, ident, reads, writes):
        self.op("pe", lambda e: e.transpose(out, in_, ident), reads, writes)

    def act(self, eng, out, in_, func, reads, writes, bias=None, scale=None, accum_out=None):
        kw = {}
        if bias is not None:
            kw["bias"] = bias
        if scale is not None:
            kw["scale"] = scale
        if accum_out is not None:
            kw["accum_out"] = accum_out
        self.op(eng, lambda e: e.activation(out, in_, func, **kw), reads, writes)

    def tt(self, eng, out, in0, in1, op, reads, writes):
        self.op(eng, lambda e: e.tensor_tensor(out, in0, in1, op), reads, writes)

    def ts(self, eng, out, in0, s1, s2, op0, op1, reads, writes, accum_out=None):
        if op1 is None:
            self.op(eng, lambda e: e.tensor_scalar(out, in0, s1, None, op0), reads, writes)
        elif accum_out is None:
            self.op(eng, lambda e: e.tensor_scalar(out, in0, s1, s2, op0, op1), reads, writes)
        else:
            self.op(eng, lambda e: e.tensor_scalar(out, in0, s1, s2, op0, op1, accum_out), reads, writes)

    def stt(self, eng, out, in0, scalar, in1, op0, op1, reads, writes):
        self.op(eng, lambda e: e.scalar_tensor_tensor(out, in0, scalar, in1, op0, op1), reads, writes)

    def cp(self, eng, out, in_, reads, writes):
        if eng == "act":
            self.op(eng, lambda e: e.copy(out, in_), reads, writes)
        else:
            self.op(eng, lambda e: e.tensor_copy(out, in_), reads, writes)

    def red(self, eng, out, in_, op, reads, writes, axis=AX.X):
        self.op(eng, lambda e: e.tensor_reduce(out, in_, axis, op), reads, writes)

    def rstd(self, t, scale, eps, extra_reads=()):
        self.act("act", t[:, 1:2], t[:, 0:1], AF.Sqrt, [t, self.consts[eps]] + list(extra_reads), [t], bias=self.eps_ap(eps, t), scale=scale)
        self.op("dve", lambda e: e.reciprocal(t[:, 2:3], t[:, 1:2]), [t], [t])

    def eps_ap(self, eps, t):
        return self.consts[eps][0:t.h.shape[0], 0:1]

    def sumsq(self, dst_col, srcs, junk, reads):
        raise NotImplementedError

    def memset(self, eng, ap, val, writes):
        self.op(eng, lambda e: e.memset(ap, val), (), writes)

    def flush(self, final=False):
        nc = self.nc
        fin = [(k, v) for k, v in self.last_ev.items() if v > 0] if final else []
        sems = self.sems
        engmap = {"pe": "tensor", "act": "scalar", "dve": "vector", "pool": "gpsimd", "sp": "sync"}
        ops = self.ops
        with nc.Block() as block:
            def make(ename):
                def body(e):
                    for waits, fn, key, inc in ops[ename]:
                        for wk, wv in waits:
                            e.wait_ge(sems[wk], wv)
                        fn(e).then_inc(sems[key], inc)
                    if ename == "sp":
                        for wk, wv in fin:
                            e.wait_ge(sems[wk], wv)
                return body
            for ename in self.engs:
                getattr(block, engmap[ename])(make(ename))
        self.ops = {e: [] for e in self.engs}


NEG = -30000.0


def emit_na(P, L, need_ctx, dram, z_d, z_res, y_d, y_res, psum, ident):
    with P.phase():
        QT = P.sb("naQT", [128, 2, TALL], BF16)
        KT = P.sb("naKT", [128, 2, TALL], BF16)
        Ve = P.sb("naVe", [128, NT, 4, 66], BF16)
        Vo = P.sb("naVo", [128, NT - 1, 4, 66], BF16)
        btab = P.sb("btab", [128, 4, 14, 64])
        maskt = P.sb("mask", [128, 64])
        zt = [P.sb("zt%d" % i, [128, 768]) for i in range(2)]
        zo = [P.sb("zo%d" % i, [128, 256]) for i in range(2)]
        sw = [P.sb("sw%d" % i, [128, 4, 64]) for i in range(2)]
        pT = [P.sb("pT%d" % i, [128, 6, 64], BF16) for i in range(2)]
        rc = [P.sb("rc%d" % i, [64, 4]) for i in range(2)]
        yo = [P.sb("yo%d" % i, [64, 4, 64]) for i in range(2)]
        import os
        stage = float(os.environ.get("NA_STAGE", "9"))
        if stage < 0.5:
            return
        P.dma("sp", btab[:], dram["na_btab"][L], (), [btab])
        P.dma("sp", maskt[:], dram["na_mask"], (), [maskt])
        P.tt("dve", btab[:].rearrange("p h d q -> p (h d) q"), btab[:].rearrange("p h d q -> p (h d) q"),
             maskt[:, :].unsqueeze(1).to_broadcast([128, 56, 64]), ALU.add, [btab, maskt], [btab])
        if stage < 0.7:
            return
        P.memset("pool", Ve[:, :, :, 64:65], 1.0, [Ve])
        P.memset("pool", Vo[:, :, :, 64:65], 1.0, [Vo])
        if stage < 0.9:
            return
        for i in range(NT):
            b = i % 2
            P.dma("sp", zt[b][:], z_d[i * 128:(i + 1) * 128, 0:768], [z_res[i]], [zt[b]])
            pst = psum[i % 2]
            for c in range(4):
                P.tr(pst[:, c * 128:(c + 1) * 128], zt[b][:, c * 128:(c + 1) * 128], ident[:], [zt[b], ident], [pst])
            pv = pst[:, :].rearrange("p (c t) -> p c t", c=4)
            if stage > 0.915:
                P.cp("act", QT[:, :, i * 128:(i + 1) * 128], pv[:, 0:2, :], [pst], [QT])
            if stage > 0.925:
                P.cp("act", KT[:, :, i * 128:(i + 1) * 128], pv[:, 2:4, :], [pst], [KT])
            if stage > 0.935:
                P.cp("dve", Ve[:, i, :, 0:64], zt[b][:, 512:768].rearrange("p (h d) -> p h d", h=4), [zt[b]], [Ve])
            if i < NT - 1 and stage > 0.945:
                P.dma("pool", zo[b][:], z_d[64 + i * 128:64 + (i + 1) * 128, 512:768], [z_res[i], z_res[i + 1]], [zo[b]])
                if stage > 0.955:
                    P.cp("dve", Vo[:, i, :, 0:64], zo[b][:].rearrange("p (h d) -> p h d", h=4), [zo[b]], [Vo])

        units = []

        def na_rows(q0, keyspec, bias_di0, out_row0):
            units.append((q0, keyspec, bias_di0, out_row0))

        def part_a(u, h):
            q0, keyspec, bias_di0, out_row0 = units[u]
            g, hp = h // 2, (h % 2) * 64
            pst = psum[(u * 4 + h) % 4]
            for c, (k0, Vt, ti) in enumerate(keyspec):
                P.mm(pst[:, c * 64:(c + 1) * 64], KT[hp:hp + 64, g, k0:k0 + 128], QT[hp:hp + 64, g, q0:q0 + 64], True, True,
                     [KT, QT], [pst])

        def part_b(u, h):
            q0, keyspec, bias_di0, out_row0 = units[u]
            nk = len(keyspec)
            pst = psum[(u * 4 + h) % 4]
            po = psum[4 + u % 2]
            p = pT[(u * 4 + h) % 2]
            c0 = 0
            if bias_di0 is not None:
                s_ = sw[(u * 4 + h) % 2]
                P.stt("dve", s_[:], pst[:, 0:256].rearrange("p (c q) -> p c q", c=4), 0.125,
                      btab[:, h, bias_di0:bias_di0 + 7:2, :], ALU.mult, ALU.add, [pst, btab], [s_])
                P.act("act", p[:, 0:4, :], s_[:], AF.Exp, [s_], [p])
                c0 = 4
            P.act("act", p[:, c0:nk, :], pst[:, c0 * 64:nk * 64].rearrange("p (c q) -> p c q", c=nk - c0), AF.Exp, [pst], [p],
                  scale=0.125)
            for c, (k0, Vt, ti) in enumerate(keyspec):
                P.mm(po[0:64, h * 65:(h + 1) * 65], p[:, c, :], Vt[:, ti, h, 0:65], c == 0, c == nk - 1, [p, Vt], [po])
            if h == 3:
                r_ = rc[u % 2]
                y_ = yo[u % 2]
                pov = po[0:64, 0:260].rearrange("p (h e) -> p h e", h=4)
                P.op("dve", lambda e: e.reciprocal(r_[:, :], pov[:, :, 64]), [po], [r_])
                P.tt("dve", y_[:], pov[:, :, 0:64], r_[:, :].unsqueeze(2).to_broadcast([64, 4, 64]), ALU.mult, [po, r_], [y_])
                P.dma("sp", y_d[out_row0:out_row0 + 64, 0:256], y_[:].rearrange("p h d -> p (h d)"), [y_], [y_res[out_row0 // 128]])

        def run_units():
            seq = [(u, h) for u in range(len(units)) for h in range(4)]
            LOOK = 2
            for k in range(len(seq) + LOOK):
                if k < len(seq):
                    part_a(*seq[k])
                if k >= LOOK:
                    part_b(*seq[k - LOOK])

        ctxkeys = [(0, Ve, 0), (128, Ve, 1)]
        if stage < 2:
            return
        if need_ctx:
            for qc in range(4):
                na_rows(qc * 64, ctxkeys, None, qc * 64)
        for r in range(64 if stage >= 3 else 0):
            w0 = min(max(r - 4, 0), 56)
            key0 = TCTX + w0 * 64
            ks = []
            for c in range(4):
                k0 = key0 + c * 128
                if k0 % 128 == 0:
                    ks.append((k0, Ve, k0 // 128))
                else:
                    ks.append((k0, Vo, (k0 - 64) // 128))
            na_rows(TCTX + r * 64, ks + ctxkeys, w0 - r + 7, TCTX + r * 64)
        run_units()


def emit_mla(P, L, need_ctx, dram, z_d, z_res, y_d, y_res, psum, ident):
    SC = 96.0 ** -0.5
    with P.phase():
        nT0 = P.sb("nT0", [128, TALL], BF16)
        nT1 = P.sb("nT1", [64, TALL], BF16)
        nkT = P.sb("nkT", [128, TALL], BF16)
        krT = P.sb("krT", [96, TALL], BF16)
        KTm = P.sb("KTm", [96, 4, TALL], BF16)
        Vm = P.sb("Vm", [128, NT, 4, 66], BF16)
        wst = P.sb("wst", [128, 768])
        wq = P.sb("wq", [128, 2, 4, 96], BF16)
        wqp = P.sb("wqp", [128, 2, 4, 96], BF16)
        wk = P.sb("wk", [128, 4, 96], BF16)
        wv = P.sb("wv", [128, 256], BF16)
        Pm = P.sb("Pm", [96, 96], BF16)
        qnb = P.sb("qnb", [128, 192])
        kvnb = P.sb("kvnb", [128, 128])
        zt = [P.sb("zt%d" % i, [128, 352]) for i in range(2)]
        nq = [P.sb("nq%d" % i, [128, 416]) for i in range(2)]
        stt_ = [P.sb("st%d" % i, [128, 8]) for i in range(2)]
        junk = P.sb("junk", [128, 192])
        ctab = [P.sb("ctab%d" % i, [96, 2, 512]) for i in range(2)]
        t1 = [P.sb("t1_%d" % i, [96, 512]) for i in range(2)]
        t2 = [P.sb("t2_%d" % i, [96, 512]) for i in range(2)]
        QTb = [P.sb("QTb%d" % i, [96, 512], BF16) for i in range(2)]
        pT = [P.sb("pT%d" % i, [128, 512], BF16) for i in range(3)]
        rc = [P.sb("rc%d" % i, [128, 4]) for i in range(2)]
        yo = [P.sb("yo%d" % i, [128, 4, 4, 64]) for i in range(2)]
        oT = [P.sb("oT%d" % i, [65, 512]) for i in range(2)]
        def ldw(dst_ap, dst_t, src, rows, cols):
            P.dma("sp", wst[0:rows, 0:cols], src, (), [wst])
            P.cp("act", dst_ap, wst[0:rows, 0:cols], [wst], [dst_t])
        ldw(wq[:, 0, :, :].rearrange("p h e -> p (h e)"), wq, dram["mla_wq_r"][L, 0:128, :], 128, 384)
        ldw(wq[0:64, 1, :, :].rearrange("p h e -> p (h e)"), wq, dram["mla_wq_r"][L, 128:192, :], 64, 384)
        ldw(wqp[:, 0, :, :].rearrange("p h e -> p (h e)"), wqp, dram["mla_wq_p"][L, 0:128, :], 128, 384)
        ldw(wqp[0:64, 1, :, :].rearrange("p h e -> p (h e)"), wqp, dram["mla_wq_p"][L, 128:192, :], 64, 384)
        ldw(wk[:, :, :].rearrange("p h e -> p (h e)"), wk, dram["mla_wk"][L], 128, 384)
        ldw(wv[:, :], wv, dram["mla_wv"][L], 128, 256)
        ldw(Pm[:, :], Pm, dram["mla_pm"], 96, 96)
        for n_ in nq:
            P.memset("pool", n_[:, 320:384], 0.0, [n_])
        P.dma("sp", qnb[:], dram["mla_q_norm"][L].partition_broadcast(128), (), [qnb])
        P.dma("sp", kvnb[:], dram["mla_kv_norm"][L].partition_broadcast(128), (), [kvnb])
        P.memset("pool", Vm[:, :, :, 64:65], 1.0, [Vm])
        for i in range(NT):
            b = i % 2
            z_, n_, s_ = zt[b], nq[b], stt_[b]
            P.dma("sp", z_[:], z_d[i * 128:(i + 1) * 128, 768:1120], [z_res[i]], [z_])
            P.act("act", junk[:, 0:192], z_[:, 0:192], AF.Square, [z_], [junk], accum_out=s_[:, 0:1])
            P.rstd(s_, 1.0 / 192, EPS, [junk])
            P.stt("dve", n_[:, 0:192], z_[:, 0:192], s_[:, 2:3], qnb[:], ALU.mult, ALU.mult, [z_, s_, qnb], [n_])
            P.act("act", junk[:, 0:128], z_[:, 192:320], AF.Square, [z_], [junk], accum_out=s_[:, 4:5])
            P.act("act", s_[:, 5:6], s_[:, 4:5], AF.Sqrt, [s_, junk, P.consts[EPS]], [s_], bias=P.consts[EPS][:, 0:1], scale=1.0 / 128)
            P.op("dve", lambda e, s_=s_: e.reciprocal(s_[:, 6:7], s_[:, 5:6]), [s_], [s_])
            P.stt("dve", n_[:, 192:320], z_[:, 192:320], s_[:, 6:7], kvnb[:], ALU.mult, ALU.mult, [z_, s_, kvnb], [n_])
            pst = psum[i % 2]
            P.tr(pst[:, 0:128], n_[:, 0:128], ident[:], [n_, ident], [pst])
            P.tr(pst[0:64, 128:256], n_[:, 128:192], ident[:], [n_, ident], [pst])
            P.tr(pst[:, 256:384], n_[:, 192:320], ident[:], [n_, ident], [pst])
            P.cp("pool", n_[:, 384:416], z_[:, 320:352], [z_], [n_])
            P.tr(pst[0:96, 384:512], n_[:, 320:416], ident[:], [n_, ident], [pst])
            ts_ = slice(i * 128, (i + 1) * 128)
            P.cp("act", nT0[:, ts_], pst[:, 0:128], [pst], [nT0])
            P.cp("act", nT1[:, ts_], pst[0:64, 128:256], [pst], [nT1])
            P.cp("act", nkT[:, ts_], pst[:, 256:384], [pst], [nkT])
            P.cp("act", krT[:, ts_], pst[0:96, 384:512], [pst], [krT])
        nblk = [(b0, min(512, TALL - b0)) for b0 in range(0, TALL, 512)]
        for bi, (b0, n) in enumerate(nblk):
            ct = ctab[bi % 2]
            P.dma("sp", ct[64:96, :, 0:n], dram["rope_tab"][:, :, b0:b0 + n].rearrange("c p t -> p c t"), (), [ct])
            for h in range(4):
                pst = psum[h % 2]
                P.mm(pst[0:96, 0:n], wk[:, h, :], nkT[:, b0:b0 + n], True, True, [wk, nkT], [pst])
                P.cp("act", KTm[0:64, h, b0:b0 + n], pst[0:64, 0:n], [pst], [KTm])
            pb = psum[2]
            P.mm(pb[0:96, 0:n], Pm[:, :], krT[:, b0:b0 + n], True, True, [Pm, krT], [pb])
            a1, a2 = t1[bi % 2], t2[bi % 2]
            P.tt("dve", a1[64:96, 0:n], krT[64:96, b0:b0 + n], ct[64:96, 0, 0:n], ALU.mult, [krT, ct], [a1])
            P.tt("dve", a2[64:96, 0:n], pb[64:96, 0:n], ct[64:96, 1, 0:n], ALU.mult, [pb, ct], [a2])
            P.tt("pool", KTm[64:96, :, b0:b0 + n], a1[64:96, 0:n].unsqueeze(1).to_broadcast([32, 4, n]),
                 a2[64:96, 0:n].unsqueeze(1).to_broadcast([32, 4, n]), ALU.add, [a1, a2], [KTm])
            for s in range(n // 128):
                ti = b0 // 128 + s
                pv = psum[3 + s % 2]
                P.mm(pv[:, 0:256], nkT[:, ti * 128:(ti + 1) * 128], wv[:, :], True, True, [nkT, wv], [pv])
                P.cp("act", Vm[:, ti, :, 0:64], pv[:, 0:256].rearrange("p (h d) -> p h d", h=4), [pv], [Vm])
        qblocks = []
        if need_ctx:
            qblocks.append((0, 256, [0, 1]))
        for b0 in range(TCTX, TALL, 512):
            qblocks.append((b0, 512, list(range(NT))))
        cnt = 0
        for qi, (b0, n, ktiles) in enumerate(qblocks):
            ct = ctab[qi % 2]
            P.dma("sp", ct[64:96, :, 0:n], dram["rope_tab"][:, :, b0:b0 + n].rearrange("c p t -> p c t"), (), [ct])
            y_ = yo[qi % 2]
            nsub = n // 128
            for h in range(4):
                pa, pb = psum[0], psum[1]
                P.mm(pa[0:96, 0:n], wq[:, 0, h, :], nT0[:, b0:b0 + n], True, False, [wq, nT0], [pa])
                P.mm(pa[0:96, 0:n], wq[0:64, 1, h, :], nT1[:, b0:b0 + n], False, True, [wq, nT1], [pa])
                P.mm(pb[0:96, 0:n], wqp[:, 0, h, :], nT0[:, b0:b0 + n], True, False, [wqp, nT0], [pb])
                P.mm(pb[0:96, 0:n], wqp[0:64, 1, h, :], nT1[:, b0:b0 + n], False, True, [wqp, nT1], [pb])
                q_ = QTb[(qi * 4 + h) % 2]
                a1, a2 = t1[h % 2], t2[h % 2]
                P.cp("act", q_[0:64, 0:n], pa[0:64, 0:n], [pa], [q_])
                P.tt("dve", a1[64:96, 0:n], pa[64:96, 0:n], ct[64:96, 0, 0:n], ALU.mult, [pa, ct], [a1])
                P.tt("dve", a2[64:96, 0:n], pb[64:96, 0:n], ct[64:96, 1, 0:n], ALU.mult, [pb, ct], [a2])
                P.tt("pool", q_[64:96, 0:n], a1[64:96, 0:n], a2[64:96, 0:n], ALU.add, [a1, a2], [q_])
                poT = psum[5 + (qi * 4 + h) % 2]
                LOOK = 2
                nk_ = len(ktiles)
                slots_ = {}
                for ki in range(nk_ + LOOK):
                    if ki < nk_:
                        kt = ktiles[ki]
                        pst = psum[2 + cnt % 3]
                        p = pT[cnt % 3]
                        cnt += 1
                        P.mm(pst[:, 0:n], KTm[:, h, kt * 128:(kt + 1) * 128], q_[:, 0:n], True, True, [KTm, q_], [pst])
                        slots_[ki] = (pst, p, kt)
                    kj = ki - LOOK
                    if kj >= 0:
                        pst, p, kt = slots_.pop(kj)
                        P.act("act", p[:, 0:n], pst[:, 0:n], AF.Exp, [pst], [p], scale=SC)
                        P.mm(poT[0:65, 0:n], Vm[:, kt, h, 0:65], p[:, 0:n], kj == 0, kj == nk_ - 1, [p, Vm], [poT])
                o_ = oT[h % 2]
                P.cp("dve", o_[:, 0:n], poT[0:65, 0:n], [poT], [o_])
                po = psum[7]
                for s in range(nsub):
                    P.tr(po[:, s * 65:(s + 1) * 65], o_[0:65, s * 128:(s + 1) * 128], ident[0:65, 0:65], [o_, ident], [po])
                r_ = rc[h % 2]
                pov = po[:, 0:nsub * 65].rearrange("p (s e) -> p s e", s=nsub)
                P.op("dve", lambda e, r_=r_, pov=pov, nsub=nsub: e.reciprocal(r_[:, 0:nsub], pov[:, :, 64]), [po], [r_])
                P.tt("dve", y_[:, 0:nsub, h, :], pov[:, :, 0:64], r_[:, 0:nsub].unsqueeze(2).to_broadcast([128, nsub, 64]), ALU.mult,
                     [po, r_], [y_])
            for s in range(nsub):
                r0 = b0 + s * 128
                P.dma("sp", y_d[r0:r0 + 128, 256:512], y_[:, s, :, :].rearrange("p h d -> p (h d)"), [y_], [y_res[r0 // 128]])


def emit_mixers(P, L, need_ctx, dram, z_d, z_res, y_d, y_res, psum, ident, dbg):
    if "skip_na" not in dbg:
        emit_na(P, L, need_ctx, dram, z_d, z_res, y_d, y_res, psum, ident)
    if "skip_mla" not in dbg:
        emit_mla(P, L, need_ctx, dram, z_d, z_res, y_d, y_res, psum, ident)
    if "skip_gla" not in dbg:
        emit_gla(P, L, need_ctx, dram, z_d, z_res, y_d, y_res, psum, ident)
    if "skip_rw" not in dbg:
        emit_rwkv(P, L, need_ctx, dram, z_d, z_res, y_d, y_res, psum, ident)


def rope_tables():
    t = np.arange(TLAT)
    row = (t // 64).astype(np.float32)
    col = (t % 64).astype(np.float32)
    inv = (10000.0 ** (-np.arange(0, 16, 2, dtype=np.float32) / 16)).astype(np.float32)
    C = np.ones((32, TALL), np.float32)
    S = np.zeros((32, TALL), np.float32)
    for part, pos in ((0, row), (1, col)):
        ang = (pos[:, None] * inv[None, :]).astype(np.float32)
        c, s = np.cos(ang).T, np.sin(ang).T
        C[part * 16:part * 16 + 8, TCTX:] = c
        C[part * 16 + 8:part * 16 + 16, TCTX:] = c
        S[part * 16:part * 16 + 8, TCTX:] = -s
        S[part * 16 + 8:part * 16 + 16, TCTX:] = s
    return np.stack([C, S], 0).astype(np.float32)


def rope_partner():
    idx = np.arange(32)
    return np.where((idx % 16) < 8, idx + 8, idx - 8)


def mix_host_inputs(inputs, m):
    part = rope_partner()
    m["rope_tab"] = rope_tables()
    m["scan_consts"] = scan_consts()
    for nm, shp in (("rw_mu", (1, 1024)), ("rw_k_k", (1, 256)), ("rw_k_a", (1, 256)), ("rw_w0", (2, 1, 256)), ("rw_a0", (2, 1, 256)),
                    ("rw_w_up", (2, 64, 256)), ("rw_a_up", (2, 64, 256)), ("rw_g_up", (128, 256)), ("rw_ln_w", (1, 256)),
                    ("rw_ln_b", (1, 256)), ("rw_r_k", (1, 256))):
        m[nm] = np.ascontiguousarray(np.asarray(inputs[nm], np.float32).reshape((NL_FULL,) + shp))
    m["gla_gate_up"] = np.asarray(inputs["gla_gate_up"], np.float32)
    m["gla_gate_b"] = np.asarray(inputs["gla_gate_b"], np.float32).reshape(NL_FULL, 2, 1, 128)
    m["gla_norm"] = np.asarray(inputs["gla_norm"], np.float32).reshape(NL_FULL, 1, 256)
    pm = np.zeros((96, 96), np.float32)
    pm[64 + part, 64 + np.arange(32)] = 1.0
    m["mla_pm"] = pm
    wuq = np.asarray(inputs["mla_w_uq"], np.float32).reshape(NL_FULL, 192, 4, 96)
    m["mla_wq_r"] = np.ascontiguousarray(wuq.reshape(NL_FULL, 192, 384))
    wp = np.concatenate([wuq[..., 0:64], wuq[..., 64:96][..., part]], axis=-1)
    m["mla_wq_p"] = np.ascontiguousarray(wp.reshape(NL_FULL, 192, 384))
    wukv = np.asarray(inputs["mla_w_ukv"], np.float32).reshape(NL_FULL, 128, 4, 128)
    wk = np.zeros((NL_FULL, 128, 4, 96), np.float32)
    wk[..., 0:64] = wukv[..., 0:64]
    m["mla_wk"] = np.ascontiguousarray(wk.reshape(NL_FULL, 128, 384))
    m["mla_wv"] = np.ascontiguousarray(wukv[..., 64:128].reshape(NL_FULL, 128, 256))
    m["mla_q_norm"] = np.asarray(inputs["mla_q_norm"], np.float32).reshape(NL_FULL, 1, 192)
    m["mla_kv_norm"] = np.asarray(inputs["mla_kv_norm"], np.float32).reshape(NL_FULL, 1, 128)
    rpb = np.asarray(inputs["na_rpb"], np.float32)
    p = np.arange(128)
    k = p % 64
    q = np.arange(64)
    coff = np.clip(k[:, None] - q[None, :], -15, 15) + 15
    di = np.arange(14)
    roff = di[None, :] + (p[:, None] >= 64)
    m["na_btab"] = np.ascontiguousarray(
        rpb[:, :, roff[:, :, None], coff[:, None, :]].transpose(0, 2, 1, 3, 4))
    cstart = np.clip(q - 8, 0, 48)
    inwin = (k[:, None] >= cstart[None, :]) & (k[:, None] < cstart[None, :] + 16)
    m["na_mask"] = np.where(inwin, 0.0, NEG).astype(np.float32)
    return m


MIX_SPECS = [
    ("rw_mu", (1, 1024), True), ("rw_k_k", (1, 256), True), ("rw_k_a", (1, 256), True), ("rw_w0", (2, 1, 256), True),
    ("rw_a0", (2, 1, 256), True), ("rw_w_up", (2, 64, 256), True), ("rw_a_up", (2, 64, 256), True), ("rw_g_up", (128, 256), True),
    ("rw_ln_w", (1, 256), True), ("rw_ln_b", (1, 256), True), ("rw_r_k", (1, 256), True),
    ("scan_consts", (2, 128, 768), False), ("gla_gate_up", (2, 16, 128), True), ("gla_gate_b", (2, 1, 128), True),
    ("gla_norm", (1, 256), True),
    ("rope_tab", (2, 32, TALL), False), ("mla_pm", (96, 96), False),
    ("mla_wq_r", (192, 384), True), ("mla_wq_p", (192, 384), True), ("mla_wk", (128, 384), True), ("mla_wv", (128, 256), True),
    ("mla_q_norm", (1, 192), True), ("mla_kv_norm", (1, 128), True),
    ("na_btab", (128, 4, 14, 64), True), ("na_mask", (128, 64), False),
]


def scan_consts():
    s = np.arange(128)[:, None]
    t = np.arange(128)[None, :]
    out = np.zeros((2, 128, 768), np.float32)
    for d in range(2):
        incl = (s <= t) if d == 0 else (s >= t)
        strict = (s < t) if d == 0 else (s > t)
        out[d, :, 0:128] = incl
        out[d, :, 128:256] = incl
        out[d, :, 256:384] = strict
        out[d, :, 384:512] = incl
        out[d, :, 512:640] = strict
        out[d, :, 640:768] = strict.T
    return out


def emit_scan(P, dram, psum, ident, pre_d, pre_res, loads, has_ab, yacc):
    ones = P.sb("ones", [128, 1])
    P.memset("pool", ones[:], 1.0, [ones])
    P.memset("pool", yacc[:], 0.0, [yacc])
    yres = [Res("yacc%d" % i) for i in range(NT)]

    def direction(d):
        B = psum[4 * d:4 * d + 4]
        Xin = [P.sb("Xin%d_%d" % (d, i), [128, 6, 256]) for i in range(2)]
        E = P.sb("E%d" % d, [128, 3, 256])
        W = [P.sb("W%d_%d" % (d, i), [128, 4, 256]) for i in range(2)]
        ft = P.sb("FT%d" % d, [128, 2, 4, 128])
        gm = P.sb("Gm%d" % d, [128, 4, 512])
        pCt = [P.sb("pC%d_%d" % (d, i), [128, 2]) for i in range(2)]
        H = P.sb("H%d" % d, [128, 2, 64])
        tmpH = P.sb("tmpH%d" % d, [128, 2, 64])
        cst = P.sb("cst%d" % d, [128, 768])
        P.dma("sp", cst[:], dram["scan_consts"][d], (), [cst])
        if has_ab:
            Xb = [P.sb("Xb%d_%d" % (d, i), [128, 4, 128]) for i in range(2)]
            Yb = [P.sb("Yb%d_%d" % (d, i), [128, 4, 128]) for i in range(2)]
            Pb = [P.sb("Pb%d_%d" % (d, i), [128, 4, 128]) for i in range(2)]
            Xs = P.sb("Xs%d" % d, [128, 256])
            Us = P.sb("Us%d" % d, [128, 256])
        order = list(range(NT)) if d == 0 else [1, 0] + list(range(NT - 1, 1, -1))
        Mc = cst[:, 0:128]
        MASK4 = cst[:, 128:640]
        P.memset("pool", H[:], 0.0, [H])
        yield
        for n, i in enumerate(order):
            b = n % 2
            xi, w_, pc = Xin[b], W[b], pCt[b]
            for (s0, ns, c0) in loads(d):
                P.dma("sp" if (s0 + d) % 2 == 0 else "pool", xi[:, s0:s0 + ns, :].rearrange("p s f -> p (s f)"),
                      pre_d[i * 128:(i + 1) * 128, c0:c0 + ns * 256], [pre_res[i]], [xi])
            pcl = B[0]
            P.mm(pcl[:, 0:256], Mc, xi[:, 3, :], True, True, [cst, xi], [pcl])
            pcp = B[1]
            for g in range(2):
                P.mm(pcp[:, 384 + g:385 + g], xi[:, 3, g * 128:(g + 1) * 128], ones[:, 0:1], True, True, [xi, ones], [pcp])
            yield
            P.act("act", E[:, 0, :], pcl[:, 0:256], AF.Exp, [pcl], [E])
            P.act("act", E[:, 1, :], pcl[:, 0:256], AF.Exp, [pcl], [E], scale=-1.0)
            P.act("act", pc[:, :], pcp[:, 384:386], AF.Exp, [pcp], [pc])
            if has_ab:
                P.tt("dve", E[:, 2, :], pcl[:, 0:256], xi[:, 3, :], ALU.subtract, [pcl, xi], [E])
                P.act("act", E[:, 2, :], E[:, 2, :], AF.Exp, [E], [E])
            yield
            P.tt("dve", w_[:, 0, :], xi[:, 0, :], E[:, 0, :], ALU.mult, [xi, E], [w_])
            P.tt("pool", w_[:, 1, :], xi[:, 1, :], E[:, 1, :], ALU.mult, [xi, E], [w_])
            if has_ab:
                P.tt("dve", w_[:, 2, :], xi[:, 4, :], E[:, 2, :], ALU.mult, [xi, E], [w_])
                P.tt("pool", w_[:, 3, :], xi[:, 5, :], E[:, 1, :], ALU.mult, [xi, E], [w_])
            yield
            slots = [(0, 0), (2, 1), (1, 2), (3, 3)] if has_ab else [(0, 0), (1, 2)]
            for g in range(2):
                pt = B[2 + g]
                for (ws, fs) in slots:
                    P.tr(pt[:, fs * 128:(fs + 1) * 128], w_[:, ws, g * 128:(g + 1) * 128], ident[:], [w_, ident], [pt])
            yield
            for g in range(2):
                pt = B[2 + g]
                if has_ab:
                    P.cp("act", ft[:, g, :, :].rearrange("p f t -> p (f t)"), pt[:, :], [pt], [ft])
                else:
                    P.cp("act", ft[:, g, 0, :], pt[:, 0:128], [pt], [ft])
                    P.cp("act", ft[:, g, 2, :], pt[:, 256:384], [pt], [ft])
            yield
            for h in range(4):
                g, hp = h // 2, (h % 2) * 64
                pg = B[2 + h % 2]
                if has_ab:
                    P.mm(pg[:, 0:256], ft[hp:hp + 64, g, 2, :], ft[hp:hp + 64, g, 0:2, :].rearrange("p f t -> p (f t)"), True, True, [ft], [pg])
                    P.mm(pg[:, 256:512], ft[hp:hp + 64, g, 3, :], ft[hp:hp + 64, g, 0:2, :].rearrange("p f t -> p (f t)"), True, True, [ft], [pg])
                    P.tt("dve", gm[:, h, :], pg[:, :], MASK4, ALU.mult, [pg, cst], [gm])
                else:
                    P.mm(pg[:, 0:128], ft[hp:hp + 64, g, 2, :], ft[hp:hp + 64, g, 0, :], True, True, [ft], [pg])
                    P.tt("dve", gm[:, h, 0:128], pg[:, 0:128], MASK4[:, 0:128], ALU.mult, [pg, cst], [gm])
                if h % 2 == 1:
                    yield
            if has_ab:
                p3 = B[0]
                for h in range(4):
                    P.tr(p3[:, h * 128:(h + 1) * 128], gm[:, h, 384:512], ident[:], [gm, ident], [p3])
                Xc = None
                Yc, Pc = Yb[0], Pb[0]
                yield
                for h in range(4):
                    P.cp("act", Yc[:, h, :], p3[:, h * 128:(h + 1) * 128], [p3], [Yc])
                    P.tt("dve", Pc[:, h, :], gm[:, h, 384:512], ident[:, :], ALU.add, [gm, ident], [Pc])
                yield
                for lv in range(1, 7):
                    Xsrc = (lambda h: gm[:, h, 384:512]) if Xc is None else (lambda h, Xc=Xc: Xc[:, h, :])
                    Xres = gm if Xc is None else Xc
                    Yn, Pn = Yb[lv % 2], Pb[lv % 2]
                    pY = B[3]
                    for h in range(4):
                        P.mm(pY[:, h * 128:(h + 1) * 128], Xsrc(h), Yc[:, h, :], True, True, [Yc, Xres], [pY])
                    if lv < 6:
                        Xn = Xb[lv % 2]
                        pX = B[2]
                        for h in range(4):
                            P.mm(pX[:, h * 128:(h + 1) * 128], Yc[:, h, :], Xsrc(h), True, True, [Yc, Xres], [pX])
                    yield
                    P.cp("dve", Yn[:].rearrange("p h s -> p (h s)"), pY[:, :], [pY], [Yn])
                    if lv < 6:
                        P.cp("act", Xn[:].rearrange("p h s -> p (h s)"), pX[:, :], [pX], [Xn])
                    yield
                    pP = B[1]
                    for h in range(4):
                        P.mm(pP[:, h * 128:(h + 1) * 128], Yn[:, h, :], Pc[:, h, :], True, True, [Yn, Pc], [pP])
                    yield
                    P.tt("dve", Pn[:].rearrange("p h s -> p (h s)"), pP[:, :], Pc[:].rearrange("p h s -> p (h s)"), ALU.add, [pP, Pc], [Pn])
                    yield
                    if lv < 6:
                        Xc = Xn
                    Yc, Pc = Yn, Pn
                pXs = B[0]
                for h in range(4):
                    g, hp = h // 2, (h % 2) * 64
                    P.mm(pXs[:, 256 + h * 64:256 + (h + 1) * 64], ft[hp:hp + 64, g, 1, :], H[hp:hp + 64, g, :], True, False, [ft, H], [pXs])
                    P.mm(pXs[:, 256 + h * 64:256 + (h + 1) * 64], gm[:, h, 128:256], xi[:, 2, h * 64:(h + 1) * 64], False, True, [gm, xi], [pXs])
                yield
                P.cp("act", Xs[:, :], pXs[:, 256:512], [pXs], [Xs])
                yield
                pU = B[2]
                for h in range(4):
                    P.mm(pU[:, h * 64:(h + 1) * 64], Pc[:, h, :], Xs[:, h * 64:(h + 1) * 64], True, True, [Pc, Xs], [pU])
                yield
                P.cp("dve", Us[:, :], pU[:, 0:256], [pU], [Us])
                yield
            pYo = B[1]
            for h in range(4):
                g, hp = h // 2, (h % 2) * 64
                vh = xi[:, 2, h * 64:(h + 1) * 64]
                P.mm(pYo[:, h * 64:(h + 1) * 64], ft[hp:hp + 64, g, 0, :], H[hp:hp + 64, g, :], True, False, [ft, H], [pYo])
                if has_ab:
                    P.mm(pYo[:, h * 64:(h + 1) * 64], gm[:, h, 256:384], Us[:, h * 64:(h + 1) * 64], False, False, [gm, Us], [pYo])
                P.mm(pYo[:, h * 64:(h + 1) * 64], gm[:, h, 0:128], vh, False, True, [gm, xi], [pYo])
            for h in range(4):
                g, hp = h // 2, (h % 2) * 64
                vh = xi[:, 2, h * 64:(h + 1) * 64]
                P.mm(pYo[hp:hp + 64, 256 + g * 64:256 + (g + 1) * 64], w_[:, 1, h * 64:(h + 1) * 64], vh, True, not has_ab, [w_, xi], [pYo])
                if has_ab:
                    P.mm(pYo[hp:hp + 64, 256 + g * 64:256 + (g + 1) * 64], w_[:, 3, h * 64:(h + 1) * 64], Us[:, h * 64:(h + 1) * 64], False, True,
                         [w_, Us], [pYo])
            yield
            P.tt("dve", yacc[:, i, :], pYo[:, 0:256], yacc[:, i, :], ALU.add, [pYo, yres[i]], [yres[i]])
            P.tt("dve", tmpH[:], pYo[:, 256:384].rearrange("p (g v) -> p g v", g=2), H[:], ALU.add, [pYo, H], [tmpH])
            P.tt("dve", H[:], tmpH[:], pc[:, :].unsqueeze(2).to_broadcast([128, 2, 64]), ALU.mult, [tmpH, pc], [H])
            yield

    gens = [direction(0), direction(1)]
    while gens:
        for g_ in list(gens):
            try:
                next(g_)
            except StopIteration:
                gens.remove(g_)
    yacc.res = Res("yacc_done")
    P.barrier_light(yres)


GLA_Z0 = 768 + 352 + 1024


def emit_gla(P, L, need_ctx, dram, z_d, z_res, y_d, y_res, psum, ident):
    pre_d = dram["gla_pre"]
    pre_res = dram.res("gla_pre")
    with P.phase():
        zt = [P.sb("zt%d" % i, [128, 528]) for i in range(2)]
        pre = [P.sb("pre%d" % i, [128, 5, 4, 64]) for i in range(2)]
        gdT = [P.sb("gdT%d" % i, [16, 128]) for i in range(2)]
        gup = P.sb("gup", [16, 2, 128])
        gb = P.sb("gb", [128, 2, 128])
        xg = [P.sb("xg%d" % i, [128, 128]) for i in range(2)]
        for p_ in pre:
            P.memset("pool", p_[:], 0.0, [p_])
        for d in range(2):
            P.dma("sp", gup[:, d, :], dram["gla_gate_up"][L, d], (), [gup])
            P.dma("sp", gb[:, d, :], dram["gla_gate_b"][L, d].partition_broadcast(128), (), [gb])
        for i in range(NT):
            b = i % 2
            z_, p_ = zt[b], pre[b]
            P.dma("sp", z_[:], z_d[i * 128:(i + 1) * 128, GLA_Z0:GLA_Z0 + 528], [z_res[i]], [z_])
            P.op("act", lambda e, z_=z_, p_=p_: e.mul(p_[:, 0, :, 0:32], z_[:, 0:128].rearrange("p (h d) -> p h d", h=4), 32.0 ** -0.5), [z_], [p_])
            P.cp("pool", p_[:, 1, :, 0:32], z_[:, 128:256].rearrange("p (h d) -> p h d", h=4), [z_], [p_])
            P.cp("pool", p_[:, 2, :, :], z_[:, 256:512].rearrange("p (h d) -> p h d", h=4), [z_], [p_])
            pt = psum[i % 2]
            P.tr(pt[0:16, 0:128], z_[:, 512:528], ident[:], [z_, ident], [pt])
            P.cp("act", gdT[b][:, :], pt[0:16, 0:128], [pt], [gdT[b]])
            for d in range(2):
                pl = psum[2 + d]
                x_ = xg[d]
                P.mm(pl[:, 0:128], gdT[b][:, :], gup[:, d, :], True, True, [gdT[b], gup], [pl])
                P.tt("dve", x_[:], pl[:, 0:128], gb[:, d, :], ALU.add, [pl, gb], [x_])
                P.act("act", x_[:], x_[:], AF.Exp, [x_], [x_], scale=-1.0)
                P.act("act", x_[:], x_[:], AF.Ln, [x_], [x_], bias=1.0)
                P.op("act", lambda e, x_=x_, p_=p_, d=d: e.mul(p_[:, 3 + d, :, 0:32], x_[:].rearrange("p (h d) -> p h d", h=4), -1.0 / 16.0), [x_], [p_])
            P.dma("pool", pre_d[i * 128:(i + 1) * 128, :], p_[:].rearrange("p s h d -> p (s h d)"), [p_], [pre_res[i]])
    with P.phase():
        yacc = P.sb("yacc", [128, NT, 256])
        emit_scan(P, dram, psum, ident, pre_d, pre_res, lambda d: [(0, 3, 0), (3, 1, 768 + d * 256)], False, yacc)
        gn = P.sb("gn", [128, 256])
        og = [P.sb("og%d" % i, [128, 256]) for i in range(2)]
        sq = [P.sb("sq%d" % i, [128, 256]) for i in range(2)]
        stg = [P.sb("stg%d" % i, [128, 12]) for i in range(2)]
        P.dma("sp", gn[:], dram["gla_norm"][L].partition_broadcast(128), (), [gn])
        for i in (range(NT) if need_ctx else range(2, NT)):
            b = i % 2
            o_, s_, g_ = og[b], sq[b], stg[b]
            yv = yacc[:, i, :]
            P.dma("sp", o_[:], z_d[i * 128:(i + 1) * 128, DIN - 256:DIN], [z_res[i]], [o_])
            P.act("act", o_[:], o_[:], AF.Silu, [o_], [o_])
            P.tt("pool", s_[:], yv, yv, ALU.mult, [yacc], [s_])
            P.red("dve", g_[:, 0:4], s_[:].rearrange("p (h d) -> p h d", h=4), ALU.add, [s_], [g_])
            P.act("act", g_[:, 4:8], g_[:, 0:4], AF.Sqrt, [g_, P.consts[EPS]], [g_], bias=P.consts[EPS][:, 0:1], scale=1.0 / 64)
            P.op("dve", lambda e, g_=g_: e.reciprocal(g_[:, 8:12], g_[:, 4:8]), [g_], [g_])
            P.tt("dve", s_[:].rearrange("p (h d) -> p h d", h=4), yv.rearrange("p (h d) -> p h d", h=4),
                 g_[:, 8:12].unsqueeze(2).to_broadcast([128, 4, 64]), ALU.mult, [yacc, g_], [s_])
            P.tt("pool", s_[:], s_[:], gn[:], ALU.mult, [s_, gn], [s_])
            P.tt("pool", s_[:], s_[:], o_[:], ALU.mult, [s_, o_], [s_])
            P.dma("sp", y_d[i * 128:(i + 1) * 128, 768:1024], s_[:], [s_], [y_res[i]])


RW_Z0 = 768 + 352


def emit_rwkv(P, L, need_ctx, dram, z_d, z_res, y_d, y_res, psum, ident):
    pre_d = dram["rw_pre"]
    pre_res = dram.res("rw_pre")
    C05 = float(-np.exp(-0.5))
    with P.phase():
        zc = [P.sb("zc%d" % i, [128, 1024]) for i in range(2)]
        zp = [P.sb("zp%d" % i, [128, 1024]) for i in range(2)]
        zn = [P.sb("zn%d" % i, [128, 1024]) for i in range(2)]
        zs = [P.sb("zs%d" % i, [128, 1024]) for i in range(2)]
        pre = [P.sb("pre%d" % i, [128, 10, 256]) for i in range(2)]
        mub = P.sb("mub", [128, 1024])
        kkb = P.sb("kkb", [128, 256])
        kab = P.sb("kab", [128, 256])
        w0b = P.sb("w0b", [128, 2, 256])
        a0b = P.sb("a0b", [128, 2, 256])
        wa = P.sb("wa", [128, 2, 256])
        gup = P.sb("gup", [128, 256])
        TWA = [P.sb("TWA%d" % i, [128, 128]) for i in range(2)]
        TG = [P.sb("TG%d" % i, [128, 128]) for i in range(2)]
        kt_ = [P.sb("kt%d" % i, [128, 256]) for i in range(2)]
        kn_ = [P.sb("kn%d" % i, [128, 256]) for i in range(2)]
        sq_ = P.sb("sq", [128, 256])
        xw = [P.sb("xw%d" % i, [128, 256]) for i in range(2)]
        xa = [P.sb("xa%d" % i, [128, 256]) for i in range(2)]
        st_ = [P.sb("st%d" % i, [128, 12]) for i in range(2)]
        P.dma("sp", mub[:], dram["rw_mu"][L].partition_broadcast(128), (), [mub])
        P.dma("sp", kkb[:], dram["rw_k_k"][L].partition_broadcast(128), (), [kkb])
        P.dma("sp", kab[:], dram["rw_k_a"][L].partition_broadcast(128), (), [kab])
        for d in range(2):
            P.dma("sp", w0b[:, d, :], dram["rw_w0"][L, d].partition_broadcast(128), (), [w0b])
            P.dma("sp", a0b[:, d, :], dram["rw_a0"][L, d].partition_broadcast(128), (), [a0b])
            P.dma("sp", wa[0:64, d, :], dram["rw_w_up"][L, d], (), [wa])
            P.dma("sp", wa[64:128, d, :], dram["rw_a_up"][L, d], (), [wa])
        P.dma("sp", gup[:], dram["rw_g_up"][L], (), [gup])
        for i in range(NT):
            b = i % 2
            c_, p_, n_, s_, pr = zc[b], zp[b], zn[b], zs[b], pre[b]
            t0 = i * 128
            P.dma("sp", c_[:], z_d[t0:t0 + 128, RW_Z0:RW_Z0 + 1024], [z_res[i]], [c_])
            if t0 in (0, TCTX):
                P.memset("pool", p_[:], 0.0, [p_])
                P.dma("pool", p_[1:128, :], z_d[t0:t0 + 127, RW_Z0:RW_Z0 + 1024], [z_res[i]], [p_])
            else:
                P.dma("pool", p_[:], z_d[t0 - 1:t0 + 127, RW_Z0:RW_Z0 + 1024], [z_res[i - 1], z_res[i]], [p_])
            if t0 + 128 in (TCTX, TALL):
                P.memset("pool", n_[:], 0.0, [n_])
                P.dma("sp", n_[0:127, :], z_d[t0 + 1:t0 + 128, RW_Z0:RW_Z0 + 1024], [z_res[i]], [n_])
            else:
                P.dma("sp", n_[:], z_d[t0 + 1:t0 + 129, RW_Z0:RW_Z0 + 1024], [z_res[i], z_res[i + 1]], [n_])
            P.tt("pool", p_[:], p_[:], n_[:], ALU.add, [p_, n_], [p_])
            P.stt("dve", p_[:], p_[:], 0.5, c_[:], ALU.mult, ALU.subtract, [p_, c_], [p_])
            P.tt("pool", p_[:], p_[:], mub[:], ALU.mult, [p_, mub], [p_])
            P.tt("dve", s_[:], p_[:], c_[:], ALU.add, [p_, c_], [s_])
            P.cp("act", pr[:, 0, :], s_[:, 0:256], [s_], [pr])
            P.cp("act", pr[:, 1, :], s_[:, 512:768], [s_], [pr])
            k_, kn, st = kt_[b], kn_[b], st_[b]
            P.tt("dve", k_[:], s_[:, 256:512], kkb[:], ALU.mult, [s_, kkb], [k_])
            P.tt("pool", sq_[:], k_[:], k_[:], ALU.mult, [k_], [sq_])
            P.red("dve", st[:, 0:4], sq_[:].rearrange("p (h d) -> p h d", h=4), ALU.add, [sq_], [st])
            P.act("act", st[:, 4:8], st[:, 0:4], AF.Sqrt, [st, P.consts[1e-12]], [st], bias=P.consts[1e-12][:, 0:1], scale=1.0)
            P.op("dve", lambda e, st=st: e.reciprocal(st[:, 8:12], st[:, 4:8]), [st], [st])
            P.tt("dve", kn[:].rearrange("p (h d) -> p h d", h=4), k_[:].rearrange("p (h d) -> p h d", h=4),
                 st[:, 8:12].unsqueeze(2).to_broadcast([128, 4, 64]), ALU.mult, [k_, st], [kn])
            P.op("act", lambda e, pr=pr, kn=kn: e.mul(pr[:, 2, :], kn[:], -1.0), [kn], [pr])
            pt = psum[i % 2]
            P.tr(pt[:, 0:128], s_[:, 768:896], ident[:], [s_, ident], [pt])
            P.tr(pt[:, 128:256], s_[:, 896:1024], ident[:], [s_, ident], [pt])
            tw, tg = TWA[b], TG[b]
            P.act("act", tw[0:64, :], pt[0:64, 0:128], AF.Tanh, [pt], [tw])
            P.cp("act", tw[64:128, :], pt[64:128, 0:128], [pt], [tw])
            P.act("act", tg[:, :], pt[:, 128:256], AF.Sigmoid, [pt], [tg])
            pgp = psum[6]
            P.mm(pgp[:, 0:256], tg[:, :], gup[:, :], True, True, [tg, gup], [pgp])
            P.cp("act", pr[:, 9, :], pgp[:, 0:256], [pgp], [pr])
            for d in range(2):
                pw, pa = psum[2 + d * 2], psum[3 + d * 2]
                P.mm(pw[:, 0:256], tw[0:64, :], wa[0:64, d, :], True, True, [tw, wa], [pw])
                P.mm(pa[:, 0:256], tw[64:128, :], wa[64:128, d, :], True, True, [tw, wa], [pa])
                w_, a_ = xw[d], xa[d]
                P.tt("dve", w_[:], pw[:, 0:256], w0b[:, d, :], ALU.add, [pw, w0b], [w_])
                P.act("act", w_[:], w_[:], AF.Sigmoid, [w_], [w_])
                P.op("act", lambda e, pr=pr, w_=w_, d=d: e.mul(pr[:, 3 + 3 * d, :], w_[:], C05), [w_], [pr])
                P.tt("dve", a_[:], pa[:, 0:256], a0b[:, d, :], ALU.add, [pa, a0b], [a_])
                P.act("act", a_[:], a_[:], AF.Sigmoid, [a_], [a_])
                P.tt("pool", pr[:, 5 + 3 * d, :], kn[:], a_[:], ALU.mult, [kn, a_], [pr])
                P.stt("dve", a_[:], a_[:], -1.0, kab[:], ALU.add, ALU.mult, [a_, kab], [a_])
                P.stt("dve", pr[:, 4 + 3 * d, :], a_[:], 1.0, s_[:, 256:512], ALU.add, ALU.mult, [a_, s_], [pr])
            P.dma("pool", pre_d[t0:t0 + 128, :], pr[:].rearrange("p s f -> p (s f)"), [pr], [pre_res[i]])
    import os
    if os.environ.get("RW_STAGE", "9") == "1":
        return
    with P.phase():
        yacc = P.sb("yacc", [128, NT, 256])
        emit_scan(P, dram, psum, ident, pre_d, pre_res,
                  lambda d: [(0, 1, 0), (2, 1, 256), (4, 1, 512), (3, 1, 768 + 768 * d), (1, 1, 1024 + 768 * d), (5, 1, 1280 + 768 * d)],
                  True, yacc)
        lnw = P.sb("lnw", [128, 256])
        lnb = P.sb("lnb", [128, 256])
        rkb = P.sb("rkb", [128, 256])
        fin = [P.sb("fin%d" % i, [128, 5, 256]) for i in range(2)]
        yc = [P.sb("yc%d" % i, [128, 256]) for i in range(2)]
        sq = [P.sb("sq%d" % i, [128, 256]) for i in range(2)]
        sf = [P.sb("sf%d" % i, [128, 24]) for i in range(2)]
        P.dma("sp", lnw[:], dram["rw_ln_w"][L].partition_broadcast(128), (), [lnw])
        P.dma("sp", lnb[:], dram["rw_ln_b"][L].partition_broadcast(128), (), [lnb])
        P.dma("sp", rkb[:], dram["rw_r_k"][L].partition_broadcast(128), (), [rkb])
        v4 = lambda ap: ap.rearrange("p (h d) -> p h d", h=4)
        bc4 = lambda ap: ap.unsqueeze(2).to_broadcast([128, 4, 64])
        for i in (range(NT) if need_ctx else range(2, NT)):
            b = i % 2
            f_, y_, s_, t_ = fin[b], yc[b], sq[b], sf[b]
            t0 = i * 128
            P.dma("sp", f_[:, 0:2, :].rearrange("p s f -> p (s f)"), pre_d[t0:t0 + 128, 0:512], [pre_res[i]], [f_])
            P.dma("pool", f_[:, 2, :], pre_d[t0:t0 + 128, 1024:1280], [pre_res[i]], [f_])
            P.dma("pool", f_[:, 3, :], pre_d[t0:t0 + 128, 1792:2048], [pre_res[i]], [f_])
            P.dma("sp", f_[:, 4, :], pre_d[t0:t0 + 128, 2304:2560], [pre_res[i]], [f_])
            yv = yacc[:, i, :]
            P.red("dve", t_[:, 0:4], v4(yv), ALU.add, [yacc], [t_])
            P.op("act", lambda e, t_=t_: e.mul(t_[:, 4:8], t_[:, 0:4], -1.0 / 64), [t_], [t_])
            P.tt("dve", v4(y_[:]), v4(yv), bc4(t_[:, 4:8]), ALU.add, [yacc, t_], [y_])
            P.tt("pool", s_[:], y_[:], y_[:], ALU.mult, [y_], [s_])
            P.red("dve", t_[:, 8:12], v4(s_[:]), ALU.add, [s_], [t_])
            P.act("act", t_[:, 12:16], t_[:, 8:12], AF.Sqrt, [t_, P.consts[64e-5]], [t_], bias=P.consts[64e-5][:, 0:1], scale=1.0 / 64)
            P.op("dve", lambda e, t_=t_: e.reciprocal(t_[:, 16:20], t_[:, 12:16]), [t_], [t_])
            P.tt("dve", v4(y_[:]), v4(y_[:]), bc4(t_[:, 16:20]), ALU.mult, [y_, t_], [y_])
            P.tt("pool", y_[:], y_[:], lnw[:], ALU.mult, [y_, lnw], [y_])
            P.tt("pool", y_[:], y_[:], lnb[:], ALU.add, [y_, lnb], [y_])
            P.tt("pool", s_[:], f_[:, 2, :], f_[:, 3, :], ALU.add, [f_], [s_])
            P.tt("dve", s_[:], s_[:], f_[:, 0, :], ALU.mult, [s_, f_], [s_])
            P.tt("pool", s_[:], s_[:], rkb[:], ALU.mult, [s_, rkb], [s_])
            P.red("dve", t_[:, 20:24], v4(s_[:]), ALU.add, [s_], [t_])
            P.tt("dve", v4(s_[:]), v4(f_[:, 1, :]), bc4(t_[:, 20:24]), ALU.mult, [f_, t_, s_], [s_])
            P.tt("pool", y_[:], y_[:], s_[:], ALU.add, [y_, s_], [y_])
            P.tt("pool", y_[:], y_[:], f_[:, 4, :], ALU.mult, [y_, f_], [y_])
            P.dma("sp", y_d[t0:t0 + 128, 512:768], y_[:], [y_], [y_res[i]])


N_CORES = 8
SCR_SPECS = {"gla_pre": 1280, "rw_pre": 2560}
W_SPECS = [
    ("w_mod", (D, 6 * D)), ("b_mod", (1, 6 * D)),
    ("g_mix_pre", (1, D)), ("g_mix_post", (1, D)), ("g_ffn_pre", (1, D)), ("g_ffn_post", (1, D)),
    ("w_in", (D, DIN)), ("w_out", (D, D)),
    ("ffn_w_up", (D, 2 * DFF)), ("ffn_conv_w", (3, 2 * DFF)), ("ffn_conv_b", (1, 2 * DFF)), ("ffn_w_down", (DFF, D)),
]


def build_program(NL=NL_FULL, dbg=None):
    dbg = dbg or {}
    NLW = dbg.get("nlw", NL_FULL)
    nc = bass.Bass("TRN2", target_bir_lowering=False)
    dram = {}

    def dtens(name, shape, dt=F32, kind="Internal"):
        return nc.dram_tensor(name, list(shape), dt, kind=kind).ap()

    def kind_of(tag, default="Internal"):
        if tag + "_in" in dbg:
            return "ExternalInput"
        if tag in dbg:
            return "ExternalOutput"
        return default

    dram["xall"] = dtens("xall", [TALL, D], kind="ExternalInput")
    dram["c2"] = dtens("c2", [2, D], kind="ExternalInput")
    dram["ident"] = dtens("ident", [128, 128], kind="ExternalInput")
    class LazyDram(dict):
        def __missing__(self, name):
            for nm, shp in W_SPECS:
                if nm == name:
                    self[name] = dtens(name, [NLW] + list(shp), kind="ExternalInput")
                    return self[name]
            for nm, shp, per_layer in MIX_SPECS:
                if nm == name:
                    self[name] = dtens(name, ([NLW] if per_layer else []) + list(shp), kind="ExternalInput")
                    return self[name]
            if name in SCR_SPECS:
                self[name] = dtens(name + "_scr", [TALL, SCR_SPECS[name]], kind=kind_of(name))
                return self[name]
            raise KeyError(name)

        def res(self, name):
            if not hasattr(self, "_res"):
                self._res = {}
            if name not in self._res:
                self._res[name] = [Res("%s%d" % (name, i)) for i in range(NT)]
            return self._res[name]
    dram = LazyDram(dram)
    out_ap = dtens("out", [TLAT, D], kind="ExternalOutput")
    z_d = dtens("z_scr", [TALL, DIN], kind=kind_of("z"))
    y_d = dtens("y_scr", [TALL, D], kind=kind_of("y"))
    xs_d = dtens("xs_scr", [TALL, D], kind=kind_of("xs"))
    mods_d = dtens("mods_scr", [NL_FULL, 2, 6 * D], kind=kind_of("mods"))
    h2T_d = dtens("h2T_scr", [128, 8, TALL], BF16)
    z_res = [Res("z%d" % i) for i in range(NT)]
    y_res = [Res("y%d" % i) for i in range(NT)]
    xs_res = [Res("x%d" % i) for i in range(NT)]
    h2_res = [Res("h2_%d" % i) for i in range(NT)]
    mods_res = [Res("mods%d" % i) for i in range(NL_FULL)]

    with contextlib.ExitStack() as st:
        P = Prog(nc, st)
        ident = P.sb("ident", [128, 128])
        P.dma("sp", ident[:], dram["ident"], (), [ident])
        psum = [P.ps("ps%d" % i) for i in range(8)]
        P.consts = {}
        for cv in (EPS, 1e-12, 64e-5):
            ct = P.sb("const%d" % len(P.consts), [128, 1])
            P.memset("pool", ct[:], cv, [ct])
            P.consts[cv] = ct
        c2T = P.sb("c2T", [128, 8, 2])
        for v in range(2):
            P.dma("sp", c2T[:, :, v], dram["c2"][v].rearrange("(kt p) -> p kt", p=128), (), [c2T],
                  allow_slow_non_contiguous=True)
        P.act("act", c2T[:], c2T[:], AF.Silu, [c2T], [c2T])

        def bcast_load(q, tile, row_ap, reads=()):
            P.dma(q, tile[:], row_ap.partition_broadcast(128), reads, [tile])

        def rms_tile(P, src, st_t, junk, A, Bv, dst, ncols=D):
            P.act("act", junk[:, 0:ncols], src[:, 0:ncols], AF.Square, [src], [junk], accum_out=st_t[:, 0:1])
            P.rstd(st_t, 1.0 / ncols, EPS, [junk])
            P.stt("dve", dst[:, 0:ncols], src[:, 0:ncols], st_t[:, 2:3], A[:, 0:ncols], ALU.mult, ALU.mult, [src, st_t, A], [dst])
            if Bv is not None:
                P.tt("pool", dst[:, 0:ncols], dst[:, 0:ncols], Bv[:, 0:ncols], ALU.add, [dst, Bv], [dst])

        def transpose8(P, src, dstT, banks):
            for half in range(2):
                pst = banks[half]
                for k4 in range(4):
                    kt = half * 4 + k4
                    P.tr(pst[:, k4 * 128:(k4 + 1) * 128], src[:, kt * 128:(kt + 1) * 128], ident[:], [src, ident], [pst])
                P.cp("act", dstT[:, half * 4:(half + 1) * 4, :], pst[:, :].rearrange("p (k t) -> p k t", k=4), [pst], [dstT])

        def load_cast(P, dst_ap, dst_t, src_ap, wst, idx, ncols):
            w = wst[idx % 2]
            P.dma("sp" if idx % 2 == 0 else "pool", w[:, 0:ncols], src_ap, (), [w])
            P.cp("act" if idx % 2 == 0 else "dve", dst_ap, w[:, 0:ncols], [w], [dst_t])

        for L in range(NL):
            need_ctx = L < NL_FULL - 1
            last = (L == NL - 1)
            mrow = lambda v, k: mods_d[L, v, k * D:(k + 1) * D]
            if "skip_p0" not in dbg:
              with P.phase():
                wst = [P.sb("wst%d" % i, [128, 512]) for i in range(2)]
                brow = P.sb("brow", [2, 6 * D])
                mrow_sb = [P.sb("mrow%d" % i, [2, 512]) for i in range(2)]
                P.dma("sp", brow[0:1, :], dram["b_mod"][L], (), [brow])
                P.dma("sp", brow[1:2, :], dram["b_mod"][L], (), [brow])
                for nb in range(12):
                    pst = psum[nb % 2]
                    for kt in range(8):
                        w = wst[(nb * 8 + kt) % 2]
                        P.dma("sp" if kt % 2 == 0 else "pool", w[:, 0:512],
                              dram["w_mod"][L, kt * 128:(kt + 1) * 128, nb * 512:(nb + 1) * 512], (), [w])
                        P.mm(pst[0:2, :], c2T[:, kt, :], w[:, 0:512], kt == 0, kt == 7, [c2T, w], [pst])
                    ms = mrow_sb[nb % 2]
                    P.tt("dve", ms[:, :], pst[0:2, :], brow[:, nb * 512:(nb + 1) * 512], ALU.add, [pst, brow], [ms])
                    P.dma("sp", mods_d[L, :, nb * 512:(nb + 1) * 512], ms[:, :], [ms], [mods_res[L]])

            if "skip_p1" not in dbg:
              with P.phase():
                x_src = dram["xall"] if L == 0 else xs_d
                wst = [P.sb("wst%d" % i, [128, 2048]) for i in range(2)]
                gB = P.sb("gB", [128, D])
                A1 = P.sb("A1", [128, D])
                B1 = P.sb("B1", [128, D])
                win_sb = P.sb("win", [128, 8, DIN], BF16)
                xt = [P.sb("xt%d" % i, [128, D]) for i in range(2)]
                xn = [P.sb("xn%d" % i, [128, D]) for i in range(2)]
                hT = [P.sb("hT%d" % i, [128, 8, 128], BF16) for i in range(2)]
                zt = [P.sb("zt%d" % i, [128, DIN]) for i in range(2)]
                st1 = [P.sb("st1_%d" % i, [128, 4]) for i in range(2)]
                junk = P.sb("junk", [128, D])
                bcast_load("sp", gB, dram["g_mix_pre"][L])
                idx = 0
                for kt in range(8):
                    for c0, cw in ((0, 2048), (2048, DIN - 2048)):
                        load_cast(P, win_sb[:, kt, c0:c0 + cw], win_sb, dram["w_in"][L, kt * 128:(kt + 1) * 128, c0:c0 + cw], wst, idx, cw)
                        idx += 1
                for i in range(NT):
                    v = 1 if i < 2 else 0
                    if i == 0 or i == 2:
                        bcast_load("sp", A1, mrow(v, 1), [mods_res[L]])
                        P.stt("dve", A1[:], A1[:], 1.0, gB[:], ALU.add, ALU.mult, [A1, gB], [A1])
                        bcast_load("sp", B1, mrow(v, 0), [mods_res[L]])
                    b = i % 2
                    P.dma("sp", xt[b][:], x_src[i * 128:(i + 1) * 128, :], [xs_res[i]], [xt[b]])
                    rms_tile(P, xt[b], st1[b], junk, A1, B1, xn[b])
                    transpose8(P, xn[b], hT[b], psum[0:2])
                    for cb in range(6):
                        c0 = cb * 512
                        cw = min(512, DIN - c0)
                        pst = psum[2 + cb]
                        for kt in range(8):
                            P.mm(pst[:, 0:cw], hT[b][:, kt, :], win_sb[:, kt, c0:c0 + cw], kt == 0, kt == 7, [hT[b], win_sb], [pst])
                        P.cp("dve" if cb % 2 == 0 else "act", zt[b][:, c0:c0 + cw], pst[:, 0:cw], [pst], [zt[b]])
                    P.dma("pool", z_d[i * 128:(i + 1) * 128, :], zt[b][:], [zt[b]], [z_res[i]])

            if "skip_mix" not in dbg:
                emit_mixers(P, L, need_ctx, dram, z_d, z_res, y_d, y_res, psum, ident, dbg)

            tiles5 = list(range(NT)) if need_ctx else list(range(2, NT))
            x_src = dram["xall"] if L == 0 else xs_d
            if "skip_p5" not in dbg:
              with P.phase():
                wst = [P.sb("wst%d" % i, [128, 1024]) for i in range(2)]
                wout_sb = P.sb("wout", [128, 8, D], BF16)
                gB = P.sb("gB", [128, D])
                G1 = P.sb("G1", [128, D])
                A2 = P.sb("A2", [128, D])
                B2 = P.sb("B2", [128, D])
                yt = [P.sb("yt%d" % i, [128, D]) for i in range(2)]
                xt = [P.sb("xt%d" % i, [128, D]) for i in range(2)]
                yT = [P.sb("yT%d" % i, [128, 8, 128], BF16) for i in range(2)]
                tmp = [P.sb("tmp%d" % i, [128, D]) for i in range(2)]
                xw = [P.sb("xw%d" % i, [128, D]) for i in range(2)]
                h2 = [P.sb("h2%d" % i, [128, D]) for i in range(2)]
                h2T = [P.sb("h2T%d" % i, [128, 8, 128], BF16) for i in range(2)]
                st5 = [P.sb("st5_%d" % i, [128, 8]) for i in range(2)]
                junk = P.sb("junk", [128, D])
                for kt in range(8):
                    load_cast(P, wout_sb[:, kt, :], wout_sb, dram["w_out"][L, kt * 128:(kt + 1) * 128, :], wst, kt, D)
                def p5_a(i):
                    v = 1 if i < 2 else 0
                    if i == tiles5[0] or i == 2:
                        bcast_load("sp", gB, dram["g_mix_post"][L])
                        bcast_load("sp", G1, mrow(v, 2), [mods_res[L]])
                        P.tt("dve", G1[:], G1[:], gB[:], ALU.mult, [G1, gB], [G1])
                        bcast_load("sp", gB, dram["g_ffn_pre"][L])
                        bcast_load("sp", A2, mrow(v, 4), [mods_res[L]])
                        P.stt("dve", A2[:], A2[:], 1.0, gB[:], ALU.add, ALU.mult, [A2, gB], [A2])
                        bcast_load("sp", B2, mrow(v, 3), [mods_res[L]])
                    b = i % 2
                    P.dma("sp", yt[b][:], y_d[i * 128:(i + 1) * 128, :], [y_res[i]], [yt[b]])
                    P.dma("pool", xt[b][:], x_src[i * 128:(i + 1) * 128, :], [xs_res[i]], [xt[b]])
                    transpose8(P, yt[b], yT[b], psum[0:2])
                    pb = [psum[2 + 2 * b], psum[3 + 2 * b]]
                    for half in range(2):
                        for kt in range(8):
                            P.mm(pb[half][:, :], yT[b][:, kt, :], wout_sb[:, kt, half * 512:(half + 1) * 512], kt == 0, kt == 7,
                                 [yT[b], wout_sb], [pb[half]])
                        P.act("act", junk[:, 0:512], pb[half][:, :], AF.Square, [pb[half]], [junk], accum_out=st5[b][:, 3 + half:4 + half])
                    P.tt("dve", st5[b][:, 0:1], st5[b][:, 3:4], st5[b][:, 4:5], ALU.add, [st5[b], junk], [st5[b]])
                    P.rstd(st5[b], 1.0 / D, EPS)
                    for half in range(2):
                        hs = slice(half * 512, (half + 1) * 512)
                        P.stt("dve", tmp[b][:, hs], pb[half][:, :], st5[b][:, 2:3], G1[:, hs], ALU.mult, ALU.mult,
                              [pb[half], st5[b], G1], [tmp[b]])
                    P.tt("pool", xw[b][:], tmp[b][:], xt[b][:], ALU.add, [tmp[b], xt[b]], [xw[b]])
                    P.dma("pool", xs_d[i * 128:(i + 1) * 128, :], xw[b][:], [xw[b]], [xs_res[i]])
                    rms_tile(P, xw[b], st5[b], junk, A2, B2, h2[b])

                def p5_b(i):
                    b = i % 2
                    transpose8(P, h2[b], h2T[b], psum[6:8])
                    P.dma("sp", h2T_d[:, :, i * 128:(i + 1) * 128], h2T[b][:], [h2T[b]], [h2_res[i]])

                for k in range(len(tiles5) + 1):
                    if k < len(tiles5):
                        p5_a(tiles5[k])
                    if k >= 1:
                        p5_b(tiles5[k - 1])

            if "skip_p6" not in dbg:
              with P.phase():
                wst = [P.sb("wst%d" % i, [128, 1024]) for i in range(2)]
                wup_sb = P.sb("wup", [128, 8, 2 * DFF], BF16)
                wdn_sb = P.sb("wdn", [128, 22, D], BF16)
                cwt = P.sb("cwt", [128, 3, 44])
                cbt = P.sb("cbt", [128, 44])
                gB = P.sb("gB", [128, D])
                G2 = P.sb("G2", [128, D])
                h2blk = [P.sb("h2blk%d" % i, [128, 8, 258], BF16) for i in range(2)]
                cv = [[P.sb("cv%d_%d" % (i, j), [128, 256]) for j in range(2)] for i in range(2)]
                aT = P.sb("aT", [128, 22, 256], BF16)
                xt = [P.sb("xt%d" % i, [128, D]) for i in range(2)]
                tmp = [P.sb("tmp%d" % i, [128, D]) for i in range(2)]
                st6 = [P.sb("st6_%d" % i, [128, 8]) for i in range(2)]
                junk = P.sb("junk", [128, 512])
                idx = 0
                for kt in range(8):
                    for c0 in range(0, 2 * DFF, 1024):
                        cw = min(1024, 2 * DFF - c0)
                        load_cast(P, wup_sb[:, kt, c0:c0 + cw], wup_sb, dram["ffn_w_up"][L, kt * 128:(kt + 1) * 128, c0:c0 + cw], wst, idx, cw)
                        idx += 1
                for j in range(22):
                    load_cast(P, wdn_sb[:, j, :], wdn_sb, dram["ffn_w_down"][L, j * 128:(j + 1) * 128, :], wst, idx, D)
                    idx += 1
                for tap in range(3):
                    P.dma("sp", cwt[:, tap, :], dram["ffn_conv_w"][L, tap].rearrange("(j p) -> p j", p=128), (), [cwt],
                          allow_slow_non_contiguous=True)
                P.dma("sp", cbt[:, :], dram["ffn_conv_b"][L, 0].rearrange("(j p) -> p j", p=128), (), [cbt],
                      allow_slow_non_contiguous=True)
                blocks = list(range(17)) if need_ctx else list(range(1, 17))
                ucount = 0
                for bi in blocks:
                    v = 1 if bi == 0 else 0
                    if bi == blocks[0] or bi == 1:
                        bcast_load("sp", gB, dram["g_ffn_post"][L])
                        bcast_load("sp", G2, mrow(v, 5), [mods_res[L]])
                        P.tt("dve", G2[:], G2[:], gB[:], ALU.mult, [G2, gB], [G2])
                    t0 = bi * 256
                    hb = h2blk[bi % 2]
                    lval = t0 not in (0, TCTX)
                    rval = (t0 + 256) not in (TCTX, TALL)
                    if not lval:
                        P.memset("pool", hb[:, :, 0:1], 0.0, [hb])
                    if not rval:
                        P.memset("pool", hb[:, :, 257:258], 0.0, [hb])
                    a = t0 - 1 if lval else t0
                    e = t0 + 257 if rval else t0 + 256
                    rtiles = sorted(set([a // 128, (e - 1) // 128, t0 // 128, t0 // 128 + 1]))
                    P.dma("sp", hb[:, :, a - (t0 - 1):e - (t0 - 1)], h2T_d[:, :, a:e], [h2_res[r] for r in rtiles], [hb])
                    for j in range(22):
                        cs = cv[j % 2]
                        for part in range(2):
                            fi = part * 22 + j
                            f0 = part * DFF + j * 128
                            pst = psum[ucount % 4]
                            ucount += 1
                            for kt in range(8):
                                P.mm(pst[:, 0:258], wup_sb[:, kt, f0:f0 + 128], hb[:, kt, :], kt == 0, kt == 7, [wup_sb, hb], [pst])
                            c = cs[part]
                            P.act("act", c[:], pst[:, 1:257], AF.Identity, [pst, cwt, cbt], [c], bias=cbt[:, fi:fi + 1], scale=cwt[:, 1, fi:fi + 1])
                            P.stt("dve", c[:], pst[:, 0:256], cwt[:, 0, fi:fi + 1], c[:], ALU.mult, ALU.add, [pst, cwt, c], [c])
                            P.stt("dve", c[:], pst[:, 2:258], cwt[:, 2, fi:fi + 1], c[:], ALU.mult, ALU.add, [pst, cwt, c], [c])
                        P.act("act", cs[1][:], cs[1][:], AF.Silu, [cs[1]], [cs[1]])
                        P.tt("pool", aT[:, j, :], cs[0][:], cs[1][:], ALU.mult, [cs[0], cs[1]], [aT])
                    for s in range(2):
                        i = bi * 2 + s
                        b = s
                        pb = [psum[4 + 2 * s], psum[5 + 2 * s]]
                        P.dma("pool", xt[b][:], xs_d[i * 128:(i + 1) * 128, :], [xs_res[i]], [xt[b]])
                        for half in range(2):
                            for j in range(22):
                                P.mm(pb[half][:, :], aT[:, j, s * 128:(s + 1) * 128], wdn_sb[:, j, half * 512:(half + 1) * 512],
                                     j == 0, j == 21, [aT, wdn_sb], [pb[half]])
                            P.act("act", junk[:, 0:512], pb[half][:, :], AF.Square, [pb[half]], [junk], accum_out=st6[b][:, 3 + half:4 + half])
                        P.tt("dve", st6[b][:, 0:1], st6[b][:, 3:4], st6[b][:, 4:5], ALU.add, [st6[b], junk], [st6[b]])
                        P.rstd(st6[b], 1.0 / D, EPS)
                        for half in range(2):
                            hs = slice(half * 512, (half + 1) * 512)
                            P.stt("dve", tmp[b][:, hs], pb[half][:, :], st6[b][:, 2:3], G2[:, hs], ALU.mult, ALU.mult,
                                  [pb[half], st6[b], G2], [tmp[b]])
                        P.tt("pool", tmp[b][:], tmp[b][:], xt[b][:], ALU.add, [tmp[b], xt[b]], [tmp[b]])
                        if last and i >= 2:
                            P.dma("sp", out_ap[(i - 2) * 128:(i - 1) * 128, :], tmp[b][:], [tmp[b]], ())
                        else:
                            P.dma("sp", xs_d[i * 128:(i + 1) * 128, :], tmp[b][:], [tmp[b]], [xs_res[i]])

        P.flush(final=True)
    return nc


def host_inputs(inputs, b):
    m = {}
    m["xall"] = np.ascontiguousarray(np.concatenate([inputs["ctx"][b], inputs["x"][b]], axis=0), dtype=np.float32)
    m["c2"] = np.ascontiguousarray(np.stack([inputs["c"][b], inputs["c_ctx"]], axis=0), dtype=np.float32)
    m["ident"] = np.eye(128, dtype=np.float32)
    for name, shp in W_SPECS:
        m[name] = np.ascontiguousarray(np.asarray(inputs[name], dtype=np.float32).reshape([NL_FULL] + list(shp)))
    mix_host_inputs(inputs, m)
    return m


def used_inputs(nc, m):
    shapes = {}
    for alloc in nc.allocations:
        try:
            if alloc.kind == "ExternalInput":
                shapes[alloc.memorylocations[0].name] = tuple(alloc.tensor_shape)
        except Exception:
            pass
    out = {}
    for k, v in m.items():
        if k in shapes:
            shp = shapes[k]
            if tuple(v.shape) != shp:
                v = np.ascontiguousarray(v[0:shp[0]])
            assert tuple(v.shape) == shp, (k, v.shape, shp)
            out[k] = v
    return out


def kernel(**inputs):
    inputs = {k: np.asarray(v) for k, v in inputs.items()}
    nc = build_program()
    shared = host_inputs(inputs, 0)
    maps = []
    for b in range(4):
        m = dict(shared)
        m["xall"] = np.ascontiguousarray(np.concatenate([inputs["ctx"][b], inputs["x"][b]], axis=0), dtype=np.float32)
        m["c2"] = np.ascontiguousarray(np.stack([inputs["c"][b], inputs["c_ctx"]], axis=0), dtype=np.float32)
        maps.append(used_inputs(nc, m))
    in_maps = [maps[b % 4] for b in range(N_CORES)]
    res = run_bass_kernel_spmd(nc, in_maps, core_ids=list(range(N_CORES)))
    out = np.stack([np.asarray(res.results[b]["out"]) for b in range(4)], axis=0)
    return out.astype(np.float32)
```

```python
import contextlib
import numpy as np
import concourse.bass as bass
import concourse.mybir as mybir
from concourse.bass_utils import run_bass_kernel_spmd

F32 = mybir.dt.float32
BF16 = mybir.dt.bfloat16
ALU = mybir.AluOpType
AF = mybir.ActivationFunctionType
AX = mybir.AxisListType

D = 1024
NL_FULL = 4
TCTX = 256
TLAT = 4096
TALL = TCTX + TLAT
NT = TALL // 128
DIN = 2928
DFF = 2816
EPS = 1e-6


class Res:
    __slots__ = ("name", "w", "r")

    def __init__(self, name=""):
        self.name = name
        self.w = None
        self.r = {}


class T:
    def __init__(self, handle, name):
        self.h = handle
        self.res = Res(name)

    def __getitem__(self, k):
        return self.h[k]


class Prog:
    EPOCH = 30000
    KD = 6

    def __init__(self, nc, st):
        self.nc = nc
        self.st = st
        self.engs = ["pe", "act", "dve", "pool", "sp"]
        self.ops = {e: [] for e in self.engs}
        self.cnt = {e: 0 for e in self.engs}
        self.seen = {e: {} for e in self.engs}
        self.dcnt = {e: 0 for e in self.engs}
        self.sems = {}
        self.n_ops = 0
        self.gst = st
        self.last_ev = {}
        self.pending = {e: [] for e in self.engs}

    def sb(self, name, shape, dt=F32):
        self.n_ops += 1
        h = self.st.enter_context(self.nc.sbuf_tensor("sb%d_%s" % (self.n_ops, name), list(shape), dt))
        return T(h, name)

    def ps(self, name, shape=(128, 512), dt=F32):
        h = self.st.enter_context(self.nc.psum_tensor("ps_" + name, list(shape), dt))
        return T(h, name)

    def _sem(self, key):
        if key not in self.sems:
            self.sems[key] = self.gst.enter_context(self.nc.semaphore("s_" + "_".join(str(k) for k in key)))
        self.last_ev[key] = max(self.last_ev.get(key, 0), 0)
        return self.sems[key]

    def barrier(self):
        evs = [(k, v) for k, v in self.last_ev.items() if v > 0]
        for e in self.engs:
            self.pending[e] = list(evs)

    def barrier_light(self, ress):
        evs = [r.w for r in ress if r.w is not None]
        for e in self.engs:
            self.pending[e] = self.pending[e] + list(evs)

    @contextlib.contextmanager
    def phase(self):
        outer = self.st
        with contextlib.ExitStack() as ph:
            self.st = ph
            yield
            self.barrier()
            self.flush()
        self.st = outer

    def _deps(self, eng, reads, writes, is_dma):
        evs = []
        for ev in self.pending[eng]:
            evs.append((ev, "bar"))
        self.pending[eng] = []
        for t in reads:
            r = t.res if isinstance(t, T) else t
            if r.w is not None:
                evs.append((r.w, "raw"))
        for t in writes:
            r = t.res if isinstance(t, T) else t
            if r.w is not None:
                evs.append((r.w, "waw"))
            for ev in r.r.values():
                evs.append((ev, "war"))
        waits = {}
        for (key, val), kind in evs:
            if key[0] == "e" and key[1] == eng and not is_dma:
                if kind != "raw" or eng == "pe":
                    continue
            if self.seen[eng].get(key, 0) >= val:
                continue
            if waits.get(key, 0) < val:
                waits[key] = val
        for k, v in waits.items():
            self.seen[eng][k] = v
        return list(waits.items())

    def _mark(self, ev, reads, writes):
        for t in reads:
            r = t.res if isinstance(t, T) else t
            r.r[ev[0]] = ev
        for t in writes:
            r = t.res if isinstance(t, T) else t
            r.w = ev
            r.r = {}

    def op(self, eng, fn, reads=(), writes=()):
        waits = self._deps(eng, reads, writes, False)
        n = self.cnt[eng]
        key = ("e", eng, n // self.EPOCH)
        ev = (key, n % self.EPOCH + 1)
        self.cnt[eng] = n + 1
        self._sem(key)
        self.last_ev[key] = ev[1]
        self.ops[eng].append((waits, fn, key, 1))
        self._mark(ev, reads, writes)
        self.n_ops += 1

    def dma(self, q, out, in_, reads=(), writes=(), **kw):
        waits = self._deps(q, reads, writes, True)
        j = self.dcnt[q]
        self.dcnt[q] = j + 1
        key = ("d", q, j % self.KD)
        val = 16 * (j // self.KD + 1)
        if j >= self.KD and self.seen[q].get(key, 0) < val - 16:
            waits.append((key, val - 16))
            self.seen[q][key] = val - 16
        self._sem(key)
        self.last_ev[key] = val
        self.ops[q].append((waits, lambda e: e.dma_start(out=out, in_=in_, **kw), key, 16))
        self._mark((key, val), reads, writes)
        self.n_ops += 1

    def mm(self, out, lhsT, rhs, start, stop, reads, writes):
        self.op("pe", lambda e: e.matmul(out, lhsT, rhs, start=start, stop=stop), reads, writes)

    def tr(self, out, in_, ident, reads, writes):
        self.op("pe", lambda e: e.transpose(out, in_, ident), reads, writes)

    def act(self, eng, out, in_, func, reads, writes, bias=None, scale=None, accum_out=None):
        kw = {}
        if bias is not None:
            kw["bias"] = bias
        if scale is not None:
            kw["scale"] = scale
        if accum_out is not None:
            kw["accum_out"] = accum_out
        self.op(eng, lambda e: e.activation(out, in_, func, **kw), reads, writes)

    def tt(self, eng, out, in0, in1, op, reads, writes):
        self.op(eng, lambda e: e.tensor_tensor(out, in0, in1, op), reads, writes)

    def ts(self, eng, out, in0, s1, s2, op0, op1, reads, writes, accum_out=None):
        if op1 is None:
            self.op(eng, lambda e: e.tensor_scalar(out, in0, s1, None, op0), reads, writes)
        elif accum_out is None:
            self.op(eng, lambda e: e.tensor_scalar(out, in0, s1, s2, op0, op1), reads, writes)
        else:
            self.op(eng, lambda e: e.tensor_scalar(out, in0, s1, s2, op0, op1, accum_out), reads, writes)

    def stt(self, eng, out, in0, scalar, in1, op0, op1, reads, writes):
        self.op(eng, lambda e: e.scalar_tensor_tensor(out, in0, scalar, in1, op0, op1), reads, writes)

    def cp(self, eng, out, in_, reads, writes):
        if eng == "act":
            self.op(eng, lambda e: e.copy(out, in_), reads, writes)
        else:
            self.op(eng, lambda e: e.tensor_copy(out, in_), reads, writes)

    def red(self, eng, out, in_, op, reads, writes, axis=AX.X):
        self.op(eng, lambda e: e.tensor_reduce(out, in_, axis, op), reads, writes)

    def rstd(self, t, scale, eps, extra_reads=()):
        self.act("act", t[:, 1:2], t[:, 0:1], AF.Sqrt, [t, self.consts[eps]] + list(extra_reads), [t], bias=self.eps_ap(eps, t), scale=scale)
        self.op("dve", lambda e: e.reciprocal(t[:, 2:3], t[:, 1:2]), [t], [t])

    def eps_ap(self, eps, t):
        return self.consts[eps][0:t.h.shape[0], 0:1]

    def sumsq(self, dst_col, srcs, junk, reads):
        raise NotImplementedError

    def memset(self, eng, ap, val, writes):
        self.op(eng, lambda e: e.memset(ap, val), (), writes)

    def flush(self, final=False):
        nc = self.nc
        fin = [(k, v) for k, v in self.last_ev.items() if v > 0] if final else []
        sems = self.sems
        engmap = {"pe": "tensor", "act": "scalar", "dve": "vector", "pool": "gpsimd", "sp": "sync"}
        ops = self.ops
        with nc.Block() as block:
            def make(ename):
                def body(e):
                    for waits, fn, key, inc in ops[ename]:
                        for wk, wv in waits:
                            e.wait_ge(sems[wk], wv)
                        fn(e).then_inc(sems[key], inc)
                    if ename == "sp":
                        for wk, wv in fin:
                            e.wait_ge(sems[wk], wv)
                return body
            for ename in self.engs:
                getattr(block, engmap[ename])(make(ename))
        self.ops = {e: [] for e in self.engs}


NEG = -30000.0


def emit_na(P, L, need_ctx, dram, z_d, z_res, y_d, y_res, psum, ident):
    with P.phase():
        QT = P.sb("naQT", [128, 2, TALL], BF16)
        KT = P.sb("naKT", [128, 2, TALL], BF16)
        Ve = P.sb("naVe", [128, NT, 4, 66], BF16)
        Vo = P.sb("naVo", [128, NT - 1, 4, 66], BF16)
        btab = P.sb("btab", [128, 4, 14, 64])
        maskt = P.sb("mask", [128, 64])
        zt = [P.sb("zt%d" % i, [128, 768]) for i in range(2)]
        zo = [P.sb("zo%d" % i, [128, 256]) for i in range(2)]
        sw = [P.sb("sw%d" % i, [128, 4, 64]) for i in range(2)]
        pT = [P.sb("pT%d" % i, [128, 6, 64], BF16) for i in range(2)]
        rc = [P.sb("rc%d" % i, [64, 4]) for i in range(2)]
        yo = [P.sb("yo%d" % i, [64, 4, 64]) for i in range(2)]
        import os
        stage = float(os.environ.get("NA_STAGE", "9"))
        if stage < 0.5:
            return
        P.dma("sp", btab[:], dram["na_btab"][L], (), [btab])
        P.dma("sp", maskt[:], dram["na_mask"], (), [maskt])
        P.tt("dve", btab[:].rearrange("p h d q -> p (h d) q"), btab[:].rearrange("p h d q -> p (h d) q"),
             maskt[:, :].unsqueeze(1).to_broadcast([128, 56, 64]), ALU.add, [btab, maskt], [btab])
        if stage < 0.7:
            return
        P.memset("pool", Ve[:, :, :, 64:65], 1.0, [Ve])
        P.memset("pool", Vo[:, :, :, 64:65], 1.0, [Vo])
        if stage < 0.9:
            return
        for i in range(NT):
            b = i % 2
            P.dma("sp", zt[b][:], z_d[i * 128:(i + 1) * 128, 0:768], [z_res[i]], [zt[b]])
            pst = psum[i % 2]
            for c in range(4):
                P.tr(pst[:, c * 128:(c + 1) * 128], zt[b][:, c * 128:(c + 1) * 128], ident[:], [zt[b], ident], [pst])
            pv = pst[:, :].rearrange("p (c t) -> p c t", c=4)
            if stage > 0.915:
                P.cp("act", QT[:, :, i * 128:(i + 1) * 128], pv[:, 0:2, :], [pst], [QT])
            if stage > 0.925:
                P.cp("act", KT[:, :, i * 128:(i + 1) * 128], pv[:, 2:4, :], [pst], [KT])
            if stage > 0.935:
                P.cp("dve", Ve[:, i, :, 0:64], zt[b][:, 512:768].rearrange("p (h d) -> p h d", h=4), [zt[b]], [Ve])
            if i < NT - 1 and stage > 0.945:
                P.dma("pool", zo[b][:], z_d[64 + i * 128:64 + (i + 1) * 128, 512:768], [z_res[i], z_res[i + 1]], [zo[b]])
                if stage > 0.955:
                    P.cp("dve", Vo[:, i, :, 0:64], zo[b][:].rearrange("p (h d) -> p h d", h=4), [zo[b]], [Vo])

        units = []

        def na_rows(q0, keyspec, bias_di0, out_row0):
            units.append((q0, keyspec, bias_di0, out_row0))

        def part_a(u, h):
            q0, keyspec, bias_di0, out_row0 = units[u]
            g, hp = h // 2, (h % 2) * 64
            pst = psum[(u * 4 + h) % 4]
            for c, (k0, Vt, ti) in enumerate(keyspec):
                P.mm(pst[:, c * 64:(c + 1) * 64], KT[hp:hp + 64, g, k0:k0 + 128], QT[hp:hp + 64, g, q0:q0 + 64], True, True,
                     [KT, QT], [pst])

        def part_b(u, h):
            q0, keyspec, bias_di0, out_row0 = units[u]
            nk = len(keyspec)
            pst = psum[(u * 4 + h) % 4]
            po = psum[4 + u % 2]
            p = pT[(u * 4 + h) % 2]
            c0 = 0
            if bias_di0 is not None:
                s_ = sw[(u * 4 + h) % 2]
                P.stt("dve", s_[:], pst[:, 0:256].rearrange("p (c q) -> p c q", c=4), 0.125,
                      btab[:, h, bias_di0:bias_di0 + 7:2, :], ALU.mult, ALU.add, [pst, btab], [s_])
                P.act("act", p[:, 0:4, :], s_[:], AF.Exp, [s_], [p])
                c0 = 4
            P.act("act", p[:, c0:nk, :], pst[:, c0 * 64:nk * 64].rearrange("p (c q) -> p c q", c=nk - c0), AF.Exp, [pst], [p],
                  scale=0.125)
            for c, (k0, Vt, ti) in enumerate(keyspec):
                P.mm(po[0:64, h * 65:(h + 1) * 65], p[:, c, :], Vt[:, ti, h, 0:65], c == 0, c == nk - 1, [p, Vt], [po])
            if h == 3:
                r_ = rc[u % 2]
                y_ = yo[u % 2]
                pov = po[0:64, 0:260].rearrange("p (h e) -> p h e", h=4)
                P.op("dve", lambda e: e.reciprocal(r_[:, :], pov[:, :, 64]), [po], [r_])
                P.tt("dve", y_[:], pov[:, :, 0:64], r_[:, :].unsqueeze(2).to_broadcast([64, 4, 64]), ALU.mult, [po, r_], [y_])
                P.dma("sp", y_d[out_row0:out_row0 + 64, 0:256], y_[:].rearrange("p h d -> p (h d)"), [y_], [y_res[out_row0 // 128]])

        def run_units():
            seq = [(u, h) for u in range(len(units)) for h in range(4)]
            LOOK = 2
            for k in range(len(seq) + LOOK):
                if k < len(seq):
                    part_a(*seq[k])
                if k >= LOOK:
                    part_b(*seq[k - LOOK])

        ctxkeys = [(0, Ve, 0), (128, Ve, 1)]
        if stage < 2:
            return
        if need_ctx:
            for qc in range(4):
                na_rows(qc * 64, ctxkeys, None, qc * 64)
        for r in range(64 if stage >= 3 else 0):
            w0 = min(max(r - 4, 0), 56)
            key0 = TCTX + w0 * 64
            ks = []
            for c in range(4):
                k0 = key0 + c * 128
                if k0 % 128 == 0:
                    ks.append((k0, Ve, k0 // 128))
                else:
                    ks.append((k0, Vo, (k0 - 64) // 128))
            na_rows(TCTX + r * 64, ks + ctxkeys, w0 - r + 7, TCTX + r * 64)
        run_units()


def emit_mla(P, L, need_ctx, dram, z_d, z_res, y_d, y_res, psum, ident):
    SC = 96.0 ** -0.5
    with P.phase():
        nT0 = P.sb("nT0", [128, TALL], BF16)
        nT1 = P.sb("nT1", [64, TALL], BF16)
        nkT = P.sb("nkT", [128, TALL], BF16)
        krT = P.sb("krT", [96, TALL], BF16)
        KTm = P.sb("KTm", [96, 4, TALL], BF16)
        Vm = P.sb("Vm", [128, NT, 4, 66], BF16)
        wst = P.sb("wst", [128, 768])
        wq = P.sb("wq", [128, 2, 4, 96], BF16)
        wqp = P.sb("wqp", [128, 2, 4, 96], BF16)
        wk = P.sb("wk", [128, 4, 96], BF16)
        wv = P.sb("wv", [128, 256], BF16)
        Pm = P.sb("Pm", [96, 96], BF16)
        qnb = P.sb("qnb", [128, 192])
        kvnb = P.sb("kvnb", [128, 128])
        zt = [P.sb("zt%d" % i, [128, 352]) for i in range(2)]
        nq = [P.sb("nq%d" % i, [128, 416]) for i in range(2)]
        stt_ = [P.sb("st%d" % i, [128, 8]) for i in range(2)]
        junk = P.sb("junk", [128, 192])
        ctab = [P.sb("ctab%d" % i, [96, 2, 512]) for i in range(2)]
        t1 = [P.sb("t1_%d" % i, [96, 512]) for i in range(2)]
        t2 = [P.sb("t2_%d" % i, [96, 512]) for i in range(2)]
        QTb = [P.sb("QTb%d" % i, [96, 512], BF16) for i in range(2)]
        pT = [P.sb("pT%d" % i, [128, 512], BF16) for i in range(3)]
        rc = [P.sb("rc%d" % i, [128, 4]) for i in range(2)]
        yo = [P.sb("yo%d" % i, [128, 4, 4, 64]) for i in range(2)]
        oT = [P.sb("oT%d" % i, [65, 512]) for i in range(2)]
        def ldw(dst_ap, dst_t, src, rows, cols):
            P.dma("sp", wst[0:rows, 0:cols], src, (), [wst])
            P.cp("act", dst_ap, wst[0:rows, 0:cols], [wst], [dst_t])
        ldw(wq[:, 0, :, :].rearrange("p h e -> p (h e)"), wq, dram["mla_wq_r"][L, 0:128, :], 128, 384)
        ldw(wq[0:64, 1, :, :].rearrange("p h e -> p (h e)"), wq, dram["mla_wq_r"][L, 128:192, :], 64, 384)
        ldw(wqp[:, 0, :, :].rearrange("p h e -> p (h e)"), wqp, dram["mla_wq_p"][L, 0:128, :], 128, 384)
        ldw(wqp[0:64, 1, :, :].rearrange("p h e -> p (h e)"), wqp, dram["mla_wq_p"][L, 128:192, :], 64, 384)
        ldw(wk[:, :, :].rearrange("p h e -> p (h e)"), wk, dram["mla_wk"][L], 128, 384)
        ldw(wv[:, :], wv, dram["mla_wv"][L], 128, 256)
        ldw(Pm[:, :], Pm, dram["mla_pm"], 96, 96)
        for n_ in nq:
            P.memset("pool", n_[:, 320:384], 0.0, [n_])
        P.dma("sp", qnb[:], dram["mla_q_norm"][L].partition_broadcast(128), (), [qnb])
        P.dma("sp", kvnb[:], dram["mla_kv_norm"][L].partition_broadcast(128), (), [kvnb])
        P.memset("pool", Vm[:, :, :, 64:65], 1.0, [Vm])
        for i in range(NT):
            b = i % 2
            z_, n_, s_ = zt[b], nq[b], stt_[b]
            P.dma("sp", z_[:], z_d[i * 128:(i + 1) * 128, 768:1120], [z_res[i]], [z_])
            P.act("act", junk[:, 0:192], z_[:, 0:192], AF.Square, [z_], [junk], accum_out=s_[:, 0:1])
            P.rstd(s_, 1.0 / 192, EPS, [junk])
            P.stt("dve", n_[:, 0:192], z_[:, 0:192], s_[:, 2:3], qnb[:], ALU.mult, ALU.mult, [z_, s_, qnb], [n_])
            P.act("act", junk[:, 0:128], z_[:, 192:320], AF.Square, [z_], [junk], accum_out=s_[:, 4:5])
            P.act("act", s_[:, 5:6], s_[:, 4:5], AF.Sqrt, [s_, junk, P.consts[EPS]], [s_], bias=P.consts[EPS][:, 0:1], scale=1.0 / 128)
            P.op("dve", lambda e, s_=s_: e.reciprocal(s_[:, 6:7], s_[:, 5:6]), [s_], [s_])
            P.stt("dve", n_[:, 192:320], z_[:, 192:320], s_[:, 6:7], kvnb[:], ALU.mult, ALU.mult, [z_, s_, kvnb], [n_])
            pst = psum[i % 2]
            P.tr(pst[:, 0:128], n_[:, 0:128], ident[:], [n_, ident], [pst])
            P.tr(pst[0:64, 128:256], n_[:, 128:192], ident[:], [n_, ident], [pst])
            P.tr(pst[:, 256:384], n_[:, 192:320], ident[:], [n_, ident], [pst])
            P.cp("pool", n_[:, 384:416], z_[:, 320:352], [z_], [n_])
            P.tr(pst[0:96, 384:512], n_[:, 320:416], ident[:], [n_, ident], [pst])
            ts_ = slice(i * 128, (i + 1) * 128)
            P.cp("act", nT0[:, ts_], pst[:, 0:128], [pst], [nT0])
            P.cp("act", nT1[:, ts_], pst[0:64, 128:256], [pst], [nT1])
            P.cp("act", nkT[:, ts_], pst[:, 256:384], [pst], [nkT])
            P.cp("act", krT[:, ts_], pst[0:96, 384:512], [pst], [krT])
        nblk = [(b0, min(512, TALL - b0)) for b0 in range(0, TALL, 512)]
        for bi, (b0, n) in enumerate(nblk):
            ct = ctab[bi % 2]
            P.dma("sp", ct[64:96, :, 0:n], dram["rope_tab"][:, :, b0:b0 + n].rearrange("c p t -> p c t"), (), [ct])
            for h in range(4):
                pst = psum[h % 2]
                P.mm(pst[0:96, 0:n], wk[:, h, :], nkT[:, b0:b0 + n], True, True, [wk, nkT], [pst])
                P.cp("act", KTm[0:64, h, b0:b0 + n], pst[0:64, 0:n], [pst], [KTm])
            pb = psum[2]
            P.mm(pb[0:96, 0:n], Pm[:, :], krT[:, b0:b0 + n], True, True, [Pm, krT], [pb])
            a1, a2 = t1[bi % 2], t2[bi % 2]
            P.tt("dve", a1[64:96, 0:n], krT[64:96, b0:b0 + n], ct[64:96, 0, 0:n], ALU.mult, [krT, ct], [a1])
            P.tt("dve", a2[64:96, 0:n], pb[64:96, 0:n], ct[64:96, 1, 0:n], ALU.mult, [pb, ct], [a2])
            P.tt("pool", KTm[64:96, :, b0:b0 + n], a1[64:96, 0:n].unsqueeze(1).to_broadcast([32, 4, n]),
                 a2[64:96, 0:n].unsqueeze(1).to_broadcast([32, 4, n]), ALU.add, [a1, a2], [KTm])
            for s in range(n // 128):
                ti = b0 // 128 + s
                pv = psum[3 + s % 2]
                P.mm(pv[:, 0:256], nkT[:, ti * 128:(ti + 1) * 128], wv[:, :], True, True, [nkT, wv], [pv])
                P.cp("act", Vm[:, ti, :, 0:64], pv[:, 0:256].rearrange("p (h d) -> p h d", h=4), [pv], [Vm])
        qblocks = []
        if need_ctx:
            qblocks.append((0, 256, [0, 1]))
        for b0 in range(TCTX, TALL, 512):
            qblocks.append((b0, 512, list(range(NT))))
        cnt = 0
        for qi, (b0, n, ktiles) in enumerate(qblocks):
            ct = ctab[qi % 2]
            P.dma("sp", ct[64:96, :, 0:n], dram["rope_tab"][:, :, b0:b0 + n].rearrange("c p t -> p c t"), (), [ct])
            y_ = yo[qi % 2]
            nsub = n // 128
            for h in range(4):
                pa, pb = psum[0], psum[1]
                P.mm(pa[0:96, 0:n], wq[:, 0, h, :], nT0[:, b0:b0 + n], True, False, [wq, nT0], [pa])
                P.mm(pa[0:96, 0:n], wq[0:64, 1, h, :], nT1[:, b0:b0 + n], False, True, [wq, nT1], [pa])
                P.mm(pb[0:96, 0:n], wqp[:, 0, h, :], nT0[:, b0:b0 + n], True, False, [wqp, nT0], [pb])
                P.mm(pb[0:96, 0:n], wqp[0:64, 1, h, :], nT1[:, b0:b0 + n], False, True, [wqp, nT1], [pb])
                q_ = QTb[(qi * 4 + h) % 2]
                a1, a2 = t1[h % 2], t2[h % 2]
                P.cp("act", q_[0:64, 0:n], pa[0:64, 0:n], [pa], [q_])
                P.tt("dve", a1[64:96, 0:n], pa[64:96, 0:n], ct[64:96, 0, 0:n], ALU.mult, [pa, ct], [a1])
                P.tt("dve", a2[64:96, 0:n], pb[64:96, 0:n], ct[64:96, 1, 0:n], ALU.mult, [pb, ct], [a2])
                P.tt("pool", q_[64:96, 0:n], a1[64:96, 0:n], a2[64:96, 0:n], ALU.add, [a1, a2], [q_])
                poT = psum[5 + (qi * 4 + h) % 2]
                LOOK = 2
                nk_ = len(ktiles)
                slots_ = {}
                for ki in range(nk_ + LOOK):
                    if ki < nk_:
                        kt = ktiles[ki]
                        pst = psum[2 + cnt % 3]
                        p = pT[cnt % 3]
                        cnt += 1
                        P.mm(pst[:, 0:n], KTm[:, h, kt * 128:(kt + 1) * 128], q_[:, 0:n], True, True, [KTm, q_], [pst])
                        slots_[ki] = (pst, p, kt)
                    kj = ki - LOOK
                    if kj >= 0:
                        pst, p, kt = slots_.pop(kj)
                        P.act("act", p[:, 0:n], pst[:, 0:n], AF.Exp, [pst], [p], scale=SC)
                        P.mm(poT[0:65, 0:n], Vm[:, kt, h, 0:65], p[:, 0:n], kj == 0, kj == nk_ - 1, [p, Vm], [poT])
                o_ = oT[h % 2]
                P.cp("dve", o_[:, 0:n], poT[0:65, 0:n], [poT], [o_])
                po = psum[7]
                for s in range(nsub):
                    P.tr(po[:, s * 65:(s + 1) * 65], o_[0:65, s * 128:(s + 1) * 128], ident[0:65, 0:65], [o_, ident], [po])
                r_ = rc[h % 2]
                pov = po[:, 0:nsub * 65].rearrange("p (s e) -> p s e", s=nsub)
                P.op("dve", lambda e, r_=r_, pov=pov, nsub=nsub: e.reciprocal(r_[:, 0:nsub], pov[:, :, 64]), [po], [r_])
                P.tt("dve", y_[:, 0:nsub, h, :], pov[:, :, 0:64], r_[:, 0:nsub].unsqueeze(2).to_broadcast([128, nsub, 64]), ALU.mult,
                     [po, r_], [y_])
            for s in range(nsub):
                r0 = b0 + s * 128
                P.dma("sp", y_d[r0:r0 + 128, 256:512], y_[:, s, :, :].rearrange("p h d -> p (h d)"), [y_], [y_res[r0 // 128]])


def emit_mixers(P, L, need_ctx, dram, z_d, z_res, y_d, y_res, psum, ident, dbg):
    if "skip_na" not in dbg:
        emit_na(P, L, need_ctx, dram, z_d, z_res, y_d, y_res, psum, ident)
    if "skip_mla" not in dbg:
        emit_mla(P, L, need_ctx, dram, z_d, z_res, y_d, y_res, psum, ident)
    if "skip_gla" not in dbg:
        emit_gla(P, L, need_ctx, dram, z_d, z_res, y_d, y_res, psum, ident)
    if "skip_rw" not in dbg:
        emit_rwkv(P, L, need_ctx, dram, z_d, z_res, y_d, y_res, psum, ident)


def rope_tables():
    t = np.arange(TLAT)
    row = (t // 64).astype(np.float32)
    col = (t % 64).astype(np.float32)
    inv = (10000.0 ** (-np.arange(0, 16, 2, dtype=np.float32) / 16)).astype(np.float32)
    C = np.ones((32, TALL), np.float32)
    S = np.zeros((32, TALL), np.float32)
    for part, pos in ((0, row), (1, col)):
        ang = (pos[:, None] * inv[None, :]).astype(np.float32)
        c, s = np.cos(ang).T, np.sin(ang).T
        C[part * 16:part * 16 + 8, TCTX:] = c
        C[part * 16 + 8:part * 16 + 16, TCTX:] = c
        S[part * 16:part * 16 + 8, TCTX:] = -s
        S[part * 16 + 8:part * 16 + 16, TCTX:] = s
    return np.stack([C, S], 0).astype(np.float32)


def rope_partner():
    idx = np.arange(32)
    return np.where((idx % 16) < 8, idx + 8, idx - 8)


def mix_host_inputs(inputs, m):
    part = rope_partner()
    m["rope_tab"] = rope_tables()
    m["scan_consts"] = scan_consts()
    for nm, shp in (("rw_mu", (1, 1024)), ("rw_k_k", (1, 256)), ("rw_k_a", (1, 256)), ("rw_w0", (2, 1, 256)), ("rw_a0", (2, 1, 256)),
                    ("rw_w_up", (2, 64, 256)), ("rw_a_up", (2, 64, 256)), ("rw_g_up", (128, 256)), ("rw_ln_w", (1, 256)),
                    ("rw_ln_b", (1, 256)), ("rw_r_k", (1, 256))):
        m[nm] = np.ascontiguousarray(np.asarray(inputs[nm], np.float32).reshape((NL_FULL,) + shp))
    m["gla_gate_up"] = np.asarray(inputs["gla_gate_up"], np.float32)
    m["gla_gate_b"] = np.asarray(inputs["gla_gate_b"], np.float32).reshape(NL_FULL, 2, 1, 128)
    m["gla_norm"] = np.asarray(inputs["gla_norm"], np.float32).reshape(NL_FULL, 1, 256)
    pm = np.zeros((96, 96), np.float32)
    pm[64 + part, 64 + np.arange(32)] = 1.0
    m["mla_pm"] = pm
    wuq = np.asarray(inputs["mla_w_uq"], np.float32).reshape(NL_FULL, 192, 4, 96)
    m["mla_wq_r"] = np.ascontiguousarray(wuq.reshape(NL_FULL, 192, 384))
    wp = np.concatenate([wuq[..., 0:64], wuq[..., 64:96][..., part]], axis=-1)
    m["mla_wq_p"] = np.ascontiguousarray(wp.reshape(NL_FULL, 192, 384))
    wukv = np.asarray(inputs["mla_w_ukv"], np.float32).reshape(NL_FULL, 128, 4, 128)
    wk = np.zeros((NL_FULL, 128, 4, 96), np.float32)
    wk[..., 0:64] = wukv[..., 0:64]
    m["mla_wk"] = np.ascontiguousarray(wk.reshape(NL_FULL, 128, 384))
    m["mla_wv"] = np.ascontiguousarray(wukv[..., 64:128].reshape(NL_FULL, 128, 256))
    m["mla_q_norm"] = np.asarray(inputs["mla_q_norm"], np.float32).reshape(NL_FULL, 1, 192)
    m["mla_kv_norm"] = np.asarray(inputs["mla_kv_norm"], np.float32).reshape(NL_FULL, 1, 128)
    rpb = np.asarray(inputs["na_rpb"], np.float32)
    p = np.arange(128)
    k = p % 64
    q = np.arange(64)
    coff = np.clip(k[:, None] - q[None, :], -15, 15) + 15
    di = np.arange(14)
    roff = di[None, :] + (p[:, None] >= 64)
    m["na_btab"] = np.ascontiguousarray(
        rpb[:, :, roff[:, :, None], coff[:, None, :]].transpose(0, 2, 1, 3, 4))
    cstart = np.clip(q - 8, 0, 48)
    inwin = (k[:, None] >= cstart[None, :]) & (k[:, None] < cstart[None, :] + 16)
    m["na_mask"] = np.where(inwin, 0.0, NEG).astype(np.float32)
    return m


MIX_SPECS = [
    ("rw_mu", (1, 1024), True), ("rw_k_k", (1, 256), True), ("rw_k_a", (1, 256), True), ("rw_w0", (2, 1, 256), True),
    ("rw_a0", (2, 1, 256), True), ("rw_w_up", (2, 64, 256), True), ("rw_a_up", (2, 64, 256), True), ("rw_g_up", (128, 256), True),
    ("rw_ln_w", (1, 256), True), ("rw_ln_b", (1, 256), True), ("rw_r_k", (1, 256), True),
    ("scan_consts", (2, 128, 768), False), ("gla_gate_up", (2, 16, 128), True), ("gla_gate_b", (2, 1, 128), True),
    ("gla_norm", (1, 256), True),
    ("rope_tab", (2, 32, TALL), False), ("mla_pm", (96, 96), False),
    ("mla_wq_r", (192, 384), True), ("mla_wq_p", (192, 384), True), ("mla_wk", (128, 384), True), ("mla_wv", (128, 256), True),
    ("mla_q_norm", (1, 192), True), ("mla_kv_norm", (1, 128), True),
    ("na_btab", (128, 4, 14, 64), True), ("na_mask", (128, 64), False),
]


def scan_consts():
    s = np.arange(128)[:, None]
    t = np.arange(128)[None, :]
    out = np.zeros((2, 128, 768), np.float32)
    for d in range(2):
        incl = (s <= t) if d == 0 else (s >= t)
        strict = (s < t) if d == 0 else (s > t)
        out[d, :, 0:128] = incl
        out[d, :, 128:256] = incl
        out[d, :, 256:384] = strict
        out[d, :, 384:512] = incl
        out[d, :, 512:640] = strict
        out[d, :, 640:768] = strict.T
    return out


INLINE_FINISH = False


def emit_scan(P, dram, psum, ident, pre_d, pre_res, loads, has_ab, yacc, finish=None):
    ones = P.sb("ones", [128, 1])
    P.memset("pool", ones[:], 1.0, [ones])
    P.memset("pool", yacc[:], 0.0, [yacc])
    yres = [Res("yacc%d" % i) for i in range(NT)]
    done = [0] * NT

    def direction(d):
        B = psum[4 * d:4 * d + 4]
        Xin = [P.sb("Xin%d_%d" % (d, i), [128, 6, 256]) for i in range(2)]
        E = P.sb("E%d" % d, [128, 3, 256])
        W = [P.sb("W%d_%d" % (d, i), [128, 4, 256]) for i in range(2)]
        ft = P.sb("FT%d" % d, [128, 2, 4, 128])
        gm = P.sb("Gm%d" % d, [128, 4, 512])
        pCt = [P.sb("pC%d_%d" % (d, i), [128, 2]) for i in range(2)]
        H = P.sb("H%d" % d, [128, 2, 64])
        tmpH = P.sb("tmpH%d" % d, [128, 2, 64])
        cst = P.sb("cst%d" % d, [128, 768])
        P.dma("sp", cst[:], dram["scan_consts"][d], (), [cst])
        if has_ab:
            Xb = [P.sb("Xb%d_%d" % (d, i), [128, 4, 128], BF16) for i in range(2)]
            Yb = [P.sb("Yb%d_%d" % (d, i), [128, 4, 128], BF16) for i in range(2)]
            Pb = [P.sb("Pb%d_%d" % (d, i), [128, 4, 128], BF16) for i in range(2)]
            P32 = [P.sb("P32%d_%d" % (d, i), [128, 4, 128]) for i in range(2)]
            Xs = P.sb("Xs%d" % d, [128, 256])
            Us = P.sb("Us%d" % d, [128, 256])
        order = list(range(NT)) if d == 0 else [1, 0] + list(range(NT - 1, 1, -1))
        Mc = cst[:, 0:128]
        MASK4 = cst[:, 128:640]
        P.memset("pool", H[:], 0.0, [H])
        yield
        for n, i in enumerate(order):
            b = n % 2
            xi, w_, pc = Xin[b], W[b], pCt[b]
            for (s0, ns, c0) in loads(d):
                P.dma("sp" if (s0 + d) % 2 == 0 else "pool", xi[:, s0:s0 + ns, :].rearrange("p s f -> p (s f)"),
                      pre_d[i * 128:(i + 1) * 128, c0:c0 + ns * 256], [pre_res[i]], [xi])
            pcl = B[0]
            P.mm(pcl[:, 0:256], Mc, xi[:, 3, :], True, True, [cst, xi], [pcl])
            pcp = B[1]
            for g in range(2):
                P.mm(pcp[:, 384 + g:385 + g], xi[:, 3, g * 128:(g + 1) * 128], ones[:, 0:1], True, True, [xi, ones], [pcp])
            yield
            P.act("act", E[:, 0, :], pcl[:, 0:256], AF.Exp, [pcl], [E])
            P.act("act", E[:, 1, :], pcl[:, 0:256], AF.Exp, [pcl], [E], scale=-1.0)
            P.act("act", pc[:, :], pcp[:, 384:386], AF.Exp, [pcp], [pc])
            if has_ab:
                P.tt("dve", E[:, 2, :], pcl[:, 0:256], xi[:, 3, :], ALU.subtract, [pcl, xi], [E])
                P.act("act", E[:, 2, :], E[:, 2, :], AF.Exp, [E], [E])
            yield
            P.tt("dve", w_[:, 0, :], xi[:, 0, :], E[:, 0, :], ALU.mult, [xi, E], [w_])
            P.tt("pool", w_[:, 1, :], xi[:, 1, :], E[:, 1, :], ALU.mult, [xi, E], [w_])
            if has_ab:
                P.tt("dve", w_[:, 2, :], xi[:, 4, :], E[:, 2, :], ALU.mult, [xi, E], [w_])
                P.tt("pool", w_[:, 3, :], xi[:, 5, :], E[:, 1, :], ALU.mult, [xi, E], [w_])
            yield
            slots = [(0, 0), (2, 1), (1, 2), (3, 3)] if has_ab else [(0, 0), (1, 2)]
            for g in range(2):
                pt = B[2 + g]
                for (ws, fs) in slots:
                    P.tr(pt[:, fs * 128:(fs + 1) * 128], w_[:, ws, g * 128:(g + 1) * 128], ident[:], [w_, ident], [pt])
            yield
            for g in range(2):
                pt = B[2 + g]
                if has_ab:
                    P.cp("act", ft[:, g, :, :].rearrange("p f t -> p (f t)"), pt[:, :], [pt], [ft])
                else:
                    P.cp("act", ft[:, g, 0, :], pt[:, 0:128], [pt], [ft])
                    P.cp("act", ft[:, g, 2, :], pt[:, 256:384], [pt], [ft])
            yield
            for h in range(4):
                g, hp = h // 2, (h % 2) * 64
                pg = B[2 + h % 2]
                if has_ab:
                    P.mm(pg[:, 0:256], ft[hp:hp + 64, g, 2, :], ft[hp:hp + 64, g, 0:2, :].rearrange("p f t -> p (f t)"), True, True, [ft], [pg])
                    P.mm(pg[:, 256:512], ft[hp:hp + 64, g, 3, :], ft[hp:hp + 64, g, 0:2, :].rearrange("p f t -> p (f t)"), True, True, [ft], [pg])
                    P.tt("dve", gm[:, h, :], pg[:, :], MASK4, ALU.mult, [pg, cst], [gm])
                else:
                    P.mm(pg[:, 0:128], ft[hp:hp + 64, g, 2, :], ft[hp:hp + 64, g, 0, :], True, True, [ft], [pg])
                    P.tt("dve", gm[:, h, 0:128], pg[:, 0:128], MASK4[:, 0:128], ALU.mult, [pg, cst], [gm])
                if h % 2 == 1:
                    yield
            if has_ab:
                p3 = B[0]
                for h in range(4):
                    P.tr(p3[:, h * 128:(h + 1) * 128], gm[:, h, 384:512], ident[:], [gm, ident], [p3])
                Xc, Yc, Pc, Pc32 = Xb[0], Yb[0], Pb[0], P32[0]
                yield
                P.cp("act", Yc[:].rearrange("p h s -> p (h s)"), p3[:, :], [p3], [Yc])
                P.cp("dve", Xc[:], gm[:, :, 384:512], [gm], [Xc])
                for h in range(4):
                    P.tt("dve", Pc32[:, h, :], gm[:, h, 384:512], ident[:, :], ALU.add, [gm, ident], [Pc32])
                P.cp("dve", Pc[:], Pc32[:], [Pc32], [Pc])
                yield
                def mm_xy(lv, Xc, Yc):
                    pY = B[3]
                    for h in range(4):
                        P.mm(pY[:, h * 128:(h + 1) * 128], Xc[:, h, :], Yc[:, h, :], True, True, [Yc, Xc], [pY])
                    if lv < 6:
                        pX = B[2]
                        for h in range(4):
                            P.mm(pX[:, h * 128:(h + 1) * 128], Yc[:, h, :], Xc[:, h, :], True, True, [Yc, Xc], [pX])

                mm_xy(1, Xc, Yc)
                yield
                for lv in range(1, 7):
                    Yn, Pn, Pn32 = Yb[lv % 2], Pb[lv % 2], P32[lv % 2]
                    Xn = Xb[lv % 2]
                    P.cp("act", Yn[:].rearrange("p h s -> p (h s)"), B[3][:, :], [B[3]], [Yn])
                    if lv < 6:
                        P.cp("act", Xn[:].rearrange("p h s -> p (h s)"), B[2][:, :], [B[2]], [Xn])
                    yield
                    pP = B[1]
                    for h in range(4):
                        P.mm(pP[:, h * 128:(h + 1) * 128], Yn[:, h, :], Pc[:, h, :], True, True, [Yn, Pc], [pP])
                    if lv < 6:
                        mm_xy(lv + 1, Xn, Yn)
                    yield
                    P.tt("dve", Pn32[:].rearrange("p h s -> p (h s)"), pP[:, :], Pc32[:].rearrange("p h s -> p (h s)"), ALU.add,
                         [pP, Pc32], [Pn32])
                    if lv < 6:
                        P.cp("dve", Pn[:], Pn32[:], [Pn32], [Pn])
                    Xc, Yc, Pc, Pc32 = Xn, Yn, Pn, Pn32
                yield
                Pc = Pc32
                pXs = B[0]
                for h in range(4):
                    g, hp = h // 2, (h % 2) * 64
                    P.mm(pXs[:, 256 + h * 64:256 + (h + 1) * 64], ft[hp:hp + 64, g, 1, :], H[hp:hp + 64, g, :], True, False, [ft, H], [pXs])
                    P.mm(pXs[:, 256 + h * 64:256 + (h + 1) * 64], gm[:, h, 128:256], xi[:, 2, h * 64:(h + 1) * 64], False, True, [gm, xi], [pXs])
                yield
                P.cp("act", Xs[:, :], pXs[:, 256:512], [pXs], [Xs])
                yield
                pU = B[2]
                for h in range(4):
                    P.mm(pU[:, h * 64:(h + 1) * 64], Pc[:, h, :], Xs[:, h * 64:(h + 1) * 64], True, True, [Pc, Xs], [pU])
                yield
                P.cp("dve", Us[:, :], pU[:, 0:256], [pU], [Us])
                yield
            pYo = B[1]
            for h in range(4):
                g, hp = h // 2, (h % 2) * 64
                vh = xi[:, 2, h * 64:(h + 1) * 64]
                P.mm(pYo[:, h * 64:(h + 1) * 64], ft[hp:hp + 64, g, 0, :], H[hp:hp + 64, g, :], True, False, [ft, H], [pYo])
                if has_ab:
                    P.mm(pYo[:, h * 64:(h + 1) * 64], gm[:, h, 256:384], Us[:, h * 64:(h + 1) * 64], False, False, [gm, Us], [pYo])
                P.mm(pYo[:, h * 64:(h + 1) * 64], gm[:, h, 0:128], vh, False, True, [gm, xi], [pYo])
            for h in range(4):
                g, hp = h // 2, (h % 2) * 64
                vh = xi[:, 2, h * 64:(h + 1) * 64]
                P.mm(pYo[hp:hp + 64, 256 + g * 64:256 + (g + 1) * 64], w_[:, 1, h * 64:(h + 1) * 64], vh, True, not has_ab, [w_, xi], [pYo])
                if has_ab:
                    P.mm(pYo[hp:hp + 64, 256 + g * 64:256 + (g + 1) * 64], w_[:, 3, h * 64:(h + 1) * 64], Us[:, h * 64:(h + 1) * 64], False, True,
                         [w_, Us], [pYo])
            yield
            P.tt("dve", yacc[:, i, :], pYo[:, 0:256], yacc[:, i, :], ALU.add, [pYo, yres[i]], [yres[i]])
            P.tt("dve", tmpH[:], pYo[:, 256:384].rearrange("p (g v) -> p g v", g=2), H[:], ALU.add, [pYo, H], [tmpH])
            P.tt("dve", H[:], tmpH[:], pc[:, :].unsqueeze(2).to_broadcast([128, 2, 64]), ALU.mult, [tmpH, pc], [H])
            done[i] += 1
            if done[i] == 2 and finish is not None and INLINE_FINISH:
                finish(i, yres[i])
            yield

    gens = [direction(0), direction(1)]
    while gens:
        for g_ in list(gens):
            try:
                next(g_)
            except StopIteration:
                gens.remove(g_)
    if finish is not None and not INLINE_FINISH:
        for i in range(NT):
            finish(i, yres[i])


GLA_Z0 = 768 + 352 + 1024


def emit_gla(P, L, need_ctx, dram, z_d, z_res, y_d, y_res, psum, ident):
    pre_d = dram["gla_pre"]
    pre_res = dram.res("gla_pre")
    with P.phase():
        zt = [P.sb("zt%d" % i, [128, 528]) for i in range(2)]
        pre = [P.sb("pre%d" % i, [128, 5, 4, 64]) for i in range(2)]
        gdT = [P.sb("gdT%d" % i, [16, 128]) for i in range(2)]
        gup = P.sb("gup", [16, 2, 128])
        gb = P.sb("gb", [128, 2, 128])
        xg = [P.sb("xg%d" % i, [128, 128]) for i in range(2)]
        for p_ in pre:
            P.memset("pool", p_[:], 0.0, [p_])
        for d in range(2):
            P.dma("sp", gup[:, d, :], dram["gla_gate_up"][L, d], (), [gup])
            P.dma("sp", gb[:, d, :], dram["gla_gate_b"][L, d].partition_broadcast(128), (), [gb])
        for i in range(NT):
            b = i % 2
            z_, p_ = zt[b], pre[b]
            P.dma("sp", z_[:], z_d[i * 128:(i + 1) * 128, GLA_Z0:GLA_Z0 + 528], [z_res[i]], [z_])
            P.op("act", lambda e, z_=z_, p_=p_: e.mul(p_[:, 0, :, 0:32], z_[:, 0:128].rearrange("p (h d) -> p h d", h=4), 32.0 ** -0.5), [z_], [p_])
            P.cp("pool", p_[:, 1, :, 0:32], z_[:, 128:256].rearrange("p (h d) -> p h d", h=4), [z_], [p_])
            P.cp("pool", p_[:, 2, :, :], z_[:, 256:512].rearrange("p (h d) -> p h d", h=4), [z_], [p_])
            pt = psum[i % 2]
            P.tr(pt[0:16, 0:128], z_[:, 512:528], ident[:], [z_, ident], [pt])
            P.cp("act", gdT[b][:, :], pt[0:16, 0:128], [pt], [gdT[b]])
            for d in range(2):
                pl = psum[2 + d]
                x_ = xg[d]
                P.mm(pl[:, 0:128], gdT[b][:, :], gup[:, d, :], True, True, [gdT[b], gup], [pl])
                P.tt("dve", x_[:], pl[:, 0:128], gb[:, d, :], ALU.add, [pl, gb], [x_])
                P.act("act", x_[:], x_[:], AF.Exp, [x_], [x_], scale=-1.0)
                P.act("act", x_[:], x_[:], AF.Ln, [x_], [x_], bias=1.0)
                P.op("act", lambda e, x_=x_, p_=p_, d=d: e.mul(p_[:, 3 + d, :, 0:32], x_[:].rearrange("p (h d) -> p h d", h=4), -1.0 / 16.0), [x_], [p_])
            P.dma("pool", pre_d[i * 128:(i + 1) * 128, :], p_[:].rearrange("p s h d -> p (s h d)"), [p_], [pre_res[i]])
    with P.phase():
        yacc = P.sb("yacc", [128, NT, 256])
        gn = P.sb("gn", [128, 256])
        og = [P.sb("og%d" % i, [128, 256]) for i in range(2)]
        sq = [P.sb("sq%d" % i, [128, 256]) for i in range(2)]
        stg = [P.sb("stg%d" % i, [128, 12]) for i in range(2)]
        P.dma("sp", gn[:], dram["gla_norm"][L].partition_broadcast(128), (), [gn])
        fcnt = [0]

        def finish(i, yr):
            if i < 2 and not need_ctx:
                return
            b = fcnt[0] % 2
            fcnt[0] += 1
            o_, s_, g_ = og[b], sq[b], stg[b]
            yv = yacc[:, i, :]
            P.dma("sp", o_[:], z_d[i * 128:(i + 1) * 128, DIN - 256:DIN], [z_res[i]], [o_])
            P.act("act", o_[:], o_[:], AF.Silu, [o_], [o_])
            P.tt("pool", s_[:], yv, yv, ALU.mult, [yr], [s_])
            P.red("dve", g_[:, 0:4], s_[:].rearrange("p (h d) -> p h d", h=4), ALU.add, [s_], [g_])
            P.act("act", g_[:, 4:8], g_[:, 0:4], AF.Sqrt, [g_, P.consts[EPS]], [g_], bias=P.consts[EPS][:, 0:1], scale=1.0 / 64)
            P.op("dve", lambda e, g_=g_: e.reciprocal(g_[:, 8:12], g_[:, 4:8]), [g_], [g_])
            P.tt("dve", s_[:].rearrange("p (h d) -> p h d", h=4), yv.rearrange("p (h d) -> p h d", h=4),
                 g_[:, 8:12].unsqueeze(2).to_broadcast([128, 4, 64]), ALU.mult, [yr, g_], [s_])
            P.tt("pool", s_[:], s_[:], gn[:], ALU.mult, [s_, gn], [s_])
            P.tt("pool", s_[:], s_[:], o_[:], ALU.mult, [s_, o_], [s_])
            P.dma("sp", y_d[i * 128:(i + 1) * 128, 768:1024], s_[:], [s_], [y_res[i]])

        emit_scan(P, dram, psum, ident, pre_d, pre_res, lambda d: [(0, 3, 0), (3, 1, 768 + d * 256)], False, yacc, finish)


RW_Z0 = 768 + 352


def emit_rwkv(P, L, need_ctx, dram, z_d, z_res, y_d, y_res, psum, ident):
    pre_d = dram["rw_pre"]
    pre_res = dram.res("rw_pre")
    C05 = float(-np.exp(-0.5))
    with P.phase():
        zc = [P.sb("zc%d" % i, [128, 1024]) for i in range(2)]
        zp = [P.sb("zp%d" % i, [128, 1024]) for i in range(2)]
        zn = [P.sb("zn%d" % i, [128, 1024]) for i in range(2)]
        zs = [P.sb("zs%d" % i, [128, 1024]) for i in range(2)]
        pre = [P.sb("pre%d" % i, [128, 10, 256]) for i in range(2)]
        mub = P.sb("mub", [128, 1024])
        kkb = P.sb("kkb", [128, 256])
        kab = P.sb("kab", [128, 256])
        w0b = P.sb("w0b", [128, 2, 256])
        a0b = P.sb("a0b", [128, 2, 256])
        wa = P.sb("wa", [128, 2, 256])
        gup = P.sb("gup", [128, 256])
        TWA = [P.sb("TWA%d" % i, [128, 128]) for i in range(2)]
        TG = [P.sb("TG%d" % i, [128, 128]) for i in range(2)]
        kt_ = [P.sb("kt%d" % i, [128, 256]) for i in range(2)]
        kn_ = [P.sb("kn%d" % i, [128, 256]) for i in range(2)]
        sq_ = P.sb("sq", [128, 256])
        xw = [P.sb("xw%d" % i, [128, 256]) for i in range(2)]
        xa = [P.sb("xa%d" % i, [128, 256]) for i in range(2)]
        st_ = [P.sb("st%d" % i, [128, 12]) for i in range(2)]
        P.dma("sp", mub[:], dram["rw_mu"][L].partition_broadcast(128), (), [mub])
        P.dma("sp", kkb[:], dram["rw_k_k"][L].partition_broadcast(128), (), [kkb])
        P.dma("sp", kab[:], dram["rw_k_a"][L].partition_broadcast(128), (), [kab])
        for d in range(2):
            P.dma("sp", w0b[:, d, :], dram["rw_w0"][L, d].partition_broadcast(128), (), [w0b])
            P.dma("sp", a0b[:, d, :], dram["rw_a0"][L, d].partition_broadcast(128), (), [a0b])
            P.dma("sp", wa[0:64, d, :], dram["rw_w_up"][L, d], (), [wa])
            P.dma("sp", wa[64:128, d, :], dram["rw_a_up"][L, d], (), [wa])
        P.dma("sp", gup[:], dram["rw_g_up"][L], (), [gup])
        def prep_a(i):
            b = i % 2
            c_, p_, n_, s_, pr = zc[b], zp[b], zn[b], zs[b], pre[b]
            t0 = i * 128
            P.dma("sp", c_[:], z_d[t0:t0 + 128, RW_Z0:RW_Z0 + 1024], [z_res[i]], [c_])
            if t0 in (0, TCTX):
                P.memset("pool", p_[:], 0.0, [p_])
                P.dma("pool", p_[1:128, :], z_d[t0:t0 + 127, RW_Z0:RW_Z0 + 1024], [z_res[i]], [p_])
            else:
                P.dma("pool", p_[:], z_d[t0 - 1:t0 + 127, RW_Z0:RW_Z0 + 1024], [z_res[i - 1], z_res[i]], [p_])
            if t0 + 128 in (TCTX, TALL):
                P.memset("pool", n_[:], 0.0, [n_])
                P.dma("sp", n_[0:127, :], z_d[t0 + 1:t0 + 128, RW_Z0:RW_Z0 + 1024], [z_res[i]], [n_])
            else:
                P.dma("sp", n_[:], z_d[t0 + 1:t0 + 129, RW_Z0:RW_Z0 + 1024], [z_res[i], z_res[i + 1]], [n_])
            P.tt("pool", p_[:], p_[:], n_[:], ALU.add, [p_, n_], [p_])
            P.stt("dve", p_[:], p_[:], 0.5, c_[:], ALU.mult, ALU.subtract, [p_, c_], [p_])
            P.tt("pool", p_[:], p_[:], mub[:], ALU.mult, [p_, mub], [p_])
            P.tt("dve", s_[:], p_[:], c_[:], ALU.add, [p_, c_], [s_])
            P.cp("act", pr[:, 0, :], s_[:, 0:256], [s_], [pr])
            P.cp("act", pr[:, 1, :], s_[:, 512:768], [s_], [pr])
            k_, kn, st = kt_[b], kn_[b], st_[b]
            P.tt("dve", k_[:], s_[:, 256:512], kkb[:], ALU.mult, [s_, kkb], [k_])
            P.tt("pool", sq_[:], k_[:], k_[:], ALU.mult, [k_], [sq_])
            P.red("dve", st[:, 0:4], sq_[:].rearrange("p (h d) -> p h d", h=4), ALU.add, [sq_], [st])
            P.act("act", st[:, 4:8], st[:, 0:4], AF.Sqrt, [st, P.consts[1e-12]], [st], bias=P.consts[1e-12][:, 0:1], scale=1.0)
            P.op("dve", lambda e, st=st: e.reciprocal(st[:, 8:12], st[:, 4:8]), [st], [st])
            P.tt("dve", kn[:].rearrange("p (h d) -> p h d", h=4), k_[:].rearrange("p (h d) -> p h d", h=4),
                 st[:, 8:12].unsqueeze(2).to_broadcast([128, 4, 64]), ALU.mult, [k_, st], [kn])
            P.op("act", lambda e, pr=pr, kn=kn: e.mul(pr[:, 2, :], kn[:], -1.0), [kn], [pr])
            pt = psum[i % 2]
            P.tr(pt[:, 0:128], s_[:, 768:896], ident[:], [s_, ident], [pt])
            P.tr(pt[:, 128:256], s_[:, 896:1024], ident[:], [s_, ident], [pt])
            tw, tg = TWA[b], TG[b]
            P.act("act", tw[0:64, :], pt[0:64, 0:128], AF.Tanh, [pt], [tw])
            P.cp("act", tw[64:128, :], pt[64:128, 0:128], [pt], [tw])
            P.act("act", tg[:, :], pt[:, 128:256], AF.Sigmoid, [pt], [tg])

        def prep_b(i):
            b = i % 2
            s_, pr = zs[b], pre[b]
            kn = kn_[b]
            tw, tg = TWA[b], TG[b]
            t0 = i * 128
            pgp = psum[6]
            P.mm(pgp[:, 0:256], tg[:, :], gup[:, :], True, True, [tg, gup], [pgp])
            P.cp("act", pr[:, 9, :], pgp[:, 0:256], [pgp], [pr])
            for d in range(2):
                pw, pa = psum[2 + d * 2], psum[3 + d * 2]
                P.mm(pw[:, 0:256], tw[0:64, :], wa[0:64, d, :], True, True, [tw, wa], [pw])
                P.mm(pa[:, 0:256], tw[64:128, :], wa[64:128, d, :], True, True, [tw, wa], [pa])
                w_, a_ = xw[d], xa[d]
                P.tt("dve", w_[:], pw[:, 0:256], w0b[:, d, :], ALU.add, [pw, w0b], [w_])
                P.act("act", w_[:], w_[:], AF.Sigmoid, [w_], [w_])
                P.op("act", lambda e, pr=pr, w_=w_, d=d: e.mul(pr[:, 3 + 3 * d, :], w_[:], C05), [w_], [pr])
                P.tt("dve", a_[:], pa[:, 0:256], a0b[:, d, :], ALU.add, [pa, a0b], [a_])
                P.act("act", a_[:], a_[:], AF.Sigmoid, [a_], [a_])
                P.tt("pool", pr[:, 5 + 3 * d, :], kn[:], a_[:], ALU.mult, [kn, a_], [pr])
                P.stt("dve", a_[:], a_[:], -1.0, kab[:], ALU.add, ALU.mult, [a_, kab], [a_])
                P.stt("dve", pr[:, 4 + 3 * d, :], a_[:], 1.0, s_[:, 256:512], ALU.add, ALU.mult, [a_, s_], [pr])
            P.dma("pool", pre_d[t0:t0 + 128, :], pr[:].rearrange("p s f -> p (s f)"), [pr], [pre_res[i]])

        for k in range(NT + 1):
            if k < NT:
                prep_a(k)
            if k >= 1:
                prep_b(k - 1)
    import os
    if os.environ.get("RW_STAGE", "9") == "1":
        return
    with P.phase():
        yacc = P.sb("yacc", [128, NT, 256])
        lnw = P.sb("lnw", [128, 256])
        lnb = P.sb("lnb", [128, 256])
        rkb = P.sb("rkb", [128, 256])
        fin = [P.sb("fin%d" % i, [128, 5, 256]) for i in range(2)]
        yc = [P.sb("yc%d" % i, [128, 256]) for i in range(2)]
        sq = [P.sb("sq%d" % i, [128, 256]) for i in range(2)]
        sf = [P.sb("sf%d" % i, [128, 24]) for i in range(2)]
        P.dma("sp", lnw[:], dram["rw_ln_w"][L].partition_broadcast(128), (), [lnw])
        P.dma("sp", lnb[:], dram["rw_ln_b"][L].partition_broadcast(128), (), [lnb])
        P.dma("sp", rkb[:], dram["rw_r_k"][L].partition_broadcast(128), (), [rkb])
        v4 = lambda ap: ap.rearrange("p (h d) -> p h d", h=4)
        bc4 = lambda ap: ap.unsqueeze(2).to_broadcast([128, 4, 64])
        fcnt = [0]

        def finish(i, yr):
            if i < 2 and not need_ctx:
                return
            b = fcnt[0] % 2
            fcnt[0] += 1
            f_, y_, s_, t_ = fin[b], yc[b], sq[b], sf[b]
            t0 = i * 128
            P.dma("sp", f_[:, 0:2, :].rearrange("p s f -> p (s f)"), pre_d[t0:t0 + 128, 0:512], [pre_res[i]], [f_])
            P.dma("pool", f_[:, 2, :], pre_d[t0:t0 + 128, 1024:1280], [pre_res[i]], [f_])
            P.dma("pool", f_[:, 3, :], pre_d[t0:t0 + 128, 1792:2048], [pre_res[i]], [f_])
            P.dma("sp", f_[:, 4, :], pre_d[t0:t0 + 128, 2304:2560], [pre_res[i]], [f_])
            yv = yacc[:, i, :]
            P.red("dve", t_[:, 0:4], v4(yv), ALU.add, [yr], [t_])
            P.op("act", lambda e, t_=t_: e.mul(t_[:, 4:8], t_[:, 0:4], -1.0 / 64), [t_], [t_])
            P.tt("dve", v4(y_[:]), v4(yv), bc4(t_[:, 4:8]), ALU.add, [yr, t_], [y_])
            P.tt("pool", s_[:], y_[:], y_[:], ALU.mult, [y_], [s_])
            P.red("dve", t_[:, 8:12], v4(s_[:]), ALU.add, [s_], [t_])
            P.act("act", t_[:, 12:16], t_[:, 8:12], AF.Sqrt, [t_, P.consts[64e-5]], [t_], bias=P.consts[64e-5][:, 0:1], scale=1.0 / 64)
            P.op("dve", lambda e, t_=t_: e.reciprocal(t_[:, 16:20], t_[:, 12:16]), [t_], [t_])
            P.tt("dve", v4(y_[:]), v4(y_[:]), bc4(t_[:, 16:20]), ALU.mult, [y_, t_], [y_])
            P.tt("pool", y_[:], y_[:], lnw[:], ALU.mult, [y_, lnw], [y_])
            P.tt("pool", y_[:], y_[:], lnb[:], ALU.add, [y_, lnb], [y_])
            P.tt("pool", s_[:], f_[:, 2, :], f_[:, 3, :], ALU.add, [f_], [s_])
            P.tt("dve", s_[:], s_[:], f_[:, 0, :], ALU.mult, [s_, f_], [s_])
            P.tt("pool", s_[:], s_[:], rkb[:], ALU.mult, [s_, rkb], [s_])
            P.red("dve", t_[:, 20:24], v4(s_[:]), ALU.add, [s_], [t_])
            P.tt("dve", v4(s_[:]), v4(f_[:, 1, :]), bc4(t_[:, 20:24]), ALU.mult, [f_, t_, s_], [s_])
            P.tt("pool", y_[:], y_[:], s_[:], ALU.add, [y_, s_], [y_])
            P.tt("pool", y_[:], y_[:], f_[:, 4, :], ALU.mult, [y_, f_], [y_])
            P.dma("sp", y_d[t0:t0 + 128, 512:768], y_[:], [y_], [y_res[i]])

        emit_scan(P, dram, psum, ident, pre_d, pre_res,
                  lambda d: [(0, 1, 0), (2, 1, 256), (4, 1, 512), (3, 1, 768 + 768 * d), (1, 1, 1024 + 768 * d), (5, 1, 1280 + 768 * d)],
                  True, yacc, finish)


N_CORES = 8
SCR_SPECS = {"gla_pre": 1280, "rw_pre": 2560}
W_SPECS = [
    ("w_mod", (D, 6 * D)), ("b_mod", (1, 6 * D)),
    ("g_mix_pre", (1, D)), ("g_mix_post", (1, D)), ("g_ffn_pre", (1, D)), ("g_ffn_post", (1, D)),
    ("w_in", (D, DIN)), ("w_out", (D, D)),
    ("ffn_w_up", (D, 2 * DFF)), ("ffn_conv_w", (3, 2 * DFF)), ("ffn_conv_b", (1, 2 * DFF)), ("ffn_w_down", (DFF, D)),
]


def build_program(NL=NL_FULL, dbg=None):
    dbg = dbg or {}
    NLW = dbg.get("nlw", NL_FULL)
    nc = bass.Bass("TRN2", target_bir_lowering=False)
    dram = {}

    def dtens(name, shape, dt=F32, kind="Internal"):
        return nc.dram_tensor(name, list(shape), dt, kind=kind).ap()

    def kind_of(tag, default="Internal"):
        if tag + "_in" in dbg:
            return "ExternalInput"
        if tag in dbg:
            return "ExternalOutput"
        return default

    dram["xall"] = dtens("xall", [TALL, D], kind="ExternalInput")
    dram["c2"] = dtens("c2", [2, D], kind="ExternalInput")
    dram["ident"] = dtens("ident", [128, 128], kind="ExternalInput")
    class LazyDram(dict):
        def __missing__(self, name):
            for nm, shp in W_SPECS:
                if nm == name:
                    self[name] = dtens(name, [NLW] + list(shp), kind="ExternalInput")
                    return self[name]
            for nm, shp, per_layer in MIX_SPECS:
                if nm == name:
                    self[name] = dtens(name, ([NLW] if per_layer else []) + list(shp), kind="ExternalInput")
                    return self[name]
            if name in SCR_SPECS:
                self[name] = dtens(name + "_scr", [TALL, SCR_SPECS[name]], kind=kind_of(name))
                return self[name]
            raise KeyError(name)

        def res(self, name):
            if not hasattr(self, "_res"):
                self._res = {}
            if name not in self._res:
                self._res[name] = [Res("%s%d" % (name, i)) for i in range(NT)]
            return self._res[name]
    dram = LazyDram(dram)
    out_ap = dtens("out", [TLAT, D], kind="ExternalOutput")
    z_d = dtens("z_scr", [TALL, DIN], kind=kind_of("z"))
    y_d = dtens("y_scr", [TALL, D], kind=kind_of("y"))
    xs_d = dtens("xs_scr", [TALL, D], kind=kind_of("xs"))
    mods_d = dtens("mods_scr", [NL_FULL, 2, 6 * D], kind=kind_of("mods"))
    h2T_d = dtens("h2T_scr", [128, 8, TALL], BF16)
    z_res = [Res("z%d" % i) for i in range(NT)]
    y_res = [Res("y%d" % i) for i in range(NT)]
    xs_res = [Res("x%d" % i) for i in range(NT)]
    h2_res = [Res("h2_%d" % i) for i in range(NT)]
    mods_res = [Res("mods%d" % i) for i in range(NL_FULL)]

    with contextlib.ExitStack() as st:
        P = Prog(nc, st)
        ident = P.sb("ident", [128, 128])
        P.dma("sp", ident[:], dram["ident"], (), [ident])
        psum = [P.ps("ps%d" % i) for i in range(8)]
        P.consts = {}
        for cv in (EPS, 1e-12, 64e-5):
            ct = P.sb("const%d" % len(P.consts), [128, 1])
            P.memset("pool", ct[:], cv, [ct])
            P.consts[cv] = ct
        c2T = P.sb("c2T", [128, 8, 2])
        for v in range(2):
            P.dma("sp", c2T[:, :, v], dram["c2"][v].rearrange("(kt p) -> p kt", p=128), (), [c2T],
                  allow_slow_non_contiguous=True)
        P.act("act", c2T[:], c2T[:], AF.Silu, [c2T], [c2T])

        def bcast_load(q, tile, row_ap, reads=()):
            P.dma(q, tile[:], row_ap.partition_broadcast(128), reads, [tile])

        def rms_tile(P, src, st_t, junk, A, Bv, dst, ncols=D):
            P.act("act", junk[:, 0:ncols], src[:, 0:ncols], AF.Square, [src], [junk], accum_out=st_t[:, 0:1])
            P.rstd(st_t, 1.0 / ncols, EPS, [junk])
            P.stt("dve", dst[:, 0:ncols], src[:, 0:ncols], st_t[:, 2:3], A[:, 0:ncols], ALU.mult, ALU.mult, [src, st_t, A], [dst])
            if Bv is not None:
                P.tt("pool", dst[:, 0:ncols], dst[:, 0:ncols], Bv[:, 0:ncols], ALU.add, [dst, Bv], [dst])

        def transpose8(P, src, dstT, banks):
            for half in range(2):
                pst = banks[half]
                for k4 in range(4):
                    kt = half * 4 + k4
                    P.tr(pst[:, k4 * 128:(k4 + 1) * 128], src[:, kt * 128:(kt + 1) * 128], ident[:], [src, ident], [pst])
                P.cp("act", dstT[:, half * 4:(half + 1) * 4, :], pst[:, :].rearrange("p (k t) -> p k t", k=4), [pst], [dstT])

        def load_cast(P, dst_ap, dst_t, src_ap, wst, idx, ncols):
            w = wst[idx % 2]
            P.dma("sp" if idx % 2 == 0 else "pool", w[:, 0:ncols], src_ap, (), [w])
            P.cp("act" if idx % 2 == 0 else "dve", dst_ap, w[:, 0:ncols], [w], [dst_t])

        for L in range(NL):
            need_ctx = L < NL_FULL - 1
            last = (L == NL - 1)
            mrow = lambda v, k: mods_d[L, v, k * D:(k + 1) * D]
            if "skip_p0" not in dbg:
              with P.phase():
                wst = [P.sb("wst%d" % i, [128, 512]) for i in range(2)]
                brow = P.sb("brow", [2, 6 * D])
                mrow_sb = [P.sb("mrow%d" % i, [2, 512]) for i in range(2)]
                P.dma("sp", brow[0:1, :], dram["b_mod"][L], (), [brow])
                P.dma("sp", brow[1:2, :], dram["b_mod"][L], (), [brow])
                for nb in range(12):
                    pst = psum[nb % 2]
                    for kt in range(8):
                        w = wst[(nb * 8 + kt) % 2]
                        P.dma("sp" if kt % 2 == 0 else "pool", w[:, 0:512],
                              dram["w_mod"][L, kt * 128:(kt + 1) * 128, nb * 512:(nb + 1) * 512], (), [w])
                        P.mm(pst[0:2, :], c2T[:, kt, :], w[:, 0:512], kt == 0, kt == 7, [c2T, w], [pst])
                    ms = mrow_sb[nb % 2]
                    P.tt("dve", ms[:, :], pst[0:2, :], brow[:, nb * 512:(nb + 1) * 512], ALU.add, [pst, brow], [ms])
                    P.dma("sp", mods_d[L, :, nb * 512:(nb + 1) * 512], ms[:, :], [ms], [mods_res[L]])

            if "skip_p1" not in dbg:
              with P.phase():
                x_src = dram["xall"] if L == 0 else xs_d
                wst = [P.sb("wst%d" % i, [128, 2048]) for i in range(2)]
                gB = P.sb("gB", [128, D])
                A1 = P.sb("A1", [128, D])
                B1 = P.sb("B1", [128, D])
                win_sb = P.sb("win", [128, 8, DIN], BF16)
                xt = [P.sb("xt%d" % i, [128, D]) for i in range(2)]
                xn = [P.sb("xn%d" % i, [128, D]) for i in range(2)]
                hT = [P.sb("hT%d" % i, [128, 8, 128], BF16) for i in range(2)]
                zt = [P.sb("zt%d" % i, [128, DIN]) for i in range(2)]
                st1 = [P.sb("st1_%d" % i, [128, 4]) for i in range(2)]
                junk = P.sb("junk", [128, D])
                bcast_load("sp", gB, dram["g_mix_pre"][L])
                win_res = [Res("win%d" % c) for c in range(6)]
                idx = 0
                for c in range(0, 6, 2):
                    c0 = c * 512
                    cw = min(1024, DIN - c0)
                    for kt in range(8):
                        w = wst[idx % 2]
                        P.dma("sp" if idx % 2 == 0 else "pool", w[:, 0:cw], dram["w_in"][L, kt * 128:(kt + 1) * 128, c0:c0 + cw], (), [w])
                        P.cp("act" if idx % 2 == 0 else "dve", win_sb[:, kt, c0:c0 + cw], w[:, 0:cw], [w], [win_res[c], win_res[c + 1]])
                        idx += 1
                for i in range(NT):
                    v = 1 if i < 2 else 0
                    if i == 0 or i == 2:
                        bcast_load("sp", A1, mrow(v, 1), [mods_res[L]])
                        P.stt("dve", A1[:], A1[:], 1.0, gB[:], ALU.add, ALU.mult, [A1, gB], [A1])
                        bcast_load("sp", B1, mrow(v, 0), [mods_res[L]])
                    b = i % 2
                    P.dma("sp", xt[b][:], x_src[i * 128:(i + 1) * 128, :], [xs_res[i]], [xt[b]])
                    rms_tile(P, xt[b], st1[b], junk, A1, B1, xn[b])
                    transpose8(P, xn[b], hT[b], psum[0:2])
                    for cb in range(6):
                        c0 = cb * 512
                        cw = min(512, DIN - c0)
                        pst = psum[2 + cb]
                        for kt in range(8):
                            P.mm(pst[:, 0:cw], hT[b][:, kt, :], win_sb[:, kt, c0:c0 + cw], kt == 0, kt == 7, [hT[b], win_res[cb]], [pst])
                        P.cp("dve" if cb % 2 == 0 else "act", zt[b][:, c0:c0 + cw], pst[:, 0:cw], [pst], [zt[b]])
                    P.dma("pool", z_d[i * 128:(i + 1) * 128, :], zt[b][:], [zt[b]], [z_res[i]])

            if "skip_mix" not in dbg:
                emit_mixers(P, L, need_ctx, dram, z_d, z_res, y_d, y_res, psum, ident, dbg)

            tiles5 = list(range(NT)) if need_ctx else list(range(2, NT))
            x_src = dram["xall"] if L == 0 else xs_d
            if "skip_p5" not in dbg:
              with P.phase():
                wst = [P.sb("wst%d" % i, [128, 1024]) for i in range(2)]
                wout_sb = P.sb("wout", [128, 8, D], BF16)
                gB = P.sb("gB", [128, D])
                G1 = P.sb("G1", [128, D])
                A2 = P.sb("A2", [128, D])
                B2 = P.sb("B2", [128, D])
                yt = [P.sb("yt%d" % i, [128, D]) for i in range(2)]
                xt = [P.sb("xt%d" % i, [128, D]) for i in range(2)]
                yT = [P.sb("yT%d" % i, [128, 8, 128], BF16) for i in range(2)]
                tmp = [P.sb("tmp%d" % i, [128, D]) for i in range(2)]
                xw = [P.sb("xw%d" % i, [128, D]) for i in range(2)]
                h2 = [P.sb("h2%d" % i, [128, D]) for i in range(2)]
                h2T = [P.sb("h2T%d" % i, [128, 8, 128], BF16) for i in range(2)]
                st5 = [P.sb("st5_%d" % i, [128, 8]) for i in range(2)]
                junk = P.sb("junk", [128, D])
                for kt in range(8):
                    load_cast(P, wout_sb[:, kt, :], wout_sb, dram["w_out"][L, kt * 128:(kt + 1) * 128, :], wst, kt, D)
                def p5_a(i):
                    v = 1 if i < 2 else 0
                    if i == tiles5[0] or i == 2:
                        bcast_load("sp", gB, dram["g_mix_post"][L])
                        bcast_load("sp", G1, mrow(v, 2), [mods_res[L]])
                        P.tt("dve", G1[:], G1[:], gB[:], ALU.mult, [G1, gB], [G1])
                        bcast_load("sp", gB, dram["g_ffn_pre"][L])
                        bcast_load("sp", A2, mrow(v, 4), [mods_res[L]])
                        P.stt("dve", A2[:], A2[:], 1.0, gB[:], ALU.add, ALU.mult, [A2, gB], [A2])
                        bcast_load("sp", B2, mrow(v, 3), [mods_res[L]])
                    b = i % 2
                    P.dma("sp", yt[b][:], y_d[i * 128:(i + 1) * 128, :], [y_res[i]], [yt[b]])
                    P.dma("pool", xt[b][:], x_src[i * 128:(i + 1) * 128, :], [xs_res[i]], [xt[b]])
                    transpose8(P, yt[b], yT[b], psum[0:2])
                    pb = [psum[2 + 2 * b], psum[3 + 2 * b]]
                    for half in range(2):
                        for kt in range(8):
                            P.mm(pb[half][:, :], yT[b][:, kt, :], wout_sb[:, kt, half * 512:(half + 1) * 512], kt == 0, kt == 7,
                                 [yT[b], wout_sb], [pb[half]])
                        P.act("act", junk[:, 0:512], pb[half][:, :], AF.Square, [pb[half]], [junk], accum_out=st5[b][:, 3 + half:4 + half])
                    P.tt("dve", st5[b][:, 0:1], st5[b][:, 3:4], st5[b][:, 4:5], ALU.add, [st5[b], junk], [st5[b]])
                    P.rstd(st5[b], 1.0 / D, EPS)
                    for half in range(2):
                        hs = slice(half * 512, (half + 1) * 512)
                        P.stt("dve", tmp[b][:, hs], pb[half][:, :], st5[b][:, 2:3], G1[:, hs], ALU.mult, ALU.mult,
                              [pb[half], st5[b], G1], [tmp[b]])
                    P.tt("pool", xw[b][:], tmp[b][:], xt[b][:], ALU.add, [tmp[b], xt[b]], [xw[b]])
                    P.dma("pool", xs_d[i * 128:(i + 1) * 128, :], xw[b][:], [xw[b]], [xs_res[i]])
                    rms_tile(P, xw[b], st5[b], junk, A2, B2, h2[b])

                def p5_b(i):
                    b = i % 2
                    transpose8(P, h2[b], h2T[b], psum[6:8])
                    P.dma("sp", h2T_d[:, :, i * 128:(i + 1) * 128], h2T[b][:], [h2T[b]], [h2_res[i]])

                for k in range(len(tiles5) + 1):
                    if k < len(tiles5):
                        p5_a(tiles5[k])
                    if k >= 1:
                        p5_b(tiles5[k - 1])

            if "skip_p6" not in dbg:
              with P.phase():
                wst = [P.sb("wst%d" % i, [128, 1024]) for i in range(2)]
                wup_sb = P.sb("wup", [128, 8, 2 * DFF], BF16)
                wdn_sb = P.sb("wdn", [128, 22, D], BF16)
                cwt = P.sb("cwt", [128, 3, 44])
                cbt = P.sb("cbt", [128, 44])
                gB = P.sb("gB", [128, D])
                G2s = [P.sb("G2_%d" % i, [128, D]) for i in range(2)]
                pending_down = []
                h2blk = [P.sb("h2blk%d" % i, [128, 8, 258], BF16) for i in range(2)]
                cv = [[P.sb("cv%d_%d" % (i, j), [128, 256]) for j in range(2)] for i in range(2)]
                aTs = [P.sb("aT%d" % i, [128, 22, 256], BF16) for i in range(2)]
                xt = [P.sb("xt%d" % i, [128, D]) for i in range(2)]
                tmp = [P.sb("tmp%d" % i, [128, D]) for i in range(2)]
                st6 = [P.sb("st6_%d" % i, [128, 8]) for i in range(2)]
                junk = P.sb("junk", [128, 512])
                wup_res = [Res("wup%d" % c) for c in range(6)]
                wdn_res = [Res("wdn%d" % j) for j in range(22)]
                idx = 0
                for c in (0, 2, 3, 1, 4, 5):
                    c0 = c * 1024
                    cw = min(1024, 2 * DFF - c0)
                    for kt in range(8):
                        load_cast(P, wup_sb[:, kt, c0:c0 + cw], wup_res[c], dram["ffn_w_up"][L, kt * 128:(kt + 1) * 128, c0:c0 + cw], wst, idx, cw)
                        idx += 1
                for j in range(22):
                    load_cast(P, wdn_sb[:, j, :], wdn_res[j], dram["ffn_w_down"][L, j * 128:(j + 1) * 128, :], wst, idx, D)
                    idx += 1
                for tap in range(3):
                    P.dma("sp", cwt[:, tap, :], dram["ffn_conv_w"][L, tap].rearrange("(j p) -> p j", p=128), (), [cwt],
                          allow_slow_non_contiguous=True)
                P.dma("sp", cbt[:, :], dram["ffn_conv_b"][L, 0].rearrange("(j p) -> p j", p=128), (), [cbt],
                      allow_slow_non_contiguous=True)
                blocks = list(range(17)) if need_ctx else list(range(1, 17))
                ucount = 0
                for bi in blocks:
                    v = 1 if bi == 0 else 0
                    if bi == blocks[0] or bi == 1:
                        bcast_load("sp", gB, dram["g_ffn_post"][L])
                        bcast_load("sp", G2s[v], mrow(v, 5), [mods_res[L]])
                        P.tt("dve", G2s[v][:], G2s[v][:], gB[:], ALU.mult, [G2s[v], gB], [G2s[v]])
                    t0 = bi * 256
                    aT = aTs[bi % 2]
                    hb = h2blk[bi % 2]
                    lval = t0 not in (0, TCTX)
                    rval = (t0 + 256) not in (TCTX, TALL)
                    if not lval:
                        P.memset("pool", hb[:, :, 0:1], 0.0, [hb])
                    if not rval:
                        P.memset("pool", hb[:, :, 257:258], 0.0, [hb])
                    a = t0 - 1 if lval else t0
                    e = t0 + 257 if rval else t0 + 256
                    rtiles = sorted(set([a // 128, (e - 1) // 128, t0 // 128, t0 // 128 + 1]))
                    P.dma("sp", hb[:, :, a - (t0 - 1):e - (t0 - 1)], h2T_d[:, :, a:e], [h2_res[r] for r in rtiles], [hb])
                    for j in range(22):
                        cs = cv[j % 2]
                        for part in range(2):
                            fi = part * 22 + j
                            f0 = part * DFF + j * 128
                            pst = psum[ucount % 4]
                            ucount += 1
                            for kt in range(8):
                                P.mm(pst[:, 0:258], wup_sb[:, kt, f0:f0 + 128], hb[:, kt, :], kt == 0, kt == 7,
                                     [wup_res[f0 // 1024], wup_res[(f0 + 127) // 1024], hb], [pst])
                            c = cs[part]
                            P.act("act", c[:], pst[:, 1:257], AF.Identity, [pst, cwt, cbt], [c], bias=cbt[:, fi:fi + 1], scale=cwt[:, 1, fi:fi + 1])
                            P.stt("dve", c[:], pst[:, 0:256], cwt[:, 0, fi:fi + 1], c[:], ALU.mult, ALU.add, [pst, cwt, c], [c])
                            P.stt("dve", c[:], pst[:, 2:258], cwt[:, 2, fi:fi + 1], c[:], ALU.mult, ALU.add, [pst, cwt, c], [c])
                        P.act("act", cs[1][:], cs[1][:], AF.Silu, [cs[1]], [cs[1]])
                        P.tt("pool", aT[:, j, :], cs[0][:], cs[1][:], ALU.mult, [cs[0], cs[1]], [aT])
                        for _ in range(5):
                            if pending_down:
                                pending_down.pop(0)()
                    def make_down(bi, aT, v):
                        items = []
                        for s in range(2):
                            i = bi * 2 + s
                            b = s
                            pb = [psum[4 + 2 * s], psum[5 + 2 * s]]
                            items.append(lambda i=i, b=b: P.dma("pool", xt[b][:], xs_d[i * 128:(i + 1) * 128, :], [xs_res[i]], [xt[b]]))
                            for half in range(2):
                                for j in range(22):
                                    items.append(lambda s=s, j=j, half=half, pb=pb: P.mm(
                                        pb[half][:, :], aT[:, j, s * 128:(s + 1) * 128], wdn_sb[:, j, half * 512:(half + 1) * 512],
                                        j == 0, j == 21, [aT, wdn_res[j]], [pb[half]]))
                                items.append(lambda b=b, half=half, pb=pb: P.act(
                                    "act", junk[:, 0:512], pb[half][:, :], AF.Square, [pb[half]], [junk], accum_out=st6[b][:, 3 + half:4 + half]))

                            def epi(i=i, b=b, pb=pb):
                                P.tt("dve", st6[b][:, 0:1], st6[b][:, 3:4], st6[b][:, 4:5], ALU.add, [st6[b], junk], [st6[b]])
                                P.rstd(st6[b], 1.0 / D, EPS)
                                for half in range(2):
                                    hs = slice(half * 512, (half + 1) * 512)
                                    P.stt("dve", tmp[b][:, hs], pb[half][:, :], st6[b][:, 2:3], G2s[v][:, hs], ALU.mult, ALU.mult,
                                          [pb[half], st6[b], G2s[v]], [tmp[b]])
                                P.tt("pool", tmp[b][:], tmp[b][:], xt[b][:], ALU.add, [tmp[b], xt[b]], [tmp[b]])
                                if last and i >= 2:
                                    P.dma("sp", out_ap[(i - 2) * 128:(i - 1) * 128, :], tmp[b][:], [tmp[b]], ())
                                else:
                                    P.dma("sp", xs_d[i * 128:(i + 1) * 128, :], tmp[b][:], [tmp[b]], [xs_res[i]])
                            items.append(epi)
                        return items
                    pending_down.extend(make_down(bi, aT, v))
                while pending_down:
                    pending_down.pop(0)()

        P.flush(final=True)
    return nc


def host_inputs(inputs, b):
    m = {}
    m["xall"] = np.ascontiguousarray(np.concatenate([inputs["ctx"][b], inputs["x"][b]], axis=0), dtype=np.float32)
    m["c2"] = np.ascontiguousarray(np.stack([inputs["c"][b], inputs["c_ctx"]], axis=0), dtype=np.float32)
    m["ident"] = np.eye(128, dtype=np.float32)
    for name, shp in W_SPECS:
        m[name] = np.ascontiguousarray(np.asarray(inputs[name], dtype=np.float32).reshape([NL_FULL] + list(shp)))
    mix_host_inputs(inputs, m)
    return m


def used_inputs(nc, m):
    shapes = {}
    for alloc in nc.allocations:
        try:
            if alloc.kind == "ExternalInput":
                shapes[alloc.memorylocations[0].name] = tuple(alloc.tensor_shape)
        except Exception:
            pass
    out = {}
    for k, v in m.items():
        if k in shapes:
            shp = shapes[k]
            if tuple(v.shape) != shp:
                v = np.ascontiguousarray(v[0:shp[0]])
            assert tuple(v.shape) == shp, (k, v.shape, shp)
            out[k] = v
    return out


def kernel(**inputs):
    inputs = {k: np.asarray(v) for k, v in inputs.items()}
    nc = build_program()
    shared = host_inputs(inputs, 0)
    maps = []
    for b in range(4):
        m = dict(shared)
        m["xall"] = np.ascontiguousarray(np.concatenate([inputs["ctx"][b], inputs["x"][b]], axis=0), dtype=np.float32)
        m["c2"] = np.ascontiguousarray(np.stack([inputs["c"][b], inputs["c_ctx"]], axis=0), dtype=np.float32)
        maps.append(used_inputs(nc, m))
    in_maps = [maps[b % 4] for b in range(N_CORES)]
    res = run_bass_kernel_spmd(nc, in_maps, core_ids=list(range(N_CORES)))
    out = np.stack([np.asarray(res.results[b]["out"]) for b in range(4)], axis=0)
    return out.astype(np.float32)
```

```python
import contextlib
import numpy as np
import concourse.bass as bass
import concourse.mybir as mybir
from concourse.bass_utils import run_bass_kernel_spmd

F32 = mybir.dt.float32
BF16 = mybir.dt.bfloat16
ALU = mybir.AluOpType
AF = mybir.ActivationFunctionType
AX = mybir.AxisListType

D = 1024
NL_FULL = 4
TCTX = 256
TLAT = 4096
TALL = TCTX + TLAT
NT = TALL // 128
DIN = 2928
DFF = 2816
EPS = 1e-6


class Res:
    __slots__ = ("name", "w", "r")

    def __init__(self, name=""):
        self.name = name
        self.w = None
        self.r = {}


class T:
    def __init__(self, handle, name):
        self.h = handle
        self.res = Res(name)

    def __getitem__(self, k):
        return self.h[k]


class Prog:
    EPOCH = 30000
    KD = 6

    def __init__(self, nc, st):
        self.nc = nc
        self.st = st
        self.engs = ["pe", "act", "dve", "pool", "sp"]
        self.ops = {e: [] for e in self.engs}
        self.cnt = {e: 0 for e in self.engs}
        self.seen = {e: {} for e in self.engs}
        self.dcnt = {e: 0 for e in self.engs}
        self.sems = {}
        self.n_ops = 0
        self.gst = st
        self.last_ev = {}
        self.pending = {e: [] for e in self.engs}

    def sb(self, name, shape, dt=F32):
        self.n_ops += 1
        h = self.st.enter_context(self.nc.sbuf_tensor("sb%d_%s" % (self.n_ops, name), list(shape), dt))
        return T(h, name)

    def ps(self, name, shape=(128, 512), dt=F32):
        h = self.st.enter_context(self.nc.psum_tensor("ps_" + name, list(shape), dt))
        return T(h, name)

    def _sem(self, key):
        if key not in self.sems:
            self.sems[key] = self.gst.enter_context(self.nc.semaphore("s_" + "_".join(str(k) for k in key)))
        self.last_ev[key] = max(self.last_ev.get(key, 0), 0)
        return self.sems[key]

    def barrier(self):
        evs = [(k, v) for k, v in self.last_ev.items() if v > 0]
        for e in self.engs:
            self.pending[e] = list(evs)

    def barrier_light(self, ress):
        evs = [r.w for r in ress if r.w is not None]
        for e in self.engs:
            self.pending[e] = self.pending[e] + list(evs)

    @contextlib.contextmanager
    def phase(self):
        outer = self.st
        with contextlib.ExitStack() as ph:
            self.st = ph
            yield
            self.barrier()
            self.flush()
        self.st = outer

    def _deps(self, eng, reads, writes, is_dma):
        evs = []
        for ev in self.pending[eng]:
            evs.append((ev, "bar"))
        self.pending[eng] = []
        for t in reads:
            r = t.res if isinstance(t, T) else t
            if r.w is not None:
                evs.append((r.w, "raw"))
        for t in writes:
            r = t.res if isinstance(t, T) else t
            if r.w is not None:
                evs.append((r.w, "waw"))
            for ev in r.r.values():
                evs.append((ev, "war"))
        waits = {}
        for (key, val), kind in evs:
            if key[0] == "e" and key[1] == eng and not is_dma:
                if kind != "raw" or eng == "pe":
                    continue
            if self.seen[eng].get(key, 0) >= val:
                continue
            if waits.get(key, 0) < val:
                waits[key] = val
        for k, v in waits.items():
            self.seen[eng][k] = v
        return list(waits.items())

    def _mark(self, ev, reads, writes):
        for t in reads:
            r = t.res if isinstance(t, T) else t
            r.r[ev[0]] = ev
        for t in writes:
            r = t.res if isinstance(t, T) else t
            r.w = ev
            r.r = {}

    def op(self, eng, fn, reads=(), writes=()):
        waits = self._deps(eng, reads, writes, False)
        n = self.cnt[eng]
        key = ("e", eng, n // self.EPOCH)
        ev = (key, n % self.EPOCH + 1)
        self.cnt[eng] = n + 1
        self._sem(key)
        self.last_ev[key] = ev[1]
        self.ops[eng].append((waits, fn, key, 1))
        self._mark(ev, reads, writes)
        self.n_ops += 1

    def dma(self, q, out, in_, reads=(), writes=(), **kw):
        waits = self._deps(q, reads, writes, True)
        j = self.dcnt[q]
        self.dcnt[q] = j + 1
        key = ("d", q, j % self.KD)
        val = 16 * (j // self.KD + 1)
        if j >= self.KD and self.seen[q].get(key, 0) < val - 16:
            waits.append((key, val - 16))
            self.seen[q][key] = val - 16
        self._sem(key)
        self.last_ev[key] = val
        self.ops[q].append((waits, lambda e: e.dma_start(out=out, in_=in_, **kw), key, 16))
        self._mark((key, val), reads, writes)
        self.n_ops += 1

    def mm(self, out, lhsT, rhs, start, stop, reads, writes):
        self.op("pe", lambda e: e.matmul(out, lhsT, rhs, start=start, stop=stop), reads, writes)

    def tr(self, out, in_, ident, reads, writes):
        self.op("pe", lambda e: e.transpose(out, in_, ident), reads, writes)

    def act(self, eng, out, in_, func, reads, writes, bias=None, scale=None, accum_out=None):
        kw = {}
        if bias is not None:
            kw["bias"] = bias
        if scale is not None:
            kw["scale"] = scale
        if accum_out is not None:
            kw["accum_out"] = accum_out
        self.op(eng, lambda e: e.activation(out, in_, func, **kw), reads, writes)

    def tt(self, eng, out, in0, in1, op, reads, writes):
        self.op(eng, lambda e: e.tensor_tensor(out, in0, in1, op), reads, writes)

    def ts(self, eng, out, in0, s1, s2, op0, op1, reads, writes, accum_out=None):
        if op1 is None:
            self.op(eng, lambda e: e.tensor_scalar(out, in0, s1, None, op0), reads, writes)
        elif accum_out is None:
            self.op(eng, lambda e: e.tensor_scalar(out, in0, s1, s2, op0, op1), reads, writes)
        else:
            self.op(eng, lambda e: e.tensor_scalar(out, in0, s1, s2, op0, op1, accum_out), reads, writes)

    def stt(self, eng, out, in0, scalar, in1, op0, op1, reads, writes):
        self.op(eng, lambda e: e.scalar_tensor_tensor(out, in0, scalar, in1, op0, op1), reads, writes)

    def cp(self, eng, out, in_, reads, writes):
        if eng == "act":
            self.op(eng, lambda e: e.copy(out, in_), reads, writes)
        else:
            self.op(eng, lambda e: e.tensor_copy(out, in_), reads, writes)

    def red(self, eng, out, in_, op, reads, writes, axis=AX.X):
        self.op(eng, lambda e: e.tensor_reduce(out, in_, axis, op), reads, writes)

    def rstd(self, t, scale, eps, extra_reads=()):
        self.act("act", t[:, 1:2], t[:, 0:1], AF.Sqrt, [t, self.consts[eps]] + list(extra_reads), [t], bias=self.eps_ap(eps, t), scale=scale)
        self.op("dve", lambda e: e.reciprocal(t[:, 2:3], t[:, 1:2]), [t], [t])

    def eps_ap(self, eps, t):
        return self.consts[eps][0:t.h.shape[0], 0:1]

    def sumsq(self, dst_col, srcs, junk, reads):
        raise NotImplementedError

    def memset(self, eng, ap, val, writes):
        self.op(eng, lambda e: e.memset(ap, val), (), writes)

    def flush(self, final=False):
        nc = self.nc
        fin = [(k, v) for k, v in self.last_ev.items() if v > 0] if final else []
        sems = self.sems
        engmap = {"pe": "tensor", "act": "scalar", "dve": "vector", "pool": "gpsimd", "sp": "sync"}
        ops = self.ops
        with nc.Block() as block:
            def make(ename):
                def body(e):
                    for waits, fn, key, inc in ops[ename]:
                        for wk, wv in waits:
                            e.wait_ge(sems[wk], wv)
                        fn(e).then_inc(sems[key], inc)
                    if ename == "sp":
                        for wk, wv in fin:
                            e.wait_ge(sems[wk], wv)
                return body
            for ename in self.engs:
                getattr(block, engmap[ename])(make(ename))
        self.ops = {e: [] for e in self.engs}


NEG = -30000.0


def emit_na(P, L, need_ctx, dram, z_d, z_res, y_d, y_res, psum, ident):
    with P.phase():
        QT = P.sb("naQT", [128, 2, TALL], BF16)
        KT = P.sb("naKT", [128, 2, TALL], BF16)
        Ve = P.sb("naVe", [128, NT, 4, 66], BF16)
        Vo = P.sb("naVo", [128, NT - 1, 4, 66], BF16)
        btab = P.sb("btab", [128, 4, 14, 64])
        maskt = P.sb("mask", [128, 64])
        zt = [P.sb("zt%d" % i, [128, 768]) for i in range(2)]
        zo = [P.sb("zo%d" % i, [128, 256]) for i in range(2)]
        sw = [P.sb("sw%d" % i, [128, 4, 64]) for i in range(2)]
        pT = [P.sb("pT%d" % i, [128, 6, 64], BF16) for i in range(2)]
        rc = [P.sb("rc%d" % i, [64, 4]) for i in range(2)]
        yo = [P.sb("yo%d" % i, [64, 4, 64]) for i in range(2)]
        import os
        stage = float(os.environ.get("NA_STAGE", "9"))
        if stage < 0.5:
            return
        P.dma("sp", btab[:], dram["na_btab"][L], (), [btab])
        P.dma("sp", maskt[:], dram["na_mask"], (), [maskt])
        P.tt("dve", btab[:].rearrange("p h d q -> p (h d) q"), btab[:].rearrange("p h d q -> p (h d) q"),
             maskt[:, :].unsqueeze(1).to_broadcast([128, 56, 64]), ALU.add, [btab, maskt], [btab])
        if stage < 0.7:
            return
        P.memset("pool", Ve[:, :, :, 64:65], 1.0, [Ve])
        P.memset("pool", Vo[:, :, :, 64:65], 1.0, [Vo])
        if stage < 0.9:
            return
        for i in range(NT):
            b = i % 2
            P.dma("sp", zt[b][:], z_d[i * 128:(i + 1) * 128, 0:768], [z_res[i]], [zt[b]])
            pst = psum[i % 2]
            for c in range(4):
                P.tr(pst[:, c * 128:(c + 1) * 128], zt[b][:, c * 128:(c + 1) * 128], ident[:], [zt[b], ident], [pst])
            pv = pst[:, :].rearrange("p (c t) -> p c t", c=4)
            if stage > 0.915:
                P.cp("act", QT[:, :, i * 128:(i + 1) * 128], pv[:, 0:2, :], [pst], [QT])
            if stage > 0.925:
                P.cp("act", KT[:, :, i * 128:(i + 1) * 128], pv[:, 2:4, :], [pst], [KT])
            if stage > 0.935:
                P.cp("dve", Ve[:, i, :, 0:64], zt[b][:, 512:768].rearrange("p (h d) -> p h d", h=4), [zt[b]], [Ve])
            if i < NT - 1 and stage > 0.945:
                P.dma("pool", zo[b][:], z_d[64 + i * 128:64 + (i + 1) * 128, 512:768], [z_res[i], z_res[i + 1]], [zo[b]])
                if stage > 0.955:
                    P.cp("dve", Vo[:, i, :, 0:64], zo[b][:].rearrange("p (h d) -> p h d", h=4), [zo[b]], [Vo])

        units = []

        def na_rows(q0, keyspec, bias_di0, out_row0):
            units.append((q0, keyspec, bias_di0, out_row0))

        def part_a(u, h):
            q0, keyspec, bias_di0, out_row0 = units[u]
            g, hp = h // 2, (h % 2) * 64
            pst = psum[(u * 4 + h) % 4]
            for c, (k0, Vt, ti) in enumerate(keyspec):
                P.mm(pst[:, c * 64:(c + 1) * 64], KT[hp:hp + 64, g, k0:k0 + 128], QT[hp:hp + 64, g, q0:q0 + 64], True, True,
                     [KT, QT], [pst])

        def part_b(u, h):
            q0, keyspec, bias_di0, out_row0 = units[u]
            nk = len(keyspec)
            pst = psum[(u * 4 + h) % 4]
            po = psum[4 + u % 2]
            p = pT[(u * 4 + h) % 2]
            c0 = 0
            if bias_di0 is not None:
                s_ = sw[(u * 4 + h) % 2]
                P.stt("dve", s_[:], pst[:, 0:256].rearrange("p (c q) -> p c q", c=4), 0.125,
                      btab[:, h, bias_di0:bias_di0 + 7:2, :], ALU.mult, ALU.add, [pst, btab], [s_])
                P.act("act", p[:, 0:4, :], s_[:], AF.Exp, [s_], [p])
                c0 = 4
            P.act("act", p[:, c0:nk, :], pst[:, c0 * 64:nk * 64].rearrange("p (c q) -> p c q", c=nk - c0), AF.Exp, [pst], [p],
                  scale=0.125)
            for c, (k0, Vt, ti) in enumerate(keyspec):
                P.mm(po[0:64, h * 65:(h + 1) * 65], p[:, c, :], Vt[:, ti, h, 0:65], c == 0, c == nk - 1, [p, Vt], [po])
            if h == 3:
                r_ = rc[u % 2]
                y_ = yo[u % 2]
                pov = po[0:64, 0:260].rearrange("p (h e) -> p h e", h=4)
                P.op("dve", lambda e: e.reciprocal(r_[:, :], pov[:, :, 64]), [po], [r_])
                P.tt("dve", y_[:], pov[:, :, 0:64], r_[:, :].unsqueeze(2).to_broadcast([64, 4, 64]), ALU.mult, [po, r_], [y_])
                P.dma("sp", y_d[out_row0:out_row0 + 64, 0:256], y_[:].rearrange("p h d -> p (h d)"), [y_], [y_res[out_row0 // 128]])

        def run_units():
            seq = [(u, h) for u in range(len(units)) for h in range(4)]
            LOOK = 2
            for k in range(len(seq) + LOOK):
                if k < len(seq):
                    part_a(*seq[k])
                if k >= LOOK:
                    part_b(*seq[k - LOOK])

        ctxkeys = [(0, Ve, 0), (128, Ve, 1)]
        if stage < 2:
            return
        if need_ctx:
            for qc in range(4):
                na_rows(qc * 64, ctxkeys, None, qc * 64)
        for r in range(64 if stage >= 3 else 0):
            w0 = min(max(r - 4, 0), 56)
            key0 = TCTX + w0 * 64
            ks = []
            for c in range(4):
                k0 = key0 + c * 128
                if k0 % 128 == 0:
                    ks.append((k0, Ve, k0 // 128))
                else:
                    ks.append((k0, Vo, (k0 - 64) // 128))
            na_rows(TCTX + r * 64, ks + ctxkeys, w0 - r + 7, TCTX + r * 64)
        run_units()


def emit_mla(P, L, need_ctx, dram, z_d, z_res, y_d, y_res, psum, ident):
    SC = 96.0 ** -0.5
    with P.phase():
        nT0 = P.sb("nT0", [128, TALL], BF16)
        nT1 = P.sb("nT1", [64, TALL], BF16)
        nkT = P.sb("nkT", [128, TALL], BF16)
        krT = P.sb("krT", [96, TALL], BF16)
        KTm = P.sb("KTm", [96, 4, TALL], BF16)
        Vm = P.sb("Vm", [128, NT, 4, 66], BF16)
        wst = P.sb("wst", [128, 768])
        wq = P.sb("wq", [128, 2, 4, 96], BF16)
        wqp = P.sb("wqp", [128, 2, 4, 96], BF16)
        wk = P.sb("wk", [128, 4, 96], BF16)
        wv = P.sb("wv", [128, 256], BF16)
        Pm = P.sb("Pm", [96, 96], BF16)
        qnb = P.sb("qnb", [128, 192])
        kvnb = P.sb("kvnb", [128, 128])
        zt = [P.sb("zt%d" % i, [128, 352]) for i in range(2)]
        nq = [P.sb("nq%d" % i, [128, 416]) for i in range(2)]
        stt_ = [P.sb("st%d" % i, [128, 8]) for i in range(2)]
        junk = P.sb("junk", [128, 192])
        ctab = [P.sb("ctab%d" % i, [96, 2, 512]) for i in range(2)]
        t1 = [P.sb("t1_%d" % i, [96, 512]) for i in range(2)]
        t2 = [P.sb("t2_%d" % i, [96, 512]) for i in range(2)]
        QTb = [P.sb("QTb%d" % i, [96, 512], BF16) for i in range(2)]
        pT = [P.sb("pT%d" % i, [128, 512], BF16) for i in range(3)]
        rc = [P.sb("rc%d" % i, [128, 4]) for i in range(2)]
        yo = [P.sb("yo%d" % i, [128, 4, 4, 64]) for i in range(2)]
        oT = [P.sb("oT%d" % i, [65, 512]) for i in range(2)]
        def ldw(dst_ap, dst_t, src, rows, cols):
            P.dma("sp", wst[0:rows, 0:cols], src, (), [wst])
            P.cp("act", dst_ap, wst[0:rows, 0:cols], [wst], [dst_t])
        ldw(wq[:, 0, :, :].rearrange("p h e -> p (h e)"), wq, dram["mla_wq_r"][L, 0:128, :], 128, 384)
        ldw(wq[0:64, 1, :, :].rearrange("p h e -> p (h e)"), wq, dram["mla_wq_r"][L, 128:192, :], 64, 384)
        ldw(wqp[:, 0, :, :].rearrange("p h e -> p (h e)"), wqp, dram["mla_wq_p"][L, 0:128, :], 128, 384)
        ldw(wqp[0:64, 1, :, :].rearrange("p h e -> p (h e)"), wqp, dram["mla_wq_p"][L, 128:192, :], 64, 384)
        ldw(wk[:, :, :].rearrange("p h e -> p (h e)"), wk, dram["mla_wk"][L], 128, 384)
        ldw(wv[:, :], wv, dram["mla_wv"][L], 128, 256)
        ldw(Pm[:, :], Pm, dram["mla_pm"], 96, 96)
        for n_ in nq:
            P.memset("pool", n_[:, 320:384], 0.0, [n_])
        P.dma("sp", qnb[:], dram["mla_q_norm"][L].partition_broadcast(128), (), [qnb])
        P.dma("sp", kvnb[:], dram["mla_kv_norm"][L].partition_broadcast(128), (), [kvnb])
        P.memset("pool", Vm[:, :, :, 64:65], 1.0, [Vm])
        for i in range(NT):
            b = i % 2
            z_, n_, s_ = zt[b], nq[b], stt_[b]
            P.dma("sp", z_[:], z_d[i * 128:(i + 1) * 128, 768:1120], [z_res[i]], [z_])
            P.act("act", junk[:, 0:192], z_[:, 0:192], AF.Square, [z_], [junk], accum_out=s_[:, 0:1])
            P.rstd(s_, 1.0 / 192, EPS, [junk])
            P.stt("dve", n_[:, 0:192], z_[:, 0:192], s_[:, 2:3], qnb[:], ALU.mult, ALU.mult, [z_, s_, qnb], [n_])
            P.act("act", junk[:, 0:128], z_[:, 192:320], AF.Square, [z_], [junk], accum_out=s_[:, 4:5])
            P.act("act", s_[:, 5:6], s_[:, 4:5], AF.Sqrt, [s_, junk, P.consts[EPS]], [s_], bias=P.consts[EPS][:, 0:1], scale=1.0 / 128)
            P.op("dve", lambda e, s_=s_: e.reciprocal(s_[:, 6:7], s_[:, 5:6]), [s_], [s_])
            P.stt("dve", n_[:, 192:320], z_[:, 192:320], s_[:, 6:7], kvnb[:], ALU.mult, ALU.mult, [z_, s_, kvnb], [n_])
            pst = psum[i % 2]
            P.tr(pst[:, 0:128], n_[:, 0:128], ident[:], [n_, ident], [pst])
            P.tr(pst[0:64, 128:256], n_[:, 128:192], ident[:], [n_, ident], [pst])
            P.tr(pst[:, 256:384], n_[:, 192:320], ident[:], [n_, ident], [pst])
            P.cp("pool", n_[:, 384:416], z_[:, 320:352], [z_], [n_])
            P.tr(pst[0:96, 384:512], n_[:, 320:416], ident[:], [n_, ident], [pst])
            ts_ = slice(i * 128, (i + 1) * 128)
            P.cp("act", nT0[:, ts_], pst[:, 0:128], [pst], [nT0])
            P.cp("act", nT1[:, ts_], pst[0:64, 128:256], [pst], [nT1])
            P.cp("act", nkT[:, ts_], pst[:, 256:384], [pst], [nkT])
            P.cp("act", krT[:, ts_], pst[0:96, 384:512], [pst], [krT])
        nblk = [(b0, min(512, TALL - b0)) for b0 in range(0, TALL, 512)]
        for bi, (b0, n) in enumerate(nblk):
            ct = ctab[bi % 2]
            P.dma("sp", ct[64:96, :, 0:n], dram["rope_tab"][:, :, b0:b0 + n].rearrange("c p t -> p c t"), (), [ct])
            for h in range(4):
                pst = psum[h % 2]
                P.mm(pst[0:96, 0:n], wk[:, h, :], nkT[:, b0:b0 + n], True, True, [wk, nkT], [pst])
                P.cp("act", KTm[0:64, h, b0:b0 + n], pst[0:64, 0:n], [pst], [KTm])
            pb = psum[2]
            P.mm(pb[0:96, 0:n], Pm[:, :], krT[:, b0:b0 + n], True, True, [Pm, krT], [pb])
            a1, a2 = t1[bi % 2], t2[bi % 2]
            P.tt("dve", a1[64:96, 0:n], krT[64:96, b0:b0 + n], ct[64:96, 0, 0:n], ALU.mult, [krT, ct], [a1])
            P.tt("dve", a2[64:96, 0:n], pb[64:96, 0:n], ct[64:96, 1, 0:n], ALU.mult, [pb, ct], [a2])
            P.tt("pool", KTm[64:96, :, b0:b0 + n], a1[64:96, 0:n].unsqueeze(1).to_broadcast([32, 4, n]),
                 a2[64:96, 0:n].unsqueeze(1).to_broadcast([32, 4, n]), ALU.add, [a1, a2], [KTm])
            for s in range(n // 128):
                ti = b0 // 128 + s
                pv = psum[3 + s % 2]
                P.mm(pv[:, 0:256], nkT[:, ti * 128:(ti + 1) * 128], wv[:, :], True, True, [nkT, wv], [pv])
                P.cp("act", Vm[:, ti, :, 0:64], pv[:, 0:256].rearrange("p (h d) -> p h d", h=4), [pv], [Vm])
        qblocks = []
        if need_ctx:
            qblocks.append((0, 256, [0, 1]))
        for b0 in range(TCTX, TALL, 512):
            qblocks.append((b0, 512, list(range(NT))))
        cnt = 0
        for qi, (b0, n, ktiles) in enumerate(qblocks):
            ct = ctab[qi % 2]
            P.dma("sp", ct[64:96, :, 0:n], dram["rope_tab"][:, :, b0:b0 + n].rearrange("c p t -> p c t"), (), [ct])
            y_ = yo[qi % 2]
            nsub = n // 128
            for h in range(4):
                pa, pb = psum[0], psum[1]
                P.mm(pa[0:96, 0:n], wq[:, 0, h, :], nT0[:, b0:b0 + n], True, False, [wq, nT0], [pa])
                P.mm(pa[0:96, 0:n], wq[0:64, 1, h, :], nT1[:, b0:b0 + n], False, True, [wq, nT1], [pa])
                P.mm(pb[0:96, 0:n], wqp[:, 0, h, :], nT0[:, b0:b0 + n], True, False, [wqp, nT0], [pb])
                P.mm(pb[0:96, 0:n], wqp[0:64, 1, h, :], nT1[:, b0:b0 + n], False, True, [wqp, nT1], [pb])
                q_ = QTb[(qi * 4 + h) % 2]
                a1, a2 = t1[h % 2], t2[h % 2]
                P.cp("act", q_[0:64, 0:n], pa[0:64, 0:n], [pa], [q_])
                P.tt("dve", a1[64:96, 0:n], pa[64:96, 0:n], ct[64:96, 0, 0:n], ALU.mult, [pa, ct], [a1])
                P.tt("dve", a2[64:96, 0:n], pb[64:96, 0:n], ct[64:96, 1, 0:n], ALU.mult, [pb, ct], [a2])
                P.tt("pool", q_[64:96, 0:n], a1[64:96, 0:n], a2[64:96, 0:n], ALU.add, [a1, a2], [q_])
                poT = psum[5 + (qi * 4 + h) % 2]
                LOOK = 2
                nk_ = len(ktiles)
                slots_ = {}
                for ki in range(nk_ + LOOK):
                    if ki < nk_:
                        kt = ktiles[ki]
                        pst = psum[2 + cnt % 3]
                        p = pT[cnt % 3]
                        cnt += 1
                        P.mm(pst[:, 0:n], KTm[:, h, kt * 128:(kt + 1) * 128], q_[:, 0:n], True, True, [KTm, q_], [pst])
                        slots_[ki] = (pst, p, kt)
                    kj = ki - LOOK
                    if kj >= 0:
                        pst, p, kt = slots_.pop(kj)
                        P.act("act", p[:, 0:n], pst[:, 0:n], AF.Exp, [pst], [p], scale=SC)
                        P.mm(poT[0:65, 0:n], Vm[:, kt, h, 0:65], p[:, 0:n], kj == 0, kj == nk_ - 1, [p, Vm], [poT])
                o_ = oT[h % 2]
                P.cp("dve", o_[:, 0:n], poT[0:65, 0:n], [poT], [o_])
                po = psum[7]
                for s in range(nsub):
                    P.tr(po[:, s * 65:(s + 1) * 65], o_[0:65, s * 128:(s + 1) * 128], ident[0:65, 0:65], [o_, ident], [po])
                r_ = rc[h % 2]
                pov = po[:, 0:nsub * 65].rearrange("p (s e) -> p s e", s=nsub)
                P.op("dve", lambda e, r_=r_, pov=pov, nsub=nsub: e.reciprocal(r_[:, 0:nsub], pov[:, :, 64]), [po], [r_])
                P.tt("dve", y_[:, 0:nsub, h, :], pov[:, :, 0:64], r_[:, 0:nsub].unsqueeze(2).to_broadcast([128, nsub, 64]), ALU.mult,
                     [po, r_], [y_])
            for s in range(nsub):
                r0 = b0 + s * 128
                P.dma("sp", y_d[r0:r0 + 128, 256:512], y_[:, s, :, :].rearrange("p h d -> p (h d)"), [y_], [y_res[r0 // 128]])


def emit_mixers(P, L, need_ctx, dram, z_d, z_res, y_d, y_res, psum, ident, dbg):
    if "skip_na" not in dbg:
        emit_na(P, L, need_ctx, dram, z_d, z_res, y_d, y_res, psum, ident)
    if "skip_mla" not in dbg:
        emit_mla(P, L, need_ctx, dram, z_d, z_res, y_d, y_res, psum, ident)
    if "skip_gla" not in dbg:
        emit_gla(P, L, need_ctx, dram, z_d, z_res, y_d, y_res, psum, ident)
    if "skip_rw" not in dbg:
        emit_rwkv(P, L, need_ctx, dram, z_d, z_res, y_d, y_res, psum, ident)


def rope_tables():
    t = np.arange(TLAT)
    row = (t // 64).astype(np.float32)
    col = (t % 64).astype(np.float32)
    inv = (10000.0 ** (-np.arange(0, 16, 2, dtype=np.float32) / 16)).astype(np.float32)
    C = np.ones((32, TALL), np.float32)
    S = np.zeros((32, TALL), np.float32)
    for part, pos in ((0, row), (1, col)):
        ang = (pos[:, None] * inv[None, :]).astype(np.float32)
        c, s = np.cos(ang).T, np.sin(ang).T
        C[part * 16:part * 16 + 8, TCTX:] = c
        C[part * 16 + 8:part * 16 + 16, TCTX:] = c
        S[part * 16:part * 16 + 8, TCTX:] = -s
        S[part * 16 + 8:part * 16 + 16, TCTX:] = s
    return np.stack([C, S], 0).astype(np.float32)


def rope_partner():
    idx = np.arange(32)
    return np.where((idx % 16) < 8, idx + 8, idx - 8)


def mix_host_inputs(inputs, m):
    part = rope_partner()
    m["rope_tab"] = rope_tables()
    m["scan_consts"] = scan_consts()
    for nm, shp in (("rw_mu", (1, 1024)), ("rw_k_k", (1, 256)), ("rw_k_a", (1, 256)), ("rw_w0", (2, 1, 256)), ("rw_a0", (2, 1, 256)),
                    ("rw_w_up", (2, 64, 256)), ("rw_a_up", (2, 64, 256)), ("rw_g_up", (128, 256)), ("rw_ln_w", (1, 256)),
                    ("rw_ln_b", (1, 256)), ("rw_r_k", (1, 256))):
        m[nm] = np.ascontiguousarray(np.asarray(inputs[nm], np.float32).reshape((NL_FULL,) + shp))
    m["gla_gate_up"] = np.asarray(inputs["gla_gate_up"], np.float32)
    m["gla_gate_b"] = np.asarray(inputs["gla_gate_b"], np.float32).reshape(NL_FULL, 2, 1, 128)
    m["gla_norm"] = np.asarray(inputs["gla_norm"], np.float32).reshape(NL_FULL, 1, 256)
    pm = np.zeros((96, 96), np.float32)
    pm[64 + part, 64 + np.arange(32)] = 1.0
    m["mla_pm"] = pm
    wuq = np.asarray(inputs["mla_w_uq"], np.float32).reshape(NL_FULL, 192, 4, 96)
    m["mla_wq_r"] = np.ascontiguousarray(wuq.reshape(NL_FULL, 192, 384))
    wp = np.concatenate([wuq[..., 0:64], wuq[..., 64:96][..., part]], axis=-1)
    m["mla_wq_p"] = np.ascontiguousarray(wp.reshape(NL_FULL, 192, 384))
    wukv = np.asarray(inputs["mla_w_ukv"], np.float32).reshape(NL_FULL, 128, 4, 128)
    wk = np.zeros((NL_FULL, 128, 4, 96), np.float32)
    wk[..., 0:64] = wukv[..., 0:64]
    m["mla_wk"] = np.ascontiguousarray(wk.reshape(NL_FULL, 128, 384))
    m["mla_wv"] = np.ascontiguousarray(wukv[..., 64:128].reshape(NL_FULL, 128, 256))
    m["mla_q_norm"] = np.asarray(inputs["mla_q_norm"], np.float32).reshape(NL_FULL, 1, 192)
    m["mla_kv_norm"] = np.asarray(inputs["mla_kv_norm"], np.float32).reshape(NL_FULL, 1, 128)
    rpb = np.asarray(inputs["na_rpb"], np.float32)
    p = np.arange(128)
    k = p % 64
    q = np.arange(64)
    coff = np.clip(k[:, None] - q[None, :], -15, 15) + 15
    di = np.arange(14)
    roff = di[None, :] + (p[:, None] >= 64)
    m["na_btab"] = np.ascontiguousarray(
        rpb[:, :, roff[:, :, None], coff[:, None, :]].transpose(0, 2, 1, 3, 4))
    cstart = np.clip(q - 8, 0, 48)
    inwin = (k[:, None] >= cstart[None, :]) & (k[:, None] < cstart[None, :] + 16)
    m["na_mask"] = np.where(inwin, 0.0, NEG).astype(np.float32)
    return m


MIX_SPECS = [
    ("rw_mu", (1, 1024), True), ("rw_k_k", (1, 256), True), ("rw_k_a", (1, 256), True), ("rw_w0", (2, 1, 256), True),
    ("rw_a0", (2, 1, 256), True), ("rw_w_up", (2, 64, 256), True), ("rw_a_up", (2, 64, 256), True), ("rw_g_up", (128, 256), True),
    ("rw_ln_w", (1, 256), True), ("rw_ln_b", (1, 256), True), ("rw_r_k", (1, 256), True),
    ("scan_consts", (2, 128, 768), False), ("gla_gate_up", (2, 16, 128), True), ("gla_gate_b", (2, 1, 128), True),
    ("gla_norm", (1, 256), True),
    ("rope_tab", (2, 32, TALL), False), ("mla_pm", (96, 96), False),
    ("mla_wq_r", (192, 384), True), ("mla_wq_p", (192, 384), True), ("mla_wk", (128, 384), True), ("mla_wv", (128, 256), True),
    ("mla_q_norm", (1, 192), True), ("mla_kv_norm", (1, 128), True),
    ("na_btab", (128, 4, 14, 64), True), ("na_mask", (128, 64), False),
]


def scan_consts():
    s = np.arange(128)[:, None]
    t = np.arange(128)[None, :]
    out = np.zeros((2, 128, 768), np.float32)
    for d in range(2):
        incl = (s <= t) if d == 0 else (s >= t)
        strict = (s < t) if d == 0 else (s > t)
        out[d, :, 0:128] = incl
        out[d, :, 128:256] = incl
        out[d, :, 256:384] = strict
        out[d, :, 384:512] = incl
        out[d, :, 512:640] = strict
        out[d, :, 640:768] = strict.T
    return out


INLINE_FINISH = False


def emit_scan(P, dram, psum, ident, pre_d, pre_res, loads, has_ab, yacc, finish=None):
    ones = P.sb("ones", [128, 1])
    P.memset("pool", ones[:], 1.0, [ones])
    P.memset("pool", yacc[:], 0.0, [yacc])
    yres = [Res("yacc%d" % i) for i in range(NT)]
    done = [0] * NT

    def direction(d):
        B = psum[4 * d:4 * d + 4]
        Xin = [P.sb("Xin%d_%d" % (d, i), [128, 6, 256]) for i in range(2)]
        E = P.sb("E%d" % d, [128, 3, 256])
        W = [P.sb("W%d_%d" % (d, i), [128, 4, 256]) for i in range(2)]
        ft = P.sb("FT%d" % d, [128, 2, 4, 128])
        gm = P.sb("Gm%d" % d, [128, 4, 512])
        pCt = [P.sb("pC%d_%d" % (d, i), [128, 2]) for i in range(2)]
        H = P.sb("H%d" % d, [128, 2, 64])
        tmpH = P.sb("tmpH%d" % d, [128, 2, 64])
        cst = P.sb("cst%d" % d, [128, 768])
        P.dma("sp", cst[:], dram["scan_consts"][d], (), [cst])
        if has_ab:
            Xb = [P.sb("Xb%d_%d" % (d, i), [128, 4, 128], BF16) for i in range(2)]
            Yb = [P.sb("Yb%d_%d" % (d, i), [128, 4, 128], BF16) for i in range(2)]
            Pb = [P.sb("Pb%d_%d" % (d, i), [128, 4, 128], BF16) for i in range(2)]
            P32 = [P.sb("P32%d_%d" % (d, i), [128, 4, 128]) for i in range(2)]
            Xs = P.sb("Xs%d" % d, [128, 256])
            Us = P.sb("Us%d" % d, [128, 256])
        order = list(range(NT)) if d == 0 else [1, 0] + list(range(NT - 1, 1, -1))
        Mc = cst[:, 0:128]
        MASK4 = cst[:, 128:640]
        P.memset("pool", H[:], 0.0, [H])
        yield
        for n, i in enumerate(order):
            b = n % 2
            xi, w_, pc = Xin[b], W[b], pCt[b]
            for (s0, ns, c0) in loads(d):
                P.dma("sp" if (s0 + d) % 2 == 0 else "pool", xi[:, s0:s0 + ns, :].rearrange("p s f -> p (s f)"),
                      pre_d[i * 128:(i + 1) * 128, c0:c0 + ns * 256], [pre_res[i]], [xi])
            pcl = B[0]
            P.mm(pcl[:, 0:256], Mc, xi[:, 3, :], True, True, [cst, xi], [pcl])
            pcp = B[1]
            for g in range(2):
                P.mm(pcp[:, 384 + g:385 + g], xi[:, 3, g * 128:(g + 1) * 128], ones[:, 0:1], True, True, [xi, ones], [pcp])
            yield
            P.act("act", E[:, 0, :], pcl[:, 0:256], AF.Exp, [pcl], [E])
            P.act("act", E[:, 1, :], pcl[:, 0:256], AF.Exp, [pcl], [E], scale=-1.0)
            P.act("act", pc[:, :], pcp[:, 384:386], AF.Exp, [pcp], [pc])
            if has_ab:
                P.tt("dve", E[:, 2, :], pcl[:, 0:256], xi[:, 3, :], ALU.subtract, [pcl, xi], [E])
                P.act("act", E[:, 2, :], E[:, 2, :], AF.Exp, [E], [E])
            yield
            P.tt("dve", w_[:, 0, :], xi[:, 0, :], E[:, 0, :], ALU.mult, [xi, E], [w_])
            P.tt("pool", w_[:, 1, :], xi[:, 1, :], E[:, 1, :], ALU.mult, [xi, E], [w_])
            if has_ab:
                P.tt("dve", w_[:, 2, :], xi[:, 4, :], E[:, 2, :], ALU.mult, [xi, E], [w_])
                P.tt("pool", w_[:, 3, :], xi[:, 5, :], E[:, 1, :], ALU.mult, [xi, E], [w_])
            yield
            slots = [(0, 0), (2, 1), (1, 2), (3, 3)] if has_ab else [(0, 0), (1, 2)]
            for g in range(2):
                pt = B[2 + g]
                for (ws, fs) in slots:
                    P.tr(pt[:, fs * 128:(fs + 1) * 128], w_[:, ws, g * 128:(g + 1) * 128], ident[:], [w_, ident], [pt])
            yield
            for g in range(2):
                pt = B[2 + g]
                if has_ab:
                    P.cp("act", ft[:, g, :, :].rearrange("p f t -> p (f t)"), pt[:, :], [pt], [ft])
                else:
                    P.cp("act", ft[:, g, 0, :], pt[:, 0:128], [pt], [ft])
                    P.cp("act", ft[:, g, 2, :], pt[:, 256:384], [pt], [ft])
            yield
            for h in range(4):
                g, hp = h // 2, (h % 2) * 64
                pg = B[2 + h % 2]
                if has_ab:
                    P.mm(pg[:, 0:256], ft[hp:hp + 64, g, 2, :], ft[hp:hp + 64, g, 0:2, :].rearrange("p f t -> p (f t)"), True, True, [ft], [pg])
                    P.mm(pg[:, 256:512], ft[hp:hp + 64, g, 3, :], ft[hp:hp + 64, g, 0:2, :].rearrange("p f t -> p (f t)"), True, True, [ft], [pg])
                    P.tt("dve", gm[:, h, :], pg[:, :], MASK4, ALU.mult, [pg, cst], [gm])
                else:
                    P.mm(pg[:, 0:128], ft[hp:hp + 64, g, 2, :], ft[hp:hp + 64, g, 0, :], True, True, [ft], [pg])
                    P.tt("dve", gm[:, h, 0:128], pg[:, 0:128], MASK4[:, 0:128], ALU.mult, [pg, cst], [gm])
                if h % 2 == 1:
                    yield
            if has_ab:
                p3 = B[0]
                for h in range(4):
                    P.tr(p3[:, h * 128:(h + 1) * 128], gm[:, h, 384:512], ident[:], [gm, ident], [p3])
                Xc, Yc, Pc, Pc32 = Xb[0], Yb[0], Pb[0], P32[0]
                yield
                P.cp("act", Yc[:].rearrange("p h s -> p (h s)"), p3[:, :], [p3], [Yc])
                P.cp("dve", Xc[:], gm[:, :, 384:512], [gm], [Xc])
                for h in range(4):
                    P.tt("dve", Pc32[:, h, :], gm[:, h, 384:512], ident[:, :], ALU.add, [gm, ident], [Pc32])
                P.cp("dve", Pc[:], Pc32[:], [Pc32], [Pc])
                yield
                def mm_xy(lv, Xc, Yc):
                    pY = B[3]
                    for h in range(4):
                        P.mm(pY[:, h * 128:(h + 1) * 128], Xc[:, h, :], Yc[:, h, :], True, True, [Yc, Xc], [pY])
                    if lv < 6:
                        pX = B[2]
                        for h in range(4):
                            P.mm(pX[:, h * 128:(h + 1) * 128], Yc[:, h, :], Xc[:, h, :], True, True, [Yc, Xc], [pX])

                mm_xy(1, Xc, Yc)
                yield
                for lv in range(1, 7):
                    Yn, Pn, Pn32 = Yb[lv % 2], Pb[lv % 2], P32[lv % 2]
                    Xn = Xb[lv % 2]
                    P.cp("act", Yn[:].rearrange("p h s -> p (h s)"), B[3][:, :], [B[3]], [Yn])
                    if lv < 6:
                        P.cp("act", Xn[:].rearrange("p h s -> p (h s)"), B[2][:, :], [B[2]], [Xn])
                    yield
                    pP = B[1]
                    for h in range(4):
                        P.mm(pP[:, h * 128:(h + 1) * 128], Yn[:, h, :], Pc[:, h, :], True, True, [Yn, Pc], [pP])
                    if lv < 6:
                        mm_xy(lv + 1, Xn, Yn)
                    yield
                    P.tt("dve", Pn32[:].rearrange("p h s -> p (h s)"), pP[:, :], Pc32[:].rearrange("p h s -> p (h s)"), ALU.add,
                         [pP, Pc32], [Pn32])
                    if lv < 6:
                        P.cp("dve", Pn[:], Pn32[:], [Pn32], [Pn])
                    Xc, Yc, Pc, Pc32 = Xn, Yn, Pn, Pn32
                yield
                Pc = Pc32
                pXs = B[0]
                for h in range(4):
                    g, hp = h // 2, (h % 2) * 64
                    P.mm(pXs[:, 256 + h * 64:256 + (h + 1) * 64], ft[hp:hp + 64, g, 1, :], H[hp:hp + 64, g, :], True, False, [ft, H], [pXs])
                    P.mm(pXs[:, 256 + h * 64:256 + (h + 1) * 64], gm[:, h, 128:256], xi[:, 2, h * 64:(h + 1) * 64], False, True, [gm, xi], [pXs])
                yield
                P.cp("act", Xs[:, :], pXs[:, 256:512], [pXs], [Xs])
                yield
                pU = B[2]
                for h in range(4):
                    P.mm(pU[:, h * 64:(h + 1) * 64], Pc[:, h, :], Xs[:, h * 64:(h + 1) * 64], True, True, [Pc, Xs], [pU])
                yield
                P.cp("dve", Us[:, :], pU[:, 0:256], [pU], [Us])
                yield
            pYo = B[1]
            for h in range(4):
                g, hp = h // 2, (h % 2) * 64
                vh = xi[:, 2, h * 64:(h + 1) * 64]
                P.mm(pYo[:, h * 64:(h + 1) * 64], ft[hp:hp + 64, g, 0, :], H[hp:hp + 64, g, :], True, False, [ft, H], [pYo])
                if has_ab:
                    P.mm(pYo[:, h * 64:(h + 1) * 64], gm[:, h, 256:384], Us[:, h * 64:(h + 1) * 64], False, False, [gm, Us], [pYo])
                P.mm(pYo[:, h * 64:(h + 1) * 64], gm[:, h, 0:128], vh, False, True, [gm, xi], [pYo])
            for h in range(4):
                g, hp = h // 2, (h % 2) * 64
                vh = xi[:, 2, h * 64:(h + 1) * 64]
                P.mm(pYo[hp:hp + 64, 256 + g * 64:256 + (g + 1) * 64], w_[:, 1, h * 64:(h + 1) * 64], vh, True, not has_ab, [w_, xi], [pYo])
                if has_ab:
                    P.mm(pYo[hp:hp + 64, 256 + g * 64:256 + (g + 1) * 64], w_[:, 3, h * 64:(h + 1) * 64], Us[:, h * 64:(h + 1) * 64], False, True,
                         [w_, Us], [pYo])
            yield
            P.tt("dve", yacc[:, i, :], pYo[:, 0:256], yacc[:, i, :], ALU.add, [pYo, yres[i]], [yres[i]])
            P.tt("dve", tmpH[:], pYo[:, 256:384].rearrange("p (g v) -> p g v", g=2), H[:], ALU.add, [pYo, H], [tmpH])
            P.tt("dve", H[:], tmpH[:], pc[:, :].unsqueeze(2).to_broadcast([128, 2, 64]), ALU.mult, [tmpH, pc], [H])
            done[i] += 1
            if done[i] == 2 and finish is not None and INLINE_FINISH:
                finish(i, yres[i])
            yield

    gens = [direction(0), direction(1)]
    while gens:
        for g_ in list(gens):
            try:
                next(g_)
            except StopIteration:
                gens.remove(g_)
    if finish is not None and not INLINE_FINISH:
        for i in range(NT):
            finish(i, yres[i])


GLA_Z0 = 768 + 352 + 1024


def emit_gla(P, L, need_ctx, dram, z_d, z_res, y_d, y_res, psum, ident):
    pre_d = dram["gla_pre"]
    pre_res = dram.res("gla_pre")
    with P.phase():
        zt = [P.sb("zt%d" % i, [128, 528]) for i in range(2)]
        pre = [P.sb("pre%d" % i, [128, 5, 4, 64]) for i in range(2)]
        gdT = [P.sb("gdT%d" % i, [16, 128]) for i in range(2)]
        gup = P.sb("gup", [16, 2, 128])
        gb = P.sb("gb", [128, 2, 128])
        xg = [P.sb("xg%d" % i, [128, 128]) for i in range(2)]
        for p_ in pre:
            P.memset("pool", p_[:], 0.0, [p_])
        for d in range(2):
            P.dma("sp", gup[:, d, :], dram["gla_gate_up"][L, d], (), [gup])
            P.dma("sp", gb[:, d, :], dram["gla_gate_b"][L, d].partition_broadcast(128), (), [gb])
        for i in range(NT):
            b = i % 2
            z_, p_ = zt[b], pre[b]
            P.dma("sp", z_[:], z_d[i * 128:(i + 1) * 128, GLA_Z0:GLA_Z0 + 528], [z_res[i]], [z_])
            P.op("act", lambda e, z_=z_, p_=p_: e.mul(p_[:, 0, :, 0:32], z_[:, 0:128].rearrange("p (h d) -> p h d", h=4), 32.0 ** -0.5), [z_], [p_])
            P.cp("pool", p_[:, 1, :, 0:32], z_[:, 128:256].rearrange("p (h d) -> p h d", h=4), [z_], [p_])
            P.cp("pool", p_[:, 2, :, :], z_[:, 256:512].rearrange("p (h d) -> p h d", h=4), [z_], [p_])
            pt = psum[i % 2]
            P.tr(pt[0:16, 0:128], z_[:, 512:528], ident[:], [z_, ident], [pt])
            P.cp("act", gdT[b][:, :], pt[0:16, 0:128], [pt], [gdT[b]])
            for d in range(2):
                pl = psum[2 + d]
                x_ = xg[d]
                P.mm(pl[:, 0:128], gdT[b][:, :], gup[:, d, :], True, True, [gdT[b], gup], [pl])
                P.tt("dve", x_[:], pl[:, 0:128], gb[:, d, :], ALU.add, [pl, gb], [x_])
                P.act("act", x_[:], x_[:], AF.Exp, [x_], [x_], scale=-1.0)
                P.act("act", x_[:], x_[:], AF.Ln, [x_], [x_], bias=1.0)
                P.op("act", lambda e, x_=x_, p_=p_, d=d: e.mul(p_[:, 3 + d, :, 0:32], x_[:].rearrange("p (h d) -> p h d", h=4), -1.0 / 16.0), [x_], [p_])
            P.dma("pool", pre_d[i * 128:(i + 1) * 128, :], p_[:].rearrange("p s h d -> p (s h d)"), [p_], [pre_res[i]])
    with P.phase():
        yacc = P.sb("yacc", [128, NT, 256])
        gn = P.sb("gn", [128, 256])
        og = [P.sb("og%d" % i, [128, 256]) for i in range(2)]
        sq = [P.sb("sq%d" % i, [128, 256]) for i in range(2)]
        stg = [P.sb("stg%d" % i, [128, 12]) for i in range(2)]
        P.dma("sp", gn[:], dram["gla_norm"][L].partition_broadcast(128), (), [gn])
        fcnt = [0]

        def finish(i, yr):
            if i < 2 and not need_ctx:
                return
            b = fcnt[0] % 2
            fcnt[0] += 1
            o_, s_, g_ = og[b], sq[b], stg[b]
            yv = yacc[:, i, :]
            P.dma("sp", o_[:], z_d[i * 128:(i + 1) * 128, DIN - 256:DIN], [z_res[i]], [o_])
            P.act("act", o_[:], o_[:], AF.Silu, [o_], [o_])
            P.tt("pool", s_[:], yv, yv, ALU.mult, [yr], [s_])
            P.red("dve", g_[:, 0:4], s_[:].rearrange("p (h d) -> p h d", h=4), ALU.add, [s_], [g_])
            P.act("act", g_[:, 4:8], g_[:, 0:4], AF.Sqrt, [g_, P.consts[EPS]], [g_], bias=P.consts[EPS][:, 0:1], scale=1.0 / 64)
            P.op("dve", lambda e, g_=g_: e.reciprocal(g_[:, 8:12], g_[:, 4:8]), [g_], [g_])
            P.tt("dve", s_[:].rearrange("p (h d) -> p h d", h=4), yv.rearrange("p (h d) -> p h d", h=4),
                 g_[:, 8:12].unsqueeze(2).to_broadcast([128, 4, 64]), ALU.mult, [yr, g_], [s_])
            P.tt("pool", s_[:], s_[:], gn[:], ALU.mult, [s_, gn], [s_])
            P.tt("pool", s_[:], s_[:], o_[:], ALU.mult, [s_, o_], [s_])
            P.dma("sp", y_d[i * 128:(i + 1) * 128, 768:1024], s_[:], [s_], [y_res[i]])

        emit_scan(P, dram, psum, ident, pre_d, pre_res, lambda d: [(0, 3, 0), (3, 1, 768 + d * 256)], False, yacc, finish)


RW_Z0 = 768 + 352


def emit_rwkv(P, L, need_ctx, dram, z_d, z_res, y_d, y_res, psum, ident):
    pre_d = dram["rw_pre"]
    pre_res = dram.res("rw_pre")
    C05 = float(-np.exp(-0.5))
    with P.phase():
        zc = [P.sb("zc%d" % i, [128, 1024]) for i in range(2)]
        zp = [P.sb("zp%d" % i, [128, 1024]) for i in range(2)]
        zn = [P.sb("zn%d" % i, [128, 1024]) for i in range(2)]
        zs = [P.sb("zs%d" % i, [128, 1024]) for i in range(2)]
        pre = [P.sb("pre%d" % i, [128, 10, 256]) for i in range(2)]
        mub = P.sb("mub", [128, 1024])
        kkb = P.sb("kkb", [128, 256])
        kab = P.sb("kab", [128, 256])
        w0b = P.sb("w0b", [128, 2, 256])
        a0b = P.sb("a0b", [128, 2, 256])
        wa = P.sb("wa", [128, 2, 256])
        gup = P.sb("gup", [128, 256])
        TWA = [P.sb("TWA%d" % i, [128, 128]) for i in range(2)]
        TG = [P.sb("TG%d" % i, [128, 128]) for i in range(2)]
        kt_ = [P.sb("kt%d" % i, [128, 256]) for i in range(2)]
        kn_ = [P.sb("kn%d" % i, [128, 256]) for i in range(2)]
        sq_ = P.sb("sq", [128, 256])
        xw = [P.sb("xw%d" % i, [128, 256]) for i in range(2)]
        xa = [P.sb("xa%d" % i, [128, 256]) for i in range(2)]
        st_ = [P.sb("st%d" % i, [128, 12]) for i in range(2)]
        P.dma("sp", mub[:], dram["rw_mu"][L].partition_broadcast(128), (), [mub])
        P.dma("sp", kkb[:], dram["rw_k_k"][L].partition_broadcast(128), (), [kkb])
        P.dma("sp", kab[:], dram["rw_k_a"][L].partition_broadcast(128), (), [kab])
        for d in range(2):
            P.dma("sp", w0b[:, d, :], dram["rw_w0"][L, d].partition_broadcast(128), (), [w0b])
            P.dma("sp", a0b[:, d, :], dram["rw_a0"][L, d].partition_broadcast(128), (), [a0b])
            P.dma("sp", wa[0:64, d, :], dram["rw_w_up"][L, d], (), [wa])
            P.dma("sp", wa[64:128, d, :], dram["rw_a_up"][L, d], (), [wa])
        P.dma("sp", gup[:], dram["rw_g_up"][L], (), [gup])
        def prep_a(i):
            b = i % 2
            c_, p_, n_, s_, pr = zc[b], zp[b], zn[b], zs[b], pre[b]
            t0 = i * 128
            P.dma("sp", c_[:], z_d[t0:t0 + 128, RW_Z0:RW_Z0 + 1024], [z_res[i]], [c_])
            if t0 in (0, TCTX):
                P.memset("pool", p_[:], 0.0, [p_])
                P.dma("pool", p_[1:128, :], z_d[t0:t0 + 127, RW_Z0:RW_Z0 + 1024], [z_res[i]], [p_])
            else:
                P.dma("pool", p_[:], z_d[t0 - 1:t0 + 127, RW_Z0:RW_Z0 + 1024], [z_res[i - 1], z_res[i]], [p_])
            if t0 + 128 in (TCTX, TALL):
                P.memset("pool", n_[:], 0.0, [n_])
                P.dma("sp", n_[0:127, :], z_d[t0 + 1:t0 + 128, RW_Z0:RW_Z0 + 1024], [z_res[i]], [n_])
            else:
                P.dma("sp", n_[:], z_d[t0 + 1:t0 + 129, RW_Z0:RW_Z0 + 1024], [z_res[i], z_res[i + 1]], [n_])
            P.tt("pool", p_[:], p_[:], n_[:], ALU.add, [p_, n_], [p_])
            P.stt("dve", p_[:], p_[:], 0.5, c_[:], ALU.mult, ALU.subtract, [p_, c_], [p_])
            P.tt("pool", p_[:], p_[:], mub[:], ALU.mult, [p_, mub], [p_])
            P.tt("dve", s_[:], p_[:], c_[:], ALU.add, [p_, c_], [s_])
            P.cp("act", pr[:, 0, :], s_[:, 0:256], [s_], [pr])
            P.cp("act", pr[:, 1, :], s_[:, 512:768], [s_], [pr])
            k_, kn, st = kt_[b], kn_[b], st_[b]
            P.tt("dve", k_[:], s_[:, 256:512], kkb[:], ALU.mult, [s_, kkb], [k_])
            P.tt("pool", sq_[:], k_[:], k_[:], ALU.mult, [k_], [sq_])
            P.red("dve", st[:, 0:4], sq_[:].rearrange("p (h d) -> p h d", h=4), ALU.add, [sq_], [st])
            P.act("act", st[:, 4:8], st[:, 0:4], AF.Sqrt, [st, P.consts[1e-12]], [st], bias=P.consts[1e-12][:, 0:1], scale=1.0)
            P.op("dve", lambda e, st=st: e.reciprocal(st[:, 8:12], st[:, 4:8]), [st], [st])
            P.tt("dve", kn[:].rearrange("p (h d) -> p h d", h=4), k_[:].rearrange("p (h d) -> p h d", h=4),
                 st[:, 8:12].unsqueeze(2).to_broadcast([128, 4, 64]), ALU.mult, [k_, st], [kn])
            P.op("act", lambda e, pr=pr, kn=kn: e.mul(pr[:, 2, :], kn[:], -1.0), [kn], [pr])
            pt = psum[i % 2]
            P.tr(pt[:, 0:128], s_[:, 768:896], ident[:], [s_, ident], [pt])
            P.tr(pt[:, 128:256], s_[:, 896:1024], ident[:], [s_, ident], [pt])
            tw, tg = TWA[b], TG[b]
            P.act("act", tw[0:64, :], pt[0:64, 0:128], AF.Tanh, [pt], [tw])
            P.cp("act", tw[64:128, :], pt[64:128, 0:128], [pt], [tw])
            P.act("act", tg[:, :], pt[:, 128:256], AF.Sigmoid, [pt], [tg])

        def prep_b(i):
            b = i % 2
            s_, pr = zs[b], pre[b]
            kn = kn_[b]
            tw, tg = TWA[b], TG[b]
            t0 = i * 128
            pgp = psum[6]
            P.mm(pgp[:, 0:256], tg[:, :], gup[:, :], True, True, [tg, gup], [pgp])
            P.cp("act", pr[:, 9, :], pgp[:, 0:256], [pgp], [pr])
            for d in range(2):
                pw, pa = psum[2 + d * 2], psum[3 + d * 2]
                P.mm(pw[:, 0:256], tw[0:64, :], wa[0:64, d, :], True, True, [tw, wa], [pw])
                P.mm(pa[:, 0:256], tw[64:128, :], wa[64:128, d, :], True, True, [tw, wa], [pa])
                w_, a_ = xw[d], xa[d]
                P.tt("dve", w_[:], pw[:, 0:256], w0b[:, d, :], ALU.add, [pw, w0b], [w_])
                P.act("act", w_[:], w_[:], AF.Sigmoid, [w_], [w_])
                P.op("act", lambda e, pr=pr, w_=w_, d=d: e.mul(pr[:, 3 + 3 * d, :], w_[:], C05), [w_], [pr])
                P.tt("dve", a_[:], pa[:, 0:256], a0b[:, d, :], ALU.add, [pa, a0b], [a_])
                P.act("act", a_[:], a_[:], AF.Sigmoid, [a_], [a_])
                P.tt("pool", pr[:, 5 + 3 * d, :], kn[:], a_[:], ALU.mult, [kn, a_], [pr])
                P.stt("dve", a_[:], a_[:], -1.0, kab[:], ALU.add, ALU.mult, [a_, kab], [a_])
                P.stt("dve", pr[:, 4 + 3 * d, :], a_[:], 1.0, s_[:, 256:512], ALU.add, ALU.mult, [a_, s_], [pr])
            P.dma("pool", pre_d[t0:t0 + 128, :], pr[:].rearrange("p s f -> p (s f)"), [pr], [pre_res[i]])

        for k in range(NT + 1):
            if k < NT:
                prep_a(k)
            if k >= 1:
                prep_b(k - 1)
    import os
    if os.environ.get("RW_STAGE", "9") == "1":
        return
    with P.phase():
        yacc = P.sb("yacc", [128, NT, 256])
        lnw = P.sb("lnw", [128, 256])
        lnb = P.sb("lnb", [128, 256])
        rkb = P.sb("rkb", [128, 256])
        fin = [P.sb("fin%d" % i, [128, 5, 256]) for i in range(2)]
        yc = [P.sb("yc%d" % i, [128, 256]) for i in range(2)]
        sq = [P.sb("sq%d" % i, [128, 256]) for i in range(2)]
        sf = [P.sb("sf%d" % i, [128, 24]) for i in range(2)]
        P.dma("sp", lnw[:], dram["rw_ln_w"][L].partition_broadcast(128), (), [lnw])
        P.dma("sp", lnb[:], dram["rw_ln_b"][L].partition_broadcast(128), (), [lnb])
        P.dma("sp", rkb[:], dram["rw_r_k"][L].partition_broadcast(128), (), [rkb])
        v4 = lambda ap: ap.rearrange("p (h d) -> p h d", h=4)
        bc4 = lambda ap: ap.unsqueeze(2).to_broadcast([128, 4, 64])
        fcnt = [0]

        def finish(i, yr):
            if i < 2 and not need_ctx:
                return
            b = fcnt[0] % 2
            fcnt[0] += 1
            f_, y_, s_, t_ = fin[b], yc[b], sq[b], sf[b]
            t0 = i * 128
            P.dma("sp", f_[:, 0:2, :].rearrange("p s f -> p (s f)"), pre_d[t0:t0 + 128, 0:512], [pre_res[i]], [f_])
            P.dma("pool", f_[:, 2, :], pre_d[t0:t0 + 128, 1024:1280], [pre_res[i]], [f_])
            P.dma("pool", f_[:, 3, :], pre_d[t0:t0 + 128, 1792:2048], [pre_res[i]], [f_])
            P.dma("sp", f_[:, 4, :], pre_d[t0:t0 + 128, 2304:2560], [pre_res[i]], [f_])
            yv = yacc[:, i, :]
            P.red("dve", t_[:, 0:4], v4(yv), ALU.add, [yr], [t_])
            P.op("act", lambda e, t_=t_: e.mul(t_[:, 4:8], t_[:, 0:4], -1.0 / 64), [t_], [t_])
            P.tt("dve", v4(y_[:]), v4(yv), bc4(t_[:, 4:8]), ALU.add, [yr, t_], [y_])
            P.tt("pool", s_[:], y_[:], y_[:], ALU.mult, [y_], [s_])
            P.red("dve", t_[:, 8:12], v4(s_[:]), ALU.add, [s_], [t_])
            P.act("act", t_[:, 12:16], t_[:, 8:12], AF.Sqrt, [t_, P.consts[64e-5]], [t_], bias=P.consts[64e-5][:, 0:1], scale=1.0 / 64)
            P.op("dve", lambda e, t_=t_: e.reciprocal(t_[:, 16:20], t_[:, 12:16]), [t_], [t_])
            P.tt("dve", v4(y_[:]), v4(y_[:]), bc4(t_[:, 16:20]), ALU.mult, [y_, t_], [y_])
            P.tt("pool", y_[:], y_[:], lnw[:], ALU.mult, [y_, lnw], [y_])
            P.tt("pool", y_[:], y_[:], lnb[:], ALU.add, [y_, lnb], [y_])
            P.tt("pool", s_[:], f_[:, 2, :], f_[:, 3, :], ALU.add, [f_], [s_])
            P.tt("dve", s_[:], s_[:], f_[:, 0, :], ALU.mult, [s_, f_], [s_])
            P.tt("pool", s_[:], s_[:], rkb[:], ALU.mult, [s_, rkb], [s_])
            P.red("dve", t_[:, 20:24], v4(s_[:]), ALU.add, [s_], [t_])
            P.tt("dve", v4(s_[:]), v4(f_[:, 1, :]), bc4(t_[:, 20:24]), ALU.mult, [f_, t_, s_], [s_])
            P.tt("pool", y_[:], y_[:], s_[:], ALU.add, [y_, s_], [y_])
            P.tt("pool", y_[:], y_[:], f_[:, 4, :], ALU.mult, [y_, f_], [y_])
            P.dma("sp", y_d[t0:t0 + 128, 512:768], y_[:], [y_], [y_res[i]])

        emit_scan(P, dram, psum, ident, pre_d, pre_res,
                  lambda d: [(0, 1, 0), (2, 1, 256), (4, 1, 512), (3, 1, 768 + 768 * d), (1, 1, 1024 + 768 * d), (5, 1, 1280 + 768 * d)],
                  True, yacc, finish)


N_CORES = 8
SCR_SPECS = {"gla_pre": 1280, "rw_pre": 2560}
W_SPECS = [
    ("w_mod", (D, 6 * D)), ("b_mod", (1, 6 * D)),
    ("g_mix_pre", (1, D)), ("g_mix_post", (1, D)), ("g_ffn_pre", (1, D)), ("g_ffn_post", (1, D)),
    ("w_in", (D, DIN)), ("w_out", (D, D)),
    ("ffn_w_up", (D, 2 * DFF)), ("ffn_conv_w", (3, 2 * DFF)), ("ffn_conv_b", (1, 2 * DFF)), ("ffn_w_down", (DFF, D)),
]


def build_program(NL=NL_FULL, dbg=None):
    dbg = dbg or {}
    NLW = dbg.get("nlw", NL_FULL)
    nc = bass.Bass("TRN2", target_bir_lowering=False)
    dram = {}

    def dtens(name, shape, dt=F32, kind="Internal"):
        return nc.dram_tensor(name, list(shape), dt, kind=kind).ap()

    def kind_of(tag, default="Internal"):
        if tag + "_in" in dbg:
            return "ExternalInput"
        if tag in dbg:
            return "ExternalOutput"
        return default

    dram["xall"] = dtens("xall", [TALL, D], kind="ExternalInput")
    dram["c2"] = dtens("c2", [2, D], kind="ExternalInput")
    dram["ident"] = dtens("ident", [128, 128], kind="ExternalInput")
    class LazyDram(dict):
        def __missing__(self, name):
            for nm, shp in W_SPECS:
                if nm == name:
                    self[name] = dtens(name, [NLW] + list(shp), kind="ExternalInput")
                    return self[name]
            for nm, shp, per_layer in MIX_SPECS:
                if nm == name:
                    self[name] = dtens(name, ([NLW] if per_layer else []) + list(shp), kind="ExternalInput")
                    return self[name]
            if name in SCR_SPECS:
                self[name] = dtens(name + "_scr", [TALL, SCR_SPECS[name]], kind=kind_of(name))
                return self[name]
            raise KeyError(name)

        def res(self, name):
            if not hasattr(self, "_res"):
                self._res = {}
            if name not in self._res:
                self._res[name] = [Res("%s%d" % (name, i)) for i in range(NT)]
            return self._res[name]
    dram = LazyDram(dram)
    out_ap = dtens("out", [TLAT, D], kind="ExternalOutput")
    z_d = dtens("z_scr", [TALL, DIN], kind=kind_of("z"))
    y_d = dtens("y_scr", [TALL, D], kind=kind_of("y"))
    xs_d = dtens("xs_scr", [TALL, D], kind=kind_of("xs"))
    mods_d = dtens("mods_scr", [NL_FULL, 2, 6 * D], kind=kind_of("mods"))
    h2T_d = dtens("h2T_scr", [128, 8, TALL], BF16)
    z_res = [Res("z%d" % i) for i in range(NT)]
    y_res = [Res("y%d" % i) for i in range(NT)]
    xs_res = [Res("x%d" % i) for i in range(NT)]
    h2_res = [Res("h2_%d" % i) for i in range(NT)]
    mods_res = [Res("mods%d" % i) for i in range(NL_FULL)]

    with contextlib.ExitStack() as st:
        P = Prog(nc, st)
        ident = P.sb("ident", [128, 128])
        P.dma("sp", ident[:], dram["ident"], (), [ident])
        psum = [P.ps("ps%d" % i) for i in range(8)]
        P.consts = {}
        for cv in (EPS, 1e-12, 64e-5):
            ct = P.sb("const%d" % len(P.consts), [128, 1])
            P.memset("pool", ct[:], cv, [ct])
            P.consts[cv] = ct
        c2T = P.sb("c2T", [128, 8, 2])
        for v in range(2):
            P.dma("sp", c2T[:, :, v], dram["c2"][v].rearrange("(kt p) -> p kt", p=128), (), [c2T],
                  allow_slow_non_contiguous=True)
        P.act("act", c2T[:], c2T[:], AF.Silu, [c2T], [c2T])

        def bcast_load(q, tile, row_ap, reads=()):
            P.dma(q, tile[:], row_ap.partition_broadcast(128), reads, [tile])

        def rms_tile(P, src, st_t, junk, A, Bv, dst, ncols=D):
            P.act("act", junk[:, 0:ncols], src[:, 0:ncols], AF.Square, [src], [junk], accum_out=st_t[:, 0:1])
            P.rstd(st_t, 1.0 / ncols, EPS, [junk])
            P.stt("dve", dst[:, 0:ncols], src[:, 0:ncols], st_t[:, 2:3], A[:, 0:ncols], ALU.mult, ALU.mult, [src, st_t, A], [dst])
            if Bv is not None:
                P.tt("pool", dst[:, 0:ncols], dst[:, 0:ncols], Bv[:, 0:ncols], ALU.add, [dst, Bv], [dst])

        def transpose8(P, src, dstT, banks):
            for half in range(2):
                pst = banks[half]
                for k4 in range(4):
                    kt = half * 4 + k4
                    P.tr(pst[:, k4 * 128:(k4 + 1) * 128], src[:, kt * 128:(kt + 1) * 128], ident[:], [src, ident], [pst])
                P.cp("act", dstT[:, half * 4:(half + 1) * 4, :], pst[:, :].rearrange("p (k t) -> p k t", k=4), [pst], [dstT])

        def load_cast(P, dst_ap, dst_t, src_ap, wst, idx, ncols):
            w = wst[idx % 2]
            P.dma("sp" if idx % 2 == 0 else "pool", w[:, 0:ncols], src_ap, (), [w])
            P.cp("act" if idx % 2 == 0 else "dve", dst_ap, w[:, 0:ncols], [w], [dst_t])

        for L in range(NL):
            need_ctx = L < NL_FULL - 1
            last = (L == NL - 1)
            mrow = lambda v, k: mods_d[L, v, k * D:(k + 1) * D]
            if "skip_p0" not in dbg:
              with P.phase():
                wst = [P.sb("wst%d" % i, [128, 2048]) for i in range(4)]
                brow = P.sb("brow", [2, 6 * D])
                mrow_sb = [P.sb("mrow%d" % i, [2, 2048]) for i in range(2)]
                P.dma("sp", brow[0:1, :], dram["b_mod"][L], (), [brow])
                P.dma("sp", brow[1:2, :], dram["b_mod"][L], (), [brow])
                widx = 0
                for nb in range(3):
                    banks = psum[4 * (nb % 2):4 * (nb % 2) + 4]
                    for kt in range(8):
                        w = wst[widx % 4]
                        P.dma("sp" if widx % 2 == 0 else "pool", w[:, :],
                              dram["w_mod"][L, kt * 128:(kt + 1) * 128, nb * 2048:(nb + 1) * 2048], (), [w])
                        widx += 1
                        for q in range(4):
                            P.mm(banks[q][0:2, :], c2T[:, kt, :], w[:, q * 512:(q + 1) * 512], kt == 0, kt == 7, [c2T, w], [banks[q]])
                    ms = mrow_sb[nb % 2]
                    for q in range(4):
                        c0 = nb * 2048 + q * 512
                        P.tt("dve", ms[:, q * 512:(q + 1) * 512], banks[q][0:2, :], brow[:, c0:c0 + 512], ALU.add, [banks[q], brow], [ms])
                    P.dma("sp", mods_d[L, :, nb * 2048:(nb + 1) * 2048], ms[:, :], [ms], [mods_res[L]])

            if "skip_p1" not in dbg:
              with P.phase():
                x_src = dram["xall"] if L == 0 else xs_d
                wst = [P.sb("wst%d" % i, [128, 2048]) for i in range(2)]
                gB = P.sb("gB", [128, D])
                A1 = P.sb("A1", [128, D])
                B1 = P.sb("B1", [128, D])
                win_sb = P.sb("win", [128, 8, DIN], BF16)
                xt = [P.sb("xt%d" % i, [128, D]) for i in range(2)]
                xn = [P.sb("xn%d" % i, [128, D]) for i in range(2)]
                hT = [P.sb("hT%d" % i, [128, 8, 128], BF16) for i in range(2)]
                zt = [P.sb("zt%d" % i, [128, DIN]) for i in range(2)]
                st1 = [P.sb("st1_%d" % i, [128, 4]) for i in range(2)]
                junk = P.sb("junk", [128, D])
                bcast_load("sp", gB, dram["g_mix_pre"][L])
                win_res = [Res("win%d" % c) for c in range(6)]
                idx = 0
                for c in range(0, 6, 2):
                    c0 = c * 512
                    cw = min(1024, DIN - c0)
                    for kt in range(8):
                        w = wst[idx % 2]
                        P.dma("sp" if idx % 2 == 0 else "pool", w[:, 0:cw], dram["w_in"][L, kt * 128:(kt + 1) * 128, c0:c0 + cw], (), [w])
                        P.cp("act" if idx % 2 == 0 else "dve", win_sb[:, kt, c0:c0 + cw], w[:, 0:cw], [w], [win_res[c], win_res[c + 1]])
                        idx += 1
                for i in range(NT):
                    v = 1 if i < 2 else 0
                    if i == 0 or i == 2:
                        bcast_load("sp", A1, mrow(v, 1), [mods_res[L]])
                        P.stt("dve", A1[:], A1[:], 1.0, gB[:], ALU.add, ALU.mult, [A1, gB], [A1])
                        bcast_load("sp", B1, mrow(v, 0), [mods_res[L]])
                    b = i % 2
                    P.dma("sp", xt[b][:], x_src[i * 128:(i + 1) * 128, :], [xs_res[i]], [xt[b]])
                    rms_tile(P, xt[b], st1[b], junk, A1, B1, xn[b])
                    transpose8(P, xn[b], hT[b], psum[0:2])
                    for cb in range(6):
                        c0 = cb * 512
                        cw = min(512, DIN - c0)
                        pst = psum[2 + cb]
                        for kt in range(8):
                            P.mm(pst[:, 0:cw], hT[b][:, kt, :], win_sb[:, kt, c0:c0 + cw], kt == 0, kt == 7, [hT[b], win_res[cb]], [pst])
                        P.cp("dve" if cb % 2 == 0 else "act", zt[b][:, c0:c0 + cw], pst[:, 0:cw], [pst], [zt[b]])
                    P.dma("pool", z_d[i * 128:(i + 1) * 128, :], zt[b][:], [zt[b]], [z_res[i]])

            if "skip_mix" not in dbg:
                emit_mixers(P, L, need_ctx, dram, z_d, z_res, y_d, y_res, psum, ident, dbg)

            tiles5 = list(range(NT)) if need_ctx else list(range(2, NT))
            x_src = dram["xall"] if L == 0 else xs_d
            if "skip_p5" not in dbg:
              with P.phase():
                wst = [P.sb("wst%d" % i, [128, 1024]) for i in range(2)]
                wout_sb = P.sb("wout", [128, 8, D], BF16)
                gB = P.sb("gB", [128, D])
                G1 = P.sb("G1", [128, D])
                A2 = P.sb("A2", [128, D])
                B2 = P.sb("B2", [128, D])
                yt = [P.sb("yt%d" % i, [128, D]) for i in range(2)]
                xt = [P.sb("xt%d" % i, [128, D]) for i in range(2)]
                yT = [P.sb("yT%d" % i, [128, 8, 128], BF16) for i in range(2)]
                tmp = [P.sb("tmp%d" % i, [128, D]) for i in range(2)]
                xw = [P.sb("xw%d" % i, [128, D]) for i in range(2)]
                h2 = [P.sb("h2%d" % i, [128, D]) for i in range(2)]
                h2T = [P.sb("h2T%d" % i, [128, 8, 128], BF16) for i in range(2)]
                st5 = [P.sb("st5_%d" % i, [128, 8]) for i in range(2)]
                junk = P.sb("junk", [128, D])
                for kt in range(8):
                    load_cast(P, wout_sb[:, kt, :], wout_sb, dram["w_out"][L, kt * 128:(kt + 1) * 128, :], wst, kt, D)
                def p5_a(i):
                    v = 1 if i < 2 else 0
                    if i == tiles5[0] or i == 2:
                        bcast_load("sp", gB, dram["g_mix_post"][L])
                        bcast_load("sp", G1, mrow(v, 2), [mods_res[L]])
                        P.tt("dve", G1[:], G1[:], gB[:], ALU.mult, [G1, gB], [G1])
                        bcast_load("sp", gB, dram["g_ffn_pre"][L])
                        bcast_load("sp", A2, mrow(v, 4), [mods_res[L]])
                        P.stt("dve", A2[:], A2[:], 1.0, gB[:], ALU.add, ALU.mult, [A2, gB], [A2])
                        bcast_load("sp", B2, mrow(v, 3), [mods_res[L]])
                    b = i % 2
                    P.dma("sp", yt[b][:], y_d[i * 128:(i + 1) * 128, :], [y_res[i]], [yt[b]])
                    P.dma("pool", xt[b][:], x_src[i * 128:(i + 1) * 128, :], [xs_res[i]], [xt[b]])
                    transpose8(P, yt[b], yT[b], psum[0:2])
                    pb = [psum[2 + 2 * b], psum[3 + 2 * b]]
                    for half in range(2):
                        for kt in range(8):
                            P.mm(pb[half][:, :], yT[b][:, kt, :], wout_sb[:, kt, half * 512:(half + 1) * 512], kt == 0, kt == 7,
                                 [yT[b], wout_sb], [pb[half]])
                        P.act("act", junk[:, 0:512], pb[half][:, :], AF.Square, [pb[half]], [junk], accum_out=st5[b][:, 3 + half:4 + half])
                    P.tt("dve", st5[b][:, 0:1], st5[b][:, 3:4], st5[b][:, 4:5], ALU.add, [st5[b], junk], [st5[b]])
                    P.rstd(st5[b], 1.0 / D, EPS)
                    for half in range(2):
                        hs = slice(half * 512, (half + 1) * 512)
                        P.stt("dve", tmp[b][:, hs], pb[half][:, :], st5[b][:, 2:3], G1[:, hs], ALU.mult, ALU.mult,
                              [pb[half], st5[b], G1], [tmp[b]])
                    P.tt("pool", xw[b][:], tmp[b][:], xt[b][:], ALU.add, [tmp[b], xt[b]], [xw[b]])
                    P.dma("pool", xs_d[i * 128:(i + 1) * 128, :], xw[b][:], [xw[b]], [xs_res[i]])
                    rms_tile(P, xw[b], st5[b], junk, A2, B2, h2[b])

                def p5_b(i):
                    b = i % 2
                    transpose8(P, h2[b], h2T[b], psum[6:8])
                    P.dma("sp", h2T_d[:, :, i * 128:(i + 1) * 128], h2T[b][:], [h2T[b]], [h2_res[i]])

                for k in range(len(tiles5) + 1):
                    if k < len(tiles5):
                        p5_a(tiles5[k])
                    if k >= 1:
                        p5_b(tiles5[k - 1])

            if "skip_p6" not in dbg:
              with P.phase():
                wst = [P.sb("wst%d" % i, [128, 1024]) for i in range(2)]
                wup_sb = P.sb("wup", [128, 8, 2 * DFF], BF16)
                wdn_sb = P.sb("wdn", [128, 22, D], BF16)
                cwt = P.sb("cwt", [128, 3, 44])
                cbt = P.sb("cbt", [128, 44])
                gB = P.sb("gB", [128, D])
                G2s = [P.sb("G2_%d" % i, [128, D]) for i in range(2)]
                pending_down = []
                h2blk = [P.sb("h2blk%d" % i, [128, 8, 258], BF16) for i in range(2)]
                cv = [[P.sb("cv%d_%d" % (i, j), [128, 256]) for j in range(2)] for i in range(2)]
                aTs = [P.sb("aT%d" % i, [128, 22, 256], BF16) for i in range(2)]
                xt = [P.sb("xt%d" % i, [128, D]) for i in range(2)]
                tmp = [P.sb("tmp%d" % i, [128, D]) for i in range(2)]
                st6 = [P.sb("st6_%d" % i, [128, 8]) for i in range(2)]
                junk = P.sb("junk", [128, 512])
                blocks = list(range(17)) if need_ctx else list(range(1, 17))

                def load_hb(bi):
                    t0 = bi * 256
                    hb = h2blk[bi % 2]
                    lval = t0 not in (0, TCTX)
                    rval = (t0 + 256) not in (TCTX, TALL)
                    if not lval:
                        P.memset("pool", hb[:, :, 0:1], 0.0, [hb])
                    if not rval:
                        P.memset("pool", hb[:, :, 257:258], 0.0, [hb])
                    a = t0 - 1 if lval else t0
                    e = t0 + 257 if rval else t0 + 256
                    rtiles = sorted(set([a // 128, (e - 1) // 128, t0 // 128, t0 // 128 + 1]))
                    P.dma("sp", hb[:, :, a - (t0 - 1):e - (t0 - 1)], h2T_d[:, :, a:e], [h2_res[r] for r in rtiles], [hb])

                load_hb(blocks[0])
                wup_res = [Res("wup%d" % c) for c in range(6)]
                wdn_res = [Res("wdn%d" % j) for j in range(22)]
                idx = 0
                for c in (0, 2, 3, 1, 4, 5):
                    c0 = c * 1024
                    cw = min(1024, 2 * DFF - c0)
                    for kt in range(8):
                        load_cast(P, wup_sb[:, kt, c0:c0 + cw], wup_res[c], dram["ffn_w_up"][L, kt * 128:(kt + 1) * 128, c0:c0 + cw], wst, idx, cw)
                        idx += 1
                for j in range(22):
                    load_cast(P, wdn_sb[:, j, :], wdn_res[j], dram["ffn_w_down"][L, j * 128:(j + 1) * 128, :], wst, idx, D)
                    idx += 1
                for tap in range(3):
                    P.dma("sp", cwt[:, tap, :], dram["ffn_conv_w"][L, tap].rearrange("(j p) -> p j", p=128), (), [cwt],
                          allow_slow_non_contiguous=True)
                P.dma("sp", cbt[:, :], dram["ffn_conv_b"][L, 0].rearrange("(j p) -> p j", p=128), (), [cbt],
                      allow_slow_non_contiguous=True)
                blocks = list(range(17)) if need_ctx else list(range(1, 17))
                ucount = 0
                for bi in blocks:
                    v = 1 if bi == 0 else 0
                    if bi == blocks[0] or bi == 1:
                        bcast_load("sp", gB, dram["g_ffn_post"][L])
                        bcast_load("sp", G2s[v], mrow(v, 5), [mods_res[L]])
                        P.tt("dve", G2s[v][:], G2s[v][:], gB[:], ALU.mult, [G2s[v], gB], [G2s[v]])
                    t0 = bi * 256
                    aT = aTs[bi % 2]
                    hb = h2blk[bi % 2]
                    nxt = blocks.index(bi) + 1
                    if nxt < len(blocks):
                        load_hb(blocks[nxt])
                    for j in range(22):
                        cs = cv[j % 2]
                        for part in range(2):
                            fi = part * 22 + j
                            f0 = part * DFF + j * 128
                            pst = psum[ucount % 4]
                            ucount += 1
                            for kt in range(8):
                                P.mm(pst[:, 0:258], wup_sb[:, kt, f0:f0 + 128], hb[:, kt, :], kt == 0, kt == 7,
                                     [wup_res[f0 // 1024], wup_res[(f0 + 127) // 1024], hb], [pst])
                            c = cs[part]
                            P.act("act", c[:], pst[:, 1:257], AF.Identity, [pst, cwt, cbt], [c], bias=cbt[:, fi:fi + 1], scale=cwt[:, 1, fi:fi + 1])
                            P.stt("dve", c[:], pst[:, 0:256], cwt[:, 0, fi:fi + 1], c[:], ALU.mult, ALU.add, [pst, cwt, c], [c])
                            P.stt("dve", c[:], pst[:, 2:258], cwt[:, 2, fi:fi + 1], c[:], ALU.mult, ALU.add, [pst, cwt, c], [c])
                        P.act("act", cs[1][:], cs[1][:], AF.Silu, [cs[1]], [cs[1]])
                        P.tt("pool", aT[:, j, :], cs[0][:], cs[1][:], ALU.mult, [cs[0], cs[1]], [aT])
                        for _ in range(5):
                            if pending_down:
                                pending_down.pop(0)()
                    def make_down(bi, aT, v):
                        items = []
                        for s in range(2):
                            i = bi * 2 + s
                            b = s
                            pb = [psum[4 + 2 * s], psum[5 + 2 * s]]
                            items.append(lambda i=i, b=b: P.dma("pool", xt[b][:], xs_d[i * 128:(i + 1) * 128, :], [xs_res[i]], [xt[b]]))
                            for half in range(2):
                                for j in range(22):
                                    items.append(lambda s=s, j=j, half=half, pb=pb: P.mm(
                                        pb[half][:, :], aT[:, j, s * 128:(s + 1) * 128], wdn_sb[:, j, half * 512:(half + 1) * 512],
                                        j == 0, j == 21, [aT, wdn_res[j]], [pb[half]]))
                                items.append(lambda b=b, half=half, pb=pb: P.act(
                                    "act", junk[:, 0:512], pb[half][:, :], AF.Square, [pb[half]], [junk], accum_out=st6[b][:, 3 + half:4 + half]))

                            def epi(i=i, b=b, pb=pb):
                                P.tt("dve", st6[b][:, 0:1], st6[b][:, 3:4], st6[b][:, 4:5], ALU.add, [st6[b], junk], [st6[b]])
                                P.rstd(st6[b], 1.0 / D, EPS)
                                for half in range(2):
                                    hs = slice(half * 512, (half + 1) * 512)
                                    P.stt("dve", tmp[b][:, hs], pb[half][:, :], st6[b][:, 2:3], G2s[v][:, hs], ALU.mult, ALU.mult,
                                          [pb[half], st6[b], G2s[v]], [tmp[b]])
                                P.tt("pool", tmp[b][:], tmp[b][:], xt[b][:], ALU.add, [tmp[b], xt[b]], [tmp[b]])
                                if last and i >= 2:
                                    P.dma("sp", out_ap[(i - 2) * 128:(i - 1) * 128, :], tmp[b][:], [tmp[b]], ())
                                else:
                                    P.dma("sp", xs_d[i * 128:(i + 1) * 128, :], tmp[b][:], [tmp[b]], [xs_res[i]])
                            items.append(epi)
                        return items
                    pending_down.extend(make_down(bi, aT, v))
                while pending_down:
                    pending_down.pop(0)()

        P.flush(final=True)
    return nc


def host_inputs(inputs, b):
    m = {}
    m["xall"] = np.ascontiguousarray(np.concatenate([inputs["ctx"][b], inputs["x"][b]], axis=0), dtype=np.float32)
    m["c2"] = np.ascontiguousarray(np.stack([inputs["c"][b], inputs["c_ctx"]], axis=0), dtype=np.float32)
    m["ident"] = np.eye(128, dtype=np.float32)
    for name, shp in W_SPECS:
        m[name] = np.ascontiguousarray(np.asarray(inputs[name], dtype=np.float32).reshape([NL_FULL] + list(shp)))
    mix_host_inputs(inputs, m)
    return m


def used_inputs(nc, m):
    shapes = {}
    for alloc in nc.allocations:
        try:
            if alloc.kind == "ExternalInput":
                shapes[alloc.memorylocations[0].name] = tuple(alloc.tensor_shape)
        except Exception:
            pass
    out = {}
    for k, v in m.items():
        if k in shapes:
            shp = shapes[k]
            if tuple(v.shape) != shp:
                v = np.ascontiguousarray(v[0:shp[0]])
            assert tuple(v.shape) == shp, (k, v.shape, shp)
            out[k] = v
    return out


def kernel(**inputs):
    inputs = {k: np.asarray(v) for k, v in inputs.items()}
    nc = build_program()
    shared = host_inputs(inputs, 0)
    maps = []
    for b in range(4):
        m = dict(shared)
        m["xall"] = np.ascontiguousarray(np.concatenate([inputs["ctx"][b], inputs["x"][b]], axis=0), dtype=np.float32)
        m["c2"] = np.ascontiguousarray(np.stack([inputs["c"][b], inputs["c_ctx"]], axis=0), dtype=np.float32)
        maps.append(used_inputs(nc, m))
    in_maps = [maps[b % 4] for b in range(N_CORES)]
    res = run_bass_kernel_spmd(nc, in_maps, core_ids=list(range(N_CORES)))
    out = np.stack([np.asarray(res.results[b]["out"]) for b in range(4)], axis=0)
    return out.astype(np.float32)
```

```python
import contextlib
import numpy as np
import concourse.bass as bass
import concourse.mybir as mybir
from concourse.bass_utils import run_bass_kernel_spmd

F32 = mybir.dt.float32
BF16 = mybir.dt.bfloat16
ALU = mybir.AluOpType
AF = mybir.ActivationFunctionType
AX = mybir.AxisListType

D = 1024
NL_FULL = 4
TCTX = 256
TLAT = 4096
TALL = TCTX + TLAT
NT = TALL // 128
DIN = 2928
DFF = 2816
EPS = 1e-6


class Res:
    __slots__ = ("name", "w", "r")

    def __init__(self, name=""):
        self.name = name
        self.w = None
        self.r = {}


class T:
    def __init__(self, handle, name):
        self.h = handle
        self.res = Res(name)

    def __getitem__(self, k):
        return self.h[k]


class Prog:
    EPOCH = 30000
    KD = 6

    def __init__(self, nc, st):
        self.nc = nc
        self.st = st
        self.engs = ["pe", "act", "dve", "pool", "sp"]
        self.ops = {e: [] for e in self.engs}
        self.cnt = {e: 0 for e in self.engs}
        self.seen = {e: {} for e in self.engs}
        self.dcnt = {e: 0 for e in self.engs}
        self.sems = {}
        self.n_ops = 0
        self.gst = st
        self.last_ev = {}
        self.pending = {e: [] for e in self.engs}

    def sb(self, name, shape, dt=F32):
        self.n_ops += 1
        h = self.st.enter_context(self.nc.sbuf_tensor("sb%d_%s" % (self.n_ops, name), list(shape), dt))
        return T(h, name)

    def ps(self, name, shape=(128, 512), dt=F32):
        h = self.st.enter_context(self.nc.psum_tensor("ps_" + name, list(shape), dt))
        return T(h, name)

    def _sem(self, key):
        if key not in self.sems:
            self.sems[key] = self.gst.enter_context(self.nc.semaphore("s_" + "_".join(str(k) for k in key)))
        self.last_ev[key] = max(self.last_ev.get(key, 0), 0)
        return self.sems[key]

    def barrier(self):
        evs = [(k, v) for k, v in self.last_ev.items() if v > 0]
        for e in self.engs:
            self.pending[e] = list(evs)

    def barrier_light(self, ress):
        evs = [r.w for r in ress if r.w is not None]
        for e in self.engs:
            self.pending[e] = self.pending[e] + list(evs)

    @contextlib.contextmanager
    def phase(self):
        outer = self.st
        with contextlib.ExitStack() as ph:
            self.st = ph
            yield
            self.barrier()
            self.flush()
        self.st = outer

    def _deps(self, eng, reads, writes, is_dma):
        evs = []
        for ev in self.pending[eng]:
            evs.append((ev, "bar"))
        self.pending[eng] = []
        for t in reads:
            r = t.res if isinstance(t, T) else t
            if r.w is not None:
                evs.append((r.w, "raw"))
        for t in writes:
            r = t.res if isinstance(t, T) else t
            if r.w is not None:
                evs.append((r.w, "waw"))
            for ev in r.r.values():
                evs.append((ev, "war"))
        waits = {}
        for (key, val), kind in evs:
            if key[0] == "e" and key[1] == eng and not is_dma:
                if kind != "raw" or eng == "pe":
                    continue
            if self.seen[eng].get(key, 0) >= val:
                continue
            if waits.get(key, 0) < val:
                waits[key] = val
        for k, v in waits.items():
            self.seen[eng][k] = v
        return list(waits.items())

    def _mark(self, ev, reads, writes):
        for t in reads:
            r = t.res if isinstance(t, T) else t
            r.r[ev[0]] = ev
        for t in writes:
            r = t.res if isinstance(t, T) else t
            r.w = ev
            r.r = {}

    def op(self, eng, fn, reads=(), writes=()):
        waits = self._deps(eng, reads, writes, False)
        n = self.cnt[eng]
        key = ("e", eng, n // self.EPOCH)
        ev = (key, n % self.EPOCH + 1)
        self.cnt[eng] = n + 1
        self._sem(key)
        self.last_ev[key] = ev[1]
        self.ops[eng].append((waits, fn, key, 1))
        self._mark(ev, reads, writes)
        self.n_ops += 1

    def dma(self, q, out, in_, reads=(), writes=(), **kw):
        waits = self._deps(q, reads, writes, True)
        j = self.dcnt[q]
        self.dcnt[q] = j + 1
        key = ("d", q, j % self.KD)
        val = 16 * (j // self.KD + 1)
        if j >= self.KD and self.seen[q].get(key, 0) < val - 16:
            waits.append((key, val - 16))
            self.seen[q][key] = val - 16
        self._sem(key)
        self.last_ev[key] = val
        self.ops[q].append((waits, lambda e: e.dma_start(out=out, in_=in_, **kw), key, 16))
        self._mark((key, val), reads, writes)
        self.n_ops += 1

    def mm(self, out, lhsT, rhs, start, stop, reads, writes):
        self.op("pe", lambda e: e.matmul(out, lhsT, rhs, start=start, stop=stop), reads, writes)

    def tr(self, out, in_, ident, reads, writes):
        self.op("pe", lambda e: e.transpose(out, in_, ident), reads, writes)

    def act(self, eng, out, in_, func, reads, writes, bias=None, scale=None, accum_out=None):
        kw = {}
        if bias is not None:
            kw["bias"] = bias
        if scale is not None:
            kw["scale"] = scale
        if accum_out is not None:
            kw["accum_out"] = accum_out
        self.op(eng, lambda e: e.activation(out, in_, func, **kw), reads, writes)

    def tt(self, eng, out, in0, in1, op, reads, writes):
        self.op(eng, lambda e: e.tensor_tensor(out, in0, in1, op), reads, writes)

    def ts(self, eng, out, in0, s1, s2, op0, op1, reads, writes, accum_out=None):
        if op1 is None:
            self.op(eng, lambda e: e.tensor_scalar(out, in0, s1, None, op0), reads, writes)
        elif accum_out is None:
            self.op(eng, lambda e: e.tensor_scalar(out, in0, s1, s2, op0, op1), reads, writes)
        else:
            self.op(eng, lambda e: e.tensor_scalar(out, in0, s1, s2, op0, op1, accum_out), reads, writes)

    def stt(self, eng, out, in0, scalar, in1, op0, op1, reads, writes):
        self.op(eng, lambda e: e.scalar_tensor_tensor(out, in0, scalar, in1, op0, op1), reads, writes)

    def cp(self, eng, out, in_, reads, writes):
        if eng == "act":
            self.op(eng, lambda e: e.copy(out, in_), reads, writes)
        else:
            self.op(eng, lambda e: e.tensor_copy(out, in_), reads, writes)

    def red(self, eng, out, in_, op, reads, writes, axis=AX.X):
        self.op(eng, lambda e: e.tensor_reduce(out, in_, axis, op), reads, writes)

    def rstd(self, t, scale, eps, extra_reads=()):
        self.act("act", t[:, 1:2], t[:, 0:1], AF.Sqrt, [t, self.consts[eps]] + list(extra_reads), [t], bias=self.eps_ap(eps, t), scale=scale)
        self.op("dve", lambda e: e.reciprocal(t[:, 2:3], t[:, 1:2]), [t], [t])

    def eps_ap(self, eps, t):
        return self.consts[eps][0:t.h.shape[0], 0:1]

    def sumsq(self, dst_col, srcs, junk, reads):
        raise NotImplementedError

    def memset(self, eng, ap, val, writes):
        self.op(eng, lambda e: e.memset(ap, val), (), writes)

    def flush(self, final=False):
        nc = self.nc
        fin = [(k, v) for k, v in self.last_ev.items() if v > 0] if final else []
        sems = self.sems
        engmap = {"pe": "tensor", "act": "scalar", "dve": "vector", "pool": "gpsimd", "sp": "sync"}
        ops = self.ops
        with nc.Block() as block:
            def make(ename):
                def body(e):
                    for waits, fn, key, inc in ops[ename]:
                        for wk, wv in waits:
                            e.wait_ge(sems[wk], wv)
                        fn(e).then_inc(sems[key], inc)
                    if ename == "sp":
                        for wk, wv in fin:
                            e.wait_ge(sems[wk], wv)
                return body
            for ename in self.engs:
                getattr(block, engmap[ename])(make(ename))
        self.ops = {e: [] for e in self.engs}


NEG = -30000.0


def emit_na(P, L, need_ctx, dram, z_d, z_res, y_d, y_res, psum, ident):
    with P.phase():
        QT = P.sb("naQT", [128, 2, TALL], BF16)
        KT = P.sb("naKT", [128, 2, TALL], BF16)
        Ve = P.sb("naVe", [128, NT, 4, 66], BF16)
        Vo = P.sb("naVo", [128, NT - 1, 4, 66], BF16)
        btab = P.sb("btab", [128, 4, 14, 64])
        maskt = P.sb("mask", [128, 64])
        zt = [P.sb("zt%d" % i, [128, 768]) for i in range(2)]
        zo = [P.sb("zo%d" % i, [128, 256]) for i in range(2)]
        sw = [P.sb("sw%d" % i, [128, 4, 64]) for i in range(2)]
        pT = [P.sb("pT%d" % i, [128, 6, 64], BF16) for i in range(2)]
        rc = [P.sb("rc%d" % i, [64, 4]) for i in range(2)]
        yo = [P.sb("yo%d" % i, [64, 4, 64]) for i in range(2)]
        import os
        stage = float(os.environ.get("NA_STAGE", "9"))
        if stage < 0.5:
            return
        P.dma("sp", btab[:], dram["na_btab"][L], (), [btab])
        P.dma("sp", maskt[:], dram["na_mask"], (), [maskt])
        P.tt("dve", btab[:].rearrange("p h d q -> p (h d) q"), btab[:].rearrange("p h d q -> p (h d) q"),
             maskt[:, :].unsqueeze(1).to_broadcast([128, 56, 64]), ALU.add, [btab, maskt], [btab])
        if stage < 0.7:
            return
        P.memset("pool", Ve[:, :, :, 64:65], 1.0, [Ve])
        P.memset("pool", Vo[:, :, :, 64:65], 1.0, [Vo])
        if stage < 0.9:
            return
        for i in range(NT):
            b = i % 2
            P.dma("sp", zt[b][:], z_d[i * 128:(i + 1) * 128, 0:768], [z_res[i]], [zt[b]])
            pst = psum[i % 2]
            for c in range(4):
                P.tr(pst[:, c * 128:(c + 1) * 128], zt[b][:, c * 128:(c + 1) * 128], ident[:], [zt[b], ident], [pst])
            pv = pst[:, :].rearrange("p (c t) -> p c t", c=4)
            if stage > 0.915:
                P.cp("act", QT[:, :, i * 128:(i + 1) * 128], pv[:, 0:2, :], [pst], [QT])
            if stage > 0.925:
                P.cp("act", KT[:, :, i * 128:(i + 1) * 128], pv[:, 2:4, :], [pst], [KT])
            if stage > 0.935:
                P.cp("dve", Ve[:, i, :, 0:64], zt[b][:, 512:768].rearrange("p (h d) -> p h d", h=4), [zt[b]], [Ve])
            if i < NT - 1 and stage > 0.945:
                P.dma("pool", zo[b][:], z_d[64 + i * 128:64 + (i + 1) * 128, 512:768], [z_res[i], z_res[i + 1]], [zo[b]])
                if stage > 0.955:
                    P.cp("dve", Vo[:, i, :, 0:64], zo[b][:].rearrange("p (h d) -> p h d", h=4), [zo[b]], [Vo])

        units = []

        def na_rows(q0, keyspec, bias_di0, out_row0):
            units.append((q0, keyspec, bias_di0, out_row0))

        def part_a(u, h):
            q0, keyspec, bias_di0, out_row0 = units[u]
            g, hp = h // 2, (h % 2) * 64
            pst = psum[(u * 4 + h) % 4]
            for c, (k0, Vt, ti) in enumerate(keyspec):
                P.mm(pst[:, c * 64:(c + 1) * 64], KT[hp:hp + 64, g, k0:k0 + 128], QT[hp:hp + 64, g, q0:q0 + 64], True, True,
                     [KT, QT], [pst])

        def part_b(u, h):
            q0, keyspec, bias_di0, out_row0 = units[u]
            nk = len(keyspec)
            pst = psum[(u * 4 + h) % 4]
            po = psum[4 + u % 2]
            p = pT[(u * 4 + h) % 2]
            c0 = 0
            if bias_di0 is not None:
                s_ = sw[(u * 4 + h) % 2]
                P.stt("dve", s_[:], pst[:, 0:256].rearrange("p (c q) -> p c q", c=4), 0.125,
                      btab[:, h, bias_di0:bias_di0 + 7:2, :], ALU.mult, ALU.add, [pst, btab], [s_])
                P.act("act", p[:, 0:4, :], s_[:], AF.Exp, [s_], [p])
                c0 = 4
            P.act("act", p[:, c0:nk, :], pst[:, c0 * 64:nk * 64].rearrange("p (c q) -> p c q", c=nk - c0), AF.Exp, [pst], [p],
                  scale=0.125)
            for c, (k0, Vt, ti) in enumerate(keyspec):
                P.mm(po[0:64, h * 65:(h + 1) * 65], p[:, c, :], Vt[:, ti, h, 0:65], c == 0, c == nk - 1, [p, Vt], [po])
            if h == 3:
                r_ = rc[u % 2]
                y_ = yo[u % 2]
                pov = po[0:64, 0:260].rearrange("p (h e) -> p h e", h=4)
                P.op("dve", lambda e: e.reciprocal(r_[:, :], pov[:, :, 64]), [po], [r_])
                P.tt("dve", y_[:], pov[:, :, 0:64], r_[:, :].unsqueeze(2).to_broadcast([64, 4, 64]), ALU.mult, [po, r_], [y_])
                P.dma("sp", y_d[out_row0:out_row0 + 64, 0:256], y_[:].rearrange("p h d -> p (h d)"), [y_], [y_res[out_row0 // 128]])

        def run_units():
            seq = [(u, h) for u in range(len(units)) for h in range(4)]
            LOOK = 2
            for k in range(len(seq) + LOOK):
                if k < len(seq):
                    part_a(*seq[k])
                if k >= LOOK:
                    part_b(*seq[k - LOOK])

        ctxkeys = [(0, Ve, 0), (128, Ve, 1)]
        if stage < 2:
            return
        if need_ctx:
            for qc in range(4):
                na_rows(qc * 64, ctxkeys, None, qc * 64)
        for r in range(64 if stage >= 3 else 0):
            w0 = min(max(r - 4, 0), 56)
            key0 = TCTX + w0 * 64
            ks = []
            for c in range(4):
                k0 = key0 + c * 128
                if k0 % 128 == 0:
                    ks.append((k0, Ve, k0 // 128))
                else:
                    ks.append((k0, Vo, (k0 - 64) // 128))
            na_rows(TCTX + r * 64, ks + ctxkeys, w0 - r + 7, TCTX + r * 64)
        run_units()


def emit_mla(P, L, need_ctx, dram, z_d, z_res, y_d, y_res, psum, ident):
    SC = 96.0 ** -0.5
    with P.phase():
        nT0 = P.sb("nT0", [128, TALL], BF16)
        nT1 = P.sb("nT1", [64, TALL], BF16)
        nkT = P.sb("nkT", [128, TALL], BF16)
        krT = P.sb("krT", [96, TALL], BF16)
        KTm = P.sb("KTm", [96, 4, TALL], BF16)
        Vm = P.sb("Vm", [128, NT, 4, 66], BF16)
        wst = P.sb("wst", [128, 768])
        wq = P.sb("wq", [128, 2, 4, 96], BF16)
        wqp = P.sb("wqp", [128, 2, 4, 96], BF16)
        wk = P.sb("wk", [128, 4, 96], BF16)
        wv = P.sb("wv", [128, 256], BF16)
        Pm = P.sb("Pm", [96, 96], BF16)
        qnb = P.sb("qnb", [128, 192])
        kvnb = P.sb("kvnb", [128, 128])
        zt = [P.sb("zt%d" % i, [128, 352]) for i in range(2)]
        nq = [P.sb("nq%d" % i, [128, 416]) for i in range(2)]
        stt_ = [P.sb("st%d" % i, [128, 8]) for i in range(2)]
        junk = P.sb("junk", [128, 192])
        ctab = [P.sb("ctab%d" % i, [96, 2, 512]) for i in range(2)]
        t1 = [P.sb("t1_%d" % i, [96, 512]) for i in range(2)]
        t2 = [P.sb("t2_%d" % i, [96, 512]) for i in range(2)]
        QTb = [P.sb("QTb%d" % i, [96, 512], BF16) for i in range(2)]
        pT = [P.sb("pT%d" % i, [128, 512], BF16) for i in range(3)]
        rc = [P.sb("rc%d" % i, [128, 4]) for i in range(2)]
        yo = [P.sb("yo%d" % i, [128, 4, 4, 64]) for i in range(2)]
        oT = [P.sb("oT%d" % i, [65, 512]) for i in range(2)]
        def ldw(dst_ap, dst_t, src, rows, cols):
            P.dma("sp", wst[0:rows, 0:cols], src, (), [wst])
            P.cp("act", dst_ap, wst[0:rows, 0:cols], [wst], [dst_t])
        ldw(wq[:, 0, :, :].rearrange("p h e -> p (h e)"), wq, dram["mla_wq_r"][L, 0:128, :], 128, 384)
        ldw(wq[0:64, 1, :, :].rearrange("p h e -> p (h e)"), wq, dram["mla_wq_r"][L, 128:192, :], 64, 384)
        ldw(wqp[:, 0, :, :].rearrange("p h e -> p (h e)"), wqp, dram["mla_wq_p"][L, 0:128, :], 128, 384)
        ldw(wqp[0:64, 1, :, :].rearrange("p h e -> p (h e)"), wqp, dram["mla_wq_p"][L, 128:192, :], 64, 384)
        ldw(wk[:, :, :].rearrange("p h e -> p (h e)"), wk, dram["mla_wk"][L], 128, 384)
        ldw(wv[:, :], wv, dram["mla_wv"][L], 128, 256)
        ldw(Pm[:, :], Pm, dram["mla_pm"], 96, 96)
        for n_ in nq:
            P.memset("pool", n_[:, 320:384], 0.0, [n_])
        P.dma("sp", qnb[:], dram["mla_q_norm"][L].partition_broadcast(128), (), [qnb])
        P.dma("sp", kvnb[:], dram["mla_kv_norm"][L].partition_broadcast(128), (), [kvnb])
        P.memset("pool", Vm[:, :, :, 64:65], 1.0, [Vm])
        for i in range(NT):
            b = i % 2
            z_, n_, s_ = zt[b], nq[b], stt_[b]
            P.dma("sp", z_[:], z_d[i * 128:(i + 1) * 128, 768:1120], [z_res[i]], [z_])
            P.act("act", junk[:, 0:192], z_[:, 0:192], AF.Square, [z_], [junk], accum_out=s_[:, 0:1])
            P.rstd(s_, 1.0 / 192, EPS, [junk])
            P.stt("dve", n_[:, 0:192], z_[:, 0:192], s_[:, 2:3], qnb[:], ALU.mult, ALU.mult, [z_, s_, qnb], [n_])
            P.act("act", junk[:, 0:128], z_[:, 192:320], AF.Square, [z_], [junk], accum_out=s_[:, 4:5])
            P.act("act", s_[:, 5:6], s_[:, 4:5], AF.Sqrt, [s_, junk, P.consts[EPS]], [s_], bias=P.consts[EPS][:, 0:1], scale=1.0 / 128)
            P.op("dve", lambda e, s_=s_: e.reciprocal(s_[:, 6:7], s_[:, 5:6]), [s_], [s_])
            P.stt("dve", n_[:, 192:320], z_[:, 192:320], s_[:, 6:7], kvnb[:], ALU.mult, ALU.mult, [z_, s_, kvnb], [n_])
            pst = psum[i % 2]
            P.tr(pst[:, 0:128], n_[:, 0:128], ident[:], [n_, ident], [pst])
            P.tr(pst[0:64, 128:256], n_[:, 128:192], ident[:], [n_, ident], [pst])
            P.tr(pst[:, 256:384], n_[:, 192:320], ident[:], [n_, ident], [pst])
            P.cp("pool", n_[:, 384:416], z_[:, 320:352], [z_], [n_])
            P.tr(pst[0:96, 384:512], n_[:, 320:416], ident[:], [n_, ident], [pst])
            ts_ = slice(i * 128, (i + 1) * 128)
            P.cp("act", nT0[:, ts_], pst[:, 0:128], [pst], [nT0])
            P.cp("act", nT1[:, ts_], pst[0:64, 128:256], [pst], [nT1])
            P.cp("act", nkT[:, ts_], pst[:, 256:384], [pst], [nkT])
            P.cp("act", krT[:, ts_], pst[0:96, 384:512], [pst], [krT])
        nblk = [(b0, min(512, TALL - b0)) for b0 in range(0, TALL, 512)]
        for bi, (b0, n) in enumerate(nblk):
            ct = ctab[bi % 2]
            P.dma("sp", ct[64:96, :, 0:n], dram["rope_tab"][:, :, b0:b0 + n].rearrange("c p t -> p c t"), (), [ct])
            for h in range(4):
                pst = psum[h % 2]
                P.mm(pst[0:96, 0:n], wk[:, h, :], nkT[:, b0:b0 + n], True, True, [wk, nkT], [pst])
                P.cp("act", KTm[0:64, h, b0:b0 + n], pst[0:64, 0:n], [pst], [KTm])
            pb = psum[2]
            P.mm(pb[0:96, 0:n], Pm[:, :], krT[:, b0:b0 + n], True, True, [Pm, krT], [pb])
            a1, a2 = t1[bi % 2], t2[bi % 2]
            P.tt("dve", a1[64:96, 0:n], krT[64:96, b0:b0 + n], ct[64:96, 0, 0:n], ALU.mult, [krT, ct], [a1])
            P.tt("dve", a2[64:96, 0:n], pb[64:96, 0:n], ct[64:96, 1, 0:n], ALU.mult, [pb, ct], [a2])
            P.tt("pool", KTm[64:96, :, b0:b0 + n], a1[64:96, 0:n].unsqueeze(1).to_broadcast([32, 4, n]),
                 a2[64:96, 0:n].unsqueeze(1).to_broadcast([32, 4, n]), ALU.add, [a1, a2], [KTm])
            for s in range(n // 128):
                ti = b0 // 128 + s
                pv = psum[3 + s % 2]
                P.mm(pv[:, 0:256], nkT[:, ti * 128:(ti + 1) * 128], wv[:, :], True, True, [nkT, wv], [pv])
                P.cp("act", Vm[:, ti, :, 0:64], pv[:, 0:256].rearrange("p (h d) -> p h d", h=4), [pv], [Vm])
        qblocks = []
        if need_ctx:
            qblocks.append((0, 256, [0, 1]))
        for b0 in range(TCTX, TALL, 512):
            qblocks.append((b0, 512, list(range(NT))))
        cnt = 0
        for qi, (b0, n, ktiles) in enumerate(qblocks):
            ct = ctab[qi % 2]
            P.dma("sp", ct[64:96, :, 0:n], dram["rope_tab"][:, :, b0:b0 + n].rearrange("c p t -> p c t"), (), [ct])
            y_ = yo[qi % 2]
            nsub = n // 128
            for h in range(4):
                pa, pb = psum[0], psum[1]
                P.mm(pa[0:96, 0:n], wq[:, 0, h, :], nT0[:, b0:b0 + n], True, False, [wq, nT0], [pa])
                P.mm(pa[0:96, 0:n], wq[0:64, 1, h, :], nT1[:, b0:b0 + n], False, True, [wq, nT1], [pa])
                P.mm(pb[0:96, 0:n], wqp[:, 0, h, :], nT0[:, b0:b0 + n], True, False, [wqp, nT0], [pb])
                P.mm(pb[0:96, 0:n], wqp[0:64, 1, h, :], nT1[:, b0:b0 + n], False, True, [wqp, nT1], [pb])
                q_ = QTb[(qi * 4 + h) % 2]
                a1, a2 = t1[h % 2], t2[h % 2]
                P.cp("act", q_[0:64, 0:n], pa[0:64, 0:n], [pa], [q_])
                P.tt("dve", a1[64:96, 0:n], pa[64:96, 0:n], ct[64:96, 0, 0:n], ALU.mult, [pa, ct], [a1])
                P.tt("dve", a2[64:96, 0:n], pb[64:96, 0:n], ct[64:96, 1, 0:n], ALU.mult, [pb, ct], [a2])
                P.tt("pool", q_[64:96, 0:n], a1[64:96, 0:n], a2[64:96, 0:n], ALU.add, [a1, a2], [q_])
                poT = psum[5 + (qi * 4 + h) % 2]
                LOOK = 2
                nk_ = len(ktiles)
                slots_ = {}
                for ki in range(nk_ + LOOK):
                    if ki < nk_:
                        kt = ktiles[ki]
                        pst = psum[2 + cnt % 3]
                        p = pT[cnt % 3]
                        cnt += 1
                        P.mm(pst[:, 0:n], KTm[:, h, kt * 128:(kt + 1) * 128], q_[:, 0:n], True, True, [KTm, q_], [pst])
                        slots_[ki] = (pst, p, kt)
                    kj = ki - LOOK
                    if kj >= 0:
                        pst, p, kt = slots_.pop(kj)
                        P.act("act", p[:, 0:n], pst[:, 0:n], AF.Exp, [pst], [p], scale=SC)
                        P.mm(poT[0:65, 0:n], Vm[:, kt, h, 0:65], p[:, 0:n], kj == 0, kj == nk_ - 1, [p, Vm], [poT])
                o_ = oT[h % 2]
                P.cp("dve", o_[:, 0:n], poT[0:65, 0:n], [poT], [o_])
                po = psum[7]
                for s in range(nsub):
                    P.tr(po[:, s * 65:(s + 1) * 65], o_[0:65, s * 128:(s + 1) * 128], ident[0:65, 0:65], [o_, ident], [po])
                r_ = rc[h % 2]
                pov = po[:, 0:nsub * 65].rearrange("p (s e) -> p s e", s=nsub)
                P.op("dve", lambda e, r_=r_, pov=pov, nsub=nsub: e.reciprocal(r_[:, 0:nsub], pov[:, :, 64]), [po], [r_])
                P.tt("dve", y_[:, 0:nsub, h, :], pov[:, :, 0:64], r_[:, 0:nsub].unsqueeze(2).to_broadcast([128, nsub, 64]), ALU.mult,
                     [po, r_], [y_])
            for s in range(nsub):
                r0 = b0 + s * 128
                P.dma("sp", y_d[r0:r0 + 128, 256:512], y_[:, s, :, :].rearrange("p h d -> p (h d)"), [y_], [y_res[r0 // 128]])


def emit_mixers(P, L, need_ctx, dram, z_d, z_res, y_d, y_res, psum, ident, dbg):
    if "skip_na" not in dbg:
        emit_na(P, L, need_ctx, dram, z_d, z_res, y_d, y_res, psum, ident)
    if "skip_mla" not in dbg:
        emit_mla(P, L, need_ctx, dram, z_d, z_res, y_d, y_res, psum, ident)
    if "skip_gla" not in dbg:
        emit_gla(P, L, need_ctx, dram, z_d, z_res, y_d, y_res, psum, ident)
    if "skip_rw" not in dbg:
        emit_rwkv(P, L, need_ctx, dram, z_d, z_res, y_d, y_res, psum, ident)


def rope_tables():
    t = np.arange(TLAT)
    row = (t // 64).astype(np.float32)
    col = (t % 64).astype(np.float32)
    inv = (10000.0 ** (-np.arange(0, 16, 2, dtype=np.float32) / 16)).astype(np.float32)
    C = np.ones((32, TALL), np.float32)
    S = np.zeros((32, TALL), np.float32)
    for part, pos in ((0, row), (1, col)):
        ang = (pos[:, None] * inv[None, :]).astype(np.float32)
        c, s = np.cos(ang).T, np.sin(ang).T
        C[part * 16:part * 16 + 8, TCTX:] = c
        C[part * 16 + 8:part * 16 + 16, TCTX:] = c
        S[part * 16:part * 16 + 8, TCTX:] = -s
        S[part * 16 + 8:part * 16 + 16, TCTX:] = s
    return np.stack([C, S], 0).astype(np.float32)


def rope_partner():
    idx = np.arange(32)
    return np.where((idx % 16) < 8, idx + 8, idx - 8)


def mix_host_inputs(inputs, m):
    part = rope_partner()
    m["rope_tab"] = rope_tables()
    m["scan_consts"] = scan_consts()
    for nm, shp in (("rw_mu", (1, 1024)), ("rw_k_k", (1, 256)), ("rw_k_a", (1, 256)), ("rw_w0", (2, 1, 256)), ("rw_a0", (2, 1, 256)),
                    ("rw_w_up", (2, 64, 256)), ("rw_a_up", (2, 64, 256)), ("rw_g_up", (128, 256)), ("rw_ln_w", (1, 256)),
                    ("rw_ln_b", (1, 256)), ("rw_r_k", (1, 256))):
        m[nm] = np.ascontiguousarray(np.asarray(inputs[nm], np.float32).reshape((NL_FULL,) + shp))
    m["gla_gate_up"] = np.asarray(inputs["gla_gate_up"], np.float32)
    m["gla_gate_b"] = np.asarray(inputs["gla_gate_b"], np.float32).reshape(NL_FULL, 2, 1, 128)
    m["gla_norm"] = np.asarray(inputs["gla_norm"], np.float32).reshape(NL_FULL, 1, 256)
    pm = np.zeros((96, 96), np.float32)
    pm[64 + part, 64 + np.arange(32)] = 1.0
    m["mla_pm"] = pm
    wuq = np.asarray(inputs["mla_w_uq"], np.float32).reshape(NL_FULL, 192, 4, 96)
    m["mla_wq_r"] = np.ascontiguousarray(wuq.reshape(NL_FULL, 192, 384))
    wp = np.concatenate([wuq[..., 0:64], wuq[..., 64:96][..., part]], axis=-1)
    m["mla_wq_p"] = np.ascontiguousarray(wp.reshape(NL_FULL, 192, 384))
    wukv = np.asarray(inputs["mla_w_ukv"], np.float32).reshape(NL_FULL, 128, 4, 128)
    wk = np.zeros((NL_FULL, 128, 4, 96), np.float32)
    wk[..., 0:64] = wukv[..., 0:64]
    m["mla_wk"] = np.ascontiguousarray(wk.reshape(NL_FULL, 128, 384))
    m["mla_wv"] = np.ascontiguousarray(wukv[..., 64:128].reshape(NL_FULL, 128, 256))
    m["mla_q_norm"] = np.asarray(inputs["mla_q_norm"], np.float32).reshape(NL_FULL, 1, 192)
    m["mla_kv_norm"] = np.asarray(inputs["mla_kv_norm"], np.float32).reshape(NL_FULL, 1, 128)
    rpb = np.asarray(inputs["na_rpb"], np.float32)
    p = np.arange(128)
    k = p % 64
    q = np.arange(64)
    coff = np.clip(k[:, None] - q[None, :], -15, 15) + 15
    di = np.arange(14)
    roff = di[None, :] + (p[:, None] >= 64)
    m["na_btab"] = np.ascontiguousarray(
        rpb[:, :, roff[:, :, None], coff[:, None, :]].transpose(0, 2, 1, 3, 4))
    cstart = np.clip(q - 8, 0, 48)
    inwin = (k[:, None] >= cstart[None, :]) & (k[:, None] < cstart[None, :] + 16)
    m["na_mask"] = np.where(inwin, 0.0, NEG).astype(np.float32)
    return m


MIX_SPECS = [
    ("rw_mu", (1, 1024), True), ("rw_k_k", (1, 256), True), ("rw_k_a", (1, 256), True), ("rw_w0", (2, 1, 256), True),
    ("rw_a0", (2, 1, 256), True), ("rw_w_up", (2, 64, 256), True), ("rw_a_up", (2, 64, 256), True), ("rw_g_up", (128, 256), True),
    ("rw_ln_w", (1, 256), True), ("rw_ln_b", (1, 256), True), ("rw_r_k", (1, 256), True),
    ("scan_consts", (2, 128, 768), False), ("gla_gate_up", (2, 16, 128), True), ("gla_gate_b", (2, 1, 128), True),
    ("gla_norm", (1, 256), True),
    ("rope_tab", (2, 32, TALL), False), ("mla_pm", (96, 96), False),
    ("mla_wq_r", (192, 384), True), ("mla_wq_p", (192, 384), True), ("mla_wk", (128, 384), True), ("mla_wv", (128, 256), True),
    ("mla_q_norm", (1, 192), True), ("mla_kv_norm", (1, 128), True),
    ("na_btab", (128, 4, 14, 64), True), ("na_mask", (128, 64), False),
]


def scan_consts():
    s = np.arange(128)[:, None]
    t = np.arange(128)[None, :]
    out = np.zeros((2, 128, 768), np.float32)
    for d in range(2):
        incl = (s <= t) if d == 0 else (s >= t)
        strict = (s < t) if d == 0 else (s > t)
        out[d, :, 0:128] = incl
        out[d, :, 128:256] = incl
        out[d, :, 256:384] = strict
        out[d, :, 384:512] = incl
        out[d, :, 512:640] = strict
        out[d, :, 640:768] = strict.T
    return out


INLINE_FINISH = False


def emit_scan(P, dram, psum, ident, pre_d, pre_res, loads, has_ab, yacc, finish=None):
    ones = P.sb("ones", [128, 1])
    P.memset("pool", ones[:], 1.0, [ones])
    P.memset("pool", yacc[:], 0.0, [yacc])
    yres = [Res("yacc%d" % i) for i in range(NT)]
    done = [0] * NT

    def direction(d):
        B = psum[4 * d:4 * d + 4]
        Xin = [P.sb("Xin%d_%d" % (d, i), [128, 6, 256]) for i in range(2)]
        E = P.sb("E%d" % d, [128, 3, 256])
        W = [P.sb("W%d_%d" % (d, i), [128, 4, 256]) for i in range(2)]
        ft = P.sb("FT%d" % d, [128, 2, 4, 128])
        gm = P.sb("Gm%d" % d, [128, 4, 512])
        pCt = [P.sb("pC%d_%d" % (d, i), [128, 2]) for i in range(2)]
        H = P.sb("H%d" % d, [128, 2, 64])
        tmpH = P.sb("tmpH%d" % d, [128, 2, 64])
        cst = P.sb("cst%d" % d, [128, 768])
        P.dma("sp", cst[:], dram["scan_consts"][d], (), [cst])
        if has_ab:
            Xb = [P.sb("Xb%d_%d" % (d, i), [128, 4, 128], BF16) for i in range(2)]
            Yb = [P.sb("Yb%d_%d" % (d, i), [128, 4, 128], BF16) for i in range(2)]
            Pb = [P.sb("Pb%d_%d" % (d, i), [128, 4, 128], BF16) for i in range(2)]
            P32 = [P.sb("P32%d_%d" % (d, i), [128, 4, 128]) for i in range(2)]
            Xs = P.sb("Xs%d" % d, [128, 256])
            Us = P.sb("Us%d" % d, [128, 256])
        order = list(range(NT)) if d == 0 else [1, 0] + list(range(NT - 1, 1, -1))
        Mc = cst[:, 0:128]
        MASK4 = cst[:, 128:640]
        P.memset("pool", H[:], 0.0, [H])
        yield
        for n, i in enumerate(order):
            b = n % 2
            xi, w_, pc = Xin[b], W[b], pCt[b]
            for (s0, ns, c0) in loads(d):
                P.dma("sp" if (s0 + d) % 2 == 0 else "pool", xi[:, s0:s0 + ns, :].rearrange("p s f -> p (s f)"),
                      pre_d[i * 128:(i + 1) * 128, c0:c0 + ns * 256], [pre_res[i]], [xi])
            pcl = B[0]
            P.mm(pcl[:, 0:256], Mc, xi[:, 3, :], True, True, [cst, xi], [pcl])
            pcp = B[1]
            for g in range(2):
                P.mm(pcp[:, 384 + g:385 + g], xi[:, 3, g * 128:(g + 1) * 128], ones[:, 0:1], True, True, [xi, ones], [pcp])
            yield
            P.act("act", E[:, 0, :], pcl[:, 0:256], AF.Exp, [pcl], [E])
            P.act("act", E[:, 1, :], pcl[:, 0:256], AF.Exp, [pcl], [E], scale=-1.0)
            P.act("act", pc[:, :], pcp[:, 384:386], AF.Exp, [pcp], [pc])
            if has_ab:
                P.tt("dve", E[:, 2, :], pcl[:, 0:256], xi[:, 3, :], ALU.subtract, [pcl, xi], [E])
                P.act("act", E[:, 2, :], E[:, 2, :], AF.Exp, [E], [E])
            yield
            P.tt("dve", w_[:, 0, :], xi[:, 0, :], E[:, 0, :], ALU.mult, [xi, E], [w_])
            P.tt("pool", w_[:, 1, :], xi[:, 1, :], E[:, 1, :], ALU.mult, [xi, E], [w_])
            if has_ab:
                P.tt("dve", w_[:, 2, :], xi[:, 4, :], E[:, 2, :], ALU.mult, [xi, E], [w_])
                P.tt("pool", w_[:, 3, :], xi[:, 5, :], E[:, 1, :], ALU.mult, [xi, E], [w_])
            yield
            slots = [(0, 0), (2, 1), (1, 2), (3, 3)] if has_ab else [(0, 0), (1, 2)]
            for g in range(2):
                pt = B[2 + g]
                for (ws, fs) in slots:
                    P.tr(pt[:, fs * 128:(fs + 1) * 128], w_[:, ws, g * 128:(g + 1) * 128], ident[:], [w_, ident], [pt])
            yield
            for g in range(2):
                pt = B[2 + g]
                if has_ab:
                    P.cp("act", ft[:, g, :, :].rearrange("p f t -> p (f t)"), pt[:, :], [pt], [ft])
                else:
                    P.cp("act", ft[:, g, 0, :], pt[:, 0:128], [pt], [ft])
                    P.cp("act", ft[:, g, 2, :], pt[:, 256:384], [pt], [ft])
            yield
            for h in range(4):
                g, hp = h // 2, (h % 2) * 64
                pg = B[2 + h % 2]
                if has_ab:
                    P.mm(pg[:, 0:256], ft[hp:hp + 64, g, 2, :], ft[hp:hp + 64, g, 0:2, :].rearrange("p f t -> p (f t)"), True, True, [ft], [pg])
                    P.mm(pg[:, 256:512], ft[hp:hp + 64, g, 3, :], ft[hp:hp + 64, g, 0:2, :].rearrange("p f t -> p (f t)"), True, True, [ft], [pg])
                    P.tt("dve", gm[:, h, :], pg[:, :], MASK4, ALU.mult, [pg, cst], [gm])
                else:
                    P.mm(pg[:, 0:128], ft[hp:hp + 64, g, 2, :], ft[hp:hp + 64, g, 0, :], True, True, [ft], [pg])
                    P.tt("dve", gm[:, h, 0:128], pg[:, 0:128], MASK4[:, 0:128], ALU.mult, [pg, cst], [gm])
                if h % 2 == 1:
                    yield
            if has_ab:
                p3 = B[0]
                for h in range(4):
                    P.tr(p3[:, h * 128:(h + 1) * 128], gm[:, h, 384:512], ident[:], [gm, ident], [p3])
                Xc, Yc, Pc, Pc32 = Xb[0], Yb[0], Pb[0], P32[0]
                yield
                P.cp("act", Yc[:].rearrange("p h s -> p (h s)"), p3[:, :], [p3], [Yc])
                P.cp("dve", Xc[:], gm[:, :, 384:512], [gm], [Xc])
                for h in range(4):
                    P.tt("dve", Pc32[:, h, :], gm[:, h, 384:512], ident[:, :], ALU.add, [gm, ident], [Pc32])
                P.cp("dve", Pc[:], Pc32[:], [Pc32], [Pc])
                yield
                def mm_xy(lv, Xc, Yc):
                    pY = B[3]
                    for h in range(4):
                        P.mm(pY[:, h * 128:(h + 1) * 128], Xc[:, h, :], Yc[:, h, :], True, True, [Yc, Xc], [pY])
                    if lv < 6:
                        pX = B[2]
                        for h in range(4):
                            P.mm(pX[:, h * 128:(h + 1) * 128], Yc[:, h, :], Xc[:, h, :], True, True, [Yc, Xc], [pX])

                mm_xy(1, Xc, Yc)
                yield
                for lv in range(1, 7):
                    Yn, Pn, Pn32 = Yb[lv % 2], Pb[lv % 2], P32[lv % 2]
                    Xn = Xb[lv % 2]
                    P.cp("act", Yn[:].rearrange("p h s -> p (h s)"), B[3][:, :], [B[3]], [Yn])
                    if lv < 6:
                        P.cp("act", Xn[:].rearrange("p h s -> p (h s)"), B[2][:, :], [B[2]], [Xn])
                    yield
                    pP = B[1]
                    for h in range(4):
                        P.mm(pP[:, h * 128:(h + 1) * 128], Yn[:, h, :], Pc[:, h, :], True, True, [Yn, Pc], [pP])
                    if lv < 6:
                        mm_xy(lv + 1, Xn, Yn)
                    yield
                    P.tt("dve", Pn32[:].rearrange("p h s -> p (h s)"), pP[:, :], Pc32[:].rearrange("p h s -> p (h s)"), ALU.add,
                         [pP, Pc32], [Pn32])
                    if lv < 6:
                        P.cp("dve", Pn[:], Pn32[:], [Pn32], [Pn])
                    Xc, Yc, Pc, Pc32 = Xn, Yn, Pn, Pn32
                yield
                Pc = Pc32
                pXs = B[0]
                for h in range(4):
                    g, hp = h // 2, (h % 2) * 64
                    P.mm(pXs[:, 256 + h * 64:256 + (h + 1) * 64], ft[hp:hp + 64, g, 1, :], H[hp:hp + 64, g, :], True, False, [ft, H], [pXs])
                    P.mm(pXs[:, 256 + h * 64:256 + (h + 1) * 64], gm[:, h, 128:256], xi[:, 2, h * 64:(h + 1) * 64], False, True, [gm, xi], [pXs])
                yield
                P.cp("act", Xs[:, :], pXs[:, 256:512], [pXs], [Xs])
                yield
                pU = B[2]
                for h in range(4):
                    P.mm(pU[:, h * 64:(h + 1) * 64], Pc[:, h, :], Xs[:, h * 64:(h + 1) * 64], True, True, [Pc, Xs], [pU])
                yield
                P.cp("dve", Us[:, :], pU[:, 0:256], [pU], [Us])
                yield
            pYo = B[1]
            for h in range(4):
                g, hp = h // 2, (h % 2) * 64
                vh = xi[:, 2, h * 64:(h + 1) * 64]
                P.mm(pYo[:, h * 64:(h + 1) * 64], ft[hp:hp + 64, g, 0, :], H[hp:hp + 64, g, :], True, False, [ft, H], [pYo])
                if has_ab:
                    P.mm(pYo[:, h * 64:(h + 1) * 64], gm[:, h, 256:384], Us[:, h * 64:(h + 1) * 64], False, False, [gm, Us], [pYo])
                P.mm(pYo[:, h * 64:(h + 1) * 64], gm[:, h, 0:128], vh, False, True, [gm, xi], [pYo])
            for h in range(4):
                g, hp = h // 2, (h % 2) * 64
                vh = xi[:, 2, h * 64:(h + 1) * 64]
                P.mm(pYo[hp:hp + 64, 256 + g * 64:256 + (g + 1) * 64], w_[:, 1, h * 64:(h + 1) * 64], vh, True, not has_ab, [w_, xi], [pYo])
                if has_ab:
                    P.mm(pYo[hp:hp + 64, 256 + g * 64:256 + (g + 1) * 64], w_[:, 3, h * 64:(h + 1) * 64], Us[:, h * 64:(h + 1) * 64], False, True,
                         [w_, Us], [pYo])
            yield
            P.tt("dve", yacc[:, i, :], pYo[:, 0:256], yacc[:, i, :], ALU.add, [pYo, yres[i]], [yres[i]])
            P.tt("dve", tmpH[:], pYo[:, 256:384].rearrange("p (g v) -> p g v", g=2), H[:], ALU.add, [pYo, H], [tmpH])
            P.tt("dve", H[:], tmpH[:], pc[:, :].unsqueeze(2).to_broadcast([128, 2, 64]), ALU.mult, [tmpH, pc], [H])
            done[i] += 1
            if done[i] == 2 and finish is not None and INLINE_FINISH:
                finish(i, yres[i])
            yield

    gens = [direction(0), direction(1)]
    while gens:
        for g_ in list(gens):
            try:
                next(g_)
            except StopIteration:
                gens.remove(g_)
    if finish is not None and not INLINE_FINISH:
        outs = [finish(i, yres[i]) for i in range(NT) ] if False else None
        i = 0
        while i < NT:
            grp = []
            for k in range(i, min(i + 4, NT)):
                g_ = finish(k, yres[k])
                if g_ is not None and hasattr(g_, "__next__"):
                    grp.append(g_)
            i += 4
            while grp:
                for g_ in list(grp):
                    try:
                        next(g_)
                    except StopIteration:
                        grp.remove(g_)


GLA_Z0 = 768 + 352 + 1024


def emit_gla(P, L, need_ctx, dram, z_d, z_res, y_d, y_res, psum, ident):
    pre_d = dram["gla_pre"]
    pre_res = dram.res("gla_pre")
    with P.phase():
        zt = [P.sb("zt%d" % i, [128, 528]) for i in range(2)]
        pre = [P.sb("pre%d" % i, [128, 5, 4, 64]) for i in range(2)]
        gdT = [P.sb("gdT%d" % i, [16, 128]) for i in range(2)]
        gup = P.sb("gup", [16, 2, 128])
        gb = P.sb("gb", [128, 2, 128])
        xg = [P.sb("xg%d" % i, [128, 128]) for i in range(2)]
        for p_ in pre:
            P.memset("pool", p_[:], 0.0, [p_])
        for d in range(2):
            P.dma("sp", gup[:, d, :], dram["gla_gate_up"][L, d], (), [gup])
            P.dma("sp", gb[:, d, :], dram["gla_gate_b"][L, d].partition_broadcast(128), (), [gb])
        def gprep_a(i):
            b = i % 2
            z_, p_ = zt[b], pre[b]
            P.dma("sp", z_[:], z_d[i * 128:(i + 1) * 128, GLA_Z0:GLA_Z0 + 528], [z_res[i]], [z_])
            P.op("act", lambda e, z_=z_, p_=p_: e.mul(p_[:, 0, :, 0:32], z_[:, 0:128].rearrange("p (h d) -> p h d", h=4), 32.0 ** -0.5), [z_], [p_])
            P.cp("pool", p_[:, 1, :, 0:32], z_[:, 128:256].rearrange("p (h d) -> p h d", h=4), [z_], [p_])
            P.cp("pool", p_[:, 2, :, :], z_[:, 256:512].rearrange("p (h d) -> p h d", h=4), [z_], [p_])
            pt = psum[i % 2]
            P.tr(pt[0:16, 0:128], z_[:, 512:528], ident[:], [z_, ident], [pt])
            P.cp("act", gdT[b][:, :], pt[0:16, 0:128], [pt], [gdT[b]])

        def gprep_b(i):
            b = i % 2
            p_ = pre[b]
            for d in range(2):
                pl = psum[2 + d]
                x_ = xg[d]
                P.mm(pl[:, 0:128], gdT[b][:, :], gup[:, d, :], True, True, [gdT[b], gup], [pl])
                P.tt("dve", x_[:], pl[:, 0:128], gb[:, d, :], ALU.add, [pl, gb], [x_])
                P.act("act", x_[:], x_[:], AF.Exp, [x_], [x_], scale=-1.0)
                P.act("act", x_[:], x_[:], AF.Ln, [x_], [x_], bias=1.0)
                P.op("act", lambda e, x_=x_, p_=p_, d=d: e.mul(p_[:, 3 + d, :, 0:32], x_[:].rearrange("p (h d) -> p h d", h=4), -1.0 / 16.0), [x_], [p_])
            P.dma("pool", pre_d[i * 128:(i + 1) * 128, :], p_[:].rearrange("p s h d -> p (s h d)"), [p_], [pre_res[i]])

        for k in range(NT + 1):
            if k < NT:
                gprep_a(k)
            if k >= 1:
                gprep_b(k - 1)
    with P.phase():
        yacc = P.sb("yacc", [128, NT, 256])
        gn = P.sb("gn", [128, 256])
        og = [P.sb("og%d" % i, [128, 256]) for i in range(2)]
        sq = [P.sb("sq%d" % i, [128, 256]) for i in range(2)]
        stg = [P.sb("stg%d" % i, [128, 12]) for i in range(2)]
        P.dma("sp", gn[:], dram["gla_norm"][L].partition_broadcast(128), (), [gn])
        fcnt = [0]

        def finish(i, yr):
            if i < 2 and not need_ctx:
                return
            b = fcnt[0] % 2
            fcnt[0] += 1
            o_, s_, g_ = og[b], sq[b], stg[b]
            yv = yacc[:, i, :]
            P.dma("sp", o_[:], z_d[i * 128:(i + 1) * 128, DIN - 256:DIN], [z_res[i]], [o_])
            P.act("act", o_[:], o_[:], AF.Silu, [o_], [o_])
            P.tt("pool", s_[:], yv, yv, ALU.mult, [yr], [s_])
            P.red("dve", g_[:, 0:4], s_[:].rearrange("p (h d) -> p h d", h=4), ALU.add, [s_], [g_])
            P.act("act", g_[:, 4:8], g_[:, 0:4], AF.Sqrt, [g_, P.consts[EPS]], [g_], bias=P.consts[EPS][:, 0:1], scale=1.0 / 64)
            P.op("dve", lambda e, g_=g_: e.reciprocal(g_[:, 8:12], g_[:, 4:8]), [g_], [g_])
            P.tt("dve", s_[:].rearrange("p (h d) -> p h d", h=4), yv.rearrange("p (h d) -> p h d", h=4),
                 g_[:, 8:12].unsqueeze(2).to_broadcast([128, 4, 64]), ALU.mult, [yr, g_], [s_])
            P.tt("pool", s_[:], s_[:], gn[:], ALU.mult, [s_, gn], [s_])
            P.tt("pool", s_[:], s_[:], o_[:], ALU.mult, [s_, o_], [s_])
            P.dma("sp", y_d[i * 128:(i + 1) * 128, 768:1024], s_[:], [s_], [y_res[i]])

        emit_scan(P, dram, psum, ident, pre_d, pre_res, lambda d: [(0, 3, 0), (3, 1, 768 + d * 256)], False, yacc, finish)


RW_Z0 = 768 + 352


def emit_rwkv(P, L, need_ctx, dram, z_d, z_res, y_d, y_res, psum, ident):
    pre_d = dram["rw_pre"]
    pre_res = dram.res("rw_pre")
    C05 = float(-np.exp(-0.5))
    with P.phase():
        zc = [P.sb("zc%d" % i, [128, 1024]) for i in range(2)]
        zp = [P.sb("zp%d" % i, [128, 1024]) for i in range(2)]
        zn = [P.sb("zn%d" % i, [128, 1024]) for i in range(2)]
        zs = [P.sb("zs%d" % i, [128, 1024]) for i in range(2)]
        pre = [P.sb("pre%d" % i, [128, 10, 256]) for i in range(2)]
        mub = P.sb("mub", [128, 1024])
        kkb = P.sb("kkb", [128, 256])
        kab = P.sb("kab", [128, 256])
        w0b = P.sb("w0b", [128, 2, 256])
        a0b = P.sb("a0b", [128, 2, 256])
        wa = P.sb("wa", [128, 2, 256])
        gup = P.sb("gup", [128, 256])
        TWA = [P.sb("TWA%d" % i, [128, 128]) for i in range(2)]
        TG = [P.sb("TG%d" % i, [128, 128]) for i in range(2)]
        kt_ = [P.sb("kt%d" % i, [128, 256]) for i in range(2)]
        kn_ = [P.sb("kn%d" % i, [128, 256]) for i in range(2)]
        sq_ = P.sb("sq", [128, 256])
        xw = [P.sb("xw%d" % i, [128, 256]) for i in range(2)]
        xa = [P.sb("xa%d" % i, [128, 256]) for i in range(2)]
        st_ = [P.sb("st%d" % i, [128, 12]) for i in range(2)]
        P.dma("sp", mub[:], dram["rw_mu"][L].partition_broadcast(128), (), [mub])
        P.dma("sp", kkb[:], dram["rw_k_k"][L].partition_broadcast(128), (), [kkb])
        P.dma("sp", kab[:], dram["rw_k_a"][L].partition_broadcast(128), (), [kab])
        for d in range(2):
            P.dma("sp", w0b[:, d, :], dram["rw_w0"][L, d].partition_broadcast(128), (), [w0b])
            P.dma("sp", a0b[:, d, :], dram["rw_a0"][L, d].partition_broadcast(128), (), [a0b])
            P.dma("sp", wa[0:64, d, :], dram["rw_w_up"][L, d], (), [wa])
            P.dma("sp", wa[64:128, d, :], dram["rw_a_up"][L, d], (), [wa])
        P.dma("sp", gup[:], dram["rw_g_up"][L], (), [gup])
        def prep_a(i):
            b = i % 2
            c_, p_, n_, s_, pr = zc[b], zp[b], zn[b], zs[b], pre[b]
            t0 = i * 128
            P.dma("sp", c_[:], z_d[t0:t0 + 128, RW_Z0:RW_Z0 + 1024], [z_res[i]], [c_])
            if t0 in (0, TCTX):
                P.memset("pool", p_[:], 0.0, [p_])
                P.dma("pool", p_[1:128, :], z_d[t0:t0 + 127, RW_Z0:RW_Z0 + 1024], [z_res[i]], [p_])
            else:
                P.dma("pool", p_[:], z_d[t0 - 1:t0 + 127, RW_Z0:RW_Z0 + 1024], [z_res[i - 1], z_res[i]], [p_])
            if t0 + 128 in (TCTX, TALL):
                P.memset("pool", n_[:], 0.0, [n_])
                P.dma("sp", n_[0:127, :], z_d[t0 + 1:t0 + 128, RW_Z0:RW_Z0 + 1024], [z_res[i]], [n_])
            else:
                P.dma("sp", n_[:], z_d[t0 + 1:t0 + 129, RW_Z0:RW_Z0 + 1024], [z_res[i], z_res[i + 1]], [n_])
            P.tt("pool", p_[:], p_[:], n_[:], ALU.add, [p_, n_], [p_])
            P.stt("dve", p_[:], p_[:], 0.5, c_[:], ALU.mult, ALU.subtract, [p_, c_], [p_])
            P.tt("pool", p_[:], p_[:], mub[:], ALU.mult, [p_, mub], [p_])
            P.tt("dve", s_[:], p_[:], c_[:], ALU.add, [p_, c_], [s_])
            P.cp("act", pr[:, 0, :], s_[:, 0:256], [s_], [pr])
            P.cp("act", pr[:, 1, :], s_[:, 512:768], [s_], [pr])
            k_, kn, st = kt_[b], kn_[b], st_[b]
            P.tt("dve", k_[:], s_[:, 256:512], kkb[:], ALU.mult, [s_, kkb], [k_])
            P.tt("pool", sq_[:], k_[:], k_[:], ALU.mult, [k_], [sq_])
            P.red("dve", st[:, 0:4], sq_[:].rearrange("p (h d) -> p h d", h=4), ALU.add, [sq_], [st])
            P.act("act", st[:, 4:8], st[:, 0:4], AF.Sqrt, [st, P.consts[1e-12]], [st], bias=P.consts[1e-12][:, 0:1], scale=1.0)
            P.op("dve", lambda e, st=st: e.reciprocal(st[:, 8:12], st[:, 4:8]), [st], [st])
            P.tt("dve", kn[:].rearrange("p (h d) -> p h d", h=4), k_[:].rearrange("p (h d) -> p h d", h=4),
                 st[:, 8:12].unsqueeze(2).to_broadcast([128, 4, 64]), ALU.mult, [k_, st], [kn])
            P.op("act", lambda e, pr=pr, kn=kn: e.mul(pr[:, 2, :], kn[:], -1.0), [kn], [pr])
            pt = psum[i % 2]
            P.tr(pt[:, 0:128], s_[:, 768:896], ident[:], [s_, ident], [pt])
            P.tr(pt[:, 128:256], s_[:, 896:1024], ident[:], [s_, ident], [pt])
            tw, tg = TWA[b], TG[b]
            P.act("act", tw[0:64, :], pt[0:64, 0:128], AF.Tanh, [pt], [tw])
            P.cp("act", tw[64:128, :], pt[64:128, 0:128], [pt], [tw])
            P.act("act", tg[:, :], pt[:, 128:256], AF.Sigmoid, [pt], [tg])

        def prep_b(i):
            b = i % 2
            s_, pr = zs[b], pre[b]
            kn = kn_[b]
            tw, tg = TWA[b], TG[b]
            t0 = i * 128
            pgp = psum[6]
            P.mm(pgp[:, 0:256], tg[:, :], gup[:, :], True, True, [tg, gup], [pgp])
            P.cp("act", pr[:, 9, :], pgp[:, 0:256], [pgp], [pr])
            for d in range(2):
                pw, pa = psum[2 + d * 2], psum[3 + d * 2]
                P.mm(pw[:, 0:256], tw[0:64, :], wa[0:64, d, :], True, True, [tw, wa], [pw])
                P.mm(pa[:, 0:256], tw[64:128, :], wa[64:128, d, :], True, True, [tw, wa], [pa])
                w_, a_ = xw[d], xa[d]
                P.tt("dve", w_[:], pw[:, 0:256], w0b[:, d, :], ALU.add, [pw, w0b], [w_])
                P.act("act", w_[:], w_[:], AF.Sigmoid, [w_], [w_])
                P.op("act", lambda e, pr=pr, w_=w_, d=d: e.mul(pr[:, 3 + 3 * d, :], w_[:], C05), [w_], [pr])
                P.tt("dve", a_[:], pa[:, 0:256], a0b[:, d, :], ALU.add, [pa, a0b], [a_])
                P.act("act", a_[:], a_[:], AF.Sigmoid, [a_], [a_])
                P.tt("pool", pr[:, 5 + 3 * d, :], kn[:], a_[:], ALU.mult, [kn, a_], [pr])
                P.stt("dve", a_[:], a_[:], -1.0, kab[:], ALU.add, ALU.mult, [a_, kab], [a_])
                P.stt("dve", pr[:, 4 + 3 * d, :], a_[:], 1.0, s_[:, 256:512], ALU.add, ALU.mult, [a_, s_], [pr])
            P.dma("pool", pre_d[t0:t0 + 128, :], pr[:].rearrange("p s f -> p (s f)"), [pr], [pre_res[i]])

        for k in range(NT + 1):
            if k < NT:
                prep_a(k)
            if k >= 1:
                prep_b(k - 1)
    import os
    if os.environ.get("RW_STAGE", "9") == "1":
        return
    with P.phase():
        yacc = P.sb("yacc", [128, NT, 256])
        lnw = P.sb("lnw", [128, 256])
        lnb = P.sb("lnb", [128, 256])
        rkb = P.sb("rkb", [128, 256])
        fin = [P.sb("fin%d" % i, [128, 5, 256]) for i in range(4)]
        yc = [P.sb("yc%d" % i, [128, 256]) for i in range(4)]
        sq = [P.sb("sq%d" % i, [128, 256]) for i in range(4)]
        sf = [P.sb("sf%d" % i, [128, 24]) for i in range(4)]
        P.dma("sp", lnw[:], dram["rw_ln_w"][L].partition_broadcast(128), (), [lnw])
        P.dma("sp", lnb[:], dram["rw_ln_b"][L].partition_broadcast(128), (), [lnb])
        P.dma("sp", rkb[:], dram["rw_r_k"][L].partition_broadcast(128), (), [rkb])
        v4 = lambda ap: ap.rearrange("p (h d) -> p h d", h=4)
        bc4 = lambda ap: ap.unsqueeze(2).to_broadcast([128, 4, 64])
        fcnt = [0]

        def finish(i, yr):
            if i < 2 and not need_ctx:
                return
            b = fcnt[0] % 4
            fcnt[0] += 1
            f_, y_, s_, t_ = fin[b], yc[b], sq[b], sf[b]
            t0 = i * 128
            P.dma("sp", f_[:, 0:2, :].rearrange("p s f -> p (s f)"), pre_d[t0:t0 + 128, 0:512], [pre_res[i]], [f_])
            yield
            P.dma("pool", f_[:, 2, :], pre_d[t0:t0 + 128, 1024:1280], [pre_res[i]], [f_])
            yield
            P.dma("pool", f_[:, 3, :], pre_d[t0:t0 + 128, 1792:2048], [pre_res[i]], [f_])
            yield
            P.dma("sp", f_[:, 4, :], pre_d[t0:t0 + 128, 2304:2560], [pre_res[i]], [f_])
            yv = yacc[:, i, :]
            yield
            P.red("dve", t_[:, 0:4], v4(yv), ALU.add, [yr], [t_])
            yield
            P.op("act", lambda e, t_=t_: e.mul(t_[:, 4:8], t_[:, 0:4], -1.0 / 64), [t_], [t_])
            yield
            P.tt("dve", v4(y_[:]), v4(yv), bc4(t_[:, 4:8]), ALU.add, [yr, t_], [y_])
            yield
            P.tt("pool", s_[:], y_[:], y_[:], ALU.mult, [y_], [s_])
            yield
            P.red("dve", t_[:, 8:12], v4(s_[:]), ALU.add, [s_], [t_])
            yield
            P.act("act", t_[:, 12:16], t_[:, 8:12], AF.Sqrt, [t_, P.consts[64e-5]], [t_], bias=P.consts[64e-5][:, 0:1], scale=1.0 / 64)
            yield
            P.op("dve", lambda e, t_=t_: e.reciprocal(t_[:, 16:20], t_[:, 12:16]), [t_], [t_])
            yield
            P.tt("dve", v4(y_[:]), v4(y_[:]), bc4(t_[:, 16:20]), ALU.mult, [y_, t_], [y_])
            yield
            P.tt("pool", y_[:], y_[:], lnw[:], ALU.mult, [y_, lnw], [y_])
            yield
            P.tt("pool", y_[:], y_[:], lnb[:], ALU.add, [y_, lnb], [y_])
            yield
            P.tt("pool", s_[:], f_[:, 2, :], f_[:, 3, :], ALU.add, [f_], [s_])
            yield
            P.tt("dve", s_[:], s_[:], f_[:, 0, :], ALU.mult, [s_, f_], [s_])
            yield
            P.tt("pool", s_[:], s_[:], rkb[:], ALU.mult, [s_, rkb], [s_])
            yield
            P.red("dve", t_[:, 20:24], v4(s_[:]), ALU.add, [s_], [t_])
            yield
            P.tt("dve", v4(s_[:]), v4(f_[:, 1, :]), bc4(t_[:, 20:24]), ALU.mult, [f_, t_, s_], [s_])
            yield
            P.tt("pool", y_[:], y_[:], s_[:], ALU.add, [y_, s_], [y_])
            yield
            P.tt("pool", y_[:], y_[:], f_[:, 4, :], ALU.mult, [y_, f_], [y_])
            yield
            P.dma("sp", y_d[t0:t0 + 128, 512:768], y_[:], [y_], [y_res[i]])

        emit_scan(P, dram, psum, ident, pre_d, pre_res,
                  lambda d: [(0, 1, 0), (2, 1, 256), (4, 1, 512), (3, 1, 768 + 768 * d), (1, 1, 1024 + 768 * d), (5, 1, 1280 + 768 * d)],
                  True, yacc, finish)


N_CORES = 8
SCR_SPECS = {"gla_pre": 1280, "rw_pre": 2560}
W_SPECS = [
    ("w_mod", (D, 6 * D)), ("b_mod", (1, 6 * D)),
    ("g_mix_pre", (1, D)), ("g_mix_post", (1, D)), ("g_ffn_pre", (1, D)), ("g_ffn_post", (1, D)),
    ("w_in", (D, DIN)), ("w_out", (D, D)),
    ("ffn_w_up", (D, 2 * DFF)), ("ffn_conv_w", (3, 2 * DFF)), ("ffn_conv_b", (1, 2 * DFF)), ("ffn_w_down", (DFF, D)),
]


def build_program(NL=NL_FULL, dbg=None):
    dbg = dbg or {}
    NLW = dbg.get("nlw", NL_FULL)
    nc = bass.Bass("TRN2", target_bir_lowering=False)
    dram = {}

    def dtens(name, shape, dt=F32, kind="Internal"):
        return nc.dram_tensor(name, list(shape), dt, kind=kind).ap()

    def kind_of(tag, default="Internal"):
        if tag + "_in" in dbg:
            return "ExternalInput"
        if tag in dbg:
            return "ExternalOutput"
        return default

    dram["xall"] = dtens("xall", [TALL, D], kind="ExternalInput")
    dram["c2"] = dtens("c2", [2, D], kind="ExternalInput")
    dram["ident"] = dtens("ident", [128, 128], kind="ExternalInput")
    class LazyDram(dict):
        def __missing__(self, name):
            for nm, shp in W_SPECS:
                if nm == name:
                    self[name] = dtens(name, [NLW] + list(shp), kind="ExternalInput")
                    return self[name]
            for nm, shp, per_layer in MIX_SPECS:
                if nm == name:
                    self[name] = dtens(name, ([NLW] if per_layer else []) + list(shp), kind="ExternalInput")
                    return self[name]
            if name in SCR_SPECS:
                self[name] = dtens(name + "_scr", [TALL, SCR_SPECS[name]], kind=kind_of(name))
                return self[name]
            raise KeyError(name)

        def res(self, name):
            if not hasattr(self, "_res"):
                self._res = {}
            if name not in self._res:
                self._res[name] = [Res("%s%d" % (name, i)) for i in range(NT)]
            return self._res[name]
    dram = LazyDram(dram)
    out_ap = dtens("out", [TLAT, D], kind="ExternalOutput")
    z_d = dtens("z_scr", [TALL, DIN], kind=kind_of("z"))
    y_d = dtens("y_scr", [TALL, D], kind=kind_of("y"))
    xs_d = dtens("xs_scr", [TALL, D], kind=kind_of("xs"))
    mods_d = dtens("mods_scr", [NL_FULL, 2, 6 * D], kind=kind_of("mods"))
    h2T_d = dtens("h2T_scr", [128, 8, TALL], BF16)
    z_res = [Res("z%d" % i) for i in range(NT)]
    y_res = [Res("y%d" % i) for i in range(NT)]
    xs_res = [Res("x%d" % i) for i in range(NT)]
    h2_res = [Res("h2_%d" % i) for i in range(NT)]
    mods_res = [Res("mods%d" % i) for i in range(NL_FULL)]

    with contextlib.ExitStack() as st:
        P = Prog(nc, st)
        ident = P.sb("ident", [128, 128])
        P.dma("sp", ident[:], dram["ident"], (), [ident])
        psum = [P.ps("ps%d" % i) for i in range(8)]
        P.consts = {}
        for cv in (EPS, 1e-12, 64e-5):
            ct = P.sb("const%d" % len(P.consts), [128, 1])
            P.memset("pool", ct[:], cv, [ct])
            P.consts[cv] = ct
        c2T = P.sb("c2T", [128, 8, 2])
        for v in range(2):
            P.dma("sp", c2T[:, :, v], dram["c2"][v].rearrange("(kt p) -> p kt", p=128), (), [c2T],
                  allow_slow_non_contiguous=True)
        P.act("act", c2T[:], c2T[:], AF.Silu, [c2T], [c2T])

        def bcast_load(q, tile, row_ap, reads=()):
            P.dma(q, tile[:], row_ap.partition_broadcast(128), reads, [tile])

        def rms_tile(P, src, st_t, junk, A, Bv, dst, ncols=D):
            P.act("act", junk[:, 0:ncols], src[:, 0:ncols], AF.Square, [src], [junk], accum_out=st_t[:, 0:1])
            P.rstd(st_t, 1.0 / ncols, EPS, [junk])
            P.stt("dve", dst[:, 0:ncols], src[:, 0:ncols], st_t[:, 2:3], A[:, 0:ncols], ALU.mult, ALU.mult, [src, st_t, A], [dst])
            if Bv is not None:
                P.tt("pool", dst[:, 0:ncols], dst[:, 0:ncols], Bv[:, 0:ncols], ALU.add, [dst, Bv], [dst])

        def transpose8(P, src, dstT, banks):
            for half in range(2):
                pst = banks[half]
                for k4 in range(4):
                    kt = half * 4 + k4
                    P.tr(pst[:, k4 * 128:(k4 + 1) * 128], src[:, kt * 128:(kt + 1) * 128], ident[:], [src, ident], [pst])
                P.cp("act", dstT[:, half * 4:(half + 1) * 4, :], pst[:, :].rearrange("p (k t) -> p k t", k=4), [pst], [dstT])

        def load_cast(P, dst_ap, dst_t, src_ap, wst, idx, ncols):
            w = wst[idx % 2]
            P.dma("sp" if idx % 2 == 0 else "pool", w[:, 0:ncols], src_ap, (), [w])
            P.cp("act" if idx % 2 == 0 else "dve", dst_ap, w[:, 0:ncols], [w], [dst_t])

        for L in range(NL):
            need_ctx = L < NL_FULL - 1
            last = (L == NL - 1)
            mrow = lambda v, k: mods_d[L, v, k * D:(k + 1) * D]
            if "skip_p0" not in dbg:
              with P.phase():
                wst = [P.sb("wst%d" % i, [128, 2048]) for i in range(4)]
                brow = P.sb("brow", [2, 6 * D])
                mrow_sb = [P.sb("mrow%d" % i, [2, 2048]) for i in range(2)]
                P.dma("sp", brow[0:1, :], dram["b_mod"][L], (), [brow])
                P.dma("sp", brow[1:2, :], dram["b_mod"][L], (), [brow])
                widx = 0
                for nb in range(3):
                    banks = psum[4 * (nb % 2):4 * (nb % 2) + 4]
                    for kt in range(8):
                        w = wst[widx % 4]
                        P.dma("sp" if widx % 2 == 0 else "pool", w[:, :],
                              dram["w_mod"][L, kt * 128:(kt + 1) * 128, nb * 2048:(nb + 1) * 2048], (), [w])
                        widx += 1
                        for q in range(4):
                            P.mm(banks[q][0:2, :], c2T[:, kt, :], w[:, q * 512:(q + 1) * 512], kt == 0, kt == 7, [c2T, w], [banks[q]])
                    ms = mrow_sb[nb % 2]
                    for q in range(4):
                        c0 = nb * 2048 + q * 512
                        P.tt("dve", ms[:, q * 512:(q + 1) * 512], banks[q][0:2, :], brow[:, c0:c0 + 512], ALU.add, [banks[q], brow], [ms])
                    P.dma("sp", mods_d[L, :, nb * 2048:(nb + 1) * 2048], ms[:, :], [ms], [mods_res[L]])

            if "skip_p1" not in dbg:
              with P.phase():
                x_src = dram["xall"] if L == 0 else xs_d
                wst = [P.sb("wst%d" % i, [128, 2048]) for i in range(2)]
                gB = P.sb("gB", [128, D])
                A1 = P.sb("A1", [128, D])
                B1 = P.sb("B1", [128, D])
                win_sb = P.sb("win", [128, 8, DIN], BF16)
                xt = [P.sb("xt%d" % i, [128, D]) for i in range(2)]
                xn = [P.sb("xn%d" % i, [128, D]) for i in range(2)]
                hT = [P.sb("hT%d" % i, [128, 8, 128], BF16) for i in range(2)]
                zt = [P.sb("zt%d" % i, [128, DIN]) for i in range(2)]
                st1 = [P.sb("st1_%d" % i, [128, 4]) for i in range(2)]
                junk = P.sb("junk", [128, D])
                bcast_load("sp", gB, dram["g_mix_pre"][L])
                win_res = [Res("win%d" % c) for c in range(6)]
                idx = 0
                for c in range(0, 6, 2):
                    c0 = c * 512
                    cw = min(1024, DIN - c0)
                    for kt in range(8):
                        w = wst[idx % 2]
                        P.dma("sp" if idx % 2 == 0 else "pool", w[:, 0:cw], dram["w_in"][L, kt * 128:(kt + 1) * 128, c0:c0 + cw], (), [w])
                        P.cp("act" if idx % 2 == 0 else "dve", win_sb[:, kt, c0:c0 + cw], w[:, 0:cw], [w], [win_res[c], win_res[c + 1]])
                        idx += 1
                for i in range(NT):
                    v = 1 if i < 2 else 0
                    if i == 0 or i == 2:
                        bcast_load("sp", A1, mrow(v, 1), [mods_res[L]])
                        P.stt("dve", A1[:], A1[:], 1.0, gB[:], ALU.add, ALU.mult, [A1, gB], [A1])
                        bcast_load("sp", B1, mrow(v, 0), [mods_res[L]])
                    b = i % 2
                    P.dma("sp", xt[b][:], x_src[i * 128:(i + 1) * 128, :], [xs_res[i]], [xt[b]])
                    rms_tile(P, xt[b], st1[b], junk, A1, B1, xn[b])
                    transpose8(P, xn[b], hT[b], psum[0:2])
                    for cb in range(6):
                        c0 = cb * 512
                        cw = min(512, DIN - c0)
                        pst = psum[2 + cb]
                        for kt in range(8):
                            P.mm(pst[:, 0:cw], hT[b][:, kt, :], win_sb[:, kt, c0:c0 + cw], kt == 0, kt == 7, [hT[b], win_res[cb]], [pst])
                        P.cp("dve" if cb % 2 == 0 else "act", zt[b][:, c0:c0 + cw], pst[:, 0:cw], [pst], [zt[b]])
                    P.dma("pool", z_d[i * 128:(i + 1) * 128, :], zt[b][:], [zt[b]], [z_res[i]])

            if "skip_mix" not in dbg:
                emit_mixers(P, L, need_ctx, dram, z_d, z_res, y_d, y_res, psum, ident, dbg)

            tiles5 = list(range(NT)) if need_ctx else list(range(2, NT))
            x_src = dram["xall"] if L == 0 else xs_d
            if "skip_p5" not in dbg:
              with P.phase():
                wst = [P.sb("wst%d" % i, [128, 1024]) for i in range(2)]
                wout_sb = P.sb("wout", [128, 8, D], BF16)
                gB = P.sb("gB", [128, D])
                G1 = P.sb("G1", [128, D])
                A2 = P.sb("A2", [128, D])
                B2 = P.sb("B2", [128, D])
                yt = [P.sb("yt%d" % i, [128, D]) for i in range(2)]
                xt = [P.sb("xt%d" % i, [128, D]) for i in range(2)]
                yT = [P.sb("yT%d" % i, [128, 8, 128], BF16) for i in range(2)]
                tmp = [P.sb("tmp%d" % i, [128, D]) for i in range(2)]
                xw = [P.sb("xw%d" % i, [128, D]) for i in range(2)]
                h2 = [P.sb("h2%d" % i, [128, D]) for i in range(2)]
                h2T = [P.sb("h2T%d" % i, [128, 8, 128], BF16) for i in range(2)]
                st5 = [P.sb("st5_%d" % i, [128, 8]) for i in range(2)]
                junk = P.sb("junk", [128, D])
                for kt in range(8):
                    load_cast(P, wout_sb[:, kt, :], wout_sb, dram["w_out"][L, kt * 128:(kt + 1) * 128, :], wst, kt, D)
                def p5_a(i):
                    v = 1 if i < 2 else 0
                    if i == tiles5[0] or i == 2:
                        bcast_load("sp", gB, dram["g_mix_post"][L])
                        bcast_load("sp", G1, mrow(v, 2), [mods_res[L]])
                        P.tt("dve", G1[:], G1[:], gB[:], ALU.mult, [G1, gB], [G1])
                        bcast_load("sp", gB, dram["g_ffn_pre"][L])
                        bcast_load("sp", A2, mrow(v, 4), [mods_res[L]])
                        P.stt("dve", A2[:], A2[:], 1.0, gB[:], ALU.add, ALU.mult, [A2, gB], [A2])
                        bcast_load("sp", B2, mrow(v, 3), [mods_res[L]])
                    b = i % 2
                    P.dma("sp", yt[b][:], y_d[i * 128:(i + 1) * 128, :], [y_res[i]], [yt[b]])
                    P.dma("pool", xt[b][:], x_src[i * 128:(i + 1) * 128, :], [xs_res[i]], [xt[b]])
                    transpose8(P, yt[b], yT[b], psum[0:2])
                    pb = [psum[2 + 2 * b], psum[3 + 2 * b]]
                    for half in range(2):
                        for kt in range(8):
                            P.mm(pb[half][:, :], yT[b][:, kt, :], wout_sb[:, kt, half * 512:(half + 1) * 512], kt == 0, kt == 7,
                                 [yT[b], wout_sb], [pb[half]])
                        P.act("act", junk[:, 0:512], pb[half][:, :], AF.Square, [pb[half]], [junk], accum_out=st5[b][:, 3 + half:4 + half])
                    P.tt("dve", st5[b][:, 0:1], st5[b][:, 3:4], st5[b][:, 4:5], ALU.add, [st5[b], junk], [st5[b]])
                    P.rstd(st5[b], 1.0 / D, EPS)
                    for half in range(2):
                        hs = slice(half * 512, (half + 1) * 512)
                        P.stt("dve", tmp[b][:, hs], pb[half][:, :], st5[b][:, 2:3], G1[:, hs], ALU.mult, ALU.mult,
                              [pb[half], st5[b], G1], [tmp[b]])
                    P.tt("pool", xw[b][:], tmp[b][:], xt[b][:], ALU.add, [tmp[b], xt[b]], [xw[b]])
                    P.dma("pool", xs_d[i * 128:(i + 1) * 128, :], xw[b][:], [xw[b]], [xs_res[i]])
                    rms_tile(P, xw[b], st5[b], junk, A2, B2, h2[b])

                def p5_b(i):
                    b = i % 2
                    transpose8(P, h2[b], h2T[b], psum[6:8])
                    P.dma("sp", h2T_d[:, :, i * 128:(i + 1) * 128], h2T[b][:], [h2T[b]], [h2_res[i]])

                for k in range(len(tiles5) + 1):
                    if k < len(tiles5):
                        p5_a(tiles5[k])
                    if k >= 1:
                        p5_b(tiles5[k - 1])

            if "skip_p6" not in dbg:
              with P.phase():
                wst = [P.sb("wst%d" % i, [128, 1024]) for i in range(2)]
                wup_sb = P.sb("wup", [128, 8, 2 * DFF], BF16)
                wdn_sb = P.sb("wdn", [128, 22, D], BF16)
                cwt = P.sb("cwt", [128, 3, 44])
                cbt = P.sb("cbt", [128, 44])
                gB = P.sb("gB", [128, D])
                G2s = [P.sb("G2_%d" % i, [128, D]) for i in range(2)]
                pending_down = []
                h2blk = [P.sb("h2blk%d" % i, [128, 8, 258], BF16) for i in range(2)]
                cv = [[P.sb("cv%d_%d" % (i, j), [128, 256]) for j in range(2)] for i in range(2)]
                aTs = [P.sb("aT%d" % i, [128, 22, 256], BF16) for i in range(2)]
                xt = [P.sb("xt%d" % i, [128, D]) for i in range(2)]
                tmp = [P.sb("tmp%d" % i, [128, D]) for i in range(2)]
                st6 = [P.sb("st6_%d" % i, [128, 8]) for i in range(2)]
                junk = P.sb("junk", [128, 512])
                blocks = list(range(17)) if need_ctx else list(range(1, 17))

                def load_hb(bi):
                    t0 = bi * 256
                    hb = h2blk[bi % 2]
                    lval = t0 not in (0, TCTX)
                    rval = (t0 + 256) not in (TCTX, TALL)
                    if not lval:
                        P.memset("pool", hb[:, :, 0:1], 0.0, [hb])
                    if not rval:
                        P.memset("pool", hb[:, :, 257:258], 0.0, [hb])
                    a = t0 - 1 if lval else t0
                    e = t0 + 257 if rval else t0 + 256
                    rtiles = sorted(set([a // 128, (e - 1) // 128, t0 // 128, t0 // 128 + 1]))
                    P.dma("sp", hb[:, :, a - (t0 - 1):e - (t0 - 1)], h2T_d[:, :, a:e], [h2_res[r] for r in rtiles], [hb])

                load_hb(blocks[0])
                wup_res = [Res("wup%d" % c) for c in range(6)]
                wdn_res = [Res("wdn%d" % j) for j in range(22)]
                idx = 0
                for c in (0, 2, 3, 1, 4, 5):
                    c0 = c * 1024
                    cw = min(1024, 2 * DFF - c0)
                    for kt in range(8):
                        load_cast(P, wup_sb[:, kt, c0:c0 + cw], wup_res[c], dram["ffn_w_up"][L, kt * 128:(kt + 1) * 128, c0:c0 + cw], wst, idx, cw)
                        idx += 1
                for j in range(22):
                    load_cast(P, wdn_sb[:, j, :], wdn_res[j], dram["ffn_w_down"][L, j * 128:(j + 1) * 128, :], wst, idx, D)
                    idx += 1
                for tap in range(3):
                    P.dma("sp", cwt[:, tap, :], dram["ffn_conv_w"][L, tap].rearrange("(j p) -> p j", p=128), (), [cwt],
                          allow_slow_non_contiguous=True)
                P.dma("sp", cbt[:, :], dram["ffn_conv_b"][L, 0].rearrange("(j p) -> p j", p=128), (), [cbt],
                      allow_slow_non_contiguous=True)
                blocks = list(range(17)) if need_ctx else list(range(1, 17))
                ucount = 0
                for bi in blocks:
                    v = 1 if bi == 0 else 0
                    if bi == blocks[0] or bi == 1:
                        bcast_load("sp", gB, dram["g_ffn_post"][L])
                        bcast_load("sp", G2s[v], mrow(v, 5), [mods_res[L]])
                        P.tt("dve", G2s[v][:], G2s[v][:], gB[:], ALU.mult, [G2s[v], gB], [G2s[v]])
                    t0 = bi * 256
                    aT = aTs[bi % 2]
                    hb = h2blk[bi % 2]
                    nxt = blocks.index(bi) + 1
                    if nxt < len(blocks):
                        load_hb(blocks[nxt])
                    for j in range(22):
                        cs = cv[j % 2]
                        for part in range(2):
                            fi = part * 22 + j
                            f0 = part * DFF + j * 128
                            pst = psum[ucount % 4]
                            ucount += 1
                            for kt in range(8):
                                P.mm(pst[:, 0:258], wup_sb[:, kt, f0:f0 + 128], hb[:, kt, :], kt == 0, kt == 7,
                                     [wup_res[f0 // 1024], wup_res[(f0 + 127) // 1024], hb], [pst])
                            c = cs[part]
                            P.act("act", c[:], pst[:, 1:257], AF.Identity, [pst, cwt, cbt], [c], bias=cbt[:, fi:fi + 1], scale=cwt[:, 1, fi:fi + 1])
                            P.stt("dve", c[:], pst[:, 0:256], cwt[:, 0, fi:fi + 1], c[:], ALU.mult, ALU.add, [pst, cwt, c], [c])
                            P.stt("dve", c[:], pst[:, 2:258], cwt[:, 2, fi:fi + 1], c[:], ALU.mult, ALU.add, [pst, cwt, c], [c])
                        P.act("act", cs[1][:], cs[1][:], AF.Silu, [cs[1]], [cs[1]])
                        P.tt("pool", aT[:, j, :], cs[0][:], cs[1][:], ALU.mult, [cs[0], cs[1]], [aT])
                        for _ in range(5):
                            if pending_down:
                                pending_down.pop(0)()
                    def make_down(bi, aT, v):
                        items = []
                        for s in range(2):
                            i = bi * 2 + s
                            b = s
                            pb = [psum[4 + 2 * s], psum[5 + 2 * s]]
                            items.append(lambda i=i, b=b: P.dma("pool", xt[b][:], xs_d[i * 128:(i + 1) * 128, :], [xs_res[i]], [xt[b]]))
                            for half in range(2):
                                for j in range(22):
                                    items.append(lambda s=s, j=j, half=half, pb=pb: P.mm(
                                        pb[half][:, :], aT[:, j, s * 128:(s + 1) * 128], wdn_sb[:, j, half * 512:(half + 1) * 512],
                                        j == 0, j == 21, [aT, wdn_res[j]], [pb[half]]))
                                items.append(lambda b=b, half=half, pb=pb: P.act(
                                    "act", junk[:, 0:512], pb[half][:, :], AF.Square, [pb[half]], [junk], accum_out=st6[b][:, 3 + half:4 + half]))

                            def epi(i=i, b=b, pb=pb):
                                P.tt("dve", st6[b][:, 0:1], st6[b][:, 3:4], st6[b][:, 4:5], ALU.add, [st6[b], junk], [st6[b]])
                                P.rstd(st6[b], 1.0 / D, EPS)
                                for half in range(2):
                                    hs = slice(half * 512, (half + 1) * 512)
                                    P.stt("dve", tmp[b][:, hs], pb[half][:, :], st6[b][:, 2:3], G2s[v][:, hs], ALU.mult, ALU.mult,
                                          [pb[half], st6[b], G2s[v]], [tmp[b]])
                                P.tt("pool", tmp[b][:], tmp[b][:], xt[b][:], ALU.add, [tmp[b], xt[b]], [tmp[b]])
                                if last and i >= 2:
                                    P.dma("sp", out_ap[(i - 2) * 128:(i - 1) * 128, :], tmp[b][:], [tmp[b]], ())
                                else:
                                    P.dma("sp", xs_d[i * 128:(i + 1) * 128, :], tmp[b][:], [tmp[b]], [xs_res[i]])
                            items.append(epi)
                        return items
                    pending_down.extend(make_down(bi, aT, v))
                while pending_down:
                    pending_down.pop(0)()

        P.flush(final=True)
    return nc


def host_inputs(inputs, b):
    m = {}
    m["xall"] = np.ascontiguousarray(np.concatenate([inputs["ctx"][b], inputs["x"][b]], axis=0), dtype=np.float32)
    m["c2"] = np.ascontiguousarray(np.stack([inputs["c"][b], inputs["c_ctx"]], axis=0), dtype=np.float32)
    m["ident"] = np.eye(128, dtype=np.float32)
    for name, shp in W_SPECS:
        m[name] = np.ascontiguousarray(np.asarray(inputs[name], dtype=np.float32).reshape([NL_FULL] + list(shp)))
    mix_host_inputs(inputs, m)
    return m


def used_inputs(nc, m):
    shapes = {}
    for alloc in nc.allocations:
        try:
            if alloc.kind == "ExternalInput":
                shapes[alloc.memorylocations[0].name] = tuple(alloc.tensor_shape)
        except Exception:
            pass
    out = {}
    for k, v in m.items():
        if k in shapes:
            shp = shapes[k]
            if tuple(v.shape) != shp:
                v = np.ascontiguousarray(v[0:shp[0]])
            assert tuple(v.shape) == shp, (k, v.shape, shp)
            out[k] = v
    return out


def kernel(**inputs):
    inputs = {k: np.asarray(v) for k, v in inputs.items()}
    nc = build_program()
    shared = host_inputs(inputs, 0)
    maps = []
    for b in range(4):
        m = dict(shared)
        m["xall"] = np.ascontiguousarray(np.concatenate([inputs["ctx"][b], inputs["x"][b]], axis=0), dtype=np.float32)
        m["c2"] = np.ascontiguousarray(np.stack([inputs["c"][b], inputs["c_ctx"]], axis=0), dtype=np.float32)
        maps.append(used_inputs(nc, m))
    in_maps = [maps[b % 4] for b in range(N_CORES)]
    res = run_bass_kernel_spmd(nc, in_maps, core_ids=list(range(N_CORES)))
    out = np.stack([np.asarray(res.results[b]["out"]) for b in range(4)], axis=0)
    return out.astype(np.float32)
```

```python
import contextlib
import numpy as np
import concourse.bass as bass
import concourse.mybir as mybir
from concourse.bass_utils import run_bass_kernel_spmd

F32 = mybir.dt.float32
BF16 = mybir.dt.bfloat16
ALU = mybir.AluOpType
AF = mybir.ActivationFunctionType
AX = mybir.AxisListType

D = 1024
NL_FULL = 4
TCTX = 256
TLAT = 4096
TALL = TCTX + TLAT
NT = TALL // 128
DIN = 2928
DFF = 2816
EPS = 1e-6


class Res:
    __slots__ = ("name", "w", "r")

    def __init__(self, name=""):
        self.name = name
        self.w = None
        self.r = {}


class T:
    def __init__(self, handle, name):
        self.h = handle
        self.res = Res(name)

    def __getitem__(self, k):
        return self.h[k]


class Prog:
    EPOCH = 30000
    KD = 6

    def __init__(self, nc, st):
        self.nc = nc
        self.st = st
        self.engs = ["pe", "act", "dve", "pool", "sp"]
        self.ops = {e: [] for e in self.engs}
        self.cnt = {e: 0 for e in self.engs}
        self.seen = {e: {} for e in self.engs}
        self.dcnt = {e: 0 for e in self.engs}
        self.sems = {}
        self.n_ops = 0
        self.gst = st
        self.last_ev = {}
        self.pending = {e: [] for e in self.engs}

    def sb(self, name, shape, dt=F32):
        self.n_ops += 1
        h = self.st.enter_context(self.nc.sbuf_tensor("sb%d_%s" % (self.n_ops, name), list(shape), dt))
        return T(h, name)

    def ps(self, name, shape=(128, 512), dt=F32):
        h = self.st.enter_context(self.nc.psum_tensor("ps_" + name, list(shape), dt))
        return T(h, name)

    def _sem(self, key):
        if key not in self.sems:
            self.sems[key] = self.gst.enter_context(self.nc.semaphore("s_" + "_".join(str(k) for k in key)))
        self.last_ev[key] = max(self.last_ev.get(key, 0), 0)
        return self.sems[key]

    def barrier(self):
        evs = [(k, v) for k, v in self.last_ev.items() if v > 0]
        for e in self.engs:
            self.pending[e] = list(evs)

    def barrier_light(self, ress):
        evs = [r.w for r in ress if r.w is not None]
        for e in self.engs:
            self.pending[e] = self.pending[e] + list(evs)

    @contextlib.contextmanager
    def phase(self):
        outer = self.st
        with contextlib.ExitStack() as ph:
            self.st = ph
            yield
            self.barrier()
            self.flush()
        self.st = outer

    def _deps(self, eng, reads, writes, is_dma):
        evs = []
        for ev in self.pending[eng]:
            evs.append((ev, "bar"))
        self.pending[eng] = []
        for t in reads:
            r = t.res if isinstance(t, T) else t
            if r.w is not None:
                evs.append((r.w, "raw"))
        for t in writes:
            r = t.res if isinstance(t, T) else t
            if r.w is not None:
                evs.append((r.w, "waw"))
            for ev in r.r.values():
                evs.append((ev, "war"))
        waits = {}
        for (key, val), kind in evs:
            if key[0] == "e" and key[1] == eng and not is_dma:
                if kind != "raw" or eng == "pe":
                    continue
            if self.seen[eng].get(key, 0) >= val:
                continue
            if waits.get(key, 0) < val:
                waits[key] = val
        for k, v in waits.items():
            self.seen[eng][k] = v
        return list(waits.items())

    def _mark(self, ev, reads, writes):
        for t in reads:
            r = t.res if isinstance(t, T) else t
            r.r[ev[0]] = ev
        for t in writes:
            r = t.res if isinstance(t, T) else t
            r.w = ev
            r.r = {}

    def op(self, eng, fn, reads=(), writes=()):
        waits = self._deps(eng, reads, writes, False)
        n = self.cnt[eng]
        key = ("e", eng, n // self.EPOCH)
        ev = (key, n % self.EPOCH + 1)
        self.cnt[eng] = n + 1
        self._sem(key)
        self.last_ev[key] = ev[1]
        self.ops[eng].append((waits, fn, key, 1))
        self._mark(ev, reads, writes)
        self.n_ops += 1

    def dma(self, q, out, in_, reads=(), writes=(), **kw):
        waits = self._deps(q, reads, writes, True)
        j = self.dcnt[q]
        self.dcnt[q] = j + 1
        key = ("d", q, j % self.KD)
        val = 16 * (j // self.KD + 1)
        if j >= self.KD and self.seen[q].get(key, 0) < val - 16:
            waits.append((key, val - 16))
            self.seen[q][key] = val - 16
        self._sem(key)
        self.last_ev[key] = val
        self.ops[q].append((waits, lambda e: e.dma_start(out=out, in_=in_, **kw), key, 16))
        self._mark((key, val), reads, writes)
        self.n_ops += 1

    def mm(self, out, lhsT, rhs, start, stop, reads, writes):
        self.op("pe", lambda e: e.matmul(out, lhsT, rhs, start=start, stop=stop), reads, writes)

    def tr(self, out, in_, ident, reads, writes):
        self.op("pe", lambda e: e.transpose(out, in_, ident), reads, writes)

    def act(self, eng, out, in_, func, reads, writes, bias=None, scale=None, accum_out=None):
        kw = {}
        if bias is not None:
            kw["bias"] = bias
        if scale is not None:
            kw["scale"] = scale
        if accum_out is not None:
            kw["accum_out"] = accum_out
        self.op(eng, lambda e: e.activation(out, in_, func, **kw), reads, writes)

    def tt(self, eng, out, in0, in1, op, reads, writes):
        self.op(eng, lambda e: e.tensor_tensor(out, in0, in1, op), reads, writes)

    def ts(self, eng, out, in0, s1, s2, op0, op1, reads, writes, accum_out=None):
        if op1 is None:
            self.op(eng, lambda e: e.tensor_scalar(out, in0, s1, None, op0), reads, writes)
        elif accum_out is None:
            self.op(eng, lambda e: e.tensor_scalar(out, in0, s1, s2, op0, op1), reads, writes)
        else:
            self.op(eng, lambda e: e.tensor_scalar(out, in0, s1, s2, op0, op1, accum_out), reads, writes)

    def stt(self, eng, out, in0, scalar, in1, op0, op1, reads, writes):
        self.op(eng, lambda e: e.scalar_tensor_tensor(out, in0, scalar, in1, op0, op1), reads, writes)

    def cp(self, eng, out, in_, reads, writes):
        if eng == "act":
            self.op(eng, lambda e: e.copy(out, in_), reads, writes)
        else:
            self.op(eng, lambda e: e.tensor_copy(out, in_), reads, writes)

    def red(self, eng, out, in_, op, reads, writes, axis=AX.X):
        self.op(eng, lambda e: e.tensor_reduce(out, in_, axis, op), reads, writes)

    def rstd(self, t, scale, eps, extra_reads=()):
        self.act("act", t[:, 1:2], t[:, 0:1], AF.Sqrt, [t, self.consts[eps]] + list(extra_reads), [t], bias=self.eps_ap(eps, t), scale=scale)
        self.op("dve", lambda e: e.reciprocal(t[:, 2:3], t[:, 1:2]), [t], [t])

    def eps_ap(self, eps, t):
        return self.consts[eps][0:t.h.shape[0], 0:1]

    def sumsq(self, dst_col, srcs, junk, reads):
        raise NotImplementedError

    def memset(self, eng, ap, val, writes):
        self.op(eng, lambda e: e.memset(ap, val), (), writes)

    def flush(self, final=False):
        nc = self.nc
        fin = [(k, v) for k, v in self.last_ev.items() if v > 0] if final else []
        sems = self.sems
        engmap = {"pe": "tensor", "act": "scalar", "dve": "vector", "pool": "gpsimd", "sp": "sync"}
        ops = self.ops
        with nc.Block() as block:
            def make(ename):
                def body(e):
                    for waits, fn, key, inc in ops[ename]:
                        for wk, wv in waits:
                            e.wait_ge(sems[wk], wv)
                        fn(e).then_inc(sems[key], inc)
                    if ename == "sp":
                        for wk, wv in fin:
                            e.wait_ge(sems[wk], wv)
                return body
            for ename in self.engs:
                getattr(block, engmap[ename])(make(ename))
        self.ops = {e: [] for e in self.engs}


NEG = -30000.0


def emit_na(P, L, need_ctx, dram, z_d, z_res, y_d, y_res, psum, ident):
    with P.phase():
        QT = P.sb("naQT", [128, 2, TALL], BF16)
        KT = P.sb("naKT", [128, 2, TALL], BF16)
        Ve = P.sb("naVe", [128, NT, 4, 66], BF16)
        Vo = P.sb("naVo", [128, NT - 1, 4, 66], BF16)
        btab = P.sb("btab", [128, 4, 14, 64])
        maskt = P.sb("mask", [128, 64])
        zt = [P.sb("zt%d" % i, [128, 768]) for i in range(2)]
        zo = [P.sb("zo%d" % i, [128, 256]) for i in range(2)]
        sw = [P.sb("sw%d" % i, [128, 4, 64]) for i in range(2)]
        pT = [P.sb("pT%d" % i, [128, 6, 64], BF16) for i in range(2)]
        rc = [P.sb("rc%d" % i, [64, 4]) for i in range(2)]
        yo = [P.sb("yo%d" % i, [64, 4, 64]) for i in range(2)]
        import os
        stage = float(os.environ.get("NA_STAGE", "9"))
        if stage < 0.5:
            return
        P.dma("sp", btab[:], dram["na_btab"][L], (), [btab])
        P.dma("sp", maskt[:], dram["na_mask"], (), [maskt])
        P.tt("dve", btab[:].rearrange("p h d q -> p (h d) q"), btab[:].rearrange("p h d q -> p (h d) q"),
             maskt[:, :].unsqueeze(1).to_broadcast([128, 56, 64]), ALU.add, [btab, maskt], [btab])
        if stage < 0.7:
            return
        P.memset("pool", Ve[:, :, :, 64:65], 1.0, [Ve])
        P.memset("pool", Vo[:, :, :, 64:65], 1.0, [Vo])
        if stage < 0.9:
            return
        for i in range(NT):
            b = i % 2
            P.dma("sp", zt[b][:], z_d[i * 128:(i + 1) * 128, 0:768], [z_res[i]], [zt[b]])
            pst = psum[i % 2]
            for c in range(4):
                P.tr(pst[:, c * 128:(c + 1) * 128], zt[b][:, c * 128:(c + 1) * 128], ident[:], [zt[b], ident], [pst])
            pv = pst[:, :].rearrange("p (c t) -> p c t", c=4)
            if stage > 0.915:
                P.cp("act", QT[:, :, i * 128:(i + 1) * 128], pv[:, 0:2, :], [pst], [QT])
            if stage > 0.925:
                P.cp("act", KT[:, :, i * 128:(i + 1) * 128], pv[:, 2:4, :], [pst], [KT])
            if stage > 0.935:
                P.cp("dve", Ve[:, i, :, 0:64], zt[b][:, 512:768].rearrange("p (h d) -> p h d", h=4), [zt[b]], [Ve])
            if i < NT - 1 and stage > 0.945:
                P.dma("pool", zo[b][:], z_d[64 + i * 128:64 + (i + 1) * 128, 512:768], [z_res[i], z_res[i + 1]], [zo[b]])
                if stage > 0.955:
                    P.cp("dve", Vo[:, i, :, 0:64], zo[b][:].rearrange("p (h d) -> p h d", h=4), [zo[b]], [Vo])

        units = []

        def na_rows(q0, keyspec, bias_di0, out_row0):
            units.append((q0, keyspec, bias_di0, out_row0))

        def part_a(u, h):
            q0, keyspec, bias_di0, out_row0 = units[u]
            g, hp = h // 2, (h % 2) * 64
            pst = psum[(u * 4 + h) % 4]
            for c, (k0, Vt, ti) in enumerate(keyspec):
                P.mm(pst[:, c * 64:(c + 1) * 64], KT[hp:hp + 64, g, k0:k0 + 128], QT[hp:hp + 64, g, q0:q0 + 64], True, True,
                     [KT, QT], [pst])

        def part_b(u, h):
            q0, keyspec, bias_di0, out_row0 = units[u]
            nk = len(keyspec)
            pst = psum[(u * 4 + h) % 4]
            po = psum[4 + u % 2]
            p = pT[(u * 4 + h) % 2]
            c0 = 0
            if bias_di0 is not None:
                s_ = sw[(u * 4 + h) % 2]
                P.stt("dve", s_[:], pst[:, 0:256].rearrange("p (c q) -> p c q", c=4), 0.125,
                      btab[:, h, bias_di0:bias_di0 + 7:2, :], ALU.mult, ALU.add, [pst, btab], [s_])
                P.act("act", p[:, 0:4, :], s_[:], AF.Exp, [s_], [p])
                c0 = 4
            P.act("act", p[:, c0:nk, :], pst[:, c0 * 64:nk * 64].rearrange("p (c q) -> p c q", c=nk - c0), AF.Exp, [pst], [p],
                  scale=0.125)
            for c, (k0, Vt, ti) in enumerate(keyspec):
                P.mm(po[0:64, h * 65:(h + 1) * 65], p[:, c, :], Vt[:, ti, h, 0:65], c == 0, c == nk - 1, [p, Vt], [po])
            if h == 3:
                r_ = rc[u % 2]
                y_ = yo[u % 2]
                pov = po[0:64, 0:260].rearrange("p (h e) -> p h e", h=4)
                P.op("dve", lambda e: e.reciprocal(r_[:, :], pov[:, :, 64]), [po], [r_])
                P.tt("dve", y_[:], pov[:, :, 0:64], r_[:, :].unsqueeze(2).to_broadcast([64, 4, 64]), ALU.mult, [po, r_], [y_])
                P.dma("sp", y_d[out_row0:out_row0 + 64, 0:256], y_[:].rearrange("p h d -> p (h d)"), [y_], [y_res[out_row0 // 128]])

        def run_units():
            seq = [(u, h) for u in range(len(units)) for h in range(4)]
            LOOK = 2
            for k in range(len(seq) + LOOK):
                if k < len(seq):
                    part_a(*seq[k])
                if k >= LOOK:
                    part_b(*seq[k - LOOK])

        ctxkeys = [(0, Ve, 0), (128, Ve, 1)]
        if stage < 2:
            return
        if need_ctx:
            for qc in range(4):
                na_rows(qc * 64, ctxkeys, None, qc * 64)
        for r in range(64 if stage >= 3 else 0):
            w0 = min(max(r - 4, 0), 56)
            key0 = TCTX + w0 * 64
            ks = []
            for c in range(4):
                k0 = key0 + c * 128
                if k0 % 128 == 0:
                    ks.append((k0, Ve, k0 // 128))
                else:
                    ks.append((k0, Vo, (k0 - 64) // 128))
            na_rows(TCTX + r * 64, ks + ctxkeys, w0 - r + 7, TCTX + r * 64)
        run_units()


def emit_mla(P, L, need_ctx, dram, z_d, z_res, y_d, y_res, psum, ident):
    SC = 96.0 ** -0.5
    with P.phase():
        nT0 = P.sb("nT0", [128, TALL], BF16)
        nT1 = P.sb("nT1", [64, TALL], BF16)
        nkT = P.sb("nkT", [128, TALL], BF16)
        krT = P.sb("krT", [96, TALL], BF16)
        KTm = P.sb("KTm", [96, 4, TALL], BF16)
        Vm = P.sb("Vm", [128, NT, 4, 66], BF16)
        wst = P.sb("wst", [128, 768])
        wq = P.sb("wq", [128, 2, 4, 96], BF16)
        wqp = P.sb("wqp", [128, 2, 4, 96], BF16)
        wk = P.sb("wk", [128, 4, 96], BF16)
        wv = P.sb("wv", [128, 256], BF16)
        Pm = P.sb("Pm", [96, 96], BF16)
        qnb = P.sb("qnb", [128, 192])
        kvnb = P.sb("kvnb", [128, 128])
        zt = [P.sb("zt%d" % i, [128, 352]) for i in range(2)]
        nq = [P.sb("nq%d" % i, [128, 416]) for i in range(2)]
        stt_ = [P.sb("st%d" % i, [128, 8]) for i in range(2)]
        junk = P.sb("junk", [128, 192])
        ctab = [P.sb("ctab%d" % i, [96, 2, 512]) for i in range(2)]
        t1 = [P.sb("t1_%d" % i, [96, 512]) for i in range(2)]
        t2 = [P.sb("t2_%d" % i, [96, 512]) for i in range(2)]
        QTb = [P.sb("QTb%d" % i, [96, 512], BF16) for i in range(2)]
        pT = [P.sb("pT%d" % i, [128, 512], BF16) for i in range(3)]
        rc = [P.sb("rc%d" % i, [128, 4]) for i in range(2)]
        yo = [P.sb("yo%d" % i, [128, 4, 4, 64]) for i in range(2)]
        oT = [P.sb("oT%d" % i, [65, 512]) for i in range(2)]
        def ldw(dst_ap, dst_t, src, rows, cols):
            P.dma("sp", wst[0:rows, 0:cols], src, (), [wst])
            P.cp("act", dst_ap, wst[0:rows, 0:cols], [wst], [dst_t])
        ldw(wq[:, 0, :, :].rearrange("p h e -> p (h e)"), wq, dram["mla_wq_r"][L, 0:128, :], 128, 384)
        ldw(wq[0:64, 1, :, :].rearrange("p h e -> p (h e)"), wq, dram["mla_wq_r"][L, 128:192, :], 64, 384)
        ldw(wqp[:, 0, :, :].rearrange("p h e -> p (h e)"), wqp, dram["mla_wq_p"][L, 0:128, :], 128, 384)
        ldw(wqp[0:64, 1, :, :].rearrange("p h e -> p (h e)"), wqp, dram["mla_wq_p"][L, 128:192, :], 64, 384)
        ldw(wk[:, :, :].rearrange("p h e -> p (h e)"), wk, dram["mla_wk"][L], 128, 384)
        ldw(wv[:, :], wv, dram["mla_wv"][L], 128, 256)
        ldw(Pm[:, :], Pm, dram["mla_pm"], 96, 96)
        for n_ in nq:
            P.memset("pool", n_[:, 320:384], 0.0, [n_])
        P.dma("sp", qnb[:], dram["mla_q_norm"][L].partition_broadcast(128), (), [qnb])
        P.dma("sp", kvnb[:], dram["mla_kv_norm"][L].partition_broadcast(128), (), [kvnb])
        P.memset("pool", Vm[:, :, :, 64:65], 1.0, [Vm])
        for i in range(NT):
            b = i % 2
            z_, n_, s_ = zt[b], nq[b], stt_[b]
            P.dma("sp", z_[:], z_d[i * 128:(i + 1) * 128, 768:1120], [z_res[i]], [z_])
            P.act("act", junk[:, 0:192], z_[:, 0:192], AF.Square, [z_], [junk], accum_out=s_[:, 0:1])
            P.rstd(s_, 1.0 / 192, EPS, [junk])
            P.stt("dve", n_[:, 0:192], z_[:, 0:192], s_[:, 2:3], qnb[:], ALU.mult, ALU.mult, [z_, s_, qnb], [n_])
            P.act("act", junk[:, 0:128], z_[:, 192:320], AF.Square, [z_], [junk], accum_out=s_[:, 4:5])
            P.act("act", s_[:, 5:6], s_[:, 4:5], AF.Sqrt, [s_, junk, P.consts[EPS]], [s_], bias=P.consts[EPS][:, 0:1], scale=1.0 / 128)
            P.op("dve", lambda e, s_=s_: e.reciprocal(s_[:, 6:7], s_[:, 5:6]), [s_], [s_])
            P.stt("dve", n_[:, 192:320], z_[:, 192:320], s_[:, 6:7], kvnb[:], ALU.mult, ALU.mult, [z_, s_, kvnb], [n_])
            pst = psum[i % 2]
            P.tr(pst[:, 0:128], n_[:, 0:128], ident[:], [n_, ident], [pst])
            P.tr(pst[0:64, 128:256], n_[:, 128:192], ident[:], [n_, ident], [pst])
            P.tr(pst[:, 256:384], n_[:, 192:320], ident[:], [n_, ident], [pst])
            P.cp("pool", n_[:, 384:416], z_[:, 320:352], [z_], [n_])
            P.tr(pst[0:96, 384:512], n_[:, 320:416], ident[:], [n_, ident], [pst])
            ts_ = slice(i * 128, (i + 1) * 128)
            P.cp("act", nT0[:, ts_], pst[:, 0:128], [pst], [nT0])
            P.cp("act", nT1[:, ts_], pst[0:64, 128:256], [pst], [nT1])
            P.cp("act", nkT[:, ts_], pst[:, 256:384], [pst], [nkT])
            P.cp("act", krT[:, ts_], pst[0:96, 384:512], [pst], [krT])
        nblk = [(b0, min(512, TALL - b0)) for b0 in range(0, TALL, 512)]
        for bi, (b0, n) in enumerate(nblk):
            ct = ctab[bi % 2]
            P.dma("sp", ct[64:96, :, 0:n], dram["rope_tab"][:, :, b0:b0 + n].rearrange("c p t -> p c t"), (), [ct])
            for h in range(4):
                pst = psum[h % 2]
                P.mm(pst[0:96, 0:n], wk[:, h, :], nkT[:, b0:b0 + n], True, True, [wk, nkT], [pst])
                P.cp("act", KTm[0:64, h, b0:b0 + n], pst[0:64, 0:n], [pst], [KTm])
            pb = psum[2]
            P.mm(pb[0:96, 0:n], Pm[:, :], krT[:, b0:b0 + n], True, True, [Pm, krT], [pb])
            a1, a2 = t1[bi % 2], t2[bi % 2]
            P.tt("dve", a1[64:96, 0:n], krT[64:96, b0:b0 + n], ct[64:96, 0, 0:n], ALU.mult, [krT, ct], [a1])
            P.tt("dve", a2[64:96, 0:n], pb[64:96, 0:n], ct[64:96, 1, 0:n], ALU.mult, [pb, ct], [a2])
            P.tt("pool", KTm[64:96, :, b0:b0 + n], a1[64:96, 0:n].unsqueeze(1).to_broadcast([32, 4, n]),
                 a2[64:96, 0:n].unsqueeze(1).to_broadcast([32, 4, n]), ALU.add, [a1, a2], [KTm])
            for s in range(n // 128):
                ti = b0 // 128 + s
                pv = psum[3 + s % 2]
                P.mm(pv[:, 0:256], nkT[:, ti * 128:(ti + 1) * 128], wv[:, :], True, True, [nkT, wv], [pv])
                P.cp("act", Vm[:, ti, :, 0:64], pv[:, 0:256].rearrange("p (h d) -> p h d", h=4), [pv], [Vm])
        qblocks = []
        if need_ctx:
            qblocks.append((0, 256, [0, 1]))
        for b0 in range(TCTX, TALL, 512):
            qblocks.append((b0, 512, list(range(NT))))
        cnt = 0
        for qi, (b0, n, ktiles) in enumerate(qblocks):
            ct = ctab[qi % 2]
            P.dma("sp", ct[64:96, :, 0:n], dram["rope_tab"][:, :, b0:b0 + n].rearrange("c p t -> p c t"), (), [ct])
            y_ = yo[qi % 2]
            nsub = n // 128
            for h in range(4):
                pa, pb = psum[0], psum[1]
                P.mm(pa[0:96, 0:n], wq[:, 0, h, :], nT0[:, b0:b0 + n], True, False, [wq, nT0], [pa])
                P.mm(pa[0:96, 0:n], wq[0:64, 1, h, :], nT1[:, b0:b0 + n], False, True, [wq, nT1], [pa])
                P.mm(pb[0:96, 0:n], wqp[:, 0, h, :], nT0[:, b0:b0 + n], True, False, [wqp, nT0], [pb])
                P.mm(pb[0:96, 0:n], wqp[0:64, 1, h, :], nT1[:, b0:b0 + n], False, True, [wqp, nT1], [pb])
                q_ = QTb[(qi * 4 + h) % 2]
                a1, a2 = t1[h % 2], t2[h % 2]
                P.cp("act", q_[0:64, 0:n], pa[0:64, 0:n], [pa], [q_])
                P.tt("dve", a1[64:96, 0:n], pa[64:96, 0:n], ct[64:96, 0, 0:n], ALU.mult, [pa, ct], [a1])
                P.tt("dve", a2[64:96, 0:n], pb[64:96, 0:n], ct[64:96, 1, 0:n], ALU.mult, [pb, ct], [a2])
                P.tt("pool", q_[64:96, 0:n], a1[64:96, 0:n], a2[64:96, 0:n], ALU.add, [a1, a2], [q_])
                poT = psum[5 + (qi * 4 + h) % 2]
                LOOK = 2
                nk_ = len(ktiles)
                slots_ = {}
                for ki in range(nk_ + LOOK):
                    if ki < nk_:
                        kt = ktiles[ki]
                        pst = psum[2 + cnt % 3]
                        p = pT[cnt % 3]
                        cnt += 1
                        P.mm(pst[:, 0:n], KTm[:, h, kt * 128:(kt + 1) * 128], q_[:, 0:n], True, True, [KTm, q_], [pst])
                        slots_[ki] = (pst, p, kt)
                    kj = ki - LOOK
                    if kj >= 0:
                        pst, p, kt = slots_.pop(kj)
                        P.act("act", p[:, 0:n], pst[:, 0:n], AF.Exp, [pst], [p], scale=SC)
                        P.mm(poT[0:65, 0:n], Vm[:, kt, h, 0:65], p[:, 0:n], kj == 0, kj == nk_ - 1, [p, Vm], [poT])
                o_ = oT[h % 2]
                P.cp("dve", o_[:, 0:n], poT[0:65, 0:n], [poT], [o_])
                po = psum[7]
                for s in range(nsub):
                    P.tr(po[:, s * 65:(s + 1) * 65], o_[0:65, s * 128:(s + 1) * 128], ident[0:65, 0:65], [o_, ident], [po])
                r_ = rc[h % 2]
                pov = po[:, 0:nsub * 65].rearrange("p (s e) -> p s e", s=nsub)
                P.op("dve", lambda e, r_=r_, pov=pov, nsub=nsub: e.reciprocal(r_[:, 0:nsub], pov[:, :, 64]), [po], [r_])
                P.tt("dve", y_[:, 0:nsub, h, :], pov[:, :, 0:64], r_[:, 0:nsub].unsqueeze(2).to_broadcast([128, nsub, 64]), ALU.mult,
                     [po, r_], [y_])
            for s in range(nsub):
                r0 = b0 + s * 128
                P.dma("sp", y_d[r0:r0 + 128, 256:512], y_[:, s, :, :].rearrange("p h d -> p (h d)"), [y_], [y_res[r0 // 128]])


def emit_mixers(P, L, need_ctx, dram, z_d, z_res, y_d, y_res, psum, ident, dbg):
    if "skip_na" not in dbg:
        emit_na(P, L, need_ctx, dram, z_d, z_res, y_d, y_res, psum, ident)
    if "skip_mla" not in dbg:
        emit_mla(P, L, need_ctx, dram, z_d, z_res, y_d, y_res, psum, ident)
    if "skip_gla" not in dbg:
        emit_gla(P, L, need_ctx, dram, z_d, z_res, y_d, y_res, psum, ident)
    if "skip_rw" not in dbg:
        emit_rwkv(P, L, need_ctx, dram, z_d, z_res, y_d, y_res, psum, ident)


def rope_tables():
    t = np.arange(TLAT)
    row = (t // 64).astype(np.float32)
    col = (t % 64).astype(np.float32)
    inv = (10000.0 ** (-np.arange(0, 16, 2, dtype=np.float32) / 16)).astype(np.float32)
    C = np.ones((32, TALL), np.float32)
    S = np.zeros((32, TALL), np.float32)
    for part, pos in ((0, row), (1, col)):
        ang = (pos[:, None] * inv[None, :]).astype(np.float32)
        c, s = np.cos(ang).T, np.sin(ang).T
        C[part * 16:part * 16 + 8, TCTX:] = c
        C[part * 16 + 8:part * 16 + 16, TCTX:] = c
        S[part * 16:part * 16 + 8, TCTX:] = -s
        S[part * 16 + 8:part * 16 + 16, TCTX:] = s
    return np.stack([C, S], 0).astype(np.float32)


def rope_partner():
    idx = np.arange(32)
    return np.where((idx % 16) < 8, idx + 8, idx - 8)


def mix_host_inputs(inputs, m):
    part = rope_partner()
    m["rope_tab"] = rope_tables()
    m["scan_consts"] = scan_consts()
    for nm, shp in (("rw_mu", (1, 1024)), ("rw_k_k", (1, 256)), ("rw_k_a", (1, 256)), ("rw_w0", (2, 1, 256)), ("rw_a0", (2, 1, 256)),
                    ("rw_w_up", (2, 64, 256)), ("rw_a_up", (2, 64, 256)), ("rw_g_up", (128, 256)), ("rw_ln_w", (1, 256)),
                    ("rw_ln_b", (1, 256)), ("rw_r_k", (1, 256))):
        m[nm] = np.ascontiguousarray(np.asarray(inputs[nm], np.float32).reshape((NL_FULL,) + shp))
    m["gla_gate_up"] = np.asarray(inputs["gla_gate_up"], np.float32)
    m["gla_gate_b"] = np.asarray(inputs["gla_gate_b"], np.float32).reshape(NL_FULL, 2, 1, 128)
    m["gla_norm"] = np.asarray(inputs["gla_norm"], np.float32).reshape(NL_FULL, 1, 256)
    pm = np.zeros((96, 96), np.float32)
    pm[64 + part, 64 + np.arange(32)] = 1.0
    m["mla_pm"] = pm
    wuq = np.asarray(inputs["mla_w_uq"], np.float32).reshape(NL_FULL, 192, 4, 96)
    m["mla_wq_r"] = np.ascontiguousarray(wuq.reshape(NL_FULL, 192, 384))
    wp = np.concatenate([wuq[..., 0:64], wuq[..., 64:96][..., part]], axis=-1)
    m["mla_wq_p"] = np.ascontiguousarray(wp.reshape(NL_FULL, 192, 384))
    wukv = np.asarray(inputs["mla_w_ukv"], np.float32).reshape(NL_FULL, 128, 4, 128)
    wk = np.zeros((NL_FULL, 128, 4, 96), np.float32)
    wk[..., 0:64] = wukv[..., 0:64]
    m["mla_wk"] = np.ascontiguousarray(wk.reshape(NL_FULL, 128, 384))
    m["mla_wv"] = np.ascontiguousarray(wukv[..., 64:128].reshape(NL_FULL, 128, 256))
    m["mla_q_norm"] = np.asarray(inputs["mla_q_norm"], np.float32).reshape(NL_FULL, 1, 192)
    m["mla_kv_norm"] = np.asarray(inputs["mla_kv_norm"], np.float32).reshape(NL_FULL, 1, 128)
    rpb = np.asarray(inputs["na_rpb"], np.float32)
    p = np.arange(128)
    k = p % 64
    q = np.arange(64)
    coff = np.clip(k[:, None] - q[None, :], -15, 15) + 15
    di = np.arange(14)
    roff = di[None, :] + (p[:, None] >= 64)
    m["na_btab"] = np.ascontiguousarray(
        rpb[:, :, roff[:, :, None], coff[:, None, :]].transpose(0, 2, 1, 3, 4))
    cstart = np.clip(q - 8, 0, 48)
    inwin = (k[:, None] >= cstart[None, :]) & (k[:, None] < cstart[None, :] + 16)
    m["na_mask"] = np.where(inwin, 0.0, NEG).astype(np.float32)
    return m


MIX_SPECS = [
    ("rw_mu", (1, 1024), True), ("rw_k_k", (1, 256), True), ("rw_k_a", (1, 256), True), ("rw_w0", (2, 1, 256), True),
    ("rw_a0", (2, 1, 256), True), ("rw_w_up", (2, 64, 256), True), ("rw_a_up", (2, 64, 256), True), ("rw_g_up", (128, 256), True),
    ("rw_ln_w", (1, 256), True), ("rw_ln_b", (1, 256), True), ("rw_r_k", (1, 256), True),
    ("scan_consts", (2, 128, 768), False), ("gla_gate_up", (2, 16, 128), True), ("gla_gate_b", (2, 1, 128), True),
    ("gla_norm", (1, 256), True),
    ("rope_tab", (2, 32, TALL), False), ("mla_pm", (96, 96), False),
    ("mla_wq_r", (192, 384), True), ("mla_wq_p", (192, 384), True), ("mla_wk", (128, 384), True), ("mla_wv", (128, 256), True),
    ("mla_q_norm", (1, 192), True), ("mla_kv_norm", (1, 128), True),
    ("na_btab", (128, 4, 14, 64), True), ("na_mask", (128, 64), False),
]


def scan_consts():
    s = np.arange(128)[:, None]
    t = np.arange(128)[None, :]
    out = np.zeros((2, 128, 768), np.float32)
    for d in range(2):
        incl = (s <= t) if d == 0 else (s >= t)
        strict = (s < t) if d == 0 else (s > t)
        out[d, :, 0:128] = incl
        out[d, :, 128:256] = incl
        out[d, :, 256:384] = strict
        out[d, :, 384:512] = incl
        out[d, :, 512:640] = strict
        out[d, :, 640:768] = strict.T
    return out


INLINE_FINISH = False


def emit_scan(P, dram, psum, ident, pre_d, pre_res, loads, has_ab, yacc, finish=None):
    ones = P.sb("ones", [128, 1])
    P.memset("pool", ones[:], 1.0, [ones])
    P.memset("pool", yacc[:], 0.0, [yacc])
    yres = [Res("yacc%d" % i) for i in range(NT)]
    done = [0] * NT

    def direction(d):
        B = psum[4 * d:4 * d + 4]
        Xin = [P.sb("Xin%d_%d" % (d, i), [128, 6, 256]) for i in range(2)]
        E = P.sb("E%d" % d, [128, 3, 256])
        W = [P.sb("W%d_%d" % (d, i), [128, 4, 256]) for i in range(2)]
        ft = P.sb("FT%d" % d, [128, 2, 4, 128])
        gm = P.sb("Gm%d" % d, [128, 4, 512])
        pCt = [P.sb("pC%d_%d" % (d, i), [128, 2]) for i in range(2)]
        H = P.sb("H%d" % d, [128, 2, 64])
        tmpH = P.sb("tmpH%d" % d, [128, 2, 64])
        cst = P.sb("cst%d" % d, [128, 768])
        P.dma("sp", cst[:], dram["scan_consts"][d], (), [cst])
        if has_ab:
            Xb = [P.sb("Xb%d_%d" % (d, i), [128, 4, 128], BF16) for i in range(2)]
            Yb = [P.sb("Yb%d_%d" % (d, i), [128, 4, 128], BF16) for i in range(2)]
            Pb = [P.sb("Pb%d_%d" % (d, i), [128, 4, 128], BF16) for i in range(2)]
            P32 = [P.sb("P32%d_%d" % (d, i), [128, 4, 128]) for i in range(2)]
            Xs = P.sb("Xs%d" % d, [128, 256])
            Us = P.sb("Us%d" % d, [128, 256])
        order = list(range(NT)) if d == 0 else [1, 0] + list(range(NT - 1, 1, -1))
        Mc = cst[:, 0:128]
        MASK4 = cst[:, 128:640]
        P.memset("pool", H[:], 0.0, [H])
        yield
        for n, i in enumerate(order):
            b = n % 2
            xi, w_, pc = Xin[b], W[b], pCt[b]
            for (s0, ns, c0) in loads(d):
                P.dma("sp" if (s0 + d) % 2 == 0 else "pool", xi[:, s0:s0 + ns, :].rearrange("p s f -> p (s f)"),
                      pre_d[i * 128:(i + 1) * 128, c0:c0 + ns * 256], [pre_res[i]], [xi])
            pcl = B[0]
            P.mm(pcl[:, 0:256], Mc, xi[:, 3, :], True, True, [cst, xi], [pcl])
            pcp = B[1]
            for g in range(2):
                P.mm(pcp[:, 384 + g:385 + g], xi[:, 3, g * 128:(g + 1) * 128], ones[:, 0:1], True, True, [xi, ones], [pcp])
            yield
            P.act("act", E[:, 0, :], pcl[:, 0:256], AF.Exp, [pcl], [E])
            P.act("act", E[:, 1, :], pcl[:, 0:256], AF.Exp, [pcl], [E], scale=-1.0)
            P.act("act", pc[:, :], pcp[:, 384:386], AF.Exp, [pcp], [pc])
            if has_ab:
                P.tt("dve", E[:, 2, :], pcl[:, 0:256], xi[:, 3, :], ALU.subtract, [pcl, xi], [E])
                P.act("act", E[:, 2, :], E[:, 2, :], AF.Exp, [E], [E])
            yield
            P.tt("dve", w_[:, 0, :], xi[:, 0, :], E[:, 0, :], ALU.mult, [xi, E], [w_])
            P.tt("pool", w_[:, 1, :], xi[:, 1, :], E[:, 1, :], ALU.mult, [xi, E], [w_])
            if has_ab:
                P.tt("dve", w_[:, 2, :], xi[:, 4, :], E[:, 2, :], ALU.mult, [xi, E], [w_])
                P.tt("pool", w_[:, 3, :], xi[:, 5, :], E[:, 1, :], ALU.mult, [xi, E], [w_])
            yield
            slots = [(0, 0), (2, 1), (1, 2), (3, 3)] if has_ab else [(0, 0), (1, 2)]
            for g in range(2):
                pt = B[2 + g]
                for (ws, fs) in slots:
                    P.tr(pt[:, fs * 128:(fs + 1) * 128], w_[:, ws, g * 128:(g + 1) * 128], ident[:], [w_, ident], [pt])
            yield
            for g in range(2):
                pt = B[2 + g]
                if has_ab:
                    P.cp("act", ft[:, g, :, :].rearrange("p f t -> p (f t)"), pt[:, :], [pt], [ft])
                else:
                    P.cp("act", ft[:, g, 0, :], pt[:, 0:128], [pt], [ft])
                    P.cp("act", ft[:, g, 2, :], pt[:, 256:384], [pt], [ft])
            yield
            for h in range(4):
                g, hp = h // 2, (h % 2) * 64
                pg = B[2 + h % 2]
                if has_ab:
                    P.mm(pg[:, 0:256], ft[hp:hp + 64, g, 2, :], ft[hp:hp + 64, g, 0:2, :].rearrange("p f t -> p (f t)"), True, True, [ft], [pg])
                    P.mm(pg[:, 256:512], ft[hp:hp + 64, g, 3, :], ft[hp:hp + 64, g, 0:2, :].rearrange("p f t -> p (f t)"), True, True, [ft], [pg])
                    P.tt("dve", gm[:, h, :], pg[:, :], MASK4, ALU.mult, [pg, cst], [gm])
                else:
                    P.mm(pg[:, 0:128], ft[hp:hp + 64, g, 2, :], ft[hp:hp + 64, g, 0, :], True, True, [ft], [pg])
                    P.tt("dve", gm[:, h, 0:128], pg[:, 0:128], MASK4[:, 0:128], ALU.mult, [pg, cst], [gm])
                if h % 2 == 1:
                    yield
            if has_ab:
                p3 = B[0]
                for h in range(4):
                    P.tr(p3[:, h * 128:(h + 1) * 128], gm[:, h, 384:512], ident[:], [gm, ident], [p3])
                Xc, Yc, Pc, Pc32 = Xb[0], Yb[0], Pb[0], P32[0]
                yield
                P.cp("act", Yc[:].rearrange("p h s -> p (h s)"), p3[:, :], [p3], [Yc])
                P.cp("dve", Xc[:], gm[:, :, 384:512], [gm], [Xc])
                for h in range(4):
                    P.tt("dve", Pc32[:, h, :], gm[:, h, 384:512], ident[:, :], ALU.add, [gm, ident], [Pc32])
                P.cp("dve", Pc[:], Pc32[:], [Pc32], [Pc])
                yield
                def mm_xy(lv, Xc, Yc):
                    pY = B[3]
                    for h in range(4):
                        P.mm(pY[:, h * 128:(h + 1) * 128], Xc[:, h, :], Yc[:, h, :], True, True, [Yc, Xc], [pY])
                    if lv < 6:
                        pX = B[2]
                        for h in range(4):
                            P.mm(pX[:, h * 128:(h + 1) * 128], Yc[:, h, :], Xc[:, h, :], True, True, [Yc, Xc], [pX])

                mm_xy(1, Xc, Yc)
                yield
                for lv in range(1, 7):
                    Yn, Pn, Pn32 = Yb[lv % 2], Pb[lv % 2], P32[lv % 2]
                    Xn = Xb[lv % 2]
                    P.cp("act", Yn[:].rearrange("p h s -> p (h s)"), B[3][:, :], [B[3]], [Yn])
                    if lv < 6:
                        P.cp("act", Xn[:].rearrange("p h s -> p (h s)"), B[2][:, :], [B[2]], [Xn])
                    yield
                    pP = B[1]
                    for h in range(4):
                        P.mm(pP[:, h * 128:(h + 1) * 128], Yn[:, h, :], Pc[:, h, :], True, True, [Yn, Pc], [pP])
                    if lv < 6:
                        mm_xy(lv + 1, Xn, Yn)
                    yield
                    P.tt("dve", Pn32[:].rearrange("p h s -> p (h s)"), pP[:, :], Pc32[:].rearrange("p h s -> p (h s)"), ALU.add,
                         [pP, Pc32], [Pn32])
                    if lv < 6:
                        P.cp("dve", Pn[:], Pn32[:], [Pn32], [Pn])
                    Xc, Yc, Pc, Pc32 = Xn, Yn, Pn, Pn32
                yield
                Pc = Pc32
                pXs = B[0]
                for h in range(4):
                    g, hp = h // 2, (h % 2) * 64
                    P.mm(pXs[:, 256 + h * 64:256 + (h + 1) * 64], ft[hp:hp + 64, g, 1, :], H[hp:hp + 64, g, :], True, False, [ft, H], [pXs])
                    P.mm(pXs[:, 256 + h * 64:256 + (h + 1) * 64], gm[:, h, 128:256], xi[:, 2, h * 64:(h + 1) * 64], False, True, [gm, xi], [pXs])
                yield
                P.cp("act", Xs[:, :], pXs[:, 256:512], [pXs], [Xs])
                yield
                pU = B[2]
                for h in range(4):
                    P.mm(pU[:, h * 64:(h + 1) * 64], Pc[:, h, :], Xs[:, h * 64:(h + 1) * 64], True, True, [Pc, Xs], [pU])
                yield
                P.cp("dve", Us[:, :], pU[:, 0:256], [pU], [Us])
                yield
            pYo = B[1]
            for h in range(4):
                g, hp = h // 2, (h % 2) * 64
                vh = xi[:, 2, h * 64:(h + 1) * 64]
                P.mm(pYo[:, h * 64:(h + 1) * 64], ft[hp:hp + 64, g, 0, :], H[hp:hp + 64, g, :], True, False, [ft, H], [pYo])
                if has_ab:
                    P.mm(pYo[:, h * 64:(h + 1) * 64], gm[:, h, 256:384], Us[:, h * 64:(h + 1) * 64], False, False, [gm, Us], [pYo])
                P.mm(pYo[:, h * 64:(h + 1) * 64], gm[:, h, 0:128], vh, False, True, [gm, xi], [pYo])
            for h in range(4):
                g, hp = h // 2, (h % 2) * 64
                vh = xi[:, 2, h * 64:(h + 1) * 64]
                P.mm(pYo[hp:hp + 64, 256 + g * 64:256 + (g + 1) * 64], w_[:, 1, h * 64:(h + 1) * 64], vh, True, not has_ab, [w_, xi], [pYo])
                if has_ab:
                    P.mm(pYo[hp:hp + 64, 256 + g * 64:256 + (g + 1) * 64], w_[:, 3, h * 64:(h + 1) * 64], Us[:, h * 64:(h + 1) * 64], False, True,
                         [w_, Us], [pYo])
            yield
            P.tt("dve", yacc[:, i, :], pYo[:, 0:256], yacc[:, i, :], ALU.add, [pYo, yres[i]], [yres[i]])
            P.tt("dve", tmpH[:], pYo[:, 256:384].rearrange("p (g v) -> p g v", g=2), H[:], ALU.add, [pYo, H], [tmpH])
            P.tt("dve", H[:], tmpH[:], pc[:, :].unsqueeze(2).to_broadcast([128, 2, 64]), ALU.mult, [tmpH, pc], [H])
            done[i] += 1
            if done[i] == 2 and finish is not None and INLINE_FINISH:
                finish(i, yres[i])
            yield

    gens = [direction(0), direction(1)]
    while gens:
        for g_ in list(gens):
            try:
                next(g_)
            except StopIteration:
                gens.remove(g_)
    if finish is not None and not INLINE_FINISH:
        outs = [finish(i, yres[i]) for i in range(NT) ] if False else None
        i = 0
        while i < NT:
            grp = []
            for k in range(i, min(i + 4, NT)):
                g_ = finish(k, yres[k])
                if g_ is not None and hasattr(g_, "__next__"):
                    grp.append(g_)
            i += 4
            while grp:
                for g_ in list(grp):
                    try:
                        next(g_)
                    except StopIteration:
                        grp.remove(g_)


GLA_Z0 = 768 + 352 + 1024


def emit_gla(P, L, need_ctx, dram, z_d, z_res, y_d, y_res, psum, ident):
    pre_d = dram["gla_pre"]
    pre_res = dram.res("gla_pre")
    with P.phase():
        zt = [P.sb("zt%d" % i, [128, 528]) for i in range(2)]
        pre = [P.sb("pre%d" % i, [128, 5, 4, 64]) for i in range(2)]
        gdT = [P.sb("gdT%d" % i, [16, 128]) for i in range(2)]
        gup = P.sb("gup", [16, 2, 128])
        gb = P.sb("gb", [128, 2, 128])
        xg = [P.sb("xg%d" % i, [128, 128]) for i in range(2)]
        for p_ in pre:
            P.memset("pool", p_[:], 0.0, [p_])
        for d in range(2):
            P.dma("sp", gup[:, d, :], dram["gla_gate_up"][L, d], (), [gup])
            P.dma("sp", gb[:, d, :], dram["gla_gate_b"][L, d].partition_broadcast(128), (), [gb])
        def gprep_a(i):
            b = i % 2
            z_, p_ = zt[b], pre[b]
            P.dma("sp", z_[:], z_d[i * 128:(i + 1) * 128, GLA_Z0:GLA_Z0 + 528], [z_res[i]], [z_])
            P.op("act", lambda e, z_=z_, p_=p_: e.mul(p_[:, 0, :, 0:32], z_[:, 0:128].rearrange("p (h d) -> p h d", h=4), 32.0 ** -0.5), [z_], [p_])
            P.cp("pool", p_[:, 1, :, 0:32], z_[:, 128:256].rearrange("p (h d) -> p h d", h=4), [z_], [p_])
            P.cp("pool", p_[:, 2, :, :], z_[:, 256:512].rearrange("p (h d) -> p h d", h=4), [z_], [p_])
            pt = psum[i % 2]
            P.tr(pt[0:16, 0:128], z_[:, 512:528], ident[:], [z_, ident], [pt])
            P.cp("act", gdT[b][:, :], pt[0:16, 0:128], [pt], [gdT[b]])

        def gprep_b(i):
            b = i % 2
            p_ = pre[b]
            for d in range(2):
                pl = psum[2 + d]
                x_ = xg[d]
                P.mm(pl[:, 0:128], gdT[b][:, :], gup[:, d, :], True, True, [gdT[b], gup], [pl])
                P.tt("dve", x_[:], pl[:, 0:128], gb[:, d, :], ALU.add, [pl, gb], [x_])
                P.act("act", x_[:], x_[:], AF.Exp, [x_], [x_], scale=-1.0)
                P.act("act", x_[:], x_[:], AF.Ln, [x_], [x_], bias=1.0)
                P.op("act", lambda e, x_=x_, p_=p_, d=d: e.mul(p_[:, 3 + d, :, 0:32], x_[:].rearrange("p (h d) -> p h d", h=4), -1.0 / 16.0), [x_], [p_])
            P.dma("pool", pre_d[i * 128:(i + 1) * 128, :], p_[:].rearrange("p s h d -> p (s h d)"), [p_], [pre_res[i]])

        for k in range(NT + 1):
            if k < NT:
                gprep_a(k)
            if k >= 1:
                gprep_b(k - 1)
    with P.phase():
        yacc = P.sb("yacc", [128, NT, 256])
        gn = P.sb("gn", [128, 256])
        og = [P.sb("og%d" % i, [128, 256]) for i in range(4)]
        sq = [P.sb("sq%d" % i, [128, 256]) for i in range(4)]
        stg = [P.sb("stg%d" % i, [128, 12]) for i in range(4)]
        P.dma("sp", gn[:], dram["gla_norm"][L].partition_broadcast(128), (), [gn])
        fcnt = [0]

        def finish(i, yr):
            if i < 2 and not need_ctx:
                return
            b = fcnt[0] % 4
            fcnt[0] += 1
            o_, s_, g_ = og[b], sq[b], stg[b]
            yv = yacc[:, i, :]
            P.dma("sp", o_[:], z_d[i * 128:(i + 1) * 128, DIN - 256:DIN], [z_res[i]], [o_])
            yield
            P.act("act", o_[:], o_[:], AF.Silu, [o_], [o_])
            yield
            P.tt("pool", s_[:], yv, yv, ALU.mult, [yr], [s_])
            yield
            P.red("dve", g_[:, 0:4], s_[:].rearrange("p (h d) -> p h d", h=4), ALU.add, [s_], [g_])
            yield
            P.act("act", g_[:, 4:8], g_[:, 0:4], AF.Sqrt, [g_, P.consts[EPS]], [g_], bias=P.consts[EPS][:, 0:1], scale=1.0 / 64)
            yield
            P.op("dve", lambda e, g_=g_: e.reciprocal(g_[:, 8:12], g_[:, 4:8]), [g_], [g_])
            yield
            P.tt("dve", s_[:].rearrange("p (h d) -> p h d", h=4), yv.rearrange("p (h d) -> p h d", h=4),
                 g_[:, 8:12].unsqueeze(2).to_broadcast([128, 4, 64]), ALU.mult, [yr, g_], [s_])
            yield
            P.tt("pool", s_[:], s_[:], gn[:], ALU.mult, [s_, gn], [s_])
            yield
            P.tt("pool", s_[:], s_[:], o_[:], ALU.mult, [s_, o_], [s_])
            yield
            P.dma("sp", y_d[i * 128:(i + 1) * 128, 768:1024], s_[:], [s_], [y_res[i]])

        emit_scan(P, dram, psum, ident, pre_d, pre_res, lambda d: [(0, 3, 0), (3, 1, 768 + d * 256)], False, yacc, finish)


RW_Z0 = 768 + 352


def emit_rwkv(P, L, need_ctx, dram, z_d, z_res, y_d, y_res, psum, ident):
    pre_d = dram["rw_pre"]
    pre_res = dram.res("rw_pre")
    C05 = float(-np.exp(-0.5))
    with P.phase():
        zc = [P.sb("zc%d" % i, [128, 1024]) for i in range(2)]
        zp = [P.sb("zp%d" % i, [128, 1024]) for i in range(2)]
        zn = [P.sb("zn%d" % i, [128, 1024]) for i in range(2)]
        zs = [P.sb("zs%d" % i, [128, 1024]) for i in range(2)]
        pre = [P.sb("pre%d" % i, [128, 10, 256]) for i in range(2)]
        mub = P.sb("mub", [128, 1024])
        kkb = P.sb("kkb", [128, 256])
        kab = P.sb("kab", [128, 256])
        w0b = P.sb("w0b", [128, 2, 256])
        a0b = P.sb("a0b", [128, 2, 256])
        wa = P.sb("wa", [128, 2, 256])
        gup = P.sb("gup", [128, 256])
        TWA = [P.sb("TWA%d" % i, [128, 128]) for i in range(2)]
        TG = [P.sb("TG%d" % i, [128, 128]) for i in range(2)]
        kt_ = [P.sb("kt%d" % i, [128, 256]) for i in range(2)]
        kn_ = [P.sb("kn%d" % i, [128, 256]) for i in range(2)]
        sq_ = P.sb("sq", [128, 256])
        xw = [P.sb("xw%d" % i, [128, 256]) for i in range(2)]
        xa = [P.sb("xa%d" % i, [128, 256]) for i in range(2)]
        st_ = [P.sb("st%d" % i, [128, 12]) for i in range(2)]
        P.dma("sp", mub[:], dram["rw_mu"][L].partition_broadcast(128), (), [mub])
        P.dma("sp", kkb[:], dram["rw_k_k"][L].partition_broadcast(128), (), [kkb])
        P.dma("sp", kab[:], dram["rw_k_a"][L].partition_broadcast(128), (), [kab])
        for d in range(2):
            P.dma("sp", w0b[:, d, :], dram["rw_w0"][L, d].partition_broadcast(128), (), [w0b])
            P.dma("sp", a0b[:, d, :], dram["rw_a0"][L, d].partition_broadcast(128), (), [a0b])
            P.dma("sp", wa[0:64, d, :], dram["rw_w_up"][L, d], (), [wa])
            P.dma("sp", wa[64:128, d, :], dram["rw_a_up"][L, d], (), [wa])
        P.dma("sp", gup[:], dram["rw_g_up"][L], (), [gup])
        def prep_a(i):
            b = i % 2
            c_, p_, n_, s_, pr = zc[b], zp[b], zn[b], zs[b], pre[b]
            t0 = i * 128
            P.dma("sp", c_[:], z_d[t0:t0 + 128, RW_Z0:RW_Z0 + 1024], [z_res[i]], [c_])
            if t0 in (0, TCTX):
                P.memset("pool", p_[:], 0.0, [p_])
                P.dma("pool", p_[1:128, :], z_d[t0:t0 + 127, RW_Z0:RW_Z0 + 1024], [z_res[i]], [p_])
            else:
                P.dma("pool", p_[:], z_d[t0 - 1:t0 + 127, RW_Z0:RW_Z0 + 1024], [z_res[i - 1], z_res[i]], [p_])
            if t0 + 128 in (TCTX, TALL):
                P.memset("pool", n_[:], 0.0, [n_])
                P.dma("sp", n_[0:127, :], z_d[t0 + 1:t0 + 128, RW_Z0:RW_Z0 + 1024], [z_res[i]], [n_])
            else:
                P.dma("sp", n_[:], z_d[t0 + 1:t0 + 129, RW_Z0:RW_Z0 + 1024], [z_res[i], z_res[i + 1]], [n_])
            P.tt("pool", p_[:], p_[:], n_[:], ALU.add, [p_, n_], [p_])
            P.stt("dve", p_[:], p_[:], 0.5, c_[:], ALU.mult, ALU.subtract, [p_, c_], [p_])
            P.tt("pool", p_[:], p_[:], mub[:], ALU.mult, [p_, mub], [p_])
            P.tt("dve", s_[:], p_[:], c_[:], ALU.add, [p_, c_], [s_])
            P.cp("act", pr[:, 0, :], s_[:, 0:256], [s_], [pr])
            P.cp("act", pr[:, 1, :], s_[:, 512:768], [s_], [pr])
            k_, kn, st = kt_[b], kn_[b], st_[b]
            P.tt("dve", k_[:], s_[:, 256:512], kkb[:], ALU.mult, [s_, kkb], [k_])
            P.tt("pool", sq_[:], k_[:], k_[:], ALU.mult, [k_], [sq_])
            P.red("dve", st[:, 0:4], sq_[:].rearrange("p (h d) -> p h d", h=4), ALU.add, [sq_], [st])
            P.act("act", st[:, 4:8], st[:, 0:4], AF.Sqrt, [st, P.consts[1e-12]], [st], bias=P.consts[1e-12][:, 0:1], scale=1.0)
            P.op("dve", lambda e, st=st: e.reciprocal(st[:, 8:12], st[:, 4:8]), [st], [st])
            P.tt("dve", kn[:].rearrange("p (h d) -> p h d", h=4), k_[:].rearrange("p (h d) -> p h d", h=4),
                 st[:, 8:12].unsqueeze(2).to_broadcast([128, 4, 64]), ALU.mult, [k_, st], [kn])
            P.op("act", lambda e, pr=pr, kn=kn: e.mul(pr[:, 2, :], kn[:], -1.0), [kn], [pr])
            pt = psum[i % 2]
            P.tr(pt[:, 0:128], s_[:, 768:896], ident[:], [s_, ident], [pt])
            P.tr(pt[:, 128:256], s_[:, 896:1024], ident[:], [s_, ident], [pt])
            tw, tg = TWA[b], TG[b]
            P.act("act", tw[0:64, :], pt[0:64, 0:128], AF.Tanh, [pt], [tw])
            P.cp("act", tw[64:128, :], pt[64:128, 0:128], [pt], [tw])
            P.act("act", tg[:, :], pt[:, 128:256], AF.Sigmoid, [pt], [tg])

        def prep_b(i):
            b = i % 2
            s_, pr = zs[b], pre[b]
            kn = kn_[b]
            tw, tg = TWA[b], TG[b]
            t0 = i * 128
            pgp = psum[6]
            P.mm(pgp[:, 0:256], tg[:, :], gup[:, :], True, True, [tg, gup], [pgp])
            P.cp("act", pr[:, 9, :], pgp[:, 0:256], [pgp], [pr])
            for d in range(2):
                pw, pa = psum[2 + d * 2], psum[3 + d * 2]
                P.mm(pw[:, 0:256], tw[0:64, :], wa[0:64, d, :], True, True, [tw, wa], [pw])
                P.mm(pa[:, 0:256], tw[64:128, :], wa[64:128, d, :], True, True, [tw, wa], [pa])
                w_, a_ = xw[d], xa[d]
                P.tt("dve", w_[:], pw[:, 0:256], w0b[:, d, :], ALU.add, [pw, w0b], [w_])
                P.act("act", w_[:], w_[:], AF.Sigmoid, [w_], [w_])
                P.op("act", lambda e, pr=pr, w_=w_, d=d: e.mul(pr[:, 3 + 3 * d, :], w_[:], C05), [w_], [pr])
                P.tt("dve", a_[:], pa[:, 0:256], a0b[:, d, :], ALU.add, [pa, a0b], [a_])
                P.act("act", a_[:], a_[:], AF.Sigmoid, [a_], [a_])
                P.tt("pool", pr[:, 5 + 3 * d, :], kn[:], a_[:], ALU.mult, [kn, a_], [pr])
                P.stt("dve", a_[:], a_[:], -1.0, kab[:], ALU.add, ALU.mult, [a_, kab], [a_])
                P.stt("dve", pr[:, 4 + 3 * d, :], a_[:], 1.0, s_[:, 256:512], ALU.add, ALU.mult, [a_, s_], [pr])
            P.dma("pool", pre_d[t0:t0 + 128, :], pr[:].rearrange("p s f -> p (s f)"), [pr], [pre_res[i]])

        for k in range(NT + 1):
            if k < NT:
                prep_a(k)
            if k >= 1:
                prep_b(k - 1)
    import os
    if os.environ.get("RW_STAGE", "9") == "1":
        return
    with P.phase():
        yacc = P.sb("yacc", [128, NT, 256])
        lnw = P.sb("lnw", [128, 256])
        lnb = P.sb("lnb", [128, 256])
        rkb = P.sb("rkb", [128, 256])
        fin = [P.sb("fin%d" % i, [128, 5, 256]) for i in range(4)]
        yc = [P.sb("yc%d" % i, [128, 256]) for i in range(4)]
        sq = [P.sb("sq%d" % i, [128, 256]) for i in range(4)]
        sf = [P.sb("sf%d" % i, [128, 24]) for i in range(4)]
        P.dma("sp", lnw[:], dram["rw_ln_w"][L].partition_broadcast(128), (), [lnw])
        P.dma("sp", lnb[:], dram["rw_ln_b"][L].partition_broadcast(128), (), [lnb])
        P.dma("sp", rkb[:], dram["rw_r_k"][L].partition_broadcast(128), (), [rkb])
        v4 = lambda ap: ap.rearrange("p (h d) -> p h d", h=4)
        bc4 = lambda ap: ap.unsqueeze(2).to_broadcast([128, 4, 64])
        fcnt = [0]

        def finish(i, yr):
            if i < 2 and not need_ctx:
                return
            b = fcnt[0] % 4
            fcnt[0] += 1
            f_, y_, s_, t_ = fin[b], yc[b], sq[b], sf[b]
            t0 = i * 128
            P.dma("sp", f_[:, 0:2, :].rearrange("p s f -> p (s f)"), pre_d[t0:t0 + 128, 0:512], [pre_res[i]], [f_])
            yield
            P.dma("pool", f_[:, 2, :], pre_d[t0:t0 + 128, 1024:1280], [pre_res[i]], [f_])
            yield
            P.dma("pool", f_[:, 3, :], pre_d[t0:t0 + 128, 1792:2048], [pre_res[i]], [f_])
            yield
            P.dma("sp", f_[:, 4, :], pre_d[t0:t0 + 128, 2304:2560], [pre_res[i]], [f_])
            yv = yacc[:, i, :]
            yield
            P.red("dve", t_[:, 0:4], v4(yv), ALU.add, [yr], [t_])
            yield
            P.op("act", lambda e, t_=t_: e.mul(t_[:, 4:8], t_[:, 0:4], -1.0 / 64), [t_], [t_])
            yield
            P.tt("dve", v4(y_[:]), v4(yv), bc4(t_[:, 4:8]), ALU.add, [yr, t_], [y_])
            yield
            P.tt("pool", s_[:], y_[:], y_[:], ALU.mult, [y_], [s_])
            yield
            P.red("dve", t_[:, 8:12], v4(s_[:]), ALU.add, [s_], [t_])
            yield
            P.act("act", t_[:, 12:16], t_[:, 8:12], AF.Sqrt, [t_, P.consts[64e-5]], [t_], bias=P.consts[64e-5][:, 0:1], scale=1.0 / 64)
            yield
            P.op("dve", lambda e, t_=t_: e.reciprocal(t_[:, 16:20], t_[:, 12:16]), [t_], [t_])
            yield
            P.tt("dve", v4(y_[:]), v4(y_[:]), bc4(t_[:, 16:20]), ALU.mult, [y_, t_], [y_])
            yield
            P.tt("pool", y_[:], y_[:], lnw[:], ALU.mult, [y_, lnw], [y_])
            yield
            P.tt("pool", y_[:], y_[:], lnb[:], ALU.add, [y_, lnb], [y_])
            yield
            P.tt("pool", s_[:], f_[:, 2, :], f_[:, 3, :], ALU.add, [f_], [s_])
            yield
            P.tt("dve", s_[:], s_[:], f_[:, 0, :], ALU.mult, [s_, f_], [s_])
            yield
            P.tt("pool", s_[:], s_[:], rkb[:], ALU.mult, [s_, rkb], [s_])
            yield
            P.red("dve", t_[:, 20:24], v4(s_[:]), ALU.add, [s_], [t_])
            yield
            P.tt("dve", v4(s_[:]), v4(f_[:, 1, :]), bc4(t_[:, 20:24]), ALU.mult, [f_, t_, s_], [s_])
            yield
            P.tt("pool", y_[:], y_[:], s_[:], ALU.add, [y_, s_], [y_])
            yield
            P.tt("pool", y_[:], y_[:], f_[:, 4, :], ALU.mult, [y_, f_], [y_])
            yield
            P.dma("sp", y_d[t0:t0 + 128, 512:768], y_[:], [y_], [y_res[i]])

        emit_scan(P, dram, psum, ident, pre_d, pre_res,
                  lambda d: [(0, 1, 0), (2, 1, 256), (4, 1, 512), (3, 1, 768 + 768 * d), (1, 1, 1024 + 768 * d), (5, 1, 1280 + 768 * d)],
                  True, yacc, finish)


N_CORES = 8
SCR_SPECS = {"gla_pre": 1280, "rw_pre": 2560}
W_SPECS = [
    ("w_mod", (D, 6 * D)), ("b_mod", (1, 6 * D)),
    ("g_mix_pre", (1, D)), ("g_mix_post", (1, D)), ("g_ffn_pre", (1, D)), ("g_ffn_post", (1, D)),
    ("w_in", (D, DIN)), ("w_out", (D, D)),
    ("ffn_w_up", (D, 2 * DFF)), ("ffn_conv_w", (3, 2 * DFF)), ("ffn_conv_b", (1, 2 * DFF)), ("ffn_w_down", (DFF, D)),
]


def build_program(NL=NL_FULL, dbg=None):
    dbg = dbg or {}
    NLW = dbg.get("nlw", NL_FULL)
    nc = bass.Bass("TRN2", target_bir_lowering=False)
    dram = {}

    def dtens(name, shape, dt=F32, kind="Internal"):
        return nc.dram_tensor(name, list(shape), dt, kind=kind).ap()

    def kind_of(tag, default="Internal"):
        if tag + "_in" in dbg:
            return "ExternalInput"
        if tag in dbg:
            return "ExternalOutput"
        return default

    dram["xall"] = dtens("xall", [TALL, D], kind="ExternalInput")
    dram["c2"] = dtens("c2", [2, D], kind="ExternalInput")
    dram["ident"] = dtens("ident", [128, 128], kind="ExternalInput")
    class LazyDram(dict):
        def __missing__(self, name):
            for nm, shp in W_SPECS:
                if nm == name:
                    self[name] = dtens(name, [NLW] + list(shp), kind="ExternalInput")
                    return self[name]
            for nm, shp, per_layer in MIX_SPECS:
                if nm == name:
                    self[name] = dtens(name, ([NLW] if per_layer else []) + list(shp), kind="ExternalInput")
                    return self[name]
            if name in SCR_SPECS:
                self[name] = dtens(name + "_scr", [TALL, SCR_SPECS[name]], kind=kind_of(name))
                return self[name]
            raise KeyError(name)

        def res(self, name):
            if not hasattr(self, "_res"):
                self._res = {}
            if name not in self._res:
                self._res[name] = [Res("%s%d" % (name, i)) for i in range(NT)]
            return self._res[name]
    dram = LazyDram(dram)
    out_ap = dtens("out", [TLAT, D], kind="ExternalOutput")
    z_d = dtens("z_scr", [TALL, DIN], kind=kind_of("z"))
    y_d = dtens("y_scr", [TALL, D], kind=kind_of("y"))
    xs_d = dtens("xs_scr", [TALL, D], kind=kind_of("xs"))
    mods_d = dtens("mods_scr", [NL_FULL, 2, 6 * D], kind=kind_of("mods"))
    h2T_d = dtens("h2T_scr", [128, 8, TALL], BF16)
    z_res = [Res("z%d" % i) for i in range(NT)]
    y_res = [Res("y%d" % i) for i in range(NT)]
    xs_res = [Res("x%d" % i) for i in range(NT)]
    h2_res = [Res("h2_%d" % i) for i in range(NT)]
    mods_res = [Res("mods%d" % i) for i in range(NL_FULL)]

    with contextlib.ExitStack() as st:
        P = Prog(nc, st)
        ident = P.sb("ident", [128, 128])
        P.dma("sp", ident[:], dram["ident"], (), [ident])
        psum = [P.ps("ps%d" % i) for i in range(8)]
        P.consts = {}
        for cv in (EPS, 1e-12, 64e-5):
            ct = P.sb("const%d" % len(P.consts), [128, 1])
            P.memset("pool", ct[:], cv, [ct])
            P.consts[cv] = ct
        c2T = P.sb("c2T", [128, 8, 2])
        for v in range(2):
            P.dma("sp", c2T[:, :, v], dram["c2"][v].rearrange("(kt p) -> p kt", p=128), (), [c2T],
                  allow_slow_non_contiguous=True)
        P.act("act", c2T[:], c2T[:], AF.Silu, [c2T], [c2T])

        def bcast_load(q, tile, row_ap, reads=()):
            P.dma(q, tile[:], row_ap.partition_broadcast(128), reads, [tile])

        def rms_tile(P, src, st_t, junk, A, Bv, dst, ncols=D):
            P.act("act", junk[:, 0:ncols], src[:, 0:ncols], AF.Square, [src], [junk], accum_out=st_t[:, 0:1])
            P.rstd(st_t, 1.0 / ncols, EPS, [junk])
            P.stt("dve", dst[:, 0:ncols], src[:, 0:ncols], st_t[:, 2:3], A[:, 0:ncols], ALU.mult, ALU.mult, [src, st_t, A], [dst])
            if Bv is not None:
                P.tt("pool", dst[:, 0:ncols], dst[:, 0:ncols], Bv[:, 0:ncols], ALU.add, [dst, Bv], [dst])

        def transpose8(P, src, dstT, banks):
            for half in range(2):
                pst = banks[half]
                for k4 in range(4):
                    kt = half * 4 + k4
                    P.tr(pst[:, k4 * 128:(k4 + 1) * 128], src[:, kt * 128:(kt + 1) * 128], ident[:], [src, ident], [pst])
                P.cp("act", dstT[:, half * 4:(half + 1) * 4, :], pst[:, :].rearrange("p (k t) -> p k t", k=4), [pst], [dstT])

        def load_cast(P, dst_ap, dst_t, src_ap, wst, idx, ncols):
            w = wst[idx % 2]
            P.dma("sp" if idx % 2 == 0 else "pool", w[:, 0:ncols], src_ap, (), [w])
            P.cp("act" if idx % 2 == 0 else "dve", dst_ap, w[:, 0:ncols], [w], [dst_t])

        for L in range(NL):
            need_ctx = L < NL_FULL - 1
            last = (L == NL - 1)
            mrow = lambda v, k: mods_d[L, v, k * D:(k + 1) * D]
            if "skip_p0" not in dbg:
              with P.phase():
                wst = [P.sb("wst%d" % i, [128, 2048]) for i in range(4)]
                brow = P.sb("brow", [2, 6 * D])
                mrow_sb = [P.sb("mrow%d" % i, [2, 2048]) for i in range(2)]
                P.dma("sp", brow[0:1, :], dram["b_mod"][L], (), [brow])
                P.dma("sp", brow[1:2, :], dram["b_mod"][L], (), [brow])
                widx = 0
                for nb in range(3):
                    banks = psum[4 * (nb % 2):4 * (nb % 2) + 4]
                    for kt in range(8):
                        w = wst[widx % 4]
                        P.dma("sp" if widx % 2 == 0 else "pool", w[:, :],
                              dram["w_mod"][L, kt * 128:(kt + 1) * 128, nb * 2048:(nb + 1) * 2048], (), [w])
                        widx += 1
                        for q in range(4):
                            P.mm(banks[q][0:2, :], c2T[:, kt, :], w[:, q * 512:(q + 1) * 512], kt == 0, kt == 7, [c2T, w], [banks[q]])
                    ms = mrow_sb[nb % 2]
                    for q in range(4):
                        c0 = nb * 2048 + q * 512
                        P.tt("dve", ms[:, q * 512:(q + 1) * 512], banks[q][0:2, :], brow[:, c0:c0 + 512], ALU.add, [banks[q], brow], [ms])
                    P.dma("sp", mods_d[L, :, nb * 2048:(nb + 1) * 2048], ms[:, :], [ms], [mods_res[L]])

            if "skip_p1" not in dbg:
              with P.phase():
                x_src = dram["xall"] if L == 0 else xs_d
                wst = [P.sb("wst%d" % i, [128, 2048]) for i in range(2)]
                gB = P.sb("gB", [128, D])
                A1 = P.sb("A1", [128, D])
                B1 = P.sb("B1", [128, D])
                win_sb = P.sb("win", [128, 8, DIN], BF16)
                xt = [P.sb("xt%d" % i, [128, D]) for i in range(2)]
                xn = [P.sb("xn%d" % i, [128, D]) for i in range(2)]
                hT = [P.sb("hT%d" % i, [128, 8, 128], BF16) for i in range(2)]
                zt = [P.sb("zt%d" % i, [128, DIN]) for i in range(2)]
                st1 = [P.sb("st1_%d" % i, [128, 4]) for i in range(2)]
                junk = P.sb("junk", [128, D])
                bcast_load("sp", gB, dram["g_mix_pre"][L])
                win_res = [Res("win%d" % c) for c in range(6)]
                idx = 0
                for c in range(0, 6, 2):
                    c0 = c * 512
                    cw = min(1024, DIN - c0)
                    for kt in range(8):
                        w = wst[idx % 2]
                        P.dma("sp" if idx % 2 == 0 else "pool", w[:, 0:cw], dram["w_in"][L, kt * 128:(kt + 1) * 128, c0:c0 + cw], (), [w])
                        P.cp("act" if idx % 2 == 0 else "dve", win_sb[:, kt, c0:c0 + cw], w[:, 0:cw], [w], [win_res[c], win_res[c + 1]])
                        idx += 1
                for i in range(NT):
                    v = 1 if i < 2 else 0
                    if i == 0 or i == 2:
                        bcast_load("sp", A1, mrow(v, 1), [mods_res[L]])
                        P.stt("dve", A1[:], A1[:], 1.0, gB[:], ALU.add, ALU.mult, [A1, gB], [A1])
                        bcast_load("sp", B1, mrow(v, 0), [mods_res[L]])
                    b = i % 2
                    P.dma("sp", xt[b][:], x_src[i * 128:(i + 1) * 128, :], [xs_res[i]], [xt[b]])
                    rms_tile(P, xt[b], st1[b], junk, A1, B1, xn[b])
                    transpose8(P, xn[b], hT[b], psum[0:2])
                    for cb in range(6):
                        c0 = cb * 512
                        cw = min(512, DIN - c0)
                        pst = psum[2 + cb]
                        for kt in range(8):
                            P.mm(pst[:, 0:cw], hT[b][:, kt, :], win_sb[:, kt, c0:c0 + cw], kt == 0, kt == 7, [hT[b], win_res[cb]], [pst])
                        P.cp("dve" if cb % 2 == 0 else "act", zt[b][:, c0:c0 + cw], pst[:, 0:cw], [pst], [zt[b]])
                    P.dma("pool", z_d[i * 128:(i + 1) * 128, :], zt[b][:], [zt[b]], [z_res[i]])

            if "skip_mix" not in dbg:
                emit_mixers(P, L, need_ctx, dram, z_d, z_res, y_d, y_res, psum, ident, dbg)

            tiles5 = list(range(NT)) if need_ctx else list(range(2, NT))
            x_src = dram["xall"] if L == 0 else xs_d
            if "skip_p5" not in dbg:
              with P.phase():
                wst = [P.sb("wst%d" % i, [128, 1024]) for i in range(2)]
                wout_sb = P.sb("wout", [128, 8, D], BF16)
                gB = P.sb("gB", [128, D])
                G1 = P.sb("G1", [128, D])
                A2 = P.sb("A2", [128, D])
                B2 = P.sb("B2", [128, D])
                yt = [P.sb("yt%d" % i, [128, D]) for i in range(2)]
                xt = [P.sb("xt%d" % i, [128, D]) for i in range(2)]
                yT = [P.sb("yT%d" % i, [128, 8, 128], BF16) for i in range(2)]
                tmp = [P.sb("tmp%d" % i, [128, D]) for i in range(2)]
                xw = [P.sb("xw%d" % i, [128, D]) for i in range(2)]
                h2 = [P.sb("h2%d" % i, [128, D]) for i in range(2)]
                h2T = [P.sb("h2T%d" % i, [128, 8, 128], BF16) for i in range(2)]
                st5 = [P.sb("st5_%d" % i, [128, 8]) for i in range(2)]
                junk = P.sb("junk", [128, D])
                for kt in range(8):
                    load_cast(P, wout_sb[:, kt, :], wout_sb, dram["w_out"][L, kt * 128:(kt + 1) * 128, :], wst, kt, D)
                def p5_a(i):
                    v = 1 if i < 2 else 0
                    if i == tiles5[0] or i == 2:
                        bcast_load("sp", gB, dram["g_mix_post"][L])
                        bcast_load("sp", G1, mrow(v, 2), [mods_res[L]])
                        P.tt("dve", G1[:], G1[:], gB[:], ALU.mult, [G1, gB], [G1])
                        bcast_load("sp", gB, dram["g_ffn_pre"][L])
                        bcast_load("sp", A2, mrow(v, 4), [mods_res[L]])
                        P.stt("dve", A2[:], A2[:], 1.0, gB[:], ALU.add, ALU.mult, [A2, gB], [A2])
                        bcast_load("sp", B2, mrow(v, 3), [mods_res[L]])
                    b = i % 2
                    P.dma("sp", yt[b][:], y_d[i * 128:(i + 1) * 128, :], [y_res[i]], [yt[b]])
                    P.dma("pool", xt[b][:], x_src[i * 128:(i + 1) * 128, :], [xs_res[i]], [xt[b]])
                    transpose8(P, yt[b], yT[b], psum[0:2])
                    pb = [psum[2 + 2 * b], psum[3 + 2 * b]]
                    for half in range(2):
                        for kt in range(8):
                            P.mm(pb[half][:, :], yT[b][:, kt, :], wout_sb[:, kt, half * 512:(half + 1) * 512], kt == 0, kt == 7,
                                 [yT[b], wout_sb], [pb[half]])
                        P.act("act", junk[:, 0:512], pb[half][:, :], AF.Square, [pb[half]], [junk], accum_out=st5[b][:, 3 + half:4 + half])
                    P.tt("dve", st5[b][:, 0:1], st5[b][:, 3:4], st5[b][:, 4:5], ALU.add, [st5[b], junk], [st5[b]])
                    P.rstd(st5[b], 1.0 / D, EPS)
                    for half in range(2):
                        hs = slice(half * 512, (half + 1) * 512)
                        P.stt("dve", tmp[b][:, hs], pb[half][:, :], st5[b][:, 2:3], G1[:, hs], ALU.mult, ALU.mult,
                              [pb[half], st5[b], G1], [tmp[b]])
                    P.tt("pool", xw[b][:], tmp[b][:], xt[b][:], ALU.add, [tmp[b], xt[b]], [xw[b]])
                    P.dma("pool", xs_d[i * 128:(i + 1) * 128, :], xw[b][:], [xw[b]], [xs_res[i]])
                    rms_tile(P, xw[b], st5[b], junk, A2, B2, h2[b])

                def p5_b(i):
                    b = i % 2
                    transpose8(P, h2[b], h2T[b], psum[6:8])
                    P.dma("sp", h2T_d[:, :, i * 128:(i + 1) * 128], h2T[b][:], [h2T[b]], [h2_res[i]])

                for k in range(len(tiles5) + 1):
                    if k < len(tiles5):
                        p5_a(tiles5[k])
                    if k >= 1:
                        p5_b(tiles5[k - 1])

            if "skip_p6" not in dbg:
              with P.phase():
                wst = [P.sb("wst%d" % i, [128, 1024]) for i in range(2)]
                wup_sb = P.sb("wup", [128, 8, 2 * DFF], BF16)
                wdn_sb = P.sb("wdn", [128, 22, D], BF16)
                cwt = P.sb("cwt", [128, 3, 44])
                cbt = P.sb("cbt", [128, 44])
                gB = P.sb("gB", [128, D])
                G2s = [P.sb("G2_%d" % i, [128, D]) for i in range(2)]
                pending_down = []
                h2blk = [P.sb("h2blk%d" % i, [128, 8, 258], BF16) for i in range(2)]
                cv = [[P.sb("cv%d_%d" % (i, j), [128, 256]) for j in range(2)] for i in range(2)]
                aTs = [P.sb("aT%d" % i, [128, 22, 256], BF16) for i in range(2)]
                xt = [P.sb("xt%d" % i, [128, D]) for i in range(2)]
                tmp = [P.sb("tmp%d" % i, [128, D]) for i in range(2)]
                st6 = [P.sb("st6_%d" % i, [128, 8]) for i in range(2)]
                junk = P.sb("junk", [128, 512])
                blocks = list(range(17)) if need_ctx else list(range(1, 17))

                def load_hb(bi):
                    t0 = bi * 256
                    hb = h2blk[bi % 2]
                    lval = t0 not in (0, TCTX)
                    rval = (t0 + 256) not in (TCTX, TALL)
                    if not lval:
                        P.memset("pool", hb[:, :, 0:1], 0.0, [hb])
                    if not rval:
                        P.memset("pool", hb[:, :, 257:258], 0.0, [hb])
                    a = t0 - 1 if lval else t0
                    e = t0 + 257 if rval else t0 + 256
                    rtiles = sorted(set([a // 128, (e - 1) // 128, t0 // 128, t0 // 128 + 1]))
                    P.dma("sp", hb[:, :, a - (t0 - 1):e - (t0 - 1)], h2T_d[:, :, a:e], [h2_res[r] for r in rtiles], [hb])

                load_hb(blocks[0])
                wup_res = [Res("wup%d" % c) for c in range(6)]
                wdn_res = [Res("wdn%d" % j) for j in range(22)]
                idx = 0
                for c in (0, 2, 3, 1, 4, 5):
                    c0 = c * 1024
                    cw = min(1024, 2 * DFF - c0)
                    for kt in range(8):
                        load_cast(P, wup_sb[:, kt, c0:c0 + cw], wup_res[c], dram["ffn_w_up"][L, kt * 128:(kt + 1) * 128, c0:c0 + cw], wst, idx, cw)
                        idx += 1
                for j in range(22):
                    load_cast(P, wdn_sb[:, j, :], wdn_res[j], dram["ffn_w_down"][L, j * 128:(j + 1) * 128, :], wst, idx, D)
                    idx += 1
                for tap in range(3):
                    P.dma("sp", cwt[:, tap, :], dram["ffn_conv_w"][L, tap].rearrange("(j p) -> p j", p=128), (), [cwt],
                          allow_slow_non_contiguous=True)
                P.dma("sp", cbt[:, :], dram["ffn_conv_b"][L, 0].rearrange("(j p) -> p j", p=128), (), [cbt],
                      allow_slow_non_contiguous=True)
                blocks = list(range(17)) if need_ctx else list(range(1, 17))
                ucount = 0
                for bi in blocks:
                    v = 1 if bi == 0 else 0
                    if bi == blocks[0] or bi == 1:
                        bcast_load("sp", gB, dram["g_ffn_post"][L])
                        bcast_load("sp", G2s[v], mrow(v, 5), [mods_res[L]])
                        P.tt("dve", G2s[v][:], G2s[v][:], gB[:], ALU.mult, [G2s[v], gB], [G2s[v]])
                    t0 = bi * 256
                    aT = aTs[bi % 2]
                    hb = h2blk[bi % 2]
                    nxt = blocks.index(bi) + 1
                    if nxt < len(blocks):
                        load_hb(blocks[nxt])
                    for j in range(22):
                        cs = cv[j % 2]
                        for part in range(2):
                            fi = part * 22 + j
                            f0 = part * DFF + j * 128
                            pst = psum[ucount % 4]
                            ucount += 1
                            for kt in range(8):
                                P.mm(pst[:, 0:258], wup_sb[:, kt, f0:f0 + 128], hb[:, kt, :], kt == 0, kt == 7,
                                     [wup_res[f0 // 1024], wup_res[(f0 + 127) // 1024], hb], [pst])
                            c = cs[part]
                            P.act("act", c[:], pst[:, 1:257], AF.Identity, [pst, cwt, cbt], [c], bias=cbt[:, fi:fi + 1], scale=cwt[:, 1, fi:fi + 1])
                            P.stt("dve", c[:], pst[:, 0:256], cwt[:, 0, fi:fi + 1], c[:], ALU.mult, ALU.add, [pst, cwt, c], [c])
                            P.stt("dve", c[:], pst[:, 2:258], cwt[:, 2, fi:fi + 1], c[:], ALU.mult, ALU.add, [pst, cwt, c], [c])
                        P.act("act", cs[1][:], cs[1][:], AF.Silu, [cs[1]], [cs[1]])
                        P.tt("pool", aT[:, j, :], cs[0][:], cs[1][:], ALU.mult, [cs[0], cs[1]], [aT])
                        for _ in range(5):
                            if pending_down:
                                pending_down.pop(0)()
                    def make_down(bi, aT, v):
                        items = []
                        for s in range(2):
                            i = bi * 2 + s
                            b = s
                            pb = [psum[4 + 2 * s], psum[5 + 2 * s]]
                            items.append(lambda i=i, b=b: P.dma("pool", xt[b][:], xs_d[i * 128:(i + 1) * 128, :], [xs_res[i]], [xt[b]]))
                            for half in range(2):
                                for j in range(22):
                                    items.append(lambda s=s, j=j, half=half, pb=pb: P.mm(
                                        pb[half][:, :], aT[:, j, s * 128:(s + 1) * 128], wdn_sb[:, j, half * 512:(half + 1) * 512],
                                        j == 0, j == 21, [aT, wdn_res[j]], [pb[half]]))
                                items.append(lambda b=b, half=half, pb=pb: P.act(
                                    "act", junk[:, 0:512], pb[half][:, :], AF.Square, [pb[half]], [junk], accum_out=st6[b][:, 3 + half:4 + half]))

                            def epi(i=i, b=b, pb=pb):
                                P.tt("dve", st6[b][:, 0:1], st6[b][:, 3:4], st6[b][:, 4:5], ALU.add, [st6[b], junk], [st6[b]])
                                P.rstd(st6[b], 1.0 / D, EPS)
                                for half in range(2):
                                    hs = slice(half * 512, (half + 1) * 512)
                                    P.stt("dve", tmp[b][:, hs], pb[half][:, :], st6[b][:, 2:3], G2s[v][:, hs], ALU.mult, ALU.mult,
                                          [pb[half], st6[b], G2s[v]], [tmp[b]])
                                P.tt("pool", tmp[b][:], tmp[b][:], xt[b][:], ALU.add, [tmp[b], xt[b]], [tmp[b]])
                                if last and i >= 2:
                                    P.dma("sp", out_ap[(i - 2) * 128:(i - 1) * 128, :], tmp[b][:], [tmp[b]], ())
                                else:
                                    P.dma("sp", xs_d[i * 128:(i + 1) * 128, :], tmp[b][:], [tmp[b]], [xs_res[i]])
                            items.append(epi)
                        return items
                    pending_down.extend(make_down(bi, aT, v))
                while pending_down:
                    pending_down.pop(0)()

        P.flush(final=True)
    return nc


def host_inputs(inputs, b):
    m = {}
    m["xall"] = np.ascontiguousarray(np.concatenate([inputs["ctx"][b], inputs["x"][b]], axis=0), dtype=np.float32)
    m["c2"] = np.ascontiguousarray(np.stack([inputs["c"][b], inputs["c_ctx"]], axis=0), dtype=np.float32)
    m["ident"] = np.eye(128, dtype=np.float32)
    for name, shp in W_SPECS:
        m[name] = np.ascontiguousarray(np.asarray(inputs[name], dtype=np.float32).reshape([NL_FULL] + list(shp)))
    mix_host_inputs(inputs, m)
    return m


def used_inputs(nc, m):
    shapes = {}
    for alloc in nc.allocations:
        try:
            if alloc.kind == "ExternalInput":
                shapes[alloc.memorylocations[0].name] = tuple(alloc.tensor_shape)
        except Exception:
            pass
    out = {}
    for k, v in m.items():
        if k in shapes:
            shp = shapes[k]
            if tuple(v.shape) != shp:
                v = np.ascontiguousarray(v[0:shp[0]])
            assert tuple(v.shape) == shp, (k, v.shape, shp)
            out[k] = v
    return out


def kernel(**inputs):
    inputs = {k: np.asarray(v) for k, v in inputs.items()}
    nc = build_program()
    shared = host_inputs(inputs, 0)
    maps = []
    for b in range(4):
        m = dict(shared)
        m["xall"] = np.ascontiguousarray(np.concatenate([inputs["ctx"][b], inputs["x"][b]], axis=0), dtype=np.float32)
        m["c2"] = np.ascontiguousarray(np.stack([inputs["c"][b], inputs["c_ctx"]], axis=0), dtype=np.float32)
        maps.append(used_inputs(nc, m))
    in_maps = [maps[b % 4] for b in range(N_CORES)]
    res = run_bass_kernel_spmd(nc, in_maps, core_ids=list(range(N_CORES)))
    out = np.stack([np.asarray(res.results[b]["out"]) for b in range(4)], axis=0)
    return out.astype(np.float32)
```
